# Optimizing a Trainium2 kernel written in Bass

```python
import jax, jax.numpy as jnp
from jax import lax
import numpy as np

D_MODEL = 1024
BATCH = 4
SEQ = 8192
DEPTH = 2

GLA_HEADS = 4
GLA_DK = D_MODEL // (2 * GLA_HEADS)
GLA_DV = D_MODEL // GLA_HEADS
GLA_RANK = 16
GLA_TAU = 16.0
GLA_CHUNK = 64
MLSTM_HEADS = 4
MLSTM_INNER = D_MODEL
MLSTM_DH = MLSTM_INNER // MLSTM_HEADS
MLSTM_CONV = 4
MLSTM_BLOCK = 4
MLSTM_NBLK = MLSTM_INNER // MLSTM_BLOCK
MLSTM_CHUNK = 64
FOX_HEADS = 8
FOX_DH = D_MODEL // FOX_HEADS
FOX_QBLOCK = 128
N_BRANCH = 3
D_FF = ((8 * D_MODEL // 3 + 255) // 256) * 256
FFN_CONV = 3
EPS = 1e-6

IN_SPLITS = (
    GLA_HEADS * GLA_DK, GLA_HEADS * GLA_DK, GLA_HEADS * GLA_DV, GLA_RANK, GLA_HEADS * GLA_DV,
    MLSTM_INNER, MLSTM_INNER,
    FOX_HEADS * FOX_DH, FOX_HEADS * FOX_DH, FOX_HEADS * FOX_DH, FOX_HEADS, FOX_HEADS * FOX_DH,
    N_BRANCH * D_MODEL,
)
N_IN = sum(IN_SPLITS)

kernel_name = 'hybrid_gla_mlstm_fox_convffn'


def rmsnorm(x, g):
    xf = x.astype(jnp.float32)
    y = xf * lax.rsqrt(jnp.mean(xf * xf, axis=-1, keepdims=True) + EPS)
    return (y * g.astype(jnp.float32)).astype(x.dtype)


def head_rmsnorm(x, g, n_heads):
    b, t, w = x.shape
    xh = x.astype(jnp.float32).reshape(b, t, n_heads, w // n_heads)
    y = xh * lax.rsqrt(jnp.mean(xh * xh, axis=-1, keepdims=True) + EPS)
    return y.reshape(b, t, w) * g.astype(jnp.float32)


def causal_dwconv(x, w, b):
    width = w.shape[0]
    t = x.shape[1]
    xp = jnp.pad(x, ((0, 0), (width - 1, 0), (0, 0)))
    y = b
    for j in range(width):
        y = y + w[j] * xp[:, j:j + t]
    return y


def to_chunks(t, n_heads, chunk):
    b, s, w = t.shape
    return t.reshape(b, s // chunk, chunk, n_heads, w // n_heads).transpose(1, 0, 3, 2, 4).astype(jnp.float32)


def from_chunks(t):
    n, b, h, l, d = t.shape
    return t.transpose(1, 0, 3, 2, 4).reshape(b, n * l, h * d)


def gla_branch(q, k, v, lr, r, w_lr, b_lr, g_norm):
    bsz = q.shape[0]
    log_a = jax.nn.log_sigmoid((lr @ w_lr + b_lr).astype(jnp.float32)) / GLA_TAU
    qc = to_chunks(q, GLA_HEADS, GLA_CHUNK) * GLA_DK ** -0.5
    kc = to_chunks(k, GLA_HEADS, GLA_CHUNK)
    vc = to_chunks(v, GLA_HEADS, GLA_CHUNK)
    bc = jnp.cumsum(to_chunks(log_a, GLA_HEADS, GLA_CHUNK), axis=3)
    b_last = bc[:, :, :, -1:, :]
    q_in = qc * jnp.exp(bc)
    k_in = kc * jnp.exp(-bc)
    k_st = kc * jnp.exp(b_last - bc)
    decay = jnp.exp(b_last[:, :, :, 0, :])
    causal = jnp.tril(jnp.ones((GLA_CHUNK, GLA_CHUNK), dtype=bool))
    att = jnp.where(causal, jnp.einsum('nbhtd,nbhsd->nbhts', q_in, k_in), 0.0)
    o_intra = jnp.einsum('nbhts,nbhsv->nbhtv', att, vc)

    def step(state, xs):
        q_c, k_c, v_c, d_c = xs
        o = jnp.einsum('bhtd,bhdv->bhtv', q_c, state)
        state = d_c[..., None] * state + jnp.einsum('bhsd,bhsv->bhdv', k_c, v_c)
        return state, o

    s0 = jnp.zeros((bsz, GLA_HEADS, GLA_DK, GLA_DV), jnp.float32)
    _, o_inter = lax.scan(step, s0, (q_in, k_st, vc, decay))
    o = from_chunks(o_intra + o_inter)
    return (head_rmsnorm(o, g_norm, GLA_HEADS) * jax.nn.silu(r.astype(jnp.float32))).astype(q.dtype)


def mlstm_cell(q, k, v, log_i, log_f):
    bsz = q.shape[0]
    qc = to_chunks(q, MLSTM_HEADS, MLSTM_CHUNK) * MLSTM_DH ** -0.5
    kc = to_chunks(k, MLSTM_HEADS, MLSTM_CHUNK)
    vc = to_chunks(v, MLSTM_HEADS, MLSTM_CHUNK)
    ic = to_chunks(log_i, MLSTM_HEADS, MLSTM_CHUNK)[..., 0]
    fc = to_chunks(log_f, MLSTM_HEADS, MLSTM_CHUNK)[..., 0]
    causal = jnp.tril(jnp.ones((MLSTM_CHUNK, MLSTM_CHUNK), dtype=bool))

    def step(carry, xs):
        c_st, n_st, m_st = carry
        q_c, k_c, v_c, i_c, f_c = xs
        b = jnp.cumsum(f_c, axis=-1)
        d_log = jnp.where(causal, b[..., :, None] - b[..., None, :] + i_c[..., None, :], -jnp.inf)
        m_inter = b + m_st[..., None]
        m_t = jnp.maximum(m_inter, jnp.max(d_log, axis=-1))
        w_intra = jnp.exp(d_log - m_t[..., None])
        w_inter = jnp.exp(m_inter - m_t)
        s = jnp.einsum('bhtd,bhsd->bhts', q_c, k_c) * w_intra
        num = jnp.einsum('bhts,bhsv->bhtv', s, v_c) + w_inter[..., None] * jnp.einsum('bhtd,bhdv->bhtv', q_c, c_st)
        qn = jnp.sum(s, axis=-1) + w_inter * jnp.einsum('bhtd,bhd->bht', q_c, n_st)
        h = num / jnp.maximum(jnp.abs(qn), jnp.exp(-m_t))[..., None]
        g = b[..., -1]
        a = g[..., None] - b + i_c
        m_new = jnp.maximum(g + m_st, jnp.max(a, axis=-1))
        wa = jnp.exp(a - m_new[..., None])
        dec = jnp.exp(g + m_st - m_new)
        c_st = dec[..., None, None] * c_st + jnp.einsum('bhs,bhsd,bhsv->bhdv', wa, k_c, v_c)
        n_st = dec[..., None] * n_st + jnp.einsum('bhs,bhsd->bhd', wa, k_c)
        return (c_st, n_st, m_new), h

    c0 = jnp.zeros((bsz, MLSTM_HEADS, MLSTM_DH, MLSTM_DH), jnp.float32)
    n0 = jnp.zeros((bsz, MLSTM_HEADS, MLSTM_DH), jnp.float32)
    m0 = jnp.zeros((bsz, MLSTM_HEADS), jnp.float32)
    _, h = lax.scan(step, (c0, n0, m0), (qc, kc, vc, ic, fc))
    return from_chunks(h)


def mlstm_branch(xm, z, conv_w, conv_b, wq, wk, wv, w_i, b_i, w_f, b_f, skip, g_norm):
    bsz, seq, _ = xm.shape
    xc = jax.nn.silu(causal_dwconv(xm, conv_w, conv_b))

    def headwise(t, w):
        tb = t.reshape(bsz, seq, MLSTM_NBLK, MLSTM_BLOCK)
        return jnp.einsum('btnc,ncd->btnd', tb, w).reshape(bsz, seq, MLSTM_INNER)

    q = headwise(xc, wq)
    k = headwise(xc, wk)
    v = headwise(xm, wv)
    qkv = jnp.concatenate([q, k, v], axis=-1)
    log_i = (qkv @ w_i + b_i).astype(jnp.float32)
    log_f = jax.nn.log_sigmoid((qkv @ w_f + b_f).astype(jnp.float32))
    h = head_rmsnorm(mlstm_cell(q, k, v, log_i, log_f), g_norm, MLSTM_HEADS)
    h = h + skip.astype(jnp.float32) * xc.astype(jnp.float32)
    return (h * jax.nn.silu(z.astype(jnp.float32))).astype(xm.dtype)


def fox_branch(q, k, v, fz, og, b_f):
    bsz, seq, _ = q.shape
    n_blk = seq // FOX_QBLOCK

    def heads(t):
        return t.reshape(bsz, seq, FOX_HEADS, FOX_DH).transpose(0, 2, 1, 3).astype(jnp.float32)

    qh = heads(q) * FOX_DH ** -0.5
    kh = heads(k)
    vh = heads(v)
    cum_f = jnp.cumsum(jax.nn.log_sigmoid(fz.astype(jnp.float32) + b_f), axis=1).transpose(0, 2, 1)
    q_blocks = qh.reshape(bsz, FOX_HEADS, n_blk, FOX_QBLOCK, FOX_DH).transpose(2, 0, 1, 3, 4)
    f_blocks = cum_f.reshape(bsz, FOX_HEADS, n_blk, FOX_QBLOCK).transpose(2, 0, 1, 3)
    key_pos = jnp.arange(seq)

    def attend(args):
        q_blk, f_blk, blk = args
        logits = jnp.einsum('bhqd,bhkd->bhqk', q_blk, kh) + f_blk[..., :, None] - cum_f[:, :, None, :]
        query_pos = blk * FOX_QBLOCK + jnp.arange(FOX_QBLOCK)
        logits = jnp.where(key_pos[None, :] <= query_pos[:, None], logits, -jnp.inf)
        return jnp.einsum('bhqk,bhkd->bhqd', jax.nn.softmax(logits, axis=-1), vh)

    o = lax.map(attend, (q_blocks, f_blocks, jnp.arange(n_blk)))
    o = o.transpose(1, 0, 3, 2, 4).reshape(bsz, seq, FOX_HEADS * FOX_DH)
    return (o * jax.nn.sigmoid(og.astype(jnp.float32))).astype(q.dtype)


def token_mixer(h, w_in, b_gate, gla_w_lr, gla_b_lr, gla_norm, mlstm_conv_w, mlstm_conv_b,
                mlstm_wq, mlstm_wk, mlstm_wv, mlstm_w_i, mlstm_b_i, mlstm_w_f, mlstm_b_f,
                mlstm_skip, mlstm_norm, fox_b_f, w_branch, w_out):
    bsz, seq, _ = h.shape
    proj = h @ w_in
    split_idx = [int(s) for s in np.cumsum(IN_SPLITS)[:-1]]
    (gq, gk, gv, glr, gr, mx, mz, fq, fk, fv, ff, fog, gates) = jnp.split(proj, split_idx, axis=-1)
    y_gla = gla_branch(gq, gk, gv, glr, gr, gla_w_lr, gla_b_lr, gla_norm)
    y_mlstm = mlstm_branch(mx, mz, mlstm_conv_w, mlstm_conv_b, mlstm_wq, mlstm_wk, mlstm_wv,
                           mlstm_w_i, mlstm_b_i, mlstm_w_f, mlstm_b_f, mlstm_skip, mlstm_norm)
    y_fox = fox_branch(fq, fk, fv, ff, fog, fox_b_f)
    ys = jnp.stack([y_gla, y_mlstm, y_fox], axis=2)
    branch = jnp.einsum('btnw,nwd->btnd', ys, w_branch)
    g = jax.nn.sigmoid(gates.reshape(bsz, seq, N_BRANCH, D_MODEL) + b_gate)
    merged = jnp.sum(branch * g, axis=2)
    return merged @ w_out


def conv_ffn(h, w_up, conv_w, conv_b, w_down):
    u = causal_dwconv(h @ w_up, conv_w, conv_b)
    a, g = jnp.split(u, 2, axis=-1)
    return (jax.nn.silu(g) * a) @ w_down


def setup_inputs(seed: int = 0) -> dict:
    key = jax.random.key(seed)
    ks = iter(jax.random.split(key, 32))

    def nrm(shape, scale):
        return jax.random.normal(next(ks), shape, jnp.float32) * scale

    def gain(shape):
        return 1.0 + nrm(shape, 0.02)

    L = DEPTH
    return {
        'x': nrm((BATCH, SEQ, D_MODEL), 1.0),
        'norm_mix': gain((L, D_MODEL)),
        'w_in': nrm((L, D_MODEL, N_IN), D_MODEL ** -0.5),
        'b_gate': nrm((L, N_BRANCH, D_MODEL), 0.1),
        'gla_w_lr': nrm((L, GLA_RANK, GLA_HEADS * GLA_DK), GLA_RANK ** -0.5),
        'gla_b_lr': nrm((L, GLA_HEADS * GLA_DK), 0.1),
        'gla_norm': gain((L, GLA_HEADS * GLA_DV)),
        'mlstm_conv_w': nrm((L, MLSTM_CONV, MLSTM_INNER), MLSTM_CONV ** -0.5),
        'mlstm_conv_b': nrm((L, MLSTM_INNER), 0.02),
        'mlstm_wq': nrm((L, MLSTM_NBLK, MLSTM_BLOCK, MLSTM_BLOCK), MLSTM_BLOCK ** -0.5),
        'mlstm_wk': nrm((L, MLSTM_NBLK, MLSTM_BLOCK, MLSTM_BLOCK), MLSTM_BLOCK ** -0.5),
        'mlstm_wv': nrm((L, MLSTM_NBLK, MLSTM_BLOCK, MLSTM_BLOCK), MLSTM_BLOCK ** -0.5),
        'mlstm_w_i': nrm((L, 3 * MLSTM_INNER, MLSTM_HEADS), (3 * MLSTM_INNER) ** -0.5),
        'mlstm_b_i': nrm((L, MLSTM_HEADS), 0.1),
        'mlstm_w_f': nrm((L, 3 * MLSTM_INNER, MLSTM_HEADS), (3 * MLSTM_INNER) ** -0.5),
        'mlstm_b_f': jnp.linspace(3.0, 6.0, MLSTM_HEADS)[None, :] + nrm((L, MLSTM_HEADS), 0.1),
        'mlstm_skip': gain((L, MLSTM_INNER)),
        'mlstm_norm': gain((L, MLSTM_INNER)),
        'fox_b_f': 2.0 + nrm((L, FOX_HEADS), 0.1),
        'w_branch': nrm((L, N_BRANCH, D_MODEL, D_MODEL), D_MODEL ** -0.5),
        'w_out': nrm((L, D_MODEL, D_MODEL), D_MODEL ** -0.5),
        'norm_ffn': gain((L, D_MODEL)),
        'ffn_w_up': nrm((L, D_MODEL, 2 * D_FF), D_MODEL ** -0.5),
        'ffn_conv_w': nrm((L, FFN_CONV, 2 * D_FF), FFN_CONV ** -0.5),
        'ffn_conv_b': nrm((L, 2 * D_FF), 0.02),
        'ffn_w_down': nrm((L, D_FF, D_MODEL), D_FF ** -0.5),
        'norm_final': gain((D_MODEL,)),
    }


def reference(x, norm_mix, w_in, b_gate, gla_w_lr, gla_b_lr, gla_norm, mlstm_conv_w, mlstm_conv_b,
              mlstm_wq, mlstm_wk, mlstm_wv, mlstm_w_i, mlstm_b_i, mlstm_w_f, mlstm_b_f, mlstm_skip,
              mlstm_norm, fox_b_f, w_branch, w_out, norm_ffn, ffn_w_up, ffn_conv_w, ffn_conv_b,
              ffn_w_down, norm_final):
    for l in range(DEPTH):
        h = rmsnorm(x, norm_mix[l])
        x = x + token_mixer(h, w_in[l], b_gate[l], gla_w_lr[l], gla_b_lr[l], gla_norm[l],
                            mlstm_conv_w[l], mlstm_conv_b[l], mlstm_wq[l], mlstm_wk[l], mlstm_wv[l],
                            mlstm_w_i[l], mlstm_b_i[l], mlstm_w_f[l], mlstm_b_f[l], mlstm_skip[l],
                            mlstm_norm[l], fox_b_f[l], w_branch[l], w_out[l])
        h = rmsnorm(x, norm_ffn[l])
        x = x + conv_ffn(h, ffn_w_up[l], ffn_conv_w[l], ffn_conv_b[l], ffn_w_down[l])
    return rmsnorm(x, norm_final)
```

```python
import numpy as np
from contextlib import ExitStack
import concourse.bass as bass
import concourse.mybir as mybir
from concourse.bass_utils import run_bass_kernel_spmd

F32 = mybir.dt.float32
BF16 = mybir.dt.bfloat16
AF = mybir.ActivationFunctionType
ALU = mybir.AluOpType
AX = mybir.AxisListType

COMPUTE = ("pe", "act", "dve", "pool")
EPOCH = 4096
NEPS = 3
NDMASEM = 20
NCSEM = 56
SAME_ENGINE_SYNC = True


def _is_ps(t):
    t0 = t[0] if isinstance(t, tuple) else t
    return isinstance(t0, str) and (t0[:2] in ("ps", "pb", "po", "pu", "pg") or t0 == "ptr")


class Op:
    __slots__ = ("eng", "fn", "deps", "flag", "sem", "target", "dma", "idx", "pre", "cc")

    def __init__(self, eng, fn, dma):
        self.eng = eng
        self.fn = fn
        self.dma = dma
        self.deps = []
        self.flag = False
        self.sem = None
        self.target = 0
        self.idx = 0
        self.pre = None
        self.cc = False


class Sched:
    def __init__(self, nc, stack):
        self.nc = nc
        self.ops = {e: [] for e in COMPUTE + ("sp",)}
        self.lastw = {}
        self.ps_true_w = {}
        self.readers = {}
        self.nops = {e: 0 for e in COMPUTE + ("sp",)}
        self.flagcnt = {e: 0 for e in COMPUTE}
        self.esem = {e: [stack.enter_context(nc.semaphore(f"s_{e}{i}")) for i in range(NEPS)] for e in COMPUTE}
        self.dsem = [stack.enter_context(nc.semaphore(f"s_dma{i}")) for i in range(NDMASEM)]
        self.ndma = 0
        self.csem = [stack.enter_context(nc.semaphore(f"s_cc{i}")) for i in range(NCSEM)]
        self.ncoll = 0
        self.dma_hist = []
        self.barrier_pending = {}
        self.last_op = {}
        self.waited = {e: {} for e in COMPUTE + ("sp",)}
        self.all_dma = []
        self.dma_barriered = 0

    def op(self, eng, fn, r=(), w=(), dma=False):
        ps_r = [t for t in r if _is_ps(t)]
        if ps_r:
            w = list(w) + [t for t in ps_r if t not in w]
        o = Op(eng, fn, dma)
        o.idx = self.nops[eng]
        self.nops[eng] += 1
        deps = {}

        def add(d, hazard):
            if d is None or d is o:
                return
            if d.dma:
                deps[id(d)] = d
            else:
                if d.eng == eng and not dma and (eng == "pe" or not SAME_ENGINE_SYNC or not hazard):
                    return
                k = ("e", d.eng)
                if k not in deps or deps[k].idx < d.idx:
                    deps[k] = d

        for t in r:
            if t in ps_r:
                add(self.ps_true_w.get(t), True)
            else:
                add(self.lastw.get(t), True)
        for t in w:
            hz = t not in ps_r
            add(self.lastw.get(t), hz)
            rd = self.readers.get(t)
            if rd:
                for d in rd.values():
                    add(d, hz)
            if hz and _is_ps(t):
                self.ps_true_w[t] = o
        bp = self.barrier_pending.pop(eng, None)
        if bp:
            for d in bp:
                add(d, True)
        for t in r:
            rd = self.readers.setdefault(t, {})
            if dma:
                rd[id(o)] = o
            else:
                rd[eng] = o
        for t in w:
            self.lastw[t] = o
            self.readers[t] = {}
        o.deps = list(deps.values())
        for d in o.deps:
            d.flag = True
        if dma:
            o.flag = True
            self.all_dma.append(o)
        self.ops[eng].append(o)
        self.last_op[eng] = o
        return o

    def pe(self, fn, r=(), w=()):
        return self.op("pe", fn, r, w)

    def act(self, fn, r=(), w=()):
        return self.op("act", fn, r, w)

    def dve(self, fn, r=(), w=()):
        return self.op("dve", fn, r, w)

    def pool(self, fn, r=(), w=()):
        return self.op("pool", fn, r, w)

    def dma(self, out, in_, r=(), w=(), eng="sp"):
        return self.op(eng, lambda e: e.dma_start(out=out, in_=in_), r, w, dma=True)

    def collective(self, kind, ins, outs, groups, r=(), w=()):
        o = self.op("pool", lambda e: e.collective_compute(kind, ALU.bypass, replica_groups=groups, ins=ins, outs=outs), r, w, dma=True)
        o.sem = self.csem[self.ncoll]
        self.ncoll += 1
        o.target = 1
        o.cc = True
        return o

    def barrier(self):
        b = [o for o in self.last_op.values() if not o.dma]
        b += [o for o in self.all_dma[self.dma_barriered:] if not o.cc]
        self.dma_barriered = len(self.all_dma)
        for o in b:
            o.flag = True
        self.barrier_pending = {e: b for e in COMPUTE + ("sp",)}

    def emit(self):
        nc = self.nc
        for e in COMPUTE:
            for o in self.ops[e]:
                if o.flag and o.sem is None:
                    c = self.flagcnt[e]
                    self.flagcnt[e] += 1
                    ep = c // EPOCH
                    o.sem = self.esem[e][ep % NEPS]
                    o.target = (ep // NEPS) * EPOCH + (c % EPOCH) + 1
        for e in COMPUTE + ("sp",):
            for o in self.ops[e]:
                if o.dma and o.sem is None:
                    n = self.ndma
                    self.ndma += 1
                    o.sem = self.dsem[n % NDMASEM]
                    o.target = 16 * (n // NDMASEM + 1)
                    if n >= NDMASEM:
                        o.pre = (o.sem, 16 * (n // NDMASEM))
        with nc.Block() as block:
            @block.tensor
            def _(eng):
                self._emit_eng("pe", eng)

            @block.scalar
            def _(eng):
                self._emit_eng("act", eng)

            @block.vector
            def _(eng):
                self._emit_eng("dve", eng)

            @block.gpsimd
            def _(eng):
                self._emit_eng("pool", eng)

            @block.sync
            def _(eng):
                self._emit_eng("sp", eng)
        for e in self.ops:
            self.ops[e] = []

    def _emit_eng(self, name, eng):
        waited = self.waited[name]
        for o in self.ops[name]:
            for d in o.deps:
                key = id(d.sem)
                if waited.get(key, 0) < d.target:
                    eng.wait_ge(d.sem, d.target)
                    waited[key] = d.target
            if o.pre is not None:
                key = id(o.pre[0])
                if waited.get(key, 0) < o.pre[1]:
                    eng.wait_ge(o.pre[0], o.pre[1])
                    waited[key] = o.pre[1]
            inst = o.fn(eng)
            if o.flag:
                inst.then_inc(o.sem, 16 if (o.dma and not o.cc) else 1)

    def finish(self, eng="sp"):
        self.barrier()
        self.op(eng, lambda e: e.nop(), r=(), w=())


class Tiles:
    CNT = [0]

    def __init__(self, nc, stack):
        self.nc = nc
        self.stack = stack

    def sb(self, shape, dtype, name=None):
        Tiles.CNT[0] += 1
        return self.stack.enter_context(self.nc.sbuf_tensor(f"{name or 't'}_{Tiles.CNT[0]}", list(shape), dtype))

    def ps(self, shape, dtype=F32, name=None):
        Tiles.CNT[0] += 1
        return self.stack.enter_context(self.nc.psum_tensor(f"{name or 'p'}_{Tiles.CNT[0]}", list(shape), dtype))


D = 1024
NFM = 3088
NTM = 2308
NA = NFM + NTM
EPS = 1e-6
FM_GQ, FM_GK, FM_MX, FM_FQ, FM_FK, FM_OG, FM_LR = 0, 256, 512, 1536, 2048, 2560, 3072
TM_GK, TM_GV, TM_GR, TM_MZ, TM_FV, TM_FF = 0, 256, 768, 1280, 1792, 2304


def cdiv(a, b):
    return (a + b - 1) // b


def rms_block(S, T_, ones_bf, xt, gcol, hT, ps_ss, tag, nb):
    xsq, std, rstd = T_["xsq"], T_["std"], T_["rstd"]
    S.act(lambda e: e.activation(out=xsq[:, :, :nb], in_=xt[:, :, :nb], func=AF.Square), r=[tag], w=["xsq"])
    for c in range(8):
        S.pe(lambda e, c=c: e.matmul(ps_ss[:, :nb], lhsT=ones_bf[:, :], rhs=xsq[:, c, :nb], start=(c == 0), stop=(c == 7)),
             r=["xsq", "ones"], w=["ps_ss"])
    S.act(lambda e: e.activation(out=std[:, :nb], in_=ps_ss[:, :nb], func=AF.Sqrt, bias=T_["epsc"][:, 0:1], scale=1.0 / D),
          r=["ps_ss", "epsc"], w=["std"])
    S.dve(lambda e: e.reciprocal(out=rstd[:, :nb], in_=std[:, :nb]), r=["std"], w=["rstd"])
    for c in range(8):
        S.dve(lambda e, c=c: e.scalar_tensor_tensor(out=hT[:, c, :nb], in0=xt[:, c, :nb], scalar=gcol[:, c:c + 1],
                                                     in1=rstd[:, :nb], op0=ALU.mult, op1=ALU.mult),
              r=[tag, "rstd", "gcol"], w=[("hT", id(hT))])


def load_weights_bf16(S, T_, wsb, wdram, ncols, wtag, stg, stgtag):
    wv = wdram.rearrange("(c p) n -> p c n", p=128)
    G = 512
    for gi in range(cdiv(ncols, G)):
        c0 = gi * G
        n = min(G, ncols - c0)
        st = stg[gi % 2]
        tg = (stgtag, gi % 2)
        S.dma(st[:, :, :n], wv[:, :, c0:c0 + n], r=[], w=[tg])
        eng = [S.pool, S.dve][gi % 2]
        eng(lambda e, st=st, c0=c0, n=n: e.tensor_copy(out=wsb[:, :, c0:c0 + n], in_=st[:, :, :n]), r=[tg], w=[wtag])


def phase_A(nc, S, T, xT, wA, gmix, pFM, pTM, xblk=None, xr=()):
    NB = 512
    with ExitStack() as st:
        A = Tiles(nc, st)
        T_ = {}
        wsb = A.sb([128, 8, NA], BF16, "wsb")
        stg = [A.sb([128, 8, 512], F32, "wstg") for _ in range(2)]
        xb = [A.sb([128, 8, NB], F32, "xb") for _ in range(2)]
        hTs = [A.sb([128, 8, NB], BF16, "hT") for _ in range(2)]
        T_["xsq"] = A.sb([128, 8, NB], BF16, "xsq")
        T_["std"] = A.sb([128, NB], F32, "std")
        T_["rstd"] = A.sb([128, NB], F32, "rstd")
        T_["epsc"] = A.sb([128, 1], F32, "epsc")
        ones_bf = A.sb([128, 128], BF16, "ones")
        gcol = A.sb([128, 8], F32, "gcol")
        fmst = [A.sb([128, NB], BF16, "fmst") for _ in range(4)]
        tmst = [A.sb([128, NTM], BF16, "tmst") for _ in range(2)]
        ps_ss = A.ps([128, NB], F32, "ps_ss")
        psr = [A.ps([128, NB], F32, "psr") for _ in range(4)]
        S.barrier()
        S.pool(lambda e: e.memset(ones_bf[:, :], 1.0), w=["ones"])
        S.pool(lambda e: e.memset(T_["epsc"][:, :], EPS), w=["epsc"])
        S.dma(gcol[:, :], gmix, w=["gcol"])
        load_weights_bf16(S, T_, wsb, wA, NA, "wsb", stg, "wstg")
        if xblk is None:
            xv = xT.rearrange("(c p) t -> p c t", p=128)
            xblk = lambda j: xv[:, :, j * NB:(j + 1) * NB]
        k = 0
        for j in range(T // NB):
            xt = xb[j % 2]
            xtag = ("xb", j % 2)
            hT = hTs[j % 2]
            htag = ("hT", id(hT))
            S.dma(xt[:, :, :], xblk(j), r=list(xr), w=[xtag])
            rms_block(S, T_, ones_bf, xt, gcol, hT, ps_ss, xtag, NB)
            for m in range(cdiv(NFM, 128)):
                mm = min(128, NFM - m * 128)
                ps = psr[k % 4]
                ptag = ("psr", k % 4)
                so = fmst[k % 4]
                stag = ("fmst", k % 4)
                for c in range(8):
                    S.pe(lambda e, c=c, ps=ps, m=m, mm=mm, hT=hT: e.matmul(ps[:mm, :], lhsT=wsb[:, c, m * 128:m * 128 + mm], rhs=hT[:, c, :],
                                                                        start=(c == 0), stop=(c == 7)), r=["wsb", htag], w=[ptag])
                if k % 2 == 0:
                    S.act(lambda e, ps=ps, so=so, mm=mm: e.copy(out=so[:mm, :], in_=ps[:mm, :]), r=[ptag], w=[stag])
                else:
                    S.dve(lambda e, ps=ps, so=so, mm=mm: e.tensor_copy(out=so[:mm, :], in_=ps[:mm, :]), r=[ptag], w=[stag])
                S.dma(pFM[m * 128:m * 128 + mm, j * NB:(j + 1) * NB], so[:mm, :], r=[stag], w=[("pFM", j)])
                k += 1
            for tt in range(NB // 128):
                ti = j * (NB // 128) + tt
                so = tmst[ti % 2]
                stag = ("tmst", ti % 2)
                for n in range(cdiv(NTM, 512)):
                    nn = min(512, NTM - n * 512)
                    ps = psr[k % 4]
                    ptag = ("psr", k % 4)
                    for c in range(8):
                        S.pe(lambda e, c=c, ps=ps, n=n, nn=nn, hT=hT, tt=tt: e.matmul(ps[:, :nn], lhsT=hT[:, c, tt * 128:(tt + 1) * 128],
                                                                                  rhs=wsb[:, c, NFM + n * 512:NFM + n * 512 + nn],
                                                                                  start=(c == 0), stop=(c == 7)), r=["wsb", htag], w=[ptag])
                    if k % 2 == 0:
                        S.act(lambda e, ps=ps, so=so, n=n, nn=nn: e.copy(out=so[:, n * 512:n * 512 + nn], in_=ps[:, :nn]), r=[ptag], w=[stag])
                    else:
                        S.dve(lambda e, ps=ps, so=so, n=n, nn=nn: e.tensor_copy(out=so[:, n * 512:n * 512 + nn], in_=ps[:, :nn]), r=[ptag], w=[stag])
                    k += 1
                S.dma(pTM[ti * 128:(ti + 1) * 128, :], so[:, :], r=[stag], w=[("pTM", ti)])
        S.emit()


class YDst:
    def __init__(self, T, yT=None, ys=None, H=0, key="y"):
        self.T, self.yT, self.ys, self.H, self.key = T, yT, ys, H, key
        self.tokens = []

    def put(self, S, row0, rows_ap_fn, j, tile_fn, rtags):
        NBK = self.T // 512
        half = NBK // 2
        tok = (self.key, row0, j)
        self.tokens.append(tok)
        if self.ys is None:
            S.dma(rows_ap_fn(self.yT, slice(512 * j, 512 * (j + 1))), tile_fn(slice(0, 512)), r=rtags, w=[tok])
            return
        H = self.H
        if j < half:
            S.dma(rows_ap_fn(self.ys[0], slice(H + 512 * j, H + 512 * (j + 1))), tile_fn(slice(0, 512)), r=rtags, w=[tok])
        else:
            S.dma(rows_ap_fn(self.ys[1], slice(H + 512 * (j - half), H + 512 * (j - half + 1))), tile_fn(slice(0, 512)), r=rtags, w=[tok])
        if j == half - 1:
            tok2 = (self.key, row0, "halo")
            self.tokens.append(tok2)
            S.dma(rows_ap_fn(self.ys[1], slice(0, H)), tile_fn(slice(512 - H, 512)), r=rtags, w=[tok2])

    def zero_halo(self, S, A):
        if self.ys is None:
            return
        z = A.sb([128, 12, self.H], BF16, "zhalo")
        S.dve(lambda e: e.memset(z[:, :, :], 0.0), w=["zhalo"])
        tok = (self.key, "zero")
        self.tokens.append(tok)
        S.dma(self.ys[0][:, 0:self.H].rearrange("(a p) t -> p a t", p=128), z[:, :, :], r=["zhalo"], w=[tok])


def make_masks(S, A):
    nc = A.nc
    C = {}
    C["ones_bf"] = A.sb([128, 128], BF16, "ones_bf")
    C["ones_f"] = A.sb([128, 128], F32, "ones_f")
    C["tri_f"] = A.sb([128, 128], F32, "tri_f")
    C["tri_bf"] = A.sb([128, 128], BF16, "tri_bf")
    C["ident_bf"] = A.sb([128, 128], BF16, "ident_bf")
    C["ident_f"] = A.sb([128, 128], F32, "ident_f")
    S.pool(lambda e: e.memset(C["ones_bf"][:, :], 1.0), w=["c_ones_bf"])
    S.pool(lambda e: e.memset(C["ones_f"][:, :], 1.0), w=["c_ones_f"])
    S.pool(lambda e: e.affine_select(out=C["tri_f"][:, :], in_=C["ones_f"][:, :], pattern=[[1, 128]],
                                     compare_op=ALU.is_ge, fill=0.0, base=0, channel_multiplier=-1),
           r=["c_ones_f"], w=["c_tri_f"])
    S.pool(lambda e: e.tensor_copy(out=C["tri_bf"][:, :], in_=C["tri_f"][:, :]), r=["c_tri_f"], w=["c_tri_bf"])
    S.pool(lambda e: e.affine_select(out=C["ident_f"][:, :], in_=C["ones_f"][:, :], pattern=[[1, 128]],
                                     compare_op=ALU.is_equal, fill=0.0, base=0, channel_multiplier=-1),
           r=["c_ones_f"], w=["c_ident_f"])
    S.pool(lambda e: e.tensor_copy(out=C["ident_bf"][:, :], in_=C["ident_f"][:, :]), r=["c_ident_f"], w=["c_ident_bf"])
    return C


def phase_fox(nc, S, T, pFM, pTM, foxbf_t, yT, zero_halo=False):
    NT = T // 128
    NQ = T // 512
    SCALE = 128 ** -0.5
    with ExitStack() as st:
        A = Tiles(nc, st)
        S.barrier()
        C = make_masks(S, A)
        if zero_halo:
            yT.zero_halo(S, A)
        ff = A.sb([128, NT, 4], BF16, "ff")
        bfb = A.sb([128, NT, 4], F32, "bfb")
        u = A.sb([128, NT, 4], F32, "u")
        sp = A.sb([128, NT, 4], F32, "sp")
        inc = A.sb([128, NT, 4], F32, "inc")
        zer = A.sb([128, NT], F32, "zer")
        Pk = A.sb([128, NT, 4], F32, "Pk")
        Bb = A.sb([128, T // 256, NT], F32, "Bb")
        kT = A.sb([128, T], BF16, "kT")
        Vall = A.sb([128, NT, 512], BF16, "Vall")
        qTb = [A.sb([128, 512], BF16, "qTb") for _ in range(2)]
        ogb = [A.sb([128, 512], BF16, "ogb") for _ in range(2)]
        PT = [A.sb([128, 512], BF16, "PT") for _ in range(3)]
        rl = A.sb([128, 512], F32, "rl")
        osb = A.sb([128, 512], F32, "osb")
        sg = A.sb([128, 512], F32, "sg")
        yb = [A.sb([128, 512], BF16, "yb") for _ in range(2)]
        ps_s = [A.ps([128, 512], F32, "ps_s") for _ in range(3)]
        ps_o = [A.ps([128, 512], F32, "ps_o") for _ in range(2)]
        ps_l = [A.ps([128, 512], F32, "ps_l") for _ in range(2)]
        ps_c = ps_s[0]
        ps_t = ps_s[1]
        ffv = pTM[:, TM_FF:TM_FF + 4].rearrange("(i p) h -> p i h", p=128)
        step = max(1, NT // 8)
        for i0 in range(0, NT, step):
            S.dma(ff[:, i0:i0 + step, :], ffv[:, i0:i0 + step, :], r=[("pTM", i) for i in range(i0, i0 + step)], w=["ff"])
        S.dma(bfb[:, :, :], foxbf_t, w=["bfb"])
        S.dve(lambda e: e.memset(zer[:, :], 0.0), w=["zer"])
        S.dve(lambda e: e.tensor_tensor(out=u[:, :, :], in0=ff[:, :, :], in1=bfb[:, :, :], op=ALU.add), r=["ff", "bfb"], w=["u"])
        S.act(lambda e: e.activation(out=u[:, :, :], in_=u[:, :, :], func=AF.Exp, scale=-1.0), r=["u"], w=["u"])
        S.act(lambda e: e.activation(out=sp[:, :, :], in_=u[:, :, :], func=AF.Ln, bias=1.0), r=["u"], w=["sp"])
        spf = sp[:, :, :].rearrange("p i h -> p (i h)")
        for n0 in range(0, NT * 4, 512):
            nn = min(512, NT * 4 - n0)
            S.pe(lambda e, n0=n0, nn=nn: e.matmul(ps_c[:, :nn], lhsT=C["tri_f"][:, :], rhs=spf[:, n0:n0 + nn], start=True, stop=True),
                 r=["sp", "c_tri_f"], w=["ps_s0"])
            S.pe(lambda e, n0=n0, nn=nn: e.matmul(ps_t[:, :nn], lhsT=C["ones_f"][:, :], rhs=spf[:, n0:n0 + nn], start=True, stop=True),
                 r=["sp", "c_ones_f"], w=["ps_s1"])
            Pf = Pk[:, :, :].rearrange("p i h -> p (i h)")
            If = inc[:, :, :].rearrange("p i h -> p (i h)")
            S.dve(lambda e, n0=n0, nn=nn, If=If: e.tensor_copy(out=If[:, n0:n0 + nn], in_=ps_t[:, :nn]), r=["ps_s1"], w=["inc"])
            S.dve(lambda e, n0=n0, nn=nn, Pf=Pf, If=If: e.tensor_tensor(out=Pf[:, n0:n0 + nn], in0=ps_c[:, :nn], in1=If[:, n0:n0 + nn], op=ALU.subtract),
                  r=["ps_s0", "inc"], w=["Pk"])
        for h in range(4):
            S.dve(lambda e, h=h: e.tensor_tensor_scan(out=inc[:, :, h], data0=inc[:, :, h], data1=zer[:, :], initial=0.0,
                                                       op0=ALU.add, op1=ALU.add), r=["inc", "zer"], w=["inc"])
        S.dve(lambda e: e.tensor_tensor(out=Pk[:, :, :], in0=Pk[:, :, :], in1=inc[:, :, :], op=ALU.add), r=["Pk", "inc"], w=["Pk"])
        vv = pTM[:, TM_FV:TM_FV + 512].rearrange("(i p) c -> p i c", p=128)
        for i0 in range(0, NT, step):
            S.dma(Vall[:, i0:i0 + step, :], vv[:, i0:i0 + step, :], r=[("pTM", i) for i in range(i0, i0 + step)], w=["Vall"])
        kq = 0
        for h in range(4):
            S.dma(kT[:, :], pFM[FM_FK + 128 * h:FM_FK + 128 * (h + 1), :], r=[("pFM", j) for j in range(NQ)], w=["kT"])
            for i2 in range(T // 256):
                nj = 2 * i2 + 2
                S.dve(lambda e, h=h, i2=i2, nj=nj: e.tensor_scalar(out=Bb[:, i2, :nj], in0=Pk[:, :nj, h], scalar1=inc[:, 2 * i2 + 1, h:h + 1],
                                                                   scalar2=None, op0=ALU.subtract), r=["Pk", "inc"], w=["Bb"])
            steps = []
            for I in range(NQ):
                nkt = 4 * I + 4
                for j in range(nkt):
                    steps.append((I, j, nkt))

            def emit_S(st_, kq_):
                I, j, nkt = st_
                qt = qTb[I % 2]
                qtag = ("qTb", I % 2)
                if j == 0:
                    og = ogb[I % 2]
                    S.dma(qt[:, :], pFM[FM_FQ + 128 * h:FM_FQ + 128 * (h + 1), 512 * I:512 * (I + 1)], r=[("pFM", I)], w=[qtag])
                    S.dma(og[:, :], pFM[FM_OG + 128 * h:FM_OG + 128 * (h + 1), 512 * I:512 * (I + 1)], r=[("pFM", I)], w=[("ogb", I % 2)])
                q0 = max(0, 128 * (j - 4 * I))
                pss = ps_s[kq_ % 3]
                S.pe(lambda e, pss=pss, j=j, qt=qt, q0=q0: e.matmul(pss[:, q0:512], lhsT=kT[:, 128 * j:128 * (j + 1)], rhs=qt[:, q0:512],
                                                                   start=True, stop=True), r=["kT", qtag], w=["ps_s%d" % (kq_ % 3)])

            def emit_rest(st_, kq_):
                I, j, nkt = st_
                r_ = j - 4 * I
                q0 = max(0, 128 * r_)
                pss = ps_s[kq_ % 3]
                pstag = "ps_s%d" % (kq_ % 3)
                pt = PT[kq_ % 3]
                po = ps_o[I % 2]
                pl = ps_l[I % 2]
                potag = ("ps_o", I % 2)
                pltag = ("ps_l", I % 2)
                for sb in range(2):
                    c0 = max(q0, 256 * sb)
                    c1 = 256 * (sb + 1)
                    if c0 >= c1:
                        continue
                    S.act(lambda e, pt=pt, pss=pss, c0=c0, c1=c1, I=I, sb=sb, j=j: e.activation(
                        out=pt[:, c0:c1], in_=pss[:, c0:c1], func=AF.Exp, scale=SCALE, bias=Bb[:, 2 * I + sb, j:j + 1]),
                        r=[pstag, "Bb"], w=[("PT", kq_ % 3, sb)])
                pttags = [("PT", kq_ % 3, 0), ("PT", kq_ % 3, 1)]
                if r_ >= 0:
                    S.pool(lambda e, pt=pt, q0=q0: e.tensor_tensor(out=pt[:, q0:q0 + 128], in0=pt[:, q0:q0 + 128], in1=C["tri_bf"][:, :], op=ALU.mult),
                           r=pttags + ["c_tri_bf"], w=pttags)
                S.pe(lambda e, po=po, j=j, pt=pt, q0=q0, nkt=nkt, h=h: e.matmul(po[:, q0:512], lhsT=Vall[:, j, 128 * h:128 * (h + 1)], rhs=pt[:, q0:512],
                                                                        start=(j == 0), stop=(j == nkt - 1)), r=["Vall"] + pttags, w=[potag])
                S.pe(lambda e, pl=pl, j=j, pt=pt, q0=q0, nkt=nkt: e.matmul(pl[:, q0:512], lhsT=C["ones_bf"][:, :], rhs=pt[:, q0:512],
                                                                        start=(j == 0), stop=(j == nkt - 1)), r=["c_ones_bf"] + pttags, w=[pltag])
                if j == nkt - 1:
                    og = ogb[I % 2]
                    ogtag = ("ogb", I % 2)
                    y = yb[I % 2]
                    ytag = ("yb", I % 2)
                    S.dve(lambda e, pl=pl: e.reciprocal(out=rl[:, :], in_=pl[:, :]), r=[pltag], w=["rl"])
                    S.dve(lambda e, po=po: e.tensor_tensor(out=osb[:, :], in0=po[:, :], in1=rl[:, :], op=ALU.mult), r=[potag, "rl"], w=["osb"])
                    S.act(lambda e, og=og: e.activation(out=sg[:, :], in_=og[:, :], func=AF.Exp, scale=-1.0), r=[ogtag], w=["sg"])
                    S.dve(lambda e: e.tensor_scalar_add(out=sg[:, :], in0=sg[:, :], scalar1=1.0), r=["sg"], w=["sg"])
                    S.dve(lambda e: e.reciprocal(out=sg[:, :], in_=sg[:, :]), r=["sg"], w=["sg"])
                    S.pool(lambda e, y=y: e.tensor_tensor(out=y[:, :], in0=osb[:, :], in1=sg[:, :], op=ALU.mult), r=["osb", "sg"], w=[ytag])
                    yT.put(S, 1024 + 128 * h, lambda d, cs_, h=h: d[1024 + 128 * h:1024 + 128 * (h + 1), cs_], I, lambda cs_, y=y: y[:, cs_], [ytag])

            LOOK = 2
            n = len(steps)
            for i in range(min(LOOK, n)):
                emit_S(steps[i], kq + i)
            for i in range(n):
                if i + LOOK < n:
                    emit_S(steps[i + LOOK], kq + i + LOOK)
                emit_rest(steps[i], kq + i)
            kq += n
        S.emit()


def phase_gla(nc, S, T, pFM, pTM, wlr_aug, gnb_d, yT, zero_halo=False):
    NBK = T // 512
    NCH = T // 128
    with ExitStack() as st:
        A = Tiles(nc, st)
        S.barrier()
        C = make_masks(S, A)
        if zero_halo:
            yT.zero_halo(S, A)
        rt_f = A.sb([128, 128], F32, "rt_f")
        S.pool(lambda e: e.affine_select(out=rt_f[:, :], in_=C["ones_f"][:, :], pattern=[[-1, 128]], compare_op=ALU.is_ge,
                                         fill=0.0, base=-1, channel_multiplier=1), r=["c_ones_f"], w=["rt_f"])
        wl_f = A.sb([17, 256], F32, "wl_f")
        wl = A.sb([17, 256], BF16, "wl")
        gnb = A.sb([128, 512], F32, "gnb")
        epsc = A.sb([128, 1], F32, "epsc")
        S.pool(lambda e: e.memset(epsc[:, :], EPS), w=["epsc"])
        S.dma(wl_f[:, :], wlr_aug, w=["wl_f"])
        S.dve(lambda e: e.tensor_copy(out=wl[:, :], in_=wl_f[:, :]), r=["wl_f"], w=["wl"])
        S.dma(gnb[:, :], gnb_d, w=["gnb"])
        laug = [A.sb([17, 512], BF16, "laug") for _ in range(2)]
        qkb = [A.sb([128, 4, 512], BF16, "qkb") for _ in range(2)]
        tmb = [A.sb([128, 4, 1280], BF16, "tmb") for _ in range(2)]
        for i in range(2):
            S.pool(lambda e, i=i: e.memset(laug[i][:, :], 1.0), w=[("laug", i)])
        e_sb = A.sb([128, 256], F32, "e_sb")
        esr = A.sb([128, 512], F32, "esr")
        sp_sb = A.sb([128, 256], F32, "sp_sb")
        ek = A.sb([128, 256], F32, "ek")
        ekk = A.sb([128, 2, 128], F32, "ekk")
        kin = A.sb([128, 2, 128], BF16, "kin")
        kst = [A.sb([128, 256], BF16, "kst") for _ in range(2)]
        eq = [A.sb([128, 2, 128], F32, "eq") for _ in range(2)]
        qin = [A.sb([128, 2, 128], BF16, "qin") for _ in range(2)]
        att = [A.sb([128, 2, 128], BF16, "att") for _ in range(2)]
        silr = [A.sb([128, 512], F32, "silr") for _ in range(2)]
        St = A.sb([128, 2, 256], F32, "St")
        Sb = A.sb([128, 2, 256], BF16, "Sb")
        junk = A.sb([128, 256], F32, "junk")
        ssum = A.sb([128, 2], F32, "ssum")
        rstd = A.sb([128, 2], F32, "rstd")
        t1 = A.sb([128, 256], F32, "t1")
        ysb = A.sb([128, 2, 256], BF16, "ysb")
        yTs = [A.sb([128, 4, 512], BF16, "yTs") for _ in range(2)]
        pb = [A.ps([128, 512], F32, "pb") for _ in range(7)]
        ptr = A.ps([128, 1024], BF16, "ptr")
        S.dve(lambda e: e.memset(St[:, :, :], 0.0), w=["St"])
        S.dve(lambda e: e.memset(Sb[:, :, :], 0.0), w=["Sb"])
        tmv = pTM[:, TM_GK:TM_GK + 1280].rearrange("(i p) c -> p i c", p=128)

        def load_block(j):
            b2 = j % 2
            bs = slice(512 * j, 512 * (j + 1))
            S.dma(laug[b2][0:16, :], pFM[FM_LR:FM_LR + 16, bs], r=[("pFM", j)], w=[("laug", b2)])
            S.dma(qkb[b2][:, 0:2, :], pFM[FM_GQ:FM_GQ + 256, bs].rearrange("(h p) t -> p h t", p=128), r=[("pFM", j)], w=[("qkb", b2)])
            S.dma(qkb[b2][:, 2:4, :], pFM[FM_GK:FM_GK + 256, bs].rearrange("(h p) t -> p h t", p=128), r=[("pFM", j)], w=[("qkb", b2)])
            S.dma(tmb[b2][:, :, :], tmv[:, 4 * j:4 * j + 4, :], r=[("pTM", 4 * j + i) for i in range(4)], w=[("tmb", b2)])

        def P(n):
            j, ch = n // 4, n % 4
            b2 = j % 2
            p2 = n % 2
            if ch == 1 and j + 1 < NBK:
                load_block(j + 1)
            cs = slice(128 * ch, 128 * (ch + 1))
            kt = tmb[b2][:, ch, 0:256]
            rt = tmb[b2][:, ch, 768:1280]
            S.pe(lambda e: e.matmul(pb[0][:, 0:256], lhsT=laug[b2][0:17, cs], rhs=wl[0:17, :], start=True, stop=True),
                 r=[("laug", b2), "wl"], w=["pg0"])
            yield
            S.act(lambda e: e.activation(out=e_sb[:, :], in_=pb[0][:, 0:256], func=AF.Exp, scale=-1.0), r=["pg0"], w=["e_sb"])
            yield
            S.act(lambda e: e.activation(out=sp_sb[:, :], in_=e_sb[:, :], func=AF.Ln, bias=1.0), r=["e_sb"], w=["sp_sb"])
            yield
            S.pe(lambda e: e.matmul(pb[0][:, 256:512], lhsT=rt_f[:, :], rhs=sp_sb[:, :], start=True, stop=True),
                 r=["rt_f", "sp_sb"], w=["pg0"])
            for h in range(2):
                S.pe(lambda e, h=h: e.matmul(pb[1][:, 128 * h:128 * (h + 1)], lhsT=sp_sb[:, 128 * h:128 * (h + 1)], rhs=C["tri_f"][:, :],
                                            start=True, stop=True), r=["sp_sb", "c_tri_f"], w=["pg1"])
            yield
            S.act(lambda e: e.activation(out=ek[:, :], in_=pb[0][:, 256:512], func=AF.Exp, scale=-1.0 / 16), r=["pg0"], w=["ek"])
            yield
            S.act(lambda e: e.activation(out=eq[p2][:, :, :].rearrange("p a b -> p (a b)"), in_=pb[1][:, 0:256], func=AF.Exp, scale=-1.0 / 16),
                  r=["pg1"], w=[("eq", p2)])
            yield
            S.act(lambda e: e.activation(out=ekk[:, :, :].rearrange("p a b -> p (a b)"), in_=pb[1][:, 0:256], func=AF.Exp, scale=1.0 / 16),
                  r=["pg1"], w=["ekk"])
            S.dve(lambda e: e.tensor_tensor(out=kst[p2][:, :], in0=kt, in1=ek[:, :], op=ALU.mult), r=[("tmb", b2), "ek"], w=[("kst", p2)])
            yield
            S.dve(lambda e: e.scalar_tensor_tensor(out=qin[p2][:, :, :], in0=qkb[b2][:, 0:2, cs], scalar=128 ** -0.5, in1=eq[p2][:, :, :],
                                                    op0=ALU.mult, op1=ALU.mult), r=[("qkb", b2), ("eq", p2)], w=[("qin", p2)])
            yield
            S.dve(lambda e: e.tensor_tensor(out=kin[:, :, :], in0=qkb[b2][:, 2:4, cs], in1=ekk[:, :, :], op=ALU.mult),
                  r=[("qkb", b2), "ekk"], w=["kin"])
            yield
            for h in range(2):
                S.pe(lambda e, h=h: e.matmul(pb[6][:, 128 * h:128 * (h + 1)], lhsT=kin[:, h, :], rhs=qin[p2][:, h, :], start=True, stop=True),
                     r=["kin", ("qin", p2)], w=["pg6"])
            yield
            S.dve(lambda e: e.tensor_tensor(out=att[p2][:, :, :], in0=pb[6][:, 0:256].rearrange("p (a b) -> p a b", a=2),
                                            in1=C["tri_f"][:, None, :].to_broadcast([128, 2, 128]), op=ALU.mult),
                  r=["pg6", "c_tri_f"], w=[("att", p2)])
            yield
            S.act(lambda e: e.activation(out=esr[:, :], in_=rt, func=AF.Exp, scale=-1.0), r=[("tmb", b2)], w=["esr"])
            yield
            S.dve(lambda e: e.tensor_scalar_add(out=esr[:, :], in0=esr[:, :], scalar1=1.0), r=["esr"], w=["esr"])
            S.dve(lambda e: e.reciprocal(out=esr[:, :], in_=esr[:, :]), r=["esr"], w=["esr"])
            yield
            S.dve(lambda e: e.tensor_tensor(out=silr[p2][:, :], in0=rt, in1=esr[:, :], op=ALU.mult), r=[("tmb", b2), "esr"], w=[("silr", p2)])
            yield

        def X(n):
            j, ch = n // 4, n % 4
            b2 = j % 2
            p2 = n % 2
            cs = slice(128 * ch, 128 * (ch + 1))
            vt = tmb[b2][:, ch, 256:768]
            for h in range(2):
                po = pb[2 + h]
                pu = pb[4 + h]
                S.pe(lambda e, h=h, po=po: e.matmul(po[:, 0:256], lhsT=att[p2][:, h, :], rhs=vt[:, 256 * h:256 * (h + 1)], start=True, stop=False),
                     r=[("att", p2), ("tmb", b2)], w=[("po", h)])
                S.pe(lambda e, h=h, po=po: e.matmul(po[:, 0:256], lhsT=qin[p2][:, h, :], rhs=Sb[:, h, :], start=False, stop=True),
                     r=[("qin", p2), ("Sb", h)], w=[("po", h)])
                S.pe(lambda e, h=h, pu=pu: e.matmul(pu[:, 0:256], lhsT=kst[p2][:, 128 * h:128 * (h + 1)], rhs=vt[:, 256 * h:256 * (h + 1)], start=True, stop=True),
                     r=[("kst", p2), ("tmb", b2)], w=[("pu", h)])
                yield
                S.dve(lambda e, h=h, pu=pu: e.scalar_tensor_tensor(out=St[:, h, :], in0=St[:, h, :], scalar=eq[p2][:, h, 127:128], in1=pu[:, 0:256],
                                                                    op0=ALU.mult, op1=ALU.add), r=[("St", h), ("eq", p2), ("pu", h)], w=[("St", h)])
                yield
                S.act(lambda e, h=h: e.copy(out=Sb[:, h, :], in_=St[:, h, :]), r=[("St", h)], w=[("Sb", h)])
                yield
                S.act(lambda e, h=h, po=po: e.activation(out=junk[:, :], in_=po[:, 0:256], func=AF.Square, accum_out=ssum[:, h:h + 1]),
                      r=[("po", h)], w=["junk", ("ssum", h)])
                yield
                S.act(lambda e, h=h: e.activation(out=ssum[:, h:h + 1], in_=ssum[:, h:h + 1], func=AF.Ln, bias=epsc[:, 0:1], scale=1.0 / 256),
                      r=[("ssum", h), "epsc"], w=[("ssum", h)])
                yield
                S.act(lambda e, h=h: e.activation(out=rstd[:, h:h + 1], in_=ssum[:, h:h + 1], func=AF.Exp, scale=-0.5), r=[("ssum", h)], w=[("rstd", h)])
                yield
                S.dve(lambda e, h=h, po=po: e.scalar_tensor_tensor(out=t1[:, :], in0=po[:, 0:256], scalar=rstd[:, h:h + 1], in1=gnb[:, 256 * h:256 * (h + 1)],
                                                                    op0=ALU.mult, op1=ALU.mult), r=[("po", h), ("rstd", h), "gnb"], w=["t1"])
                yield
                S.pool(lambda e, h=h: e.tensor_tensor(out=ysb[:, h, :], in0=t1[:, :], in1=silr[p2][:, 256 * h:256 * (h + 1)], op=ALU.mult),
                       r=["t1", ("silr", p2)], w=[("ysb", h)])
                yield
                for vc in range(2):
                    S.pe(lambda e, h=h, vc=vc: e.transpose(ptr[:, 128 * vc:128 * (vc + 1)], ysb[:, h, 128 * vc:128 * (vc + 1)], C["ident_bf"][:, :]),
                         r=[("ysb", h), "c_ident_bf"], w=["ptr"])
                yield
                S.act(lambda e, h=h: e.copy(out=yTs[b2][:, 2 * h:2 * h + 2, cs], in_=ptr[:, 0:256].rearrange("p (a b) -> p a b", a=2)),
                      r=["ptr"], w=[("yTs", b2)])
                yield
            if ch == 3:
                yT.put(S, 0, lambda d, cs_: d[0:512, cs_].rearrange("(a p) t -> p a t", p=128), j, lambda cs_, b2=b2: yTs[b2][:, :, cs_], [("yTs", b2)])

        load_block(0)
        for _ in P(0):
            pass
        for n in range(NCH):
            gp = P(n + 1) if n + 1 < NCH else iter(())
            gx = X(n)
            while True:
                a_ = next(gp, "end")
                b_ = next(gx, "end")
                if a_ == "end" and b_ == "end":
                    break
        S.emit()


MLSTM_STOP = 0


def phase_mlstm(nc, S, T, pFM, pTM, P, yT):
    NBK = T // 512
    with ExitStack() as st:
        A = Tiles(nc, st)
        S.barrier()
        C = make_masks(S, A)
        epsc = A.sb([128, 1], F32, "epsc")
        S.pool(lambda e: e.memset(epsc[:, :], EPS), w=["epsc"])
        cw = A.sb([128, 8, 4], F32, "cw")
        cb = A.sb([128, 8], F32, "cb")
        gb = A.sb([128, 4], F32, "gb")
        skc = A.sb([128, 4], F32, "skc")
        gnb = A.sb([128, 512], F32, "gnb")
        wif = A.sb([128, 3, 8, 4], F32, "wif")
        for nm, tl in (("cw", cw), ("cb", cb), ("gb", gb), ("skc", skc), ("gnb", gnb), ("wif", wif)):
            S.dma(tl[tuple(slice(None) for _ in tl.shape)], P[nm], w=[nm])
        wbd_f = A.sb([128, 3, 4, 128], F32, "wbd_f")
        wbd = A.sb([128, 3, 4, 128], BF16, "wbd")
        wT_f = A.sb([128, 3, 8, 128], F32, "wT_f")
        for i, nm in enumerate(("wq_bd", "wk_bd", "wv_bd")):
            S.dma(wbd_f[:, i, :, :], P[nm].rearrange("c p o -> p c o"), w=["wbd_f"])
        S.dve(lambda e: e.tensor_copy(out=wbd[:, :, :, :], in_=wbd_f[:, :, :, :]), r=["wbd_f"], w=["wbd"])
        for i, nm in enumerate(("wqT_bd", "wkT_bd", "wvT_bd")):
            S.dma(wT_f[:, i, :, :], P[nm].rearrange("c p o -> p c o"), w=["wT_f"])
        dcw = A.sb([128, 8, 4, 128], BF16, "dcw")
        dsk = A.sb([128, 4, 128], BF16, "dsk")
        for c in range(8):
            for j in range(4):
                S.dve(lambda e, c=c, j=j: e.tensor_scalar(out=dcw[:, c, j, :], in0=C["ident_f"][:, :], scalar1=cw[:, c, j:j + 1], scalar2=None, op0=ALU.mult),
                      r=["c_ident_f", "cw"], w=["dcw"])
        for c in range(4):
            S.dve(lambda e, c=c: e.tensor_scalar(out=dsk[:, c, :], in0=C["ident_f"][:, :], scalar1=skc[:, c:c + 1], scalar2=None, op0=ALU.mult),
                  r=["c_ident_f", "skc"], w=["dsk"])
        pb = [A.ps([128, 512], F32, "pb") for _ in range(7)]
        ptr = A.ps([128, 1024], BF16, "ptr")
        weff = A.sb([128, 2, 8, 4], BF16, "weff")
        for c in range(8):
            S.pe(lambda e, c=c: e.matmul(pb[0][:, 8 * c:8 * c + 4], lhsT=wT_f[:, 0, c, :], rhs=wif[:, 0, c, :], start=True, stop=False), r=["wT_f", "wif"], w=["pb0"])
            S.pe(lambda e, c=c: e.matmul(pb[0][:, 8 * c:8 * c + 4], lhsT=wT_f[:, 1, c, :], rhs=wif[:, 1, c, :], start=False, stop=True), r=["wT_f", "wif"], w=["pb0"])
            S.pe(lambda e, c=c: e.matmul(pb[0][:, 8 * c + 4:8 * c + 8], lhsT=wT_f[:, 2, c, :], rhs=wif[:, 2, c, :], start=True, stop=True), r=["wT_f", "wif"], w=["pb0"])
        pw = pb[0][:, 0:64].rearrange("p (c k f) -> p k c f", c=8, k=2)
        S.dve(lambda e: e.tensor_copy(out=weff[:, :, :, :], in_=pw), r=["pb0"], w=["weff"])
        if MLSTM_STOP == 1:
            S.emit()
            return
        mxb = [A.sb([128, 8, 516], BF16, "mxb") for _ in range(2)]
        mxs = A.sb([128, 8, 516], BF16, "mxs")
        mzb = [A.sb([128, 4, 512], BF16, "mzb") for _ in range(2)]
        xcT = [A.sb([128, 8, 512], BF16, "xcT") for _ in range(2)]
        gsb = A.sb([128, 4], F32, "gsb")
        ef = A.sb([128, 2], F32, "ef")
        nlf = A.sb([128, 2], F32, "nlf")
        tmp2 = A.sb([128, 2], F32, "tmp2")
        wv = A.sb([128, 2], F32, "wv")
        wveg = A.sb([128, 2], F32, "wveg")
        eb = [A.sb([128, 2], F32, "eb") for _ in range(2)]
        eg = [A.sb([128, 2], F32, "eg") for _ in range(2)]
        silz = [A.sb([128, 512], F32, "silz") for _ in range(2)]
        qk = [[A.sb([128, 4, 128], BF16, "qk") for _ in range(2)] for _ in range(2)]
        ksb = [[A.sb([128, 256], BF16, "ksb") for _ in range(2)] for _ in range(2)]
        vw = [[A.sb([128, 260], BF16, "vw") for _ in range(2)] for _ in range(2)]
        vw2 = [[A.sb([128, 260], BF16, "vw2") for _ in range(2)] for _ in range(2)]
        att = [[A.sb([128, 128], BF16, "att") for _ in range(2)] for _ in range(2)]
        Cst = A.sb([128, 2, 2, 257], F32, "Cst")
        Cb = A.sb([128, 2, 2, 260], BF16, "Cb")
        den = A.sb([128, 1], F32, "den")
        fac = A.sb([128, 1], F32, "fac")
        ss = A.sb([128, 1], F32, "ss")
        fr = A.sb([128, 1], F32, "fr")
        junk = A.sb([128, 256], F32, "junk")
        t1 = A.sb([128, 256], F32, "t1")
        ysb = A.sb([128, 2, 256], BF16, "ysb")
        yTs = [A.sb([128, 4, 512], BF16, "yTs") for _ in range(2)]
        ln16c = A.sb([128, 1], F32, "ln16c")
        ncb = A.sb([128, 8], F32, "ncb")
        S.dve(lambda e: e.tensor_scalar(out=ncb[:, :], in0=cb[:, :], scalar1=-1.0, scalar2=None, op0=ALU.mult), r=["cb"], w=["ncb"])
        ecv = A.sb([128, 512], F32, "ecv")
        zcv = A.sb([128, 512], F32, "zcv")
        esz = A.sb([128, 512], F32, "esz")
        S.pool(lambda e: e.memset(ln16c[:, :], float(np.log(1.0 / 16.0))), w=["ln16c"])
        S.dve(lambda e: e.memset(Cst[:, :, :, :], 0.0), w=["Cst"])
        S.dve(lambda e: e.memset(Cb[:, :, :, :], 0.0), w=["Cb"])
        for p_ in range(2):
            for h_ in range(2):
                S.dve(lambda e, p_=p_, h_=h_: e.memset(vw[p_][h_][:, :], 0.0), w=[("vw", p_, h_)])
                S.dve(lambda e, p_=p_, h_=h_: e.memset(vw2[p_][h_][:, :], 0.0), w=[("vw2", p_, h_)])
        S.pool(lambda e: e.memset(mxb[0][:, :, 0:4], 0.0), w=[("mxb", 0)])
        S.pool(lambda e: e.memset(mxs[:, :, :], 0.0), w=["mxs"])
        mzv = pTM[:, TM_MZ:TM_MZ + 512].rearrange("(i p) c -> p i c", p=128)
        NTL = T // 128

        def load_block(j):
            b2 = j % 2
            bs = slice(512 * j, 512 * (j + 1))
            S.dma(mxb[b2][:, :, 4:516], pFM[FM_MX:FM_MX + 1024, bs].rearrange("(c p) t -> p c t", p=128), r=[("pFM", j)], w=[("mxb", b2)])
            S.dma(mzb[b2][:, :, :], mzv[:, 4 * j:4 * j + 4, :], r=[("pTM", 4 * j + i) for i in range(4)], w=[("mzb", b2)])

        def P(n):
            j, tt = n // 4, n % 4
            b2 = j % 2
            p2 = n % 2
            mx = mxb[b2]
            xc = xcT[b2]
            xctag = ("xcT", b2)
            if tt == 1 and j + 1 < NBK:
                load_block(j + 1)
            if tt == 0:
                if j > 0:
                    S.dve(lambda e: e.tensor_copy(out=mxb[b2][:, :, 0:4], in_=mxb[1 - b2][:, :, 512:516]), r=[("mxb", 1 - b2)], w=[("mxb", b2)])
                S.dve(lambda e: e.tensor_copy(out=mxs[:, :, 0:514], in_=mx[:, :, 1:515]), r=[("mxb", b2)], w=["mxs"])
                yield
                for c in range(8):
                    pbi = (2, 4)[c % 2]
                    pc = pb[pbi]
                    for tp in range(4):
                        S.pe(lambda e, c=c, tp=tp, pc=pc: e.matmul(pc[:, :], lhsT=dcw[:, c, tp, :], rhs=(mx[:, c, tp + 1:tp + 513] if tp % 2 == 1 else mxs[:, c, tp:tp + 512]),
                                                                  start=(tp == 0), stop=(tp == 3)), r=["dcw", ("mxb", b2), "mxs"], w=["pb%d" % pbi])
                    S.act(lambda e, c=c, pc=pc: e.activation(out=ecv[:, :], in_=pc[:, :], func=AF.Exp, scale=-1.0, bias=ncb[:, c:c + 1]), r=["pb%d" % pbi, "ncb"], w=["ecv"])
                    S.act(lambda e, c=c, pc=pc: e.activation(out=zcv[:, :], in_=pc[:, :], func=AF.Identity, bias=cb[:, c:c + 1]), r=["pb%d" % pbi, "cb"], w=["zcv"])
                    yield
                    S.dve(lambda e: e.tensor_scalar_add(out=ecv[:, :], in0=ecv[:, :], scalar1=1.0), r=["ecv"], w=["ecv"])
                    S.dve(lambda e: e.reciprocal(out=ecv[:, :], in_=ecv[:, :]), r=["ecv"], w=["ecv"])
                    S.dve(lambda e, c=c: e.tensor_tensor(out=xc[:, c, :], in0=zcv[:, :], in1=ecv[:, :], op=ALU.mult), r=["zcv", "ecv"], w=[xctag])
                    yield
            ts_ = slice(128 * tt, 128 * (tt + 1))
            tsx = slice(4 + 128 * tt, 4 + 128 * (tt + 1))
            for c in range(8):
                S.pe(lambda e, c=c: e.matmul(pb[3][:, 0:4], lhsT=xc[:, c, ts_], rhs=weff[:, 0, c, :], start=(c == 0), stop=False), r=[xctag, "weff"], w=["pb3"])
            for c in range(8):
                S.pe(lambda e, c=c: e.matmul(pb[3][:, 0:4], lhsT=mx[:, c, tsx], rhs=weff[:, 1, c, :], start=False, stop=(c == 7)), r=[("mxb", b2), "weff"], w=["pb3"])
            yield
            S.dve(lambda e: e.tensor_tensor(out=gsb[:, :], in0=pb[3][:, 0:4], in1=gb[:, :], op=ALU.add), r=["pb3", "gb"], w=["gsb"])
            yield
            S.act(lambda e: e.activation(out=ef[:, :], in_=gsb[:, 2:4], func=AF.Exp, scale=-1.0), r=["gsb"], w=["ef"])
            yield
            S.act(lambda e: e.activation(out=nlf[:, :], in_=ef[:, :], func=AF.Ln, bias=1.0), r=["ef"], w=["nlf"])
            yield
            S.pe(lambda e: e.matmul(pb[3][:, 8:10], lhsT=C["tri_f"][:, :], rhs=nlf[:, :], start=True, stop=True), r=["c_tri_f", "nlf"], w=["pb3"])
            S.pe(lambda e: e.matmul(pb[3][:, 16:18], lhsT=C["ones_f"][:, :], rhs=nlf[:, :], start=True, stop=True), r=["c_ones_f", "nlf"], w=["pb3"])
            yield
            S.dve(lambda e: e.tensor_tensor(out=tmp2[:, :], in0=pb[3][:, 8:10], in1=gsb[:, 0:2], op=ALU.add), r=["pb3", "gsb"], w=["tmp2"])
            yield
            S.act(lambda e: e.activation(out=wv[:, :], in_=tmp2[:, :], func=AF.Exp), r=["tmp2"], w=["wv"])
            S.act(lambda e: e.activation(out=eb[p2][:, :], in_=pb[3][:, 8:10], func=AF.Exp, scale=-1.0, bias=ln16c[:, 0:1]), r=["pb3", "ln16c"], w=[("eb", p2)])
            S.act(lambda e: e.activation(out=eg[p2][:, :], in_=pb[3][:, 16:18], func=AF.Exp, scale=-1.0), r=["pb3"], w=[("eg", p2)])
            yield
            S.dve(lambda e: e.tensor_tensor(out=wveg[:, :], in0=wv[:, :], in1=eg[p2][:, :], op=ALU.mult), r=["wv", ("eg", p2)], w=["wveg"])
            yield
            for h in range(2):
                pA, pB = pb[3], pb[4]
                for dc in range(2):
                    c = 2 * h + dc
                    S.pe(lambda e, c=c, dc=dc: e.matmul(pA[:, 128 * dc:128 * (dc + 1)], lhsT=wbd[:, 0, c, :], rhs=xc[:, c, ts_], start=True, stop=True), r=["wbd", xctag], w=["pb3"])
                    S.pe(lambda e, c=c, dc=dc: e.matmul(pA[:, 256 + 128 * dc:256 + 128 * (dc + 1)], lhsT=wbd[:, 1, c, :], rhs=xc[:, c, ts_], start=True, stop=True), r=["wbd", xctag], w=["pb3"])
                    S.pe(lambda e, c=c, dc=dc: e.matmul(pB[:, 128 * dc:128 * (dc + 1)], lhsT=xc[:, c, ts_], rhs=wbd[:, 1, c, :], start=True, stop=True), r=["wbd", xctag], w=["pb4"])
                    S.pe(lambda e, c=c, dc=dc: e.matmul(pB[:, 256 + 128 * dc:256 + 128 * (dc + 1)], lhsT=mx[:, c, tsx], rhs=wbd[:, 2, c, :], start=True, stop=True), r=["wbd", ("mxb", b2)], w=["pb4"])
                yield
                S.act(lambda e, h=h: e.copy(out=qk[p2][h][:, :, :].rearrange("p a b -> p (a b)"), in_=pA[:, :]), r=["pb3"], w=[("qk", p2, h)])
                yield
                S.act(lambda e, h=h: e.copy(out=ksb[p2][h][:, :], in_=pB[:, 0:256]), r=["pb4"], w=[("ksb", p2, h)])
                S.dve(lambda e, h=h: e.tensor_scalar(out=vw[p2][h][:, 0:256], in0=pB[:, 256:512], scalar1=wv[:, h:h + 1], scalar2=None, op0=ALU.mult), r=["pb4", "wv"], w=[("vw", p2, h)])
                yield
                S.dve(lambda e, h=h: e.tensor_scalar(out=vw2[p2][h][:, 0:256], in0=pB[:, 256:512], scalar1=wveg[:, h:h + 1], scalar2=None, op0=ALU.mult), r=["pb4", "wveg"], w=[("vw2", p2, h)])
                S.pool(lambda e, h=h: e.tensor_copy(out=vw[p2][h][:, 256:257], in_=wv[:, h:h + 1]), r=["wv"], w=[("vw", p2, h)])
                S.pool(lambda e, h=h: e.tensor_copy(out=vw2[p2][h][:, 256:257], in_=wveg[:, h:h + 1]), r=["wveg"], w=[("vw2", p2, h)])
                yield
                for dc in range(2):
                    S.pe(lambda e, dc=dc, h=h: e.matmul(pb[2][:, 0:128], lhsT=qk[p2][h][:, 2 + dc, :], rhs=qk[p2][h][:, dc, :], start=(dc == 0), stop=(dc == 1)), r=[("qk", p2, h)], w=["pb2"])
                yield
                S.dve(lambda e, h=h: e.tensor_tensor(out=att[p2][h][:, :], in0=pb[2][:, 0:128], in1=C["tri_f"][:, :], op=ALU.mult), r=["pb2", "c_tri_f"], w=[("att", p2, h)])
                yield
            S.act(lambda e: e.activation(out=esz[:, :], in_=mzb[b2][:, tt, :], func=AF.Exp, scale=-1.0), r=[("mzb", b2)], w=["esz"])
            yield
            S.dve(lambda e: e.tensor_scalar_add(out=esz[:, :], in0=esz[:, :], scalar1=1.0), r=["esz"], w=["esz"])
            S.dve(lambda e: e.reciprocal(out=esz[:, :], in_=esz[:, :]), r=["esz"], w=["esz"])
            yield
            S.dve(lambda e: e.tensor_tensor(out=silz[p2][:, :], in0=mzb[b2][:, tt, :], in1=esz[:, :], op=ALU.mult), r=[("mzb", b2), "esz"], w=[("silz", p2)])
            yield

        def X(n):
            j, tt = n // 4, n % 4
            b2 = j % 2
            p2 = n % 2
            xc = xcT[b2]
            xctag = ("xcT", b2)
            ts_ = slice(128 * tt, 128 * (tt + 1))
            for h in range(2):
                pD, pE, pF, pG = pb[5], pb[6], pb[0], pb[1]
                S.pe(lambda e, h=h: e.matmul(pD[:, 0:258], lhsT=att[p2][h][:, :], rhs=vw[p2][h][:, 0:258], start=True, stop=False), r=[("att", p2, h), ("vw", p2, h)], w=["pb5"])
                for dc in range(2):
                    S.pe(lambda e, dc=dc, h=h: e.matmul(pD[:, 0:258], lhsT=qk[p2][h][:, dc, :], rhs=Cb[:, h, dc, 0:258], start=False, stop=(dc == 1)), r=[("qk", p2, h), ("Cb", h)], w=["pb5"])
                for dc, pU in ((0, pE), (1, pF)):
                    S.pe(lambda e, dc=dc, pU=pU, h=h: e.matmul(pU[:, 0:258], lhsT=ksb[p2][h][:, 128 * dc:128 * (dc + 1)], rhs=vw2[p2][h][:, 0:258], start=True, stop=True),
                         r=[("ksb", p2, h), ("vw2", p2, h)], w=[("pb6", "pb0")[dc]])
                yield
                for dc, pU in ((0, pE), (1, pF)):
                    S.dve(lambda e, dc=dc, pU=pU, h=h: e.scalar_tensor_tensor(out=Cst[:, h, dc, :], in0=Cst[:, h, dc, :], scalar=eg[p2][:, h:h + 1], in1=pU[:, 0:257],
                                                                             op0=ALU.mult, op1=ALU.add), r=[("Cst", h), ("eg", p2), ("pb6", "pb0")[dc]], w=[("Cst", h)])
                yield
                S.act(lambda e, h=h: e.copy(out=Cb[:, h, :, 0:257], in_=Cst[:, h, :, :]), r=[("Cst", h)], w=[("Cb", h)])
                yield
                S.act(lambda e, h=h: e.activation(out=den[:, :], in_=pD[:, 256:257], func=AF.Abs, scale=eb[p2][:, h:h + 1]), r=["pb5", ("eb", p2)], w=["den"])
                yield
                S.dve(lambda e: e.tensor_scalar_max(out=den[:, :], in0=den[:, :], scalar1=1.0), r=["den"], w=["den"])
                S.dve(lambda e: e.reciprocal(out=den[:, :], in_=den[:, :]), r=["den"], w=["den"])
                S.dve(lambda e, h=h: e.tensor_tensor(out=fac[:, :], in0=den[:, :], in1=eb[p2][:, h:h + 1], op=ALU.mult), r=["den", ("eb", p2)], w=["fac"])
                yield
                S.act(lambda e: e.activation(out=junk[:, :], in_=pD[:, 0:256], func=AF.Square, scale=fac[:, 0:1], accum_out=ss[:, 0:1]), r=["pb5", "fac"], w=["junk", "ss"])
                yield
                S.act(lambda e: e.activation(out=ss[:, :], in_=ss[:, :], func=AF.Ln, bias=epsc[:, 0:1], scale=1.0 / 256), r=["ss", "epsc"], w=["ss"])
                yield
                S.act(lambda e: e.activation(out=fr[:, :], in_=ss[:, :], func=AF.Exp, scale=-0.5), r=["ss"], w=["fr"])
                S.dve(lambda e: e.tensor_tensor(out=fr[:, :], in0=fr[:, :], in1=fac[:, :], op=ALU.mult), r=["fr", "fac"], w=["fr"])
                S.dve(lambda e, h=h: e.scalar_tensor_tensor(out=t1[:, :], in0=pD[:, 0:256], scalar=fr[:, 0:1], in1=gnb[:, 256 * h:256 * (h + 1)],
                                                             op0=ALU.mult, op1=ALU.mult), r=["pb5", "fr", "gnb"], w=["t1"])
                for dc in range(2):
                    c = 2 * h + dc
                    S.pe(lambda e, c=c, dc=dc: e.matmul(pG[:, 128 * dc:128 * (dc + 1)], lhsT=xc[:, c, ts_], rhs=dsk[:, c, :], start=True, stop=True),
                         r=[xctag, "dsk"], w=["pb1"])
                yield
                S.dve(lambda e: e.tensor_tensor(out=t1[:, :], in0=pG[:, 0:256], in1=t1[:, :], op=ALU.add), r=["pb1", "t1"], w=["t1"])
                yield
                S.pool(lambda e, h=h: e.tensor_tensor(out=ysb[:, h, :], in0=t1[:, :], in1=silz[p2][:, 256 * h:256 * (h + 1)], op=ALU.mult), r=["t1", ("silz", p2)], w=[("ysb", h)])
                yield
                for vc in range(2):
                    S.pe(lambda e, h=h, vc=vc: e.transpose(ptr[:, 128 * vc:128 * (vc + 1)], ysb[:, h, 128 * vc:128 * (vc + 1)], C["ident_bf"][:, :]),
                         r=[("ysb", h), "c_ident_bf"], w=["ptr"])
                yield
                S.act(lambda e, h=h: e.copy(out=yTs[b2][:, 2 * h:2 * h + 2, ts_], in_=ptr[:, 0:256].rearrange("p (a b) -> p a b", a=2)),
                      r=["ptr"], w=[("yTs", b2)])
                yield
            if tt == 3:
                yT.put(S, 512, lambda d, cs_: d[512:1024, cs_].rearrange("(a p) t -> p a t", p=128), j, lambda cs_, b2=b2: yTs[b2][:, :, cs_], [("yTs", b2)])

        load_block(0)
        for _ in P(0):
            pass
        for n in range(NTL):
            gp = P(n + 1) if n + 1 < NTL else iter(())
            gx = X(n)
            while True:
                a_ = next(gp, "end")
                b_ = next(gx, "end")
                if a_ == "end" and b_ == "end":
                    break
        S.emit()


def bcast(v, n=128):
    v = np.asarray(v, np.float32)
    return np.ascontiguousarray(np.broadcast_to(v, (n,) + v.shape))


def block_diag_full(w):
    W = np.zeros((1024, 1024), np.float32)
    for c in range(4):
        for d in range(4):
            W[np.arange(256) * 4 + c, np.arange(256) * 4 + d] = w[:, c, d]
    return W


def prep_mlstm(p, conv_w, conv_b, wq, wk, wv, w_i, b_i, w_f, b_f, skip, norm):
    own = np.arange(512 * p, 512 * p + 512)
    oth = np.arange(512 * (1 - p), 512 * (1 - p) + 512)
    perm = np.concatenate([own, oth])
    P = {}
    P["cw"] = np.ascontiguousarray(conv_w[:, perm].reshape(4, 8, 128).transpose(2, 1, 0))
    P["cb"] = np.ascontiguousarray(conv_b[perm].reshape(8, 128).T)
    for nm, w in (("q", wq), ("k", wk), ("v", wv)):
        W = block_diag_full(w)[perm][:, perm]
        blocks = np.stack([W[128 * c:128 * (c + 1), 128 * c:128 * (c + 1)] for c in range(8)])
        P["w%s_bd" % nm] = np.ascontiguousarray(blocks[:4])
        P["w%sT_bd" % nm] = np.ascontiguousarray(blocks.transpose(0, 2, 1))
    cols = np.stack([w_i[:, 2 * p], w_i[:, 2 * p + 1], w_f[:, 2 * p], w_f[:, 2 * p + 1]], axis=1)
    wif = np.stack([cols[part * 1024 + perm] for part in range(3)])
    P["wif"] = np.ascontiguousarray(wif.reshape(3, 8, 128, 4).transpose(2, 0, 1, 3))
    P["gb"] = bcast(np.array([b_i[2 * p], b_i[2 * p + 1], b_f[2 * p], b_f[2 * p + 1]], np.float32))
    P["skc"] = np.ascontiguousarray(skip[own].reshape(4, 128).T)
    P["gnb"] = bcast(norm[own])
    return {k: np.ascontiguousarray(v, dtype=np.float32) for k, v in P.items()}, perm


def load_w_generic(S, wsb_view_fn, wdram_view, nchunks, ncols, G, stg, stgtag, wtag, k0=0):
    k = k0
    for c0 in range(0, ncols, G):
        n = min(G, ncols - c0)
        st = stg[k % 2]
        tg = (stgtag, k % 2)
        sv = st[:, 0:nchunks * n].rearrange("p (c n) -> p c n", c=nchunks)
        S.dma(sv, wdram_view[:, :, c0:c0 + n], w=[tg])
        eng = [S.pool, S.dve, S.act][k % 3]
        if k % 3 == 2:
            eng(lambda e, sv=sv, c0=c0, n=n: e.copy(out=wsb_view_fn(c0, n), in_=sv), r=[tg], w=[wtag])
        else:
            eng(lambda e, sv=sv, c0=c0, n=n: e.tensor_copy(out=wsb_view_fn(c0, n), in_=sv), r=[tg], w=[wtag])
        k += 1
    return k


def blocks_of(H, TO, NB):
    bl = [(0, H)] if H > 0 else []
    for k in range(TO // NB):
        bl.append((H + NB * k, NB))
    return bl


def phase_C1(nc, S, H, TO, xTo, ysrc, hmask_d, Wg, bg_d, Wb, Wout, gmix, x1T, yr=(), xr=(), ypre=False):
    NB = 256
    with ExitStack() as st:
        A = Tiles(nc, st)
        S.barrier()
        T_ = {}
        wg = A.sb([128, 8, 3072], BF16, "wg")
        wb = A.sb([128, 3, 8, 1024], BF16, "wb")
        wo = A.sb([128, 8, 1024], BF16, "wo")
        stg = [A.sb([128, 2048], F32, "stg") for _ in range(2)]
        xb = [A.sb([128, 8, NB], F32, "xb") for _ in range(1)]
        yb = [A.sb([128, 24, NB], BF16, "yb") for _ in range(2)]
        yb2 = A.sb([128, 24, NB], BF16, "yb2") if len(ysrc) == 2 else None
        hT = A.sb([128, 8, NB], BF16, "hT")
        T_["xsq"] = A.sb([128, 8, NB], BF16, "xsq")
        T_["std"] = A.sb([128, NB], F32, "std")
        T_["rstd"] = A.sb([128, NB], F32, "rstd")
        T_["epsc"] = A.sb([128, 1], F32, "epsc")
        ones_bf = A.sb([128, 128], BF16, "ones")
        gcol = A.sb([128, 8], F32, "gcol")
        bg = A.sb([128, 24], F32, "bg")
        hmask = A.sb([128, 2], F32, "hmask")
        mg = A.sb([128, 8, NB], F32, "mg")
        mgT = A.sb([128, 8, NB], BF16, "mgT")
        sg = [A.sb([128, NB], F32, "sg") for _ in range(2)]
        tmp = [A.sb([128, NB], F32, "tmp") for _ in range(2)]
        ps_ss = A.ps([128, 512], F32, "ps_ss")
        psg = [A.ps([128, 512], F32, "psg") for _ in range(2)]
        psb = [A.ps([128, 512], F32, "psb") for _ in range(2)]
        pso = [A.ps([128, 512], F32, "pso") for _ in range(2)]
        S.pool(lambda e: e.memset(ones_bf[:, :], 1.0), w=["ones"])
        S.pool(lambda e: e.memset(T_["epsc"][:, :], EPS), w=["epsc"])
        S.dma(gcol[:, :], gmix, w=["gcol"])
        S.dma(bg[:, :], bg_d, w=["bg"])
        S.dma(hmask[:, :], hmask_d, w=["hmask"])
        k = load_w_generic(S, lambda c0, n: wg[:, :, c0:c0 + n], Wg.rearrange("(c p) n -> p c n", p=128), 8, 3072, 256, stg, "stg", "wg")
        for n_ in range(3):
            k = load_w_generic(S, lambda c0, n, n_=n_: wb[:, n_, :, c0:c0 + n], Wb[n_].rearrange("(c p) n -> p c n", p=128), 8, 1024, 256, stg, "stg", "wb", k)
        k = load_w_generic(S, lambda c0, n: wo[:, :, c0:c0 + n], Wout.rearrange("(c p) n -> p c n", p=128), 8, 1024, 256, stg, "stg", "wo", k)
        xv = xTo.rearrange("(c p) t -> p c t", p=128)
        yvs = ysrc if ypre else [y_.rearrange("(c p) t -> p c t", p=128) for y_ in ysrc]
        yview = (lambda t_: t_.rearrange("p (r k) n -> p r k n", r=2)) if ypre else (lambda t_: t_)
        ov = x1T.rearrange("(c p) t -> p c t", p=128)
        kk = 0
        for bi, (c0, nb) in enumerate(blocks_of(H, TO, NB)):
            xt = xb[0]
            xtag = ("xb", 0)
            yt = yb[bi % 2]
            ytag = ("yb", bi % 2)
            S.dma(xt[:, :, :nb], xv[:, :, c0:c0 + nb], r=list(xr), w=[xtag])
            if ypre:
                for r_ in range(2):
                    S.dma(yt[:, 12 * r_:12 * (r_ + 1), :nb], yvs[0][:, r_, :, c0:c0 + nb], r=list(yr), w=[ytag])
            else:
                S.dma(yt[:, :, :nb], yvs[0][:, :, c0:c0 + nb], r=list(yr), w=[ytag])
            if len(ysrc) == 2:
                for r_ in range(2):
                    S.dma(yb2[:, 12 * r_:12 * (r_ + 1), :nb], yvs[1][:, r_, :, c0:c0 + nb], r=list(yr), w=["yb2"])
                S.dve(lambda e, yt=yt, nb=nb: e.tensor_scalar(out=yt[:, :, :nb], in0=yt[:, :, :nb], scalar1=hmask[:, 0:1], scalar2=None, op0=ALU.mult),
                      r=[ytag, "hmask"], w=[ytag])
                S.dve(lambda e, yt=yt, nb=nb: e.scalar_tensor_tensor(out=yt[:, :, :nb], in0=yb2[:, :, :nb], scalar=hmask[:, 1:2], in1=yt[:, :, :nb], op0=ALU.mult, op1=ALU.add),
                      r=[ytag, "yb2", "hmask"], w=[ytag])
            if bi == 0 and H > 0:
                S.dve(lambda e, xt=xt, nb=nb: e.tensor_scalar(out=xt[:, :, :nb], in0=xt[:, :, :nb], scalar1=hmask[:, 1:2], scalar2=None, op0=ALU.mult),
                      r=[xtag, "hmask"], w=[xtag])
            rms_block(S, T_, ones_bf, xt, gcol, hT, ps_ss, xtag, nb)
            htag = ("hT", id(hT))
            for dc in range(8):
                for n_ in range(3):
                    pg = psg[kk % 2]
                    pgt = ("psg", kk % 2)
                    pbk = psb[kk % 2]
                    pbt = ("psb", kk % 2)
                    sgt = sg[kk % 2]
                    sgtag = ("sg", kk % 2)
                    tm = tmp[kk % 2]
                    tmtag = ("tmp", kk % 2)
                    kk += 1
                    for c in range(8):
                        S.pe(lambda e, c=c, pg=pg, n_=n_, dc=dc, nb=nb: e.matmul(pg[:, :nb], lhsT=wg[:, c, n_ * 1024 + dc * 128:n_ * 1024 + (dc + 1) * 128], rhs=hT[:, c, :nb],
                                                                               start=(c == 0), stop=(c == 7)), r=["wg", htag], w=[pgt])
                    S.act(lambda e, pg=pg, sgt=sgt, n_=n_, dc=dc, nb=nb: e.activation(out=sgt[:, :nb], in_=pg[:, :nb], func=AF.Sigmoid, bias=bg[:, n_ * 8 + dc:n_ * 8 + dc + 1]),
                          r=[pgt, "bg"], w=[sgtag])
                    for c in range(8):
                        ych = (c // 4) * 12 + n_ * 4 + (c % 4)
                        S.pe(lambda e, c=c, pbk=pbk, n_=n_, dc=dc, nb=nb, ych=ych, yt=yt: e.matmul(pbk[:, :nb], lhsT=wb[:, n_, c, dc * 128:(dc + 1) * 128], rhs=yt[:, ych, :nb],
                                                                                                start=(c == 0), stop=(c == 7)), r=["wb", ytag], w=[pbt])
                    if n_ == 0:
                        S.dve(lambda e, pbk=pbk, sgt=sgt, dc=dc, nb=nb: e.tensor_tensor(out=mg[:, dc, :nb], in0=pbk[:, :nb], in1=sgt[:, :nb], op=ALU.mult),
                              r=[pbt, sgtag], w=[("mg", dc)])
                    else:
                        S.dve(lambda e, pbk=pbk, sgt=sgt, tm=tm, nb=nb: e.tensor_tensor(out=tm[:, :nb], in0=pbk[:, :nb], in1=sgt[:, :nb], op=ALU.mult),
                              r=[pbt, sgtag], w=[tmtag])
                        dst = mg if n_ == 1 else mgT
                        S.pool(lambda e, tm=tm, dc=dc, nb=nb, dst=dst: e.tensor_tensor(out=dst[:, dc, :nb], in0=mg[:, dc, :nb], in1=tm[:, :nb], op=ALU.add),
                               r=[("mg", dc), tmtag], w=[("mg", dc), ("mgT", dc)])
            for dc in range(8):
                po = pso[dc % 2]
                pot = ("pso", dc % 2)
                for c in range(8):
                    S.pe(lambda e, c=c, po=po, dc=dc, nb=nb: e.matmul(po[:, :nb], lhsT=wo[:, c, dc * 128:(dc + 1) * 128], rhs=mgT[:, c, :nb], start=(c == 0), stop=(c == 7)),
                         r=["wo"] + [("mgT", c_) for c_ in range(8)], w=[pot])
                S.dve(lambda e, po=po, dc=dc, nb=nb, xt=xt: e.tensor_tensor(out=xt[:, dc, :nb], in0=po[:, :nb], in1=xt[:, dc, :nb], op=ALU.add), r=[pot, xtag, htag], w=[xtag])
            S.dma(ov[:, :, c0:c0 + nb], xt[:, :, :nb], r=[xtag], w=[("x1T", c0)])
        S.emit()


def phase_C2(nc, S, H, TO, x1T, Wup, cw_d, cb_d, Wdown, gffn, x2T, final_g=None, outT=None, xsend=None):
    NB = 256
    with ExitStack() as st:
        A = Tiles(nc, st)
        S.barrier()
        T_ = {}
        wu = A.sb([128, 8, 5632], BF16, "wu")
        wd = A.sb([128, 22, 1024], BF16, "wd")
        stg = [A.sb([128, 2048], F32, "stg") for _ in range(2)]
        xb = [A.sb([128, 8, NB], F32, "xb") for _ in range(2)]
        hT = A.sb([128, 8, NB], BF16, "hT")
        T_["xsq"] = A.sb([128, 8, NB], BF16, "xsq")
        T_["std"] = A.sb([128, NB], F32, "std")
        T_["rstd"] = A.sb([128, NB], F32, "rstd")
        T_["epsc"] = A.sb([128, 1], F32, "epsc")
        ones_bf = A.sb([128, 128], BF16, "ones")
        gcol = A.sb([128, 8], F32, "gcol")
        gfin = A.sb([128, 8], F32, "gfin")
        cw = A.sb([128, 44, 3], F32, "cw")
        cb = A.sb([128, 44], F32, "cb")
        halo = A.sb([128, 44, 2], F32, "halo")
        ua = [A.sb([128, NB + 2], F32, "ua") for _ in range(2)]
        ug = [A.sb([128, NB + 2], F32, "ug") for _ in range(2)]
        aa = [A.sb([128, NB], F32, "aa") for _ in range(2)]
        ag = [A.sb([128, NB], F32, "ag") for _ in range(2)]
        actT = A.sb([128, 22, NB], BF16, "actT")
        oT = A.sb([128, 8, NB], F32, "oT")
        ps_ss = A.ps([128, 512], F32, "ps_ss")
        psa = [A.ps([128, 512], F32, "psa") for _ in range(2)]
        psgt = [A.ps([128, 512], F32, "psgt") for _ in range(2)]
        psd = [A.ps([128, 512], F32, "psd") for _ in range(2)]
        S.pool(lambda e: e.memset(ones_bf[:, :], 1.0), w=["ones"])
        S.pool(lambda e: e.memset(T_["epsc"][:, :], EPS), w=["epsc"])
        S.dve(lambda e: e.memset(halo[:, :, :], 0.0), w=["halo"])
        S.dma(gcol[:, :], gffn, w=["gcol"])
        if final_g is not None:
            S.dma(gfin[:, :], final_g, w=["gfin"])
        S.dma(cw[:, :, :], cw_d, w=["cw"])
        S.dma(cb[:, :], cb_d, w=["cb"])
        k = load_w_generic(S, lambda c0, n: wu[:, :, c0:c0 + n], Wup.rearrange("(c p) n -> p c n", p=128), 8, 5632, 256, stg, "stg", "wu")
        k = load_w_generic(S, lambda c0, n: wd[:, :, c0:c0 + n], Wdown.rearrange("(c p) n -> p c n", p=128), 22, 1024, 64, stg, "stg", "wd", k)
        xv = x1T.rearrange("(c p) t -> p c t", p=128)
        ov = x2T.rearrange("(c p) t -> p c t", p=128)
        kk = 0
        for bi, (c0, nb) in enumerate(blocks_of(H, TO, NB)):
            xt = xb[bi % 2]
            xtag = ("xb", bi % 2)
            S.dma(xt[:, :, :nb], xv[:, :, c0:c0 + nb], r=[("x1T", c0)], w=[xtag])
            rms_block(S, T_, ones_bf, xt, gcol, hT, ps_ss, xtag, nb)
            htag = ("hT", id(hT))
            for fc in range(22):
                b2 = kk % 2
                kk += 1
                for (ps, pst, off, ut, utag, acc, atag, hc) in ((psa[b2], ("psa", b2), 0, ua[b2], ("ua", b2), aa[b2], ("aa", b2), fc),
                                                                (psgt[b2], ("psgt", b2), 2816, ug[b2], ("ug", b2), ag[b2], ("ag", b2), 22 + fc)):
                    for c in range(8):
                        S.pe(lambda e, c=c, ps=ps, off=off, fc=fc, nb=nb: e.matmul(ps[:, :nb], lhsT=wu[:, c, off + fc * 128:off + (fc + 1) * 128], rhs=hT[:, c, :nb],
                                                                                 start=(c == 0), stop=(c == 7)), r=["wu", htag], w=[pst])
                    S.act(lambda e, ut=ut, hc=hc: e.copy(out=ut[:, 0:2], in_=halo[:, hc, :]), r=[("halo", hc)], w=[utag])
                    S.act(lambda e, ut=ut, ps=ps, nb=nb: e.copy(out=ut[:, 2:2 + nb], in_=ps[:, :nb]), r=[pst], w=[utag])
                    S.act(lambda e, ut=ut, hc=hc, nb=nb: e.copy(out=halo[:, hc, :], in_=ut[:, nb:nb + 2]), r=[utag], w=[("halo", hc)])
                    S.dve(lambda e, ut=ut, acc=acc, hc=hc, nb=nb: e.tensor_scalar(out=acc[:, :nb], in0=ut[:, 0:nb], scalar1=cw[:, hc, 0:1], scalar2=cb[:, hc:hc + 1],
                                                                                op0=ALU.mult, op1=ALU.add), r=[utag, "cw", "cb"], w=[atag])
                    for j in (1, 2):
                        S.dve(lambda e, ut=ut, acc=acc, hc=hc, nb=nb, j=j: e.scalar_tensor_tensor(out=acc[:, :nb], in0=ut[:, j:j + nb], scalar=cw[:, hc, j:j + 1], in1=acc[:, :nb],
                                                                                                  op0=ALU.mult, op1=ALU.add), r=[utag, "cw", atag], w=[atag])
                S.act(lambda e, b2=b2, nb=nb: e.activation(out=ag[b2][:, :nb], in_=ag[b2][:, :nb], func=AF.Silu), r=[("ag", b2)], w=[("ag", b2)])
                S.pool(lambda e, b2=b2, fc=fc, nb=nb: e.tensor_tensor(out=actT[:, fc, :nb], in0=aa[b2][:, :nb], in1=ag[b2][:, :nb], op=ALU.mult),
                       r=[("aa", b2), ("ag", b2)], w=[("actT", fc)])
            for dc in range(8):
                po = psd[dc % 2]
                pot = ("psd", dc % 2)
                for fc in range(22):
                    S.pe(lambda e, fc=fc, po=po, dc=dc, nb=nb: e.matmul(po[:, :nb], lhsT=wd[:, fc, dc * 128:(dc + 1) * 128], rhs=actT[:, fc, :nb], start=(fc == 0), stop=(fc == 21)),
                         r=["wd"] + [("actT", f_) for f_ in range(22)], w=[pot])
                S.dve(lambda e, po=po, dc=dc, nb=nb, xt=xt: e.tensor_tensor(out=xt[:, dc, :nb], in0=po[:, :nb], in1=xt[:, dc, :nb], op=ALU.add), r=[pot, xtag, htag], w=[xtag])
            if final_g is None:
                S.dma(ov[:, :, c0:c0 + nb], xt[:, :, :nb], r=[xtag], w=[("x2T", c0)])
                if xsend is not None and not (bi == 0 and H > 0):
                    S.dma(xsend.rearrange("(c p) t -> p c t", p=128)[:, :, c0 - H:c0 - H + nb], xt[:, :, :nb], r=[xtag], w=[("xsend", c0)])
            elif not (bi == 0 and H > 0):
                xsq, std, rstd = T_["xsq"], T_["std"], T_["rstd"]
                S.act(lambda e, xt=xt, nb=nb: e.activation(out=xsq[:, :, :nb], in_=xt[:, :, :nb], func=AF.Square), r=[xtag], w=["xsq"])
                for c in range(8):
                    S.pe(lambda e, c=c, nb=nb: e.matmul(ps_ss[:, :nb], lhsT=ones_bf[:, :], rhs=xsq[:, c, :nb], start=(c == 0), stop=(c == 7)), r=["xsq", "ones"], w=["ps_ss"])
                S.act(lambda e, nb=nb: e.activation(out=std[:, :nb], in_=ps_ss[:, :nb], func=AF.Sqrt, bias=T_["epsc"][:, 0:1], scale=1.0 / D), r=["ps_ss", "epsc"], w=["std"])
                S.dve(lambda e, nb=nb: e.reciprocal(out=rstd[:, :nb], in_=std[:, :nb]), r=["std"], w=["rstd"])
                for c in range(8):
                    S.dve(lambda e, c=c, nb=nb, xt=xt: e.scalar_tensor_tensor(out=oT[:, c, :nb], in0=xt[:, c, :nb], scalar=gfin[:, c:c + 1], in1=rstd[:, :nb], op0=ALU.mult, op1=ALU.mult),
                          r=[xtag, "rstd", "gfin"], w=["oT"])
                S.dma(outT.rearrange("(c p) t -> p c t", p=128)[:, :, c0 - H:c0 - H + nb], oT[:, :, :nb], r=["oT"], w=[("outT", c0)])
        S.emit()


T_FULL = 8192
TO_FULL = 4096
ML_SH = {"cw": [128, 8, 4], "cb": [128, 8], "wq_bd": [4, 128, 128], "wk_bd": [4, 128, 128], "wv_bd": [4, 128, 128], "wqT_bd": [8, 128, 128],
         "wkT_bd": [8, 128, 128], "wvT_bd": [8, 128, 128], "wif": [128, 3, 8, 4], "gb": [128, 4], "skc": [128, 4], "gnb": [128, 512]}


def build_AB(T):
    nc = bass.Bass("TRN2", target_bir_lowering=False)
    dt = lambda n, s, d, k: nc.dram_tensor(n, s, d, kind=k).ap()
    xT = dt("xT", [D, T], F32, "ExternalInput")
    wA = dt("wA", [D, NA], F32, "ExternalInput")
    gm = dt("gmix", [128, 8], F32, "ExternalInput")
    fb = dt("foxbf_t", [128, T // 128, 4], F32, "ExternalInput")
    wl = dt("wlr_aug", [17, 256], F32, "ExternalInput")
    gn = dt("gla_gnb", [128, 512], F32, "ExternalInput")
    P = {k: dt("m_" + k, v, F32, "ExternalInput") for k, v in ML_SH.items()}
    yT = dt("yT", [1536, T], BF16, "ExternalOutput")
    pFM = dt("pFM", [NFM, T], BF16, "Internal")
    pTM = dt("pTM", [T, NTM], BF16, "Internal")
    with ExitStack() as st:
        S = Sched(nc, st)
        phase_A(nc, S, T, xT, wA, gm, pFM, pTM)
        yd = YDst(T, yT=yT)
        phase_gla(nc, S, T, pFM, pTM, wl, gn, yd)
        phase_mlstm(nc, S, T, pFM, pTM, P, yd)
        phase_fox(nc, S, T, pFM, pTM, fb, yd)
        S.finish()
        S.emit()
    return nc


def build_C(H, TO, final):
    nc = bass.Bass("TRN2", target_bir_lowering=False)
    dt = lambda n, s, d, k: nc.dram_tensor(n, s, d, kind=k).ap()
    W = H + TO
    xTo = dt("xTo", [D, W], F32, "ExternalInput")
    yTall = dt("yTall", [3072, W], BF16, "ExternalInput")
    hm = dt("hmask", [128, 2], F32, "ExternalInput")
    Wg = dt("Wg", [D, 3072], F32, "ExternalInput")
    bg = dt("bg", [128, 24], F32, "ExternalInput")
    Wb = dt("Wb", [3, D, D], F32, "ExternalInput")
    Wo = dt("Wout", [D, D], F32, "ExternalInput")
    gm = dt("gmix", [128, 8], F32, "ExternalInput")
    Wup = dt("Wup", [D, 5632], F32, "ExternalInput")
    cw = dt("fcw", [128, 44, 3], F32, "ExternalInput")
    cb = dt("fcb", [128, 44], F32, "ExternalInput")
    Wd = dt("Wdown", [2816, D], F32, "ExternalInput")
    gf = dt("gffn", [128, 8], F32, "ExternalInput")
    x1T = dt("x1T", [D, W], F32, "Internal")
    if final:
        gfin = dt("gfin", [128, 8], F32, "ExternalInput")
        outT = dt("outT", [D, TO], F32, "ExternalOutput")
        x2T = x1T
    else:
        gfin, outT = None, None
        x2T = dt("x2T", [D, W], F32, "ExternalOutput")
    with ExitStack() as st:
        S = Sched(nc, st)
        phase_C1(nc, S, H, TO, xTo, [yTall], hm, Wg, bg, Wb, Wo, gm, x1T)
        phase_C2(nc, S, H, TO, x1T, Wup, cw, cb, Wd, gf, x2T, gfin, outT)
        S.finish()
        S.emit()
    return nc


def colvec(v):
    v = np.asarray(v, np.float32)
    return np.ascontiguousarray(v.reshape(-1, 128).T)


SPL = np.cumsum([0, 512, 512, 1024, 16, 1024, 1024, 1024, 1024, 1024, 1024, 8, 1024, 3072])
O_GQ, O_GK, O_GV, O_GLR, O_GR, O_MX, O_MZ, O_FQ, O_FK, O_FV, O_FF, O_FOG, O_GATES = [int(v) for v in SPL[:13]]


def prep_AB_inputs(p, l, inp, T):
    w_in = inp["w_in"][l]
    own = np.arange(512 * p, 512 * p + 512)
    oth = np.arange(512 * (1 - p), 512 * (1 - p) + 512)
    perm = np.concatenate([own, oth])
    r256 = np.arange(256 * p, 256 * p + 256)
    cols = np.concatenate([O_GQ + r256, O_GK + r256, O_MX + perm, O_FQ + own, O_FK + own, O_FOG + own, O_GLR + np.arange(16),
                           O_GK + r256, O_GV + own, O_GR + own, O_MZ + own, O_FV + own, O_FF + np.arange(4 * p, 4 * p + 4)])
    assert cols.size == NA
    d = {}
    d["wA"] = np.ascontiguousarray(w_in[:, cols])
    d["gmix"] = colvec(inp["norm_mix"][l])
    d["foxbf_t"] = np.ascontiguousarray(np.broadcast_to(inp["fox_b_f"][l][4 * p:4 * p + 4], (128, T // 128, 4))).astype(np.float32)
    d["wlr_aug"] = np.ascontiguousarray(np.concatenate([inp["gla_w_lr"][l][:, r256], inp["gla_b_lr"][l][r256][None]], 0)).astype(np.float32)
    d["gla_gnb"] = bcast(inp["gla_norm"][l][own])
    P, _ = prep_mlstm(p, inp["mlstm_conv_w"][l], inp["mlstm_conv_b"][l], inp["mlstm_wq"][l], inp["mlstm_wk"][l], inp["mlstm_wv"][l],
                      inp["mlstm_w_i"][l], inp["mlstm_b_i"][l], inp["mlstm_w_f"][l], inp["mlstm_b_f"][l], inp["mlstm_skip"][l], inp["mlstm_norm"][l])
    for k, v in P.items():
        d["m_" + k] = v
    return d


def prep_C_inputs(l, inp, final):
    d = {}
    d["Wg"] = np.ascontiguousarray(inp["w_in"][l][:, O_GATES:O_GATES + 3072])
    d["bg"] = colvec(inp["b_gate"][l].reshape(-1))
    d["Wb"] = np.ascontiguousarray(inp["w_branch"][l])
    d["Wout"] = np.ascontiguousarray(inp["w_out"][l])
    d["gmix"] = colvec(inp["norm_mix"][l])
    d["Wup"] = np.ascontiguousarray(inp["ffn_w_up"][l])
    d["fcw"] = np.ascontiguousarray(inp["ffn_conv_w"][l].reshape(3, 44, 128).transpose(2, 1, 0))
    d["fcb"] = colvec(inp["ffn_conv_b"][l])
    d["Wdown"] = np.ascontiguousarray(inp["ffn_w_down"][l])
    d["gffn"] = colvec(inp["norm_ffn"][l])
    if final:
        d["gfin"] = colvec(inp["norm_final"])
    return d


_NC_CACHE = {}


def _get_nc(key, fn):
    if key not in _NC_CACHE:
        _NC_CACHE[key] = fn()
    return _NC_CACHE[key]


def kernel_unfused(inp):
    x = inp["x"]
    B, T, _ = x.shape
    TO = T // 2
    ncore = 2 * B
    HS = [4, 2]
    xT_full = [np.ascontiguousarray(x[b].T) for b in range(B)]
    def own_slice(arr, p, H):
        if p == 0:
            return np.ascontiguousarray(np.concatenate([np.zeros((arr.shape[0], H), arr.dtype), arr[:, :TO]], axis=1))
        return np.ascontiguousarray(arr[:, TO - H:])
    xTo = [own_slice(xT_full[c // 2], c % 2, HS[0]) for c in range(ncore)]
    out = None
    for l in range(2):
        H = HS[l]
        final = (l == 1)
        nc_ab = _get_nc(("AB", T), lambda: build_AB(T))
        in_maps = []
        for c in range(ncore):
            d = prep_AB_inputs(c % 2, l, inp, T)
            d["xT"] = xT_full[c // 2]
            in_maps.append(d)
        res = run_bass_kernel_spmd(nc_ab, in_maps, core_ids=list(range(ncore)))
        yTs = [np.asarray(r["yT"]) for r in res.results]
        nc_c = _get_nc(("C", H, TO, final), lambda: build_C(H, TO, final))
        cw = prep_C_inputs(l, inp, final)
        in_maps = []
        for c in range(ncore):
            b, p = c // 2, c % 2
            d = dict(cw)
            d["xTo"] = xTo[c]
            d["yTall"] = np.ascontiguousarray(np.concatenate([own_slice(yTs[2 * b + rr], p, H) for rr in range(2)], axis=0))
            d["hmask"] = np.ascontiguousarray(np.broadcast_to(np.array([1.0 - p, float(p)], np.float32), (128, 2)))
            in_maps.append(d)
        res = run_bass_kernel_spmd(nc_c, in_maps, core_ids=list(range(ncore)))
        if not final:
            x2 = [np.asarray(r["x2T"]) for r in res.results]
            xT_full = [np.ascontiguousarray(np.concatenate([x2[2 * b][:, H:], x2[2 * b + 1][:, H:]], axis=1)) for b in range(B)]
            xTo = [np.ascontiguousarray(x2[c][:, H - HS[1]:]) for c in range(ncore)]
        else:
            o = [np.asarray(r["outT"]) for r in res.results]
            out = np.stack([np.concatenate([o[2 * b], o[2 * b + 1]], axis=1).T for b in range(B)]).astype(np.float32)
    return np.ascontiguousarray(out)


AB_IN = [("wA", [D, NA]), ("gmix", [128, 8]), ("foxbf_t", None), ("wlr_aug", [17, 256]), ("gla_gnb", [128, 512])] + [("m_" + k, v) for k, v in ML_SH.items()]
C_IN = [("Wg", [D, 3072]), ("bg", [128, 24]), ("Wb", [3, D, D]), ("Wout", [D, D]), ("Wup", [D, 5632]), ("fcw", [128, 44, 3]), ("fcb", [128, 44]),
        ("Wdown", [2816, D]), ("gffn", [128, 8])]


def build_fused(T, ncore):
    TO = T // 2
    HS = [4, 2]
    NBK = T // 512
    half = NBK // 2
    groups = [[i, i + 1] for i in range(0, ncore, 2)]
    nc = bass.Bass("TRN2", target_bir_lowering=False)
    dt = lambda n, s, d, k: nc.dram_tensor(n, s, d, kind=k).ap()
    xT = dt("xT", [D, T], F32, "ExternalInput")
    xTo = dt("xTo", [D, HS[0] + TO], F32, "ExternalInput")
    hm = dt("hmask", [128, 2], F32, "ExternalInput")
    gfin = dt("gfin", [128, 8], F32, "ExternalInput")
    outT = dt("outT", [D, TO], F32, "ExternalOutput")
    W = []
    for l in range(2):
        d = {}
        for n, shp in AB_IN:
            d[n] = dt(f"{n}_l{l}", shp if shp is not None else [128, T // 128, 4], F32, "ExternalInput")
        for n, shp in C_IN:
            d[n] = dt(f"{n}_l{l}", shp, F32, "ExternalInput")
        W.append(d)
    pFM = dt("pFM", [NFM, T], BF16, "Internal")
    pTM = dt("pTM", [T, NTM], BF16, "Internal")
    ys = [[dt(f"ys_{l}_{s_}", [1536, HS[l] + TO], BF16, "Internal") for s_ in range(2)] for l in range(2)]
    yg = [[dt(f"yg_{l}_{s_}", [12, 2, 128, HS[l] + TO], BF16, "Internal") for s_ in range(2)] for l in range(2)]
    x1T = [dt(f"x1T_{l}", [D, HS[l] + TO], F32, "Internal") for l in range(2)]
    x2T0 = dt("x2T_0", [D, HS[0] + TO], F32, "Internal")
    xsend = dt("xsend", [D, TO], F32, "Internal")
    sgT = dt("sgT", [3072, HS[0] + TO], BF16, "Internal")
    actD = dt("actD", [2816, HS[0] + TO], BF16, "Internal")
    yselD = dt("yselD", [3072, HS[0] + TO], BF16, "Internal")
    xg = dt("xg", [8, 2, 128, TO], F32, "Internal")
    with ExitStack() as st:
        S = Sched(nc, st)
        for l in range(2):
            H = HS[l]
            w = W[l]
            if l == 0:
                phase_A(nc, S, T, xT, w["wA"], w["gmix"], pFM, pTM)
            else:
                xgv = xg.rearrange("c r p t -> r p c t")
                xblk = lambda j: xgv[j // half][:, :, 512 * (j % half):512 * (j % half + 1)]
                phase_A(nc, S, T, None, w["wA"], w["gmix"], pFM, pTM, xblk=xblk, xr=["xg"])
            yd = YDst(T, ys=ys[l], H=H, key=("y", l))
            P = {k: w["m_" + k] for k in ML_SH}
            def gather_rows(i0, i1, row_lo, row_hi):
                toks = [t for t in yd.tokens if t[1] == "zero" or (isinstance(t[1], int) and row_lo <= t[1] < row_hi)]
                for s_ in range(2):
                    for i in range(i0, i1):
                        S.collective("AllGather", [ys[l][s_][128 * i:128 * (i + 1), :]], [yg[l][s_][i].rearrange("r p w -> (r p) w")], groups,
                                     r=toks, w=[("yg", l, s_, i)])

            phase_gla(nc, S, T, pFM, pTM, w["wlr_aug"], w["gla_gnb"], yd, zero_halo=True)
            gather_rows(0, 4, 0, 512)
            phase_mlstm(nc, S, T, pFM, pTM, P, yd)
            gather_rows(4, 8, 512, 1024)
            phase_fox(nc, S, T, pFM, pTM, w["foxbf_t"], yd)
            gather_rows(8, 12, 1024, 1536)
            xin = xTo if l == 0 else x2T0[:, HS[0] - HS[1]:]
            xr = [] if l == 0 else [("x2T", c0) for (c0, nb) in blocks_of(HS[0], TO, 256)]
            W_ = H + TO
            sg_l = sgT[:, 0:W_]
            act_l = actD[:, 0:W_]
            blk = blocks_of(HS[0], TO, 512)
            xr2 = [] if l == 0 else [("x2T", c0) for (c0, nb) in blk]
            ysel_l = yselD[:, 0:W_]
            phase_C1a(nc, S, H, TO, xin, hm, w["Wg"], w["bg"], w["gmix"], sg_l, xr=xr2,
                      ysrc=[y_.rearrange("k r p w -> p r k w") for y_ in yg[l]], ysel=ysel_l, yr=[("yg", l, s_, i) for s_ in range(2) for i in range(12)])
            phase_C1b(nc, S, H, TO, xin, ysel_l, hm, sg_l, w["Wb"], w["Wout"], x1T[l], xr=xr2)
            phase_C2a(nc, S, H, TO, x1T[l], w["Wup"], w["fcw"], w["fcb"], w["gffn"], act_l)
            if l == 0:
                phase_C2b(nc, S, H, TO, x1T[l], act_l, w["Wdown"], x2T0, xsend=xsend)
                for i in range(8):
                    S.collective("AllGather", [xsend[128 * i:128 * (i + 1), :]], [xg[i].rearrange("r p t -> (r p) t")], groups,
                                 r=[("xsend", c0) for (c0, nb) in blocks_of(H, TO, 512)], w=["xg"])
            else:
                phase_C2b(nc, S, H, TO, x1T[l], act_l, w["Wdown"], x1T[l], gfin, outT)
        S.finish()
        S.emit()
    return nc


def kernel_fused(inp):
    x = inp["x"]
    B, T, _ = x.shape
    TO = T // 2
    ncore = 2 * B
    nc = _get_nc(("F", T, ncore), lambda: build_fused(T, ncore))
    in_maps = []
    per_rank = {}
    for p in range(2):
        d = {}
        for l in range(2):
            for k, v in prep_AB_inputs(p, l, inp, T).items():
                d[f"{k}_l{l}"] = v
            for k, v in prep_C_inputs(l, inp, False).items():
                if k != "gmix":
                    d[f"{k}_l{l}"] = v
        d["gfin"] = colvec(inp["norm_final"])
        d["hmask"] = np.ascontiguousarray(np.broadcast_to(np.array([1.0 - p, float(p)], np.float32), (128, 2)))
        per_rank[p] = d
    for c in range(ncore):
        b, p = c // 2, c % 2
        d = dict(per_rank[p])
        xTb = np.ascontiguousarray(x[b].T)
        d["xT"] = xTb
        if p == 0:
            d["xTo"] = np.ascontiguousarray(np.concatenate([np.zeros((D, 4), np.float32), xTb[:, :TO]], axis=1))
        else:
            d["xTo"] = np.ascontiguousarray(xTb[:, TO - 4:])
        in_maps.append(d)
    res = run_bass_kernel_spmd(nc, in_maps, core_ids=list(range(ncore)))
    o = [np.asarray(r["outT"]) for r in res.results]
    out = np.stack([np.concatenate([o[2 * b], o[2 * b + 1]], axis=1).T for b in range(B)]).astype(np.float32)
    return np.ascontiguousarray(out)


FUSED = True


def kernel(**inputs):
    inp = {k: np.asarray(v, dtype=np.float32) for k, v in inputs.items()}
    if FUSED:
        return kernel_fused(inp)
    return kernel_unfused(inp)


def phase_C1a(nc, S, H, TO, xTo, hmask_d, Wg, bg_d, gmix, sgT, xr=(), ysrc=None, ysel=None, yr=()):
    NB = 512
    with ExitStack() as st:
        A = Tiles(nc, st)
        S.barrier()
        T_ = {}
        wg = A.sb([128, 8, 3072], BF16, "wg")
        stg = [A.sb([128, 2048], F32, "stg") for _ in range(2)]
        xb = [A.sb([128, 8, NB], F32, "xb") for _ in range(2)]
        hT = A.sb([128, 8, NB], BF16, "hT")
        T_["xsq"] = A.sb([128, 8, NB], BF16, "xsq")
        T_["std"] = A.sb([128, NB], F32, "std")
        T_["rstd"] = A.sb([128, NB], F32, "rstd")
        T_["epsc"] = A.sb([128, 1], F32, "epsc")
        ones_bf = A.sb([128, 128], BF16, "ones")
        gcol = A.sb([128, 8], F32, "gcol")
        bg = A.sb([128, 24], F32, "bg")
        hmask = A.sb([128, 2], F32, "hmask")
        sgo = [A.sb([128, NB], BF16, "sgo") for _ in range(4)]
        if ysrc is not None:
            ya = A.sb([128, 24, NB], BF16, "ya")
            yb_ = A.sb([128, 24, NB], BF16, "yb_")
            ysv = ysel.rearrange("(c p) t -> p c t", p=128)
        ps_ss = A.ps([128, 512], F32, "ps_ss")
        psg = [A.ps([128, 512], F32, "psg") for _ in range(4)]
        S.pool(lambda e: e.memset(ones_bf[:, :], 1.0), w=["ones"])
        S.pool(lambda e: e.memset(T_["epsc"][:, :], EPS), w=["epsc"])
        S.dma(gcol[:, :], gmix, w=["gcol"])
        S.dma(bg[:, :], bg_d, w=["bg"])
        S.dma(hmask[:, :], hmask_d, w=["hmask"])
        load_w_generic(S, lambda c0, n: wg[:, :, c0:c0 + n], Wg.rearrange("(c p) n -> p c n", p=128), 8, 3072, 256, stg, "stg", "wg")
        xv = xTo.rearrange("(c p) t -> p c t", p=128)
        kk = 0
        blks = blocks_of(H, TO, NB)

        def load_x(bi):
            c0_, nb_ = blks[bi]
            S.dma(xb[bi % 2][:, :, :nb_], xv[:, :, c0_:c0_ + nb_], r=list(xr), w=[("xb", bi % 2)])

        load_x(0)
        for bi, (c0, nb) in enumerate(blks):
            xt = xb[bi % 2]
            xtag = ("xb", bi % 2)
            if bi + 1 < len(blks):
                load_x(bi + 1)
            if ysrc is not None:
                for r_ in range(2):
                    S.dma(ya[:, 12 * r_:12 * (r_ + 1), :nb], ysrc[0][:, r_, :, c0:c0 + nb], r=list(yr), w=["ya"])
                    S.dma(yb_[:, 12 * r_:12 * (r_ + 1), :nb], ysrc[1][:, r_, :, c0:c0 + nb], r=list(yr), w=["yb_"])
                S.dve(lambda e, nb=nb: e.tensor_scalar(out=ya[:, :, :nb], in0=ya[:, :, :nb], scalar1=hmask[:, 0:1], scalar2=None, op0=ALU.mult),
                      r=["ya", "hmask"], w=["ya"])
                S.dve(lambda e, nb=nb: e.scalar_tensor_tensor(out=ya[:, :, :nb], in0=yb_[:, :, :nb], scalar=hmask[:, 1:2], in1=ya[:, :, :nb], op0=ALU.mult, op1=ALU.add),
                      r=["ya", "yb_", "hmask"], w=["ya"])
                S.dma(ysv[:, :, c0:c0 + nb], ya[:, :, :nb], r=["ya"], w=[("ysel", c0)])
            if bi == 0 and H > 0:
                S.dve(lambda e, xt=xt, nb=nb: e.tensor_scalar(out=xt[:, :, :nb], in0=xt[:, :, :nb], scalar1=hmask[:, 1:2], scalar2=None, op0=ALU.mult),
                      r=[xtag, "hmask"], w=[xtag])
            rms_block(S, T_, ones_bf, xt, gcol, hT, ps_ss, xtag, nb)
            htag = ("hT", id(hT))
            for n_ in range(3):
                for dc in range(8):
                    pg = psg[kk % 4]
                    pgt = ("psg", kk % 4)
                    so = sgo[kk % 4]
                    sot = ("sgo", kk % 4)
                    kk += 1
                    for c in range(8):
                        S.pe(lambda e, c=c, pg=pg, n_=n_, dc=dc, nb=nb: e.matmul(pg[:, :nb], lhsT=wg[:, c, n_ * 1024 + dc * 128:n_ * 1024 + (dc + 1) * 128], rhs=hT[:, c, :nb],
                                                                               start=(c == 0), stop=(c == 7)), r=["wg", htag], w=[pgt])
                    S.act(lambda e, pg=pg, so=so, n_=n_, dc=dc, nb=nb: e.activation(out=so[:, :nb], in_=pg[:, :nb], func=AF.Sigmoid, bias=bg[:, n_ * 8 + dc:n_ * 8 + dc + 1]),
                          r=[pgt, "bg"], w=[sot])
                    k_ = n_ * 8 + dc
                    S.dma(sgT[k_ * 128:(k_ + 1) * 128, c0:c0 + nb], so[:, :nb], r=[sot], w=[("sgT", c0)])
        S.emit()


def phase_C1b(nc, S, H, TO, xTo, ysel, hmask_d, sgT, Wb, Wout, x1T, xr=()):
    NB = 512
    with ExitStack() as st:
        A = Tiles(nc, st)
        S.barrier()
        wb = A.sb([128, 3, 8, 1024], BF16, "wb")
        wo = A.sb([128, 8, 1024], BF16, "wo")
        stg = [A.sb([128, 2048], F32, "stg") for _ in range(2)]
        xb = A.sb([128, 8, NB], F32, "xb")
        yb = [A.sb([128, 24, NB], BF16, "yb") for _ in range(2)]
        sgr = [A.sb([128, 3, NB], BF16, "sgr") for _ in range(4)]
        hmask = A.sb([128, 2], F32, "hmask")
        mg = [A.sb([128, NB], F32, "mg") for _ in range(2)]
        mgT = A.sb([128, 8, NB], BF16, "mgT")
        tmp = [A.sb([128, NB], F32, "tmp") for _ in range(2)]
        psb = [A.ps([128, 512], F32, "psb") for _ in range(4)]
        pso = [A.ps([128, 512], F32, "pso") for _ in range(2)]
        S.dma(hmask[:, :], hmask_d, w=["hmask"])
        k = 0
        for n_ in range(3):
            k = load_w_generic(S, lambda c0, n, n_=n_: wb[:, n_, :, c0:c0 + n], Wb[n_].rearrange("(c p) n -> p c n", p=128), 8, 1024, 256, stg, "stg", "wb", k)
        k = load_w_generic(S, lambda c0, n: wo[:, :, c0:c0 + n], Wout.rearrange("(c p) n -> p c n", p=128), 8, 1024, 256, stg, "stg", "wo", k)
        xv = xTo.rearrange("(c p) t -> p c t", p=128)
        yv = ysel.rearrange("(c p) t -> p c t", p=128)
        sv = sgT.rearrange("(n d p) t -> p d n t", n=3, p=128)
        ov = x1T.rearrange("(c p) t -> p c t", p=128)
        blks = blocks_of(H, TO, NB)

        def load_y(bi):
            c0_, nb_ = blks[bi]
            S.dma(yb[bi % 2][:, :, :nb_], yv[:, :, c0_:c0_ + nb_], r=[("ysel", c0_)], w=[("yb", bi % 2)])

        kk = 0
        ks = 0
        load_y(0)
        for bi, (c0, nb) in enumerate(blks):
            yt = yb[bi % 2]
            ytag = ("yb", bi % 2)
            if bi + 1 < len(blks):
                load_y(bi + 1)
            S.dma(xb[:, :, :nb], xv[:, :, c0:c0 + nb], r=list(xr), w=["xb"])
            if bi == 0 and H > 0:
                S.dve(lambda e, nb=nb: e.tensor_scalar(out=xb[:, :, :nb], in0=xb[:, :, :nb], scalar1=hmask[:, 1:2], scalar2=None, op0=ALU.mult),
                      r=["xb", "hmask"], w=["xb"])
            for dc in range(8):
                sg = sgr[ks % 4]
                sgtag = ("sgr", ks % 4)
                ks += 1
                S.dma(sg[:, :, :nb], sv[:, dc, :, c0:c0 + nb], r=[("sgT", c0)], w=[sgtag])
                mgd = mg[dc % 2]
                mgtag = ("mg", dc % 2)
                for n_ in range(3):
                    pbk = psb[kk % 4]
                    pbt = ("psb", kk % 4)
                    tm = tmp[kk % 2]
                    tmtag = ("tmp", kk % 2)
                    kk += 1
                    for c in range(8):
                        ych = (c // 4) * 12 + n_ * 4 + (c % 4)
                        S.pe(lambda e, c=c, pbk=pbk, n_=n_, dc=dc, nb=nb, ych=ych, yt=yt: e.matmul(pbk[:, :nb], lhsT=wb[:, n_, c, dc * 128:(dc + 1) * 128], rhs=yt[:, ych, :nb],
                                                                                                start=(c == 0), stop=(c == 7)), r=["wb", ytag], w=[pbt])
                    if n_ == 0:
                        S.dve(lambda e, pbk=pbk, nb=nb, sg=sg, mgd=mgd: e.tensor_tensor(out=mgd[:, :nb], in0=pbk[:, :nb], in1=sg[:, 0, :nb], op=ALU.mult),
                              r=[pbt, sgtag], w=[mgtag])
                    else:
                        S.dve(lambda e, pbk=pbk, tm=tm, nb=nb, sg=sg, n_=n_: e.tensor_tensor(out=tm[:, :nb], in0=pbk[:, :nb], in1=sg[:, n_, :nb], op=ALU.mult),
                              r=[pbt, sgtag], w=[tmtag])
                        if n_ == 1:
                            S.pool(lambda e, tm=tm, nb=nb, mgd=mgd: e.tensor_tensor(out=mgd[:, :nb], in0=mgd[:, :nb], in1=tm[:, :nb], op=ALU.add),
                                   r=[mgtag, tmtag], w=[mgtag])
                        else:
                            S.pool(lambda e, tm=tm, dc=dc, nb=nb, mgd=mgd: e.tensor_tensor(out=mgT[:, dc, :nb], in0=mgd[:, :nb], in1=tm[:, :nb], op=ALU.add),
                                   r=[mgtag, tmtag], w=[("mgT", dc)])
            for dc in range(8):
                po = pso[dc % 2]
                pot = ("pso", dc % 2)
                for c in range(8):
                    S.pe(lambda e, c=c, po=po, dc=dc, nb=nb: e.matmul(po[:, :nb], lhsT=wo[:, c, dc * 128:(dc + 1) * 128], rhs=mgT[:, c, :nb], start=(c == 0), stop=(c == 7)),
                         r=["wo"] + [("mgT", c_) for c_ in range(8)], w=[pot])
                S.dve(lambda e, po=po, dc=dc, nb=nb: e.tensor_tensor(out=xb[:, dc, :nb], in0=po[:, :nb], in1=xb[:, dc, :nb], op=ALU.add), r=[pot, "xb"], w=["xb"])
            S.dma(ov[:, :, c0:c0 + nb], xb[:, :, :nb], r=["xb"], w=[("x1T", c0)])
        S.emit()


def phase_C2a(nc, S, H, TO, x1T, Wup, cw_d, cb_d, gffn, actD):
    NB = 512
    with ExitStack() as st:
        A = Tiles(nc, st)
        S.barrier()
        T_ = {}
        wu = A.sb([128, 8, 5632], BF16, "wu")
        stg = [A.sb([128, 2048], F32, "stg") for _ in range(2)]
        xb = [A.sb([128, 8, NB], F32, "xb") for _ in range(2)]
        hT = A.sb([128, 8, NB], BF16, "hT")
        T_["xsq"] = A.sb([128, 8, NB], BF16, "xsq")
        T_["std"] = A.sb([128, NB], F32, "std")
        T_["rstd"] = A.sb([128, NB], F32, "rstd")
        T_["epsc"] = A.sb([128, 1], F32, "epsc")
        ones_bf = A.sb([128, 128], BF16, "ones")
        gcol = A.sb([128, 8], F32, "gcol")
        cw = A.sb([128, 44, 3], F32, "cw")
        cb = A.sb([128, 44], F32, "cb")
        halo = A.sb([128, 44, 2], F32, "halo")
        ua = [A.sb([128, NB + 2], F32, "ua") for _ in range(2)]
        ug = [A.sb([128, NB + 2], F32, "ug") for _ in range(2)]
        aa = [A.sb([128, NB], F32, "aa") for _ in range(2)]
        ag = [A.sb([128, NB], F32, "ag") for _ in range(2)]
        acto = [A.sb([128, NB], BF16, "acto") for _ in range(4)]
        ps_ss = A.ps([128, 512], F32, "ps_ss")
        psa = [A.ps([128, 512], F32, "psa") for _ in range(3)]
        psgt = [A.ps([128, 512], F32, "psgt") for _ in range(3)]
        S.pool(lambda e: e.memset(ones_bf[:, :], 1.0), w=["ones"])
        S.pool(lambda e: e.memset(T_["epsc"][:, :], EPS), w=["epsc"])
        S.dve(lambda e: e.memset(halo[:, :, :], 0.0), w=["halo"])
        S.dma(gcol[:, :], gffn, w=["gcol"])
        S.dma(cw[:, :, :], cw_d, w=["cw"])
        S.dma(cb[:, :], cb_d, w=["cb"])
        load_w_generic(S, lambda c0, n: wu[:, :, c0:c0 + n], Wup.rearrange("(c p) n -> p c n", p=128), 8, 5632, 256, stg, "stg", "wu")
        xv = x1T.rearrange("(c p) t -> p c t", p=128)
        kk = 0
        blks = blocks_of(H, TO, NB)

        def load_x(bi):
            c0_, nb_ = blks[bi]
            S.dma(xb[bi % 2][:, :, :nb_], xv[:, :, c0_:c0_ + nb_], r=[("x1T", c0_)], w=[("xb", bi % 2)])

        load_x(0)
        for bi, (c0, nb) in enumerate(blks):
            xt = xb[bi % 2]
            xtag = ("xb", bi % 2)
            if bi + 1 < len(blks):
                load_x(bi + 1)
            rms_block(S, T_, ones_bf, xt, gcol, hT, ps_ss, xtag, nb)
            htag = ("hT", id(hT))
            for fc in range(22):
                b2 = kk % 2
                b3 = kk % 3
                b4 = kk % 4
                kk += 1
                for (ps, pst, off, ut, utag, acc, atag, hc) in ((psa[b3], ("psa", b3), 0, ua[b2], ("ua", b2), aa[b2], ("aa", b2), fc),
                                                                (psgt[b3], ("psgt", b3), 2816, ug[b2], ("ug", b2), ag[b2], ("ag", b2), 22 + fc)):
                    for c in range(8):
                        S.pe(lambda e, c=c, ps=ps, off=off, fc=fc, nb=nb: e.matmul(ps[:, :nb], lhsT=wu[:, c, off + fc * 128:off + (fc + 1) * 128], rhs=hT[:, c, :nb],
                                                                                 start=(c == 0), stop=(c == 7)), r=["wu", htag], w=[pst])
                    S.act(lambda e, ut=ut, hc=hc: e.copy(out=ut[:, 0:2], in_=halo[:, hc, :]), r=[("halo", hc)], w=[utag])
                    S.act(lambda e, ut=ut, ps=ps, nb=nb: e.copy(out=ut[:, 2:2 + nb], in_=ps[:, :nb]), r=[pst], w=[utag])
                    S.act(lambda e, ut=ut, hc=hc, nb=nb: e.copy(out=halo[:, hc, :], in_=ut[:, nb:nb + 2]), r=[utag], w=[("halo", hc)])
                    S.dve(lambda e, ut=ut, acc=acc, hc=hc, nb=nb: e.tensor_scalar(out=acc[:, :nb], in0=ut[:, 0:nb], scalar1=cw[:, hc, 0:1], scalar2=cb[:, hc:hc + 1],
                                                                                op0=ALU.mult, op1=ALU.add), r=[utag, "cw", "cb"], w=[atag])
                    for j in (1, 2):
                        S.dve(lambda e, ut=ut, acc=acc, hc=hc, nb=nb, j=j: e.scalar_tensor_tensor(out=acc[:, :nb], in0=ut[:, j:j + nb], scalar=cw[:, hc, j:j + 1], in1=acc[:, :nb],
                                                                                                  op0=ALU.mult, op1=ALU.add), r=[utag, "cw", atag], w=[atag])
                S.act(lambda e, b2=b2, nb=nb: e.activation(out=ag[b2][:, :nb], in_=ag[b2][:, :nb], func=AF.Silu), r=[("ag", b2)], w=[("ag", b2)])
                S.pool(lambda e, b2=b2, b4=b4, nb=nb: e.tensor_tensor(out=acto[b4][:, :nb], in0=aa[b2][:, :nb], in1=ag[b2][:, :nb], op=ALU.mult),
                       r=[("aa", b2), ("ag", b2)], w=[("acto", b4)])
                S.dma(actD[fc * 128:(fc + 1) * 128, c0:c0 + nb], acto[b4][:, :nb], r=[("acto", b4)], w=[("actD", c0)])
        S.emit()


def phase_C2b(nc, S, H, TO, x1T, actD, Wdown, x2T, final_g=None, outT=None, xsend=None):
    NB = 512
    with ExitStack() as st:
        A = Tiles(nc, st)
        S.barrier()
        T_ = {}
        wd = A.sb([128, 22, 1024], BF16, "wd")
        stg = [A.sb([128, 2048], F32, "stg") for _ in range(2)]
        xb = [A.sb([128, 8, NB], F32, "xb") for _ in range(2)]
        actb = [A.sb([128, 22, NB], BF16, "actb") for _ in range(2)]
        T_["xsq"] = A.sb([128, 8, NB], BF16, "xsq")
        T_["std"] = A.sb([128, NB], F32, "std")
        T_["rstd"] = A.sb([128, NB], F32, "rstd")
        T_["epsc"] = A.sb([128, 1], F32, "epsc")
        ones_bf = A.sb([128, 128], BF16, "ones")
        gfin = A.sb([128, 8], F32, "gfin")
        oT = A.sb([128, 8, NB], F32, "oT")
        ps_ss = A.ps([128, 512], F32, "ps_ss")
        psd = [A.ps([128, 512], F32, "psd") for _ in range(4)]
        S.pool(lambda e: e.memset(ones_bf[:, :], 1.0), w=["ones"])
        S.pool(lambda e: e.memset(T_["epsc"][:, :], EPS), w=["epsc"])
        if final_g is not None:
            S.dma(gfin[:, :], final_g, w=["gfin"])
        load_w_generic(S, lambda c0, n: wd[:, :, c0:c0 + n], Wdown.rearrange("(c p) n -> p c n", p=128), 22, 1024, 64, stg, "stg", "wd")
        xv = x1T.rearrange("(c p) t -> p c t", p=128)
        av = actD.rearrange("(c p) t -> p c t", p=128)
        ov = x2T.rearrange("(c p) t -> p c t", p=128)
        kk = 0
        blks = blocks_of(H, TO, NB)

        def load_xa(bi):
            c0_, nb_ = blks[bi]
            S.dma(xb[bi % 2][:, :, :nb_], xv[:, :, c0_:c0_ + nb_], r=[("x1T", c0_)], w=[("xb", bi % 2)])
            S.dma(actb[bi % 2][:, :, :nb_], av[:, :, c0_:c0_ + nb_], r=[("actD", c0_)], w=[("actb", bi % 2)])

        load_xa(0)
        for bi, (c0, nb) in enumerate(blks):
            xt = xb[bi % 2]
            xtag = ("xb", bi % 2)
            at = actb[bi % 2]
            attag = ("actb", bi % 2)
            if bi + 1 < len(blks):
                load_xa(bi + 1)
            for dc in range(8):
                po = psd[kk % 4]
                pot = ("psd", kk % 4)
                kk += 1
                for fc in range(22):
                    S.pe(lambda e, fc=fc, po=po, dc=dc, nb=nb, at=at: e.matmul(po[:, :nb], lhsT=wd[:, fc, dc * 128:(dc + 1) * 128], rhs=at[:, fc, :nb], start=(fc == 0), stop=(fc == 21)),
                         r=["wd", attag], w=[pot])
                S.dve(lambda e, po=po, dc=dc, nb=nb, xt=xt: e.tensor_tensor(out=xt[:, dc, :nb], in0=po[:, :nb], in1=xt[:, dc, :nb], op=ALU.add), r=[pot, xtag], w=[xtag])
            if final_g is None:
                S.dma(ov[:, :, c0:c0 + nb], xt[:, :, :nb], r=[xtag], w=[("x2T", c0)])
                if xsend is not None and not (bi == 0 and H > 0):
                    S.dma(xsend.rearrange("(c p) t -> p c t", p=128)[:, :, c0 - H:c0 - H + nb], xt[:, :, :nb], r=[xtag], w=[("xsend", c0)])
            elif not (bi == 0 and H > 0):
                xsq, std, rstd = T_["xsq"], T_["std"], T_["rstd"]
                S.act(lambda e, xt=xt, nb=nb: e.activation(out=xsq[:, :, :nb], in_=xt[:, :, :nb], func=AF.Square), r=[xtag], w=["xsq"])
                for c in range(8):
                    S.pe(lambda e, c=c, nb=nb: e.matmul(ps_ss[:, :nb], lhsT=ones_bf[:, :], rhs=xsq[:, c, :nb], start=(c == 0), stop=(c == 7)), r=["xsq", "ones"], w=["ps_ss"])
                S.act(lambda e, nb=nb: e.activation(out=std[:, :nb], in_=ps_ss[:, :nb], func=AF.Sqrt, bias=T_["epsc"][:, 0:1], scale=1.0 / D), r=["ps_ss", "epsc"], w=["std"])
                S.dve(lambda e, nb=nb: e.reciprocal(out=rstd[:, :nb], in_=std[:, :nb]), r=["std"], w=["rstd"])
                for c in range(8):
                    S.dve(lambda e, c=c, nb=nb, xt=xt: e.scalar_tensor_tensor(out=oT[:, c, :nb], in0=xt[:, c, :nb], scalar=gfin[:, c:c + 1], in1=rstd[:, :nb], op0=ALU.mult, op1=ALU.mult),
                          r=[xtag, "rstd", "gfin"], w=["oT"])
                S.dma(outT.rearrange("(c p) t -> p c t", p=128)[:, :, c0 - H:c0 - H + nb], oT[:, :, :nb], r=["oT"], w=[("outT", c0)])
        S.emit()
```

```python
import numpy as np
from contextlib import ExitStack
import concourse.bass as bass
import concourse.mybir as mybir
from concourse.bass_utils import run_bass_kernel_spmd

F32 = mybir.dt.float32
BF16 = mybir.dt.bfloat16
AF = mybir.ActivationFunctionType
ALU = mybir.AluOpType
AX = mybir.AxisListType

COMPUTE = ("pe", "act", "dve", "pool")
EPOCH = 4096
NEPS = 3
NDMASEM = 20
NCSEM = 56
SAME_ENGINE_SYNC = True


def _is_ps(t):
    t0 = t[0] if isinstance(t, tuple) else t
    return isinstance(t0, str) and (t0[:2] in ("ps", "pb", "po", "pu", "pg") or t0 == "ptr")


class Op:
    __slots__ = ("eng", "fn", "deps", "flag", "sem", "target", "dma", "idx", "pre", "cc")

    def __init__(self, eng, fn, dma):
        self.eng = eng
        self.fn = fn
        self.dma = dma
        self.deps = []
        self.flag = False
        self.sem = None
        self.target = 0
        self.idx = 0
        self.pre = None
        self.cc = False


class Sched:
    def __init__(self, nc, stack):
        self.nc = nc
        self.ops = {e: [] for e in COMPUTE + ("sp",)}
        self.lastw = {}
        self.ps_true_w = {}
        self.readers = {}
        self.nops = {e: 0 for e in COMPUTE + ("sp",)}
        self.flagcnt = {e: 0 for e in COMPUTE}
        self.esem = {e: [stack.enter_context(nc.semaphore(f"s_{e}{i}")) for i in range(NEPS)] for e in COMPUTE}
        self.dsem = [stack.enter_context(nc.semaphore(f"s_dma{i}")) for i in range(NDMASEM)]
        self.ndma = 0
        self.csem = [stack.enter_context(nc.semaphore(f"s_cc{i}")) for i in range(NCSEM)]
        self.ncoll = 0
        self.dma_hist = []
        self.barrier_pending = {}
        self.last_op = {}
        self.waited = {e: {} for e in COMPUTE + ("sp",)}
        self.all_dma = []
        self.dma_barriered = 0

    def op(self, eng, fn, r=(), w=(), dma=False):
        ps_r = [t for t in r if _is_ps(t)]
        if ps_r:
            w = list(w) + [t for t in ps_r if t not in w]
        o = Op(eng, fn, dma)
        o.idx = self.nops[eng]
        self.nops[eng] += 1
        deps = {}

        def add(d, hazard):
            if d is None or d is o:
                return
            if d.dma:
                deps[id(d)] = d
            else:
                if d.eng == eng and not dma and (eng == "pe" or not SAME_ENGINE_SYNC or not hazard):
                    return
                k = ("e", d.eng)
                if k not in deps or deps[k].idx < d.idx:
                    deps[k] = d

        for t in r:
            if t in ps_r:
                add(self.ps_true_w.get(t), True)
            else:
                add(self.lastw.get(t), True)
        for t in w:
            hz = t not in ps_r
            add(self.lastw.get(t), hz)
            rd = self.readers.get(t)
            if rd:
                for d in rd.values():
                    add(d, hz)
            if hz and _is_ps(t):
                self.ps_true_w[t] = o
        bp = self.barrier_pending.pop(eng, None)
        if bp:
            for d in bp:
                add(d, True)
        for t in r:
            rd = self.readers.setdefault(t, {})
            if dma:
                rd[id(o)] = o
            else:
                rd[eng] = o
        for t in w:
            self.lastw[t] = o
            self.readers[t] = {}
        o.deps = list(deps.values())
        for d in o.deps:
            d.flag = True
        if dma:
            o.flag = True
            self.all_dma.append(o)
        self.ops[eng].append(o)
        self.last_op[eng] = o
        return o

    def pe(self, fn, r=(), w=()):
        return self.op("pe", fn, r, w)

    def act(self, fn, r=(), w=()):
        return self.op("act", fn, r, w)

    def dve(self, fn, r=(), w=()):
        return self.op("dve", fn, r, w)

    def pool(self, fn, r=(), w=()):
        return self.op("pool", fn, r, w)

    def dma(self, out, in_, r=(), w=(), eng="sp"):
        return self.op(eng, lambda e: e.dma_start(out=out, in_=in_), r, w, dma=True)

    def collective(self, kind, ins, outs, groups, r=(), w=()):
        o = self.op("pool", lambda e: e.collective_compute(kind, ALU.bypass, replica_groups=groups, ins=ins, outs=outs), r, w, dma=True)
        o.sem = self.csem[self.ncoll]
        self.ncoll += 1
        o.target = 1
        o.cc = True
        return o

    def barrier(self):
        b = [o for o in self.last_op.values() if not o.dma]
        b += [o for o in self.all_dma[self.dma_barriered:] if not o.cc]
        self.dma_barriered = len(self.all_dma)
        for o in b:
            o.flag = True
        self.barrier_pending = {e: b for e in COMPUTE + ("sp",)}

    def emit(self):
        nc = self.nc
        for e in COMPUTE:
            for o in self.ops[e]:
                if o.flag and o.sem is None:
                    c = self.flagcnt[e]
                    self.flagcnt[e] += 1
                    ep = c // EPOCH
                    o.sem = self.esem[e][ep % NEPS]
                    o.target = (ep // NEPS) * EPOCH + (c % EPOCH) + 1
        for e in COMPUTE + ("sp",):
            for o in self.ops[e]:
                if o.dma and o.sem is None:
                    n = self.ndma
                    self.ndma += 1
                    o.sem = self.dsem[n % NDMASEM]
                    o.target = 16 * (n // NDMASEM + 1)
                    if n >= NDMASEM:
                        o.pre = (o.sem, 16 * (n // NDMASEM))
        with nc.Block() as block:
            @block.tensor
            def _(eng):
                self._emit_eng("pe", eng)

            @block.scalar
            def _(eng):
                self._emit_eng("act", eng)

            @block.vector
            def _(eng):
                self._emit_eng("dve", eng)

            @block.gpsimd
            def _(eng):
                self._emit_eng("pool", eng)

            @block.sync
            def _(eng):
                self._emit_eng("sp", eng)
        for e in self.ops:
            self.ops[e] = []

    def _emit_eng(self, name, eng):
        waited = self.waited[name]
        for o in self.ops[name]:
            for d in o.deps:
                key = id(d.sem)
                if waited.get(key, 0) < d.target:
                    eng.wait_ge(d.sem, d.target)
                    waited[key] = d.target
            if o.pre is not None:
                key = id(o.pre[0])
                if waited.get(key, 0) < o.pre[1]:
                    eng.wait_ge(o.pre[0], o.pre[1])
                    waited[key] = o.pre[1]
            inst = o.fn(eng)
            if o.flag:
                inst.then_inc(o.sem, 16 if (o.dma and not o.cc) else 1)

    def finish(self, eng="sp"):
        self.barrier()
        self.op(eng, lambda e: e.nop(), r=(), w=())


class Tiles:
    CNT = [0]

    def __init__(self, nc, stack):
        self.nc = nc
        self.stack = stack

    def sb(self, shape, dtype, name=None):
        Tiles.CNT[0] += 1
        return self.stack.enter_context(self.nc.sbuf_tensor(f"{name or 't'}_{Tiles.CNT[0]}", list(shape), dtype))

    def ps(self, shape, dtype=F32, name=None):
        Tiles.CNT[0] += 1
        return self.stack.enter_context(self.nc.psum_tensor(f"{name or 'p'}_{Tiles.CNT[0]}", list(shape), dtype))


D = 1024
NFM = 3088
NTM = 2308
NA = NFM + NTM
EPS = 1e-6
FM_GQ, FM_GK, FM_MX, FM_FQ, FM_FK, FM_OG, FM_LR = 0, 256, 512, 1536, 2048, 2560, 3072
TM_GK, TM_GV, TM_GR, TM_MZ, TM_FV, TM_FF = 0, 256, 768, 1280, 1792, 2304


def cdiv(a, b):
    return (a + b - 1) // b


def rms_block(S, T_, ones_bf, xt, gcol, hT, ps_ss, tag, nb):
    xsq, std, rstd = T_["xsq"], T_["std"], T_["rstd"]
    S.act(lambda e: e.activation(out=xsq[:, :, :nb], in_=xt[:, :, :nb], func=AF.Square), r=[tag], w=["xsq"])
    for c in range(8):
        S.pe(lambda e, c=c: e.matmul(ps_ss[:, :nb], lhsT=ones_bf[:, :], rhs=xsq[:, c, :nb], start=(c == 0), stop=(c == 7)),
             r=["xsq", "ones"], w=["ps_ss"])
    S.act(lambda e: e.activation(out=std[:, :nb], in_=ps_ss[:, :nb], func=AF.Sqrt, bias=T_["epsc"][:, 0:1], scale=1.0 / D),
          r=["ps_ss", "epsc"], w=["std"])
    S.dve(lambda e: e.reciprocal(out=rstd[:, :nb], in_=std[:, :nb]), r=["std"], w=["rstd"])
    for c in range(8):
        S.dve(lambda e, c=c: e.scalar_tensor_tensor(out=hT[:, c, :nb], in0=xt[:, c, :nb], scalar=gcol[:, c:c + 1],
                                                     in1=rstd[:, :nb], op0=ALU.mult, op1=ALU.mult),
              r=[tag, "rstd", "gcol"], w=[("hT", id(hT))])


def load_weights_bf16(S, T_, wsb, wdram, ncols, wtag, stg, stgtag):
    wv = wdram.rearrange("(c p) n -> p c n", p=128)
    G = 512
    for gi in range(cdiv(ncols, G)):
        c0 = gi * G
        n = min(G, ncols - c0)
        st = stg[gi % 2]
        tg = (stgtag, gi % 2)
        S.dma(st[:, :, :n], wv[:, :, c0:c0 + n], r=[], w=[tg])
        eng = [S.pool, S.dve][gi % 2]
        eng(lambda e, st=st, c0=c0, n=n: e.tensor_copy(out=wsb[:, :, c0:c0 + n], in_=st[:, :, :n]), r=[tg], w=[wtag])


def phase_A(nc, S, T, xT, wA, gmix, pFM, pTM, xblk=None, xr=()):
    NB = 512
    with ExitStack() as st:
        A = Tiles(nc, st)
        T_ = {}
        wsb = A.sb([128, 8, NA], BF16, "wsb")
        stg = [A.sb([128, 8, 512], F32, "wstg") for _ in range(2)]
        xb = [A.sb([128, 8, NB], F32, "xb") for _ in range(2)]
        hTs = [A.sb([128, 8, NB], BF16, "hT") for _ in range(2)]
        T_["xsq"] = A.sb([128, 8, NB], BF16, "xsq")
        T_["std"] = A.sb([128, NB], F32, "std")
        T_["rstd"] = A.sb([128, NB], F32, "rstd")
        T_["epsc"] = A.sb([128, 1], F32, "epsc")
        ones_bf = A.sb([128, 128], BF16, "ones")
        gcol = A.sb([128, 8], F32, "gcol")
        fmst = [A.sb([128, NB], BF16, "fmst") for _ in range(4)]
        tmst = [A.sb([128, NTM], BF16, "tmst") for _ in range(2)]
        ps_ss = A.ps([128, NB], F32, "ps_ss")
        psr = [A.ps([128, NB], F32, "psr") for _ in range(4)]
        S.barrier()
        S.pool(lambda e: e.memset(ones_bf[:, :], 1.0), w=["ones"])
        S.pool(lambda e: e.memset(T_["epsc"][:, :], EPS), w=["epsc"])
        S.dma(gcol[:, :], gmix, w=["gcol"])
        load_weights_bf16(S, T_, wsb, wA, NA, "wsb", stg, "wstg")
        if xblk is None:
            xv = xT.rearrange("(c p) t -> p c t", p=128)
            xblk = lambda j: xv[:, :, j * NB:(j + 1) * NB]
        k = 0
        for j in range(T // NB):
            xt = xb[j % 2]
            xtag = ("xb", j % 2)
            hT = hTs[j % 2]
            htag = ("hT", id(hT))
            S.dma(xt[:, :, :], xblk(j), r=list(xr), w=[xtag])
            rms_block(S, T_, ones_bf, xt, gcol, hT, ps_ss, xtag, NB)
            for m in range(cdiv(NFM, 128)):
                mm = min(128, NFM - m * 128)
                ps = psr[k % 4]
                ptag = ("psr", k % 4)
                so = fmst[k % 4]
                stag = ("fmst", k % 4)
                for c in range(8):
                    S.pe(lambda e, c=c, ps=ps, m=m, mm=mm, hT=hT: e.matmul(ps[:mm, :], lhsT=wsb[:, c, m * 128:m * 128 + mm], rhs=hT[:, c, :],
                                                                        start=(c == 0), stop=(c == 7)), r=["wsb", htag], w=[ptag])
                if k % 2 == 0:
                    S.act(lambda e, ps=ps, so=so, mm=mm: e.copy(out=so[:mm, :], in_=ps[:mm, :]), r=[ptag], w=[stag])
                else:
                    S.dve(lambda e, ps=ps, so=so, mm=mm: e.tensor_copy(out=so[:mm, :], in_=ps[:mm, :]), r=[ptag], w=[stag])
                S.dma(pFM[m * 128:m * 128 + mm, j * NB:(j + 1) * NB], so[:mm, :], r=[stag], w=[("pFM", j)])
                k += 1
            for tt in range(NB // 128):
                ti = j * (NB // 128) + tt
                so = tmst[ti % 2]
                stag = ("tmst", ti % 2)
                for n in range(cdiv(NTM, 512)):
                    nn = min(512, NTM - n * 512)
                    ps = psr[k % 4]
                    ptag = ("psr", k % 4)
                    for c in range(8):
                        S.pe(lambda e, c=c, ps=ps, n=n, nn=nn, hT=hT, tt=tt: e.matmul(ps[:, :nn], lhsT=hT[:, c, tt * 128:(tt + 1) * 128],
                                                                                  rhs=wsb[:, c, NFM + n * 512:NFM + n * 512 + nn],
                                                                                  start=(c == 0), stop=(c == 7)), r=["wsb", htag], w=[ptag])
                    if k % 2 == 0:
                        S.act(lambda e, ps=ps, so=so, n=n, nn=nn: e.copy(out=so[:, n * 512:n * 512 + nn], in_=ps[:, :nn]), r=[ptag], w=[stag])
                    else:
                        S.dve(lambda e, ps=ps, so=so, n=n, nn=nn: e.tensor_copy(out=so[:, n * 512:n * 512 + nn], in_=ps[:, :nn]), r=[ptag], w=[stag])
                    k += 1
                S.dma(pTM[ti * 128:(ti + 1) * 128, :], so[:, :], r=[stag], w=[("pTM", ti)])
        S.emit()


class YDst:
    def __init__(self, T, yT=None, ys=None, H=0, key="y"):
        self.T, self.yT, self.ys, self.H, self.key = T, yT, ys, H, key
        self.tokens = []

    def put(self, S, row0, rows_ap_fn, j, tile_fn, rtags):
        NBK = self.T // 512
        half = NBK // 2
        tok = (self.key, row0, j)
        self.tokens.append(tok)
        if self.ys is None:
            S.dma(rows_ap_fn(self.yT, slice(512 * j, 512 * (j + 1))), tile_fn(slice(0, 512)), r=rtags, w=[tok])
            return
        H = self.H
        if j < half:
            S.dma(rows_ap_fn(self.ys[0], slice(H + 512 * j, H + 512 * (j + 1))), tile_fn(slice(0, 512)), r=rtags, w=[tok])
        else:
            S.dma(rows_ap_fn(self.ys[1], slice(H + 512 * (j - half), H + 512 * (j - half + 1))), tile_fn(slice(0, 512)), r=rtags, w=[tok])
        if j == half - 1:
            tok2 = (self.key, row0, "halo")
            self.tokens.append(tok2)
            S.dma(rows_ap_fn(self.ys[1], slice(0, H)), tile_fn(slice(512 - H, 512)), r=rtags, w=[tok2])

    def zero_halo(self, S, A):
        if self.ys is None:
            return
        z = A.sb([128, 12, self.H], BF16, "zhalo")
        S.dve(lambda e: e.memset(z[:, :, :], 0.0), w=["zhalo"])
        tok = (self.key, "zero")
        self.tokens.append(tok)
        S.dma(self.ys[0][:, 0:self.H].rearrange("(a p) t -> p a t", p=128), z[:, :, :], r=["zhalo"], w=[tok])


def make_masks(S, A):
    nc = A.nc
    C = {}
    C["ones_bf"] = A.sb([128, 128], BF16, "ones_bf")
    C["ones_f"] = A.sb([128, 128], F32, "ones_f")
    C["tri_f"] = A.sb([128, 128], F32, "tri_f")
    C["tri_bf"] = A.sb([128, 128], BF16, "tri_bf")
    C["ident_bf"] = A.sb([128, 128], BF16, "ident_bf")
    C["ident_f"] = A.sb([128, 128], F32, "ident_f")
    S.pool(lambda e: e.memset(C["ones_bf"][:, :], 1.0), w=["c_ones_bf"])
    S.pool(lambda e: e.memset(C["ones_f"][:, :], 1.0), w=["c_ones_f"])
    S.pool(lambda e: e.affine_select(out=C["tri_f"][:, :], in_=C["ones_f"][:, :], pattern=[[1, 128]],
                                     compare_op=ALU.is_ge, fill=0.0, base=0, channel_multiplier=-1),
           r=["c_ones_f"], w=["c_tri_f"])
    S.pool(lambda e: e.tensor_copy(out=C["tri_bf"][:, :], in_=C["tri_f"][:, :]), r=["c_tri_f"], w=["c_tri_bf"])
    S.pool(lambda e: e.affine_select(out=C["ident_f"][:, :], in_=C["ones_f"][:, :], pattern=[[1, 128]],
                                     compare_op=ALU.is_equal, fill=0.0, base=0, channel_multiplier=-1),
           r=["c_ones_f"], w=["c_ident_f"])
    S.pool(lambda e: e.tensor_copy(out=C["ident_bf"][:, :], in_=C["ident_f"][:, :]), r=["c_ident_f"], w=["c_ident_bf"])
    return C


def phase_fox(nc, S, T, pFM, pTM, foxbf_t, yT, zero_halo=False):
    NT = T // 128
    NQ = T // 512
    SCALE = 128 ** -0.5
    with ExitStack() as st:
        A = Tiles(nc, st)
        S.barrier()
        C = make_masks(S, A)
        if zero_halo:
            yT.zero_halo(S, A)
        ff = A.sb([128, NT, 4], BF16, "ff")
        bfb = A.sb([128, NT, 4], F32, "bfb")
        u = A.sb([128, NT, 4], F32, "u")
        sp = A.sb([128, NT, 4], F32, "sp")
        inc = A.sb([128, NT, 4], F32, "inc")
        zer = A.sb([128, NT], F32, "zer")
        Pk = A.sb([128, NT, 4], F32, "Pk")
        Bb = A.sb([128, T // 256, NT], F32, "Bb")
        kT = A.sb([128, T], BF16, "kT")
        Vall = A.sb([128, NT, 512], BF16, "Vall")
        qTb = [A.sb([128, 512], BF16, "qTb") for _ in range(2)]
        ogb = [A.sb([128, 512], BF16, "ogb") for _ in range(2)]
        PT = [A.sb([128, 512], BF16, "PT") for _ in range(3)]
        rl = A.sb([128, 512], F32, "rl")
        osb = A.sb([128, 512], F32, "osb")
        sg = A.sb([128, 512], F32, "sg")
        yb = [A.sb([128, 512], BF16, "yb") for _ in range(2)]
        ps_s = [A.ps([128, 512], F32, "ps_s") for _ in range(3)]
        ps_o = [A.ps([128, 512], F32, "ps_o") for _ in range(2)]
        ps_l = [A.ps([128, 512], F32, "ps_l") for _ in range(2)]
        ps_c = ps_s[0]
        ps_t = ps_s[1]
        ffv = pTM[:, TM_FF:TM_FF + 4].rearrange("(i p) h -> p i h", p=128)
        step = max(1, NT // 8)
        for i0 in range(0, NT, step):
            S.dma(ff[:, i0:i0 + step, :], ffv[:, i0:i0 + step, :], r=[("pTM", i) for i in range(i0, i0 + step)], w=["ff"])
        S.dma(bfb[:, :, :], foxbf_t, w=["bfb"])
        S.dve(lambda e: e.memset(zer[:, :], 0.0), w=["zer"])
        S.dve(lambda e: e.tensor_tensor(out=u[:, :, :], in0=ff[:, :, :], in1=bfb[:, :, :], op=ALU.add), r=["ff", "bfb"], w=["u"])
        S.act(lambda e: e.activation(out=u[:, :, :], in_=u[:, :, :], func=AF.Exp, scale=-1.0), r=["u"], w=["u"])
        S.act(lambda e: e.activation(out=sp[:, :, :], in_=u[:, :, :], func=AF.Ln, bias=1.0), r=["u"], w=["sp"])
        spf = sp[:, :, :].rearrange("p i h -> p (i h)")
        for n0 in range(0, NT * 4, 512):
            nn = min(512, NT * 4 - n0)
            S.pe(lambda e, n0=n0, nn=nn: e.matmul(ps_c[:, :nn], lhsT=C["tri_f"][:, :], rhs=spf[:, n0:n0 + nn], start=True, stop=True),
                 r=["sp", "c_tri_f"], w=["ps_s0"])
            S.pe(lambda e, n0=n0, nn=nn: e.matmul(ps_t[:, :nn], lhsT=C["ones_f"][:, :], rhs=spf[:, n0:n0 + nn], start=True, stop=True),
                 r=["sp", "c_ones_f"], w=["ps_s1"])
            Pf = Pk[:, :, :].rearrange("p i h -> p (i h)")
            If = inc[:, :, :].rearrange("p i h -> p (i h)")
            S.dve(lambda e, n0=n0, nn=nn, If=If: e.tensor_copy(out=If[:, n0:n0 + nn], in_=ps_t[:, :nn]), r=["ps_s1"], w=["inc"])
            S.dve(lambda e, n0=n0, nn=nn, Pf=Pf, If=If: e.tensor_tensor(out=Pf[:, n0:n0 + nn], in0=ps_c[:, :nn], in1=If[:, n0:n0 + nn], op=ALU.subtract),
                  r=["ps_s0", "inc"], w=["Pk"])
        for h in range(4):
            S.dve(lambda e, h=h: e.tensor_tensor_scan(out=inc[:, :, h], data0=inc[:, :, h], data1=zer[:, :], initial=0.0,
                                                       op0=ALU.add, op1=ALU.add), r=["inc", "zer"], w=["inc"])
        S.dve(lambda e: e.tensor_tensor(out=Pk[:, :, :], in0=Pk[:, :, :], in1=inc[:, :, :], op=ALU.add), r=["Pk", "inc"], w=["Pk"])
        vv = pTM[:, TM_FV:TM_FV + 512].rearrange("(i p) c -> p i c", p=128)
        for i0 in range(0, NT, step):
            S.dma(Vall[:, i0:i0 + step, :], vv[:, i0:i0 + step, :], r=[("pTM", i) for i in range(i0, i0 + step)], w=["Vall"])
        kq = 0
        for h in range(4):
            S.dma(kT[:, :], pFM[FM_FK + 128 * h:FM_FK + 128 * (h + 1), :], r=[("pFM", j) for j in range(NQ)], w=["kT"])
            for i2 in range(T // 256):
                nj = 2 * i2 + 2
                S.dve(lambda e, h=h, i2=i2, nj=nj: e.tensor_scalar(out=Bb[:, i2, :nj], in0=Pk[:, :nj, h], scalar1=inc[:, 2 * i2 + 1, h:h + 1],
                                                                   scalar2=None, op0=ALU.subtract), r=["Pk", "inc"], w=["Bb"])
            steps = []
            for I in range(NQ):
                nkt = 4 * I + 4
                for j in range(nkt):
                    steps.append((I, j, nkt))

            def emit_S(st_, kq_):
                I, j, nkt = st_
                qt = qTb[I % 2]
                qtag = ("qTb", I % 2)
                if j == 0:
                    og = ogb[I % 2]
                    S.dma(qt[:, :], pFM[FM_FQ + 128 * h:FM_FQ + 128 * (h + 1), 512 * I:512 * (I + 1)], r=[("pFM", I)], w=[qtag])
                    S.dma(og[:, :], pFM[FM_OG + 128 * h:FM_OG + 128 * (h + 1), 512 * I:512 * (I + 1)], r=[("pFM", I)], w=[("ogb", I % 2)])
                q0 = max(0, 128 * (j - 4 * I))
                pss = ps_s[kq_ % 3]
                S.pe(lambda e, pss=pss, j=j, qt=qt, q0=q0: e.matmul(pss[:, q0:512], lhsT=kT[:, 128 * j:128 * (j + 1)], rhs=qt[:, q0:512],
                                                                   start=True, stop=True), r=["kT", qtag], w=["ps_s%d" % (kq_ % 3)])

            def emit_rest(st_, kq_):
                I, j, nkt = st_
                r_ = j - 4 * I
                q0 = max(0, 128 * r_)
                pss = ps_s[kq_ % 3]
                pstag = "ps_s%d" % (kq_ % 3)
                pt = PT[kq_ % 3]
                po = ps_o[I % 2]
                pl = ps_l[I % 2]
                potag = ("ps_o", I % 2)
                pltag = ("ps_l", I % 2)
                for sb in range(2):
                    c0 = max(q0, 256 * sb)
                    c1 = 256 * (sb + 1)
                    if c0 >= c1:
                        continue
                    S.act(lambda e, pt=pt, pss=pss, c0=c0, c1=c1, I=I, sb=sb, j=j: e.activation(
                        out=pt[:, c0:c1], in_=pss[:, c0:c1], func=AF.Exp, scale=SCALE, bias=Bb[:, 2 * I + sb, j:j + 1]),
                        r=[pstag, "Bb"], w=[("PT", kq_ % 3, sb)])
                pttags = [("PT", kq_ % 3, 0), ("PT", kq_ % 3, 1)]
                if r_ >= 0:
                    S.pool(lambda e, pt=pt, q0=q0: e.tensor_tensor(out=pt[:, q0:q0 + 128], in0=pt[:, q0:q0 + 128], in1=C["tri_bf"][:, :], op=ALU.mult),
                           r=pttags + ["c_tri_bf"], w=pttags)
                S.pe(lambda e, po=po, j=j, pt=pt, q0=q0, nkt=nkt, h=h: e.matmul(po[:, q0:512], lhsT=Vall[:, j, 128 * h:128 * (h + 1)], rhs=pt[:, q0:512],
                                                                        start=(j == 0), stop=(j == nkt - 1)), r=["Vall"] + pttags, w=[potag])
                S.pe(lambda e, pl=pl, j=j, pt=pt, q0=q0, nkt=nkt: e.matmul(pl[:, q0:512], lhsT=C["ones_bf"][:, :], rhs=pt[:, q0:512],
                                                                        start=(j == 0), stop=(j == nkt - 1)), r=["c_ones_bf"] + pttags, w=[pltag])
                if j == nkt - 1:
                    og = ogb[I % 2]
                    ogtag = ("ogb", I % 2)
                    y = yb[I % 2]
                    ytag = ("yb", I % 2)
                    S.dve(lambda e, pl=pl: e.reciprocal(out=rl[:, :], in_=pl[:, :]), r=[pltag], w=["rl"])
                    S.dve(lambda e, po=po: e.tensor_tensor(out=osb[:, :], in0=po[:, :], in1=rl[:, :], op=ALU.mult), r=[potag, "rl"], w=["osb"])
                    S.act(lambda e, og=og: e.activation(out=sg[:, :], in_=og[:, :], func=AF.Exp, scale=-1.0), r=[ogtag], w=["sg"])
                    S.dve(lambda e: e.tensor_scalar_add(out=sg[:, :], in0=sg[:, :], scalar1=1.0), r=["sg"], w=["sg"])
                    S.dve(lambda e: e.reciprocal(out=sg[:, :], in_=sg[:, :]), r=["sg"], w=["sg"])
                    S.pool(lambda e, y=y: e.tensor_tensor(out=y[:, :], in0=osb[:, :], in1=sg[:, :], op=ALU.mult), r=["osb", "sg"], w=[ytag])
                    yT.put(S, 1024 + 128 * h, lambda d, cs_, h=h: d[1024 + 128 * h:1024 + 128 * (h + 1), cs_], I, lambda cs_, y=y: y[:, cs_], [ytag])

            LOOK = 2
            n = len(steps)
            for i in range(min(LOOK, n)):
                emit_S(steps[i], kq + i)
            for i in range(n):
                if i + LOOK < n:
                    emit_S(steps[i + LOOK], kq + i + LOOK)
                emit_rest(steps[i], kq + i)
            kq += n
        S.emit()


def phase_gla(nc, S, T, pFM, pTM, wlr_aug, gnb_d, yT, zero_halo=False):
    NBK = T // 512
    NCH = T // 128
    with ExitStack() as st:
        A = Tiles(nc, st)
        S.barrier()
        C = make_masks(S, A)
        if zero_halo:
            yT.zero_halo(S, A)
        rt_f = A.sb([128, 128], F32, "rt_f")
        S.pool(lambda e: e.affine_select(out=rt_f[:, :], in_=C["ones_f"][:, :], pattern=[[-1, 128]], compare_op=ALU.is_ge,
                                         fill=0.0, base=-1, channel_multiplier=1), r=["c_ones_f"], w=["rt_f"])
        wl_f = A.sb([17, 256], F32, "wl_f")
        wl = A.sb([17, 256], BF16, "wl")
        gnb = A.sb([128, 512], F32, "gnb")
        epsc = A.sb([128, 1], F32, "epsc")
        S.pool(lambda e: e.memset(epsc[:, :], EPS), w=["epsc"])
        S.dma(wl_f[:, :], wlr_aug, w=["wl_f"])
        S.dve(lambda e: e.tensor_copy(out=wl[:, :], in_=wl_f[:, :]), r=["wl_f"], w=["wl"])
        S.dma(gnb[:, :], gnb_d, w=["gnb"])
        laug = [A.sb([17, 512], BF16, "laug") for _ in range(2)]
        qkb = [A.sb([128, 4, 512], BF16, "qkb") for _ in range(2)]
        tmb = [A.sb([128, 4, 1280], BF16, "tmb") for _ in range(2)]
        for i in range(2):
            S.pool(lambda e, i=i: e.memset(laug[i][:, :], 1.0), w=[("laug", i)])
        e_sb = A.sb([128, 256], F32, "e_sb")
        esr = A.sb([128, 512], F32, "esr")
        sp_sb = A.sb([128, 256], F32, "sp_sb")
        ek = A.sb([128, 256], F32, "ek")
        ekk = A.sb([128, 2, 128], F32, "ekk")
        kin = A.sb([128, 2, 128], BF16, "kin")
        kst = [A.sb([128, 256], BF16, "kst") for _ in range(2)]
        eq = [A.sb([128, 2, 128], F32, "eq") for _ in range(2)]
        qin = [A.sb([128, 2, 128], BF16, "qin") for _ in range(2)]
        att = [A.sb([128, 2, 128], BF16, "att") for _ in range(2)]
        silr = [A.sb([128, 512], F32, "silr") for _ in range(2)]
        St = A.sb([128, 2, 256], F32, "St")
        Sb = A.sb([128, 2, 256], BF16, "Sb")
        junk = A.sb([128, 256], F32, "junk")
        ssum = A.sb([128, 2], F32, "ssum")
        rstd = A.sb([128, 2], F32, "rstd")
        t1 = A.sb([128, 256], F32, "t1")
        ysb = A.sb([128, 2, 256], BF16, "ysb")
        yTs = [A.sb([128, 4, 512], BF16, "yTs") for _ in range(2)]
        pb = [A.ps([128, 512], F32, "pb") for _ in range(7)]
        ptr = A.ps([128, 1024], BF16, "ptr")
        S.dve(lambda e: e.memset(St[:, :, :], 0.0), w=["St"])
        S.dve(lambda e: e.memset(Sb[:, :, :], 0.0), w=["Sb"])
        tmv = pTM[:, TM_GK:TM_GK + 1280].rearrange("(i p) c -> p i c", p=128)

        def load_block(j):
            b2 = j % 2
            bs = slice(512 * j, 512 * (j + 1))
            S.dma(laug[b2][0:16, :], pFM[FM_LR:FM_LR + 16, bs], r=[("pFM", j)], w=[("laug", b2)])
            S.dma(qkb[b2][:, 0:2, :], pFM[FM_GQ:FM_GQ + 256, bs].rearrange("(h p) t -> p h t", p=128), r=[("pFM", j)], w=[("qkb", b2)])
            S.dma(qkb[b2][:, 2:4, :], pFM[FM_GK:FM_GK + 256, bs].rearrange("(h p) t -> p h t", p=128), r=[("pFM", j)], w=[("qkb", b2)])
            S.dma(tmb[b2][:, :, :], tmv[:, 4 * j:4 * j + 4, :], r=[("pTM", 4 * j + i) for i in range(4)], w=[("tmb", b2)])

        def P(n):
            j, ch = n // 4, n % 4
            b2 = j % 2
            p2 = n % 2
            if ch == 1 and j + 1 < NBK:
                load_block(j + 1)
            cs = slice(128 * ch, 128 * (ch + 1))
            kt = tmb[b2][:, ch, 0:256]
            rt = tmb[b2][:, ch, 768:1280]
            S.pe(lambda e: e.matmul(pb[0][:, 0:256], lhsT=laug[b2][0:17, cs], rhs=wl[0:17, :], start=True, stop=True),
                 r=[("laug", b2), "wl"], w=["pg0"])
            yield
            S.act(lambda e: e.activation(out=e_sb[:, :], in_=pb[0][:, 0:256], func=AF.Exp, scale=-1.0), r=["pg0"], w=["e_sb"])
            yield
            S.act(lambda e: e.activation(out=sp_sb[:, :], in_=e_sb[:, :], func=AF.Ln, bias=1.0), r=["e_sb"], w=["sp_sb"])
            yield
            S.pe(lambda e: e.matmul(pb[0][:, 256:512], lhsT=rt_f[:, :], rhs=sp_sb[:, :], start=True, stop=True),
                 r=["rt_f", "sp_sb"], w=["pg0"])
            for h in range(2):
                S.pe(lambda e, h=h: e.matmul(pb[1][:, 128 * h:128 * (h + 1)], lhsT=sp_sb[:, 128 * h:128 * (h + 1)], rhs=C["tri_f"][:, :],
                                            start=True, stop=True), r=["sp_sb", "c_tri_f"], w=["pg1"])
            yield
            S.act(lambda e: e.activation(out=ek[:, :], in_=pb[0][:, 256:512], func=AF.Exp, scale=-1.0 / 16), r=["pg0"], w=["ek"])
            yield
            S.act(lambda e: e.activation(out=eq[p2][:, :, :].rearrange("p a b -> p (a b)"), in_=pb[1][:, 0:256], func=AF.Exp, scale=-1.0 / 16),
                  r=["pg1"], w=[("eq", p2)])
            yield
            S.act(lambda e: e.activation(out=ekk[:, :, :].rearrange("p a b -> p (a b)"), in_=pb[1][:, 0:256], func=AF.Exp, scale=1.0 / 16),
                  r=["pg1"], w=["ekk"])
            S.dve(lambda e: e.tensor_tensor(out=kst[p2][:, :], in0=kt, in1=ek[:, :], op=ALU.mult), r=[("tmb", b2), "ek"], w=[("kst", p2)])
            yield
            S.dve(lambda e: e.scalar_tensor_tensor(out=qin[p2][:, :, :], in0=qkb[b2][:, 0:2, cs], scalar=128 ** -0.5, in1=eq[p2][:, :, :],
                                                    op0=ALU.mult, op1=ALU.mult), r=[("qkb", b2), ("eq", p2)], w=[("qin", p2)])
            yield
            S.dve(lambda e: e.tensor_tensor(out=kin[:, :, :], in0=qkb[b2][:, 2:4, cs], in1=ekk[:, :, :], op=ALU.mult),
                  r=[("qkb", b2), "ekk"], w=["kin"])
            yield
            for h in range(2):
                S.pe(lambda e, h=h: e.matmul(pb[6][:, 128 * h:128 * (h + 1)], lhsT=kin[:, h, :], rhs=qin[p2][:, h, :], start=True, stop=True),
                     r=["kin", ("qin", p2)], w=["pg6"])
            yield
            S.dve(lambda e: e.tensor_tensor(out=att[p2][:, :, :], in0=pb[6][:, 0:256].rearrange("p (a b) -> p a b", a=2),
                                            in1=C["tri_f"][:, None, :].to_broadcast([128, 2, 128]), op=ALU.mult),
                  r=["pg6", "c_tri_f"], w=[("att", p2)])
            yield
            S.act(lambda e: e.activation(out=silr[p2][:, :], in_=rt, func=AF.Silu), r=[("tmb", b2)], w=[("silr", p2)])
            yield

        def X(n):
            j, ch = n // 4, n % 4
            b2 = j % 2
            p2 = n % 2
            cs = slice(128 * ch, 128 * (ch + 1))
            vt = tmb[b2][:, ch, 256:768]
            for h in range(2):
                po = pb[2 + h]
                pu = pb[4 + h]
                S.pe(lambda e, h=h, po=po: e.matmul(po[:, 0:256], lhsT=att[p2][:, h, :], rhs=vt[:, 256 * h:256 * (h + 1)], start=True, stop=False),
                     r=[("att", p2), ("tmb", b2)], w=[("po", h)])
                S.pe(lambda e, h=h, po=po: e.matmul(po[:, 0:256], lhsT=qin[p2][:, h, :], rhs=Sb[:, h, :], start=False, stop=True),
                     r=[("qin", p2), ("Sb", h)], w=[("po", h)])
                S.pe(lambda e, h=h, pu=pu: e.matmul(pu[:, 0:256], lhsT=kst[p2][:, 128 * h:128 * (h + 1)], rhs=vt[:, 256 * h:256 * (h + 1)], start=True, stop=True),
                     r=[("kst", p2), ("tmb", b2)], w=[("pu", h)])
                yield
                S.dve(lambda e, h=h, pu=pu: e.scalar_tensor_tensor(out=St[:, h, :], in0=St[:, h, :], scalar=eq[p2][:, h, 127:128], in1=pu[:, 0:256],
                                                                    op0=ALU.mult, op1=ALU.add), r=[("St", h), ("eq", p2), ("pu", h)], w=[("St", h)])
                yield
                S.act(lambda e, h=h: e.copy(out=Sb[:, h, :], in_=St[:, h, :]), r=[("St", h)], w=[("Sb", h)])
                yield
                S.act(lambda e, h=h, po=po: e.activation(out=junk[:, :], in_=po[:, 0:256], func=AF.Square, accum_out=ssum[:, h:h + 1]),
                      r=[("po", h)], w=["junk", ("ssum", h)])
                yield
                S.act(lambda e, h=h: e.activation(out=ssum[:, h:h + 1], in_=ssum[:, h:h + 1], func=AF.Ln, bias=epsc[:, 0:1], scale=1.0 / 256),
                      r=[("ssum", h), "epsc"], w=[("ssum", h)])
                yield
                S.act(lambda e, h=h: e.activation(out=rstd[:, h:h + 1], in_=ssum[:, h:h + 1], func=AF.Exp, scale=-0.5), r=[("ssum", h)], w=[("rstd", h)])
                yield
                S.dve(lambda e, h=h, po=po: e.scalar_tensor_tensor(out=t1[:, :], in0=po[:, 0:256], scalar=rstd[:, h:h + 1], in1=gnb[:, 256 * h:256 * (h + 1)],
                                                                    op0=ALU.mult, op1=ALU.mult), r=[("po", h), ("rstd", h), "gnb"], w=["t1"])
                yield
                S.pool(lambda e, h=h: e.tensor_tensor(out=ysb[:, h, :], in0=t1[:, :], in1=silr[p2][:, 256 * h:256 * (h + 1)], op=ALU.mult),
                       r=["t1", ("silr", p2)], w=[("ysb", h)])
                yield
                for vc in range(2):
                    S.pe(lambda e, h=h, vc=vc: e.transpose(ptr[:, 128 * vc:128 * (vc + 1)], ysb[:, h, 128 * vc:128 * (vc + 1)], C["ident_bf"][:, :]),
                         r=[("ysb", h), "c_ident_bf"], w=["ptr"])
                yield
                S.act(lambda e, h=h: e.copy(out=yTs[b2][:, 2 * h:2 * h + 2, cs], in_=ptr[:, 0:256].rearrange("p (a b) -> p a b", a=2)),
                      r=["ptr"], w=[("yTs", b2)])
                yield
            if ch == 3:
                yT.put(S, 0, lambda d, cs_: d[0:512, cs_].rearrange("(a p) t -> p a t", p=128), j, lambda cs_, b2=b2: yTs[b2][:, :, cs_], [("yTs", b2)])

        load_block(0)
        for _ in P(0):
            pass
        for n in range(NCH):
            gp = P(n + 1) if n + 1 < NCH else iter(())
            gx = X(n)
            while True:
                a_ = next(gp, "end")
                b_ = next(gx, "end")
                if a_ == "end" and b_ == "end":
                    break
        S.emit()


MLSTM_STOP = 0


def phase_mlstm(nc, S, T, pFM, pTM, P, yT):
    NBK = T // 512
    with ExitStack() as st:
        A = Tiles(nc, st)
        S.barrier()
        C = make_masks(S, A)
        epsc = A.sb([128, 1], F32, "epsc")
        S.pool(lambda e: e.memset(epsc[:, :], EPS), w=["epsc"])
        cw = A.sb([128, 8, 4], F32, "cw")
        cb = A.sb([128, 8], F32, "cb")
        gb = A.sb([128, 4], F32, "gb")
        skc = A.sb([128, 4], F32, "skc")
        gnb = A.sb([128, 512], F32, "gnb")
        wif = A.sb([128, 3, 8, 4], F32, "wif")
        for nm, tl in (("cw", cw), ("cb", cb), ("gb", gb), ("skc", skc), ("gnb", gnb), ("wif", wif)):
            S.dma(tl[tuple(slice(None) for _ in tl.shape)], P[nm], w=[nm])
        wbd_f = A.sb([128, 3, 4, 128], F32, "wbd_f")
        wbd = A.sb([128, 3, 4, 128], BF16, "wbd")
        wT_f = A.sb([128, 3, 8, 128], F32, "wT_f")
        for i, nm in enumerate(("wq_bd", "wk_bd", "wv_bd")):
            S.dma(wbd_f[:, i, :, :], P[nm].rearrange("c p o -> p c o"), w=["wbd_f"])
        S.dve(lambda e: e.tensor_copy(out=wbd[:, :, :, :], in_=wbd_f[:, :, :, :]), r=["wbd_f"], w=["wbd"])
        for i, nm in enumerate(("wqT_bd", "wkT_bd", "wvT_bd")):
            S.dma(wT_f[:, i, :, :], P[nm].rearrange("c p o -> p c o"), w=["wT_f"])
        dcw = A.sb([128, 8, 4, 128], BF16, "dcw")
        dsk = A.sb([128, 4, 128], BF16, "dsk")
        for c in range(8):
            for j in range(4):
                S.dve(lambda e, c=c, j=j: e.tensor_scalar(out=dcw[:, c, j, :], in0=C["ident_f"][:, :], scalar1=cw[:, c, j:j + 1], scalar2=None, op0=ALU.mult),
                      r=["c_ident_f", "cw"], w=["dcw"])
        for c in range(4):
            S.dve(lambda e, c=c: e.tensor_scalar(out=dsk[:, c, :], in0=C["ident_f"][:, :], scalar1=skc[:, c:c + 1], scalar2=None, op0=ALU.mult),
                  r=["c_ident_f", "skc"], w=["dsk"])
        pb = [A.ps([128, 512], F32, "pb") for _ in range(7)]
        ptr = A.ps([128, 1024], BF16, "ptr")
        weff = A.sb([128, 2, 8, 4], BF16, "weff")
        for c in range(8):
            S.pe(lambda e, c=c: e.matmul(pb[0][:, 8 * c:8 * c + 4], lhsT=wT_f[:, 0, c, :], rhs=wif[:, 0, c, :], start=True, stop=False), r=["wT_f", "wif"], w=["pb0"])
            S.pe(lambda e, c=c: e.matmul(pb[0][:, 8 * c:8 * c + 4], lhsT=wT_f[:, 1, c, :], rhs=wif[:, 1, c, :], start=False, stop=True), r=["wT_f", "wif"], w=["pb0"])
            S.pe(lambda e, c=c: e.matmul(pb[0][:, 8 * c + 4:8 * c + 8], lhsT=wT_f[:, 2, c, :], rhs=wif[:, 2, c, :], start=True, stop=True), r=["wT_f", "wif"], w=["pb0"])
        pw = pb[0][:, 0:64].rearrange("p (c k f) -> p k c f", c=8, k=2)
        S.dve(lambda e: e.tensor_copy(out=weff[:, :, :, :], in_=pw), r=["pb0"], w=["weff"])
        if MLSTM_STOP == 1:
            S.emit()
            return
        mxb = [A.sb([128, 8, 516], BF16, "mxb") for _ in range(2)]
        mxs = A.sb([128, 8, 516], BF16, "mxs")
        mzb = [A.sb([128, 4, 512], BF16, "mzb") for _ in range(2)]
        xcT = [A.sb([128, 8, 512], BF16, "xcT") for _ in range(2)]
        gsb = A.sb([128, 4], F32, "gsb")
        ef = A.sb([128, 2], F32, "ef")
        nlf = A.sb([128, 2], F32, "nlf")
        tmp2 = A.sb([128, 2], F32, "tmp2")
        wv = A.sb([128, 2], F32, "wv")
        wveg = A.sb([128, 2], F32, "wveg")
        eb = [A.sb([128, 2], F32, "eb") for _ in range(2)]
        eg = [A.sb([128, 2], F32, "eg") for _ in range(2)]
        silz = [A.sb([128, 512], F32, "silz") for _ in range(2)]
        qk = [[A.sb([128, 4, 128], BF16, "qk") for _ in range(2)] for _ in range(2)]
        ksb = [[A.sb([128, 256], BF16, "ksb") for _ in range(2)] for _ in range(2)]
        vw = [[A.sb([128, 260], BF16, "vw") for _ in range(2)] for _ in range(2)]
        vw2 = [[A.sb([128, 260], BF16, "vw2") for _ in range(2)] for _ in range(2)]
        att = [[A.sb([128, 128], BF16, "att") for _ in range(2)] for _ in range(2)]
        Cst = A.sb([128, 2, 2, 257], F32, "Cst")
        Cb = A.sb([128, 2, 2, 260], BF16, "Cb")
        den = A.sb([128, 1], F32, "den")
        fac = A.sb([128, 1], F32, "fac")
        ss = A.sb([128, 1], F32, "ss")
        fr = A.sb([128, 1], F32, "fr")
        junk = A.sb([128, 256], F32, "junk")
        t1 = A.sb([128, 256], F32, "t1")
        ysb = A.sb([128, 2, 256], BF16, "ysb")
        yTs = [A.sb([128, 4, 512], BF16, "yTs") for _ in range(2)]
        ln16c = A.sb([128, 1], F32, "ln16c")
        ncb = A.sb([128, 8], F32, "ncb")
        S.dve(lambda e: e.tensor_scalar(out=ncb[:, :], in0=cb[:, :], scalar1=-1.0, scalar2=None, op0=ALU.mult), r=["cb"], w=["ncb"])
        ecv = A.sb([128, 512], F32, "ecv")
        zcv = A.sb([128, 512], F32, "zcv")
        esz = A.sb([128, 512], F32, "esz")
        S.pool(lambda e: e.memset(ln16c[:, :], float(np.log(1.0 / 16.0))), w=["ln16c"])
        S.dve(lambda e: e.memset(Cst[:, :, :, :], 0.0), w=["Cst"])
        S.dve(lambda e: e.memset(Cb[:, :, :, :], 0.0), w=["Cb"])
        for p_ in range(2):
            for h_ in range(2):
                S.dve(lambda e, p_=p_, h_=h_: e.memset(vw[p_][h_][:, :], 0.0), w=[("vw", p_, h_)])
                S.dve(lambda e, p_=p_, h_=h_: e.memset(vw2[p_][h_][:, :], 0.0), w=[("vw2", p_, h_)])
        S.pool(lambda e: e.memset(mxb[0][:, :, 0:4], 0.0), w=[("mxb", 0)])
        S.pool(lambda e: e.memset(mxs[:, :, :], 0.0), w=["mxs"])
        mzv = pTM[:, TM_MZ:TM_MZ + 512].rearrange("(i p) c -> p i c", p=128)
        NTL = T // 128

        def load_block(j):
            b2 = j % 2
            bs = slice(512 * j, 512 * (j + 1))
            S.dma(mxb[b2][:, :, 4:516], pFM[FM_MX:FM_MX + 1024, bs].rearrange("(c p) t -> p c t", p=128), r=[("pFM", j)], w=[("mxb", b2)])
            S.dma(mzb[b2][:, :, :], mzv[:, 4 * j:4 * j + 4, :], r=[("pTM", 4 * j + i) for i in range(4)], w=[("mzb", b2)])

        def P(n):
            j, tt = n // 4, n % 4
            b2 = j % 2
            p2 = n % 2
            mx = mxb[b2]
            xc = xcT[b2]
            xctag = ("xcT", b2)
            if tt == 1 and j + 1 < NBK:
                load_block(j + 1)
            if tt == 0:
                if j > 0:
                    S.dve(lambda e: e.tensor_copy(out=mxb[b2][:, :, 0:4], in_=mxb[1 - b2][:, :, 512:516]), r=[("mxb", 1 - b2)], w=[("mxb", b2)])
                S.dve(lambda e: e.tensor_copy(out=mxs[:, :, 0:514], in_=mx[:, :, 1:515]), r=[("mxb", b2)], w=["mxs"])
                yield
                for c in range(8):
                    pbi = (2, 4)[c % 2]
                    pc = pb[pbi]
                    for tp in range(4):
                        S.pe(lambda e, c=c, tp=tp, pc=pc: e.matmul(pc[:, :], lhsT=dcw[:, c, tp, :], rhs=(mx[:, c, tp + 1:tp + 513] if tp % 2 == 1 else mxs[:, c, tp:tp + 512]),
                                                                  start=(tp == 0), stop=(tp == 3)), r=["dcw", ("mxb", b2), "mxs"], w=["pb%d" % pbi])
                    S.act(lambda e, c=c, pc=pc: e.activation(out=xc[:, c, :], in_=pc[:, :], func=AF.Silu, bias=cb[:, c:c + 1]), r=["pb%d" % pbi, "cb"], w=[xctag])
                    yield
            ts_ = slice(128 * tt, 128 * (tt + 1))
            tsx = slice(4 + 128 * tt, 4 + 128 * (tt + 1))
            for c in range(8):
                S.pe(lambda e, c=c: e.matmul(pb[3][:, 0:4], lhsT=xc[:, c, ts_], rhs=weff[:, 0, c, :], start=(c == 0), stop=False), r=[xctag, "weff"], w=["pb3"])
            for c in range(8):
                S.pe(lambda e, c=c: e.matmul(pb[3][:, 0:4], lhsT=mx[:, c, tsx], rhs=weff[:, 1, c, :], start=False, stop=(c == 7)), r=[("mxb", b2), "weff"], w=["pb3"])
            yield
            S.dve(lambda e: e.tensor_tensor(out=gsb[:, :], in0=pb[3][:, 0:4], in1=gb[:, :], op=ALU.add), r=["pb3", "gb"], w=["gsb"])
            yield
            S.act(lambda e: e.activation(out=ef[:, :], in_=gsb[:, 2:4], func=AF.Exp, scale=-1.0), r=["gsb"], w=["ef"])
            yield
            S.act(lambda e: e.activation(out=nlf[:, :], in_=ef[:, :], func=AF.Ln, bias=1.0), r=["ef"], w=["nlf"])
            yield
            S.pe(lambda e: e.matmul(pb[3][:, 8:10], lhsT=C["tri_f"][:, :], rhs=nlf[:, :], start=True, stop=True), r=["c_tri_f", "nlf"], w=["pb3"])
            S.pe(lambda e: e.matmul(pb[3][:, 16:18], lhsT=C["ones_f"][:, :], rhs=nlf[:, :], start=True, stop=True), r=["c_ones_f", "nlf"], w=["pb3"])
            yield
            S.dve(lambda e: e.tensor_tensor(out=tmp2[:, :], in0=pb[3][:, 8:10], in1=gsb[:, 0:2], op=ALU.add), r=["pb3", "gsb"], w=["tmp2"])
            yield
            S.act(lambda e: e.activation(out=wv[:, :], in_=tmp2[:, :], func=AF.Exp), r=["tmp2"], w=["wv"])
            S.act(lambda e: e.activation(out=eb[p2][:, :], in_=pb[3][:, 8:10], func=AF.Exp, scale=-1.0, bias=ln16c[:, 0:1]), r=["pb3", "ln16c"], w=[("eb", p2)])
            S.act(lambda e: e.activation(out=eg[p2][:, :], in_=pb[3][:, 16:18], func=AF.Exp, scale=-1.0), r=["pb3"], w=[("eg", p2)])
            yield
            S.dve(lambda e: e.tensor_tensor(out=wveg[:, :], in0=wv[:, :], in1=eg[p2][:, :], op=ALU.mult), r=["wv", ("eg", p2)], w=["wveg"])
            yield
            for h in range(2):
                pA, pB = pb[3], pb[4]
                for dc in range(2):
                    c = 2 * h + dc
                    S.pe(lambda e, c=c, dc=dc: e.matmul(pA[:, 128 * dc:128 * (dc + 1)], lhsT=wbd[:, 0, c, :], rhs=xc[:, c, ts_], start=True, stop=True), r=["wbd", xctag], w=["pb3"])
                    S.pe(lambda e, c=c, dc=dc: e.matmul(pA[:, 256 + 128 * dc:256 + 128 * (dc + 1)], lhsT=wbd[:, 1, c, :], rhs=xc[:, c, ts_], start=True, stop=True), r=["wbd", xctag], w=["pb3"])
                    S.pe(lambda e, c=c, dc=dc: e.matmul(pB[:, 128 * dc:128 * (dc + 1)], lhsT=xc[:, c, ts_], rhs=wbd[:, 1, c, :], start=True, stop=True), r=["wbd", xctag], w=["pb4"])
                    S.pe(lambda e, c=c, dc=dc: e.matmul(pB[:, 256 + 128 * dc:256 + 128 * (dc + 1)], lhsT=mx[:, c, tsx], rhs=wbd[:, 2, c, :], start=True, stop=True), r=["wbd", ("mxb", b2)], w=["pb4"])
                yield
                S.act(lambda e, h=h: e.copy(out=qk[p2][h][:, :, :].rearrange("p a b -> p (a b)"), in_=pA[:, :]), r=["pb3"], w=[("qk", p2, h)])
                yield
                S.act(lambda e, h=h: e.copy(out=ksb[p2][h][:, :], in_=pB[:, 0:256]), r=["pb4"], w=[("ksb", p2, h)])
                S.dve(lambda e, h=h: e.tensor_scalar(out=vw[p2][h][:, 0:256], in0=pB[:, 256:512], scalar1=wv[:, h:h + 1], scalar2=None, op0=ALU.mult), r=["pb4", "wv"], w=[("vw", p2, h)])
                yield
                S.dve(lambda e, h=h: e.tensor_scalar(out=vw2[p2][h][:, 0:256], in0=pB[:, 256:512], scalar1=wveg[:, h:h + 1], scalar2=None, op0=ALU.mult), r=["pb4", "wveg"], w=[("vw2", p2, h)])
                S.pool(lambda e, h=h: e.tensor_copy(out=vw[p2][h][:, 256:257], in_=wv[:, h:h + 1]), r=["wv"], w=[("vw", p2, h)])
                S.pool(lambda e, h=h: e.tensor_copy(out=vw2[p2][h][:, 256:257], in_=wveg[:, h:h + 1]), r=["wveg"], w=[("vw2", p2, h)])
                yield
                for dc in range(2):
                    S.pe(lambda e, dc=dc, h=h: e.matmul(pb[2][:, 0:128], lhsT=qk[p2][h][:, 2 + dc, :], rhs=qk[p2][h][:, dc, :], start=(dc == 0), stop=(dc == 1)), r=[("qk", p2, h)], w=["pb2"])
                yield
                S.dve(lambda e, h=h: e.tensor_tensor(out=att[p2][h][:, :], in0=pb[2][:, 0:128], in1=C["tri_f"][:, :], op=ALU.mult), r=["pb2", "c_tri_f"], w=[("att", p2, h)])
                yield
            S.act(lambda e: e.activation(out=silz[p2][:, :], in_=mzb[b2][:, tt, :], func=AF.Silu), r=[("mzb", b2)], w=[("silz", p2)])
            yield

        def X(n):
            j, tt = n // 4, n % 4
            b2 = j % 2
            p2 = n % 2
            xc = xcT[b2]
            xctag = ("xcT", b2)
            ts_ = slice(128 * tt, 128 * (tt + 1))
            for h in range(2):
                pD, pE, pF, pG = pb[5], pb[6], pb[0], pb[1]
                S.pe(lambda e, h=h: e.matmul(pD[:, 0:258], lhsT=att[p2][h][:, :], rhs=vw[p2][h][:, 0:258], start=True, stop=False), r=[("att", p2, h), ("vw", p2, h)], w=["pb5"])
                for dc in range(2):
                    S.pe(lambda e, dc=dc, h=h: e.matmul(pD[:, 0:258], lhsT=qk[p2][h][:, dc, :], rhs=Cb[:, h, dc, 0:258], start=False, stop=(dc == 1)), r=[("qk", p2, h), ("Cb", h)], w=["pb5"])
                for dc, pU in ((0, pE), (1, pF)):
                    S.pe(lambda e, dc=dc, pU=pU, h=h: e.matmul(pU[:, 0:258], lhsT=ksb[p2][h][:, 128 * dc:128 * (dc + 1)], rhs=vw2[p2][h][:, 0:258], start=True, stop=True),
                         r=[("ksb", p2, h), ("vw2", p2, h)], w=[("pb6", "pb0")[dc]])
                yield
                for dc, pU in ((0, pE), (1, pF)):
                    S.dve(lambda e, dc=dc, pU=pU, h=h: e.scalar_tensor_tensor(out=Cst[:, h, dc, :], in0=Cst[:, h, dc, :], scalar=eg[p2][:, h:h + 1], in1=pU[:, 0:257],
                                                                             op0=ALU.mult, op1=ALU.add), r=[("Cst", h), ("eg", p2), ("pb6", "pb0")[dc]], w=[("Cst", h)])
                yield
                S.act(lambda e, h=h: e.copy(out=Cb[:, h, :, 0:257], in_=Cst[:, h, :, :]), r=[("Cst", h)], w=[("Cb", h)])
                yield
                S.act(lambda e, h=h: e.activation(out=den[:, :], in_=pD[:, 256:257], func=AF.Abs, scale=eb[p2][:, h:h + 1]), r=["pb5", ("eb", p2)], w=["den"])
                yield
                S.dve(lambda e: e.tensor_scalar_max(out=den[:, :], in0=den[:, :], scalar1=1.0), r=["den"], w=["den"])
                S.dve(lambda e: e.reciprocal(out=den[:, :], in_=den[:, :]), r=["den"], w=["den"])
                S.dve(lambda e, h=h: e.tensor_tensor(out=fac[:, :], in0=den[:, :], in1=eb[p2][:, h:h + 1], op=ALU.mult), r=["den", ("eb", p2)], w=["fac"])
                yield
                S.act(lambda e: e.activation(out=junk[:, :], in_=pD[:, 0:256], func=AF.Square, scale=fac[:, 0:1], accum_out=ss[:, 0:1]), r=["pb5", "fac"], w=["junk", "ss"])
                yield
                S.act(lambda e: e.activation(out=ss[:, :], in_=ss[:, :], func=AF.Ln, bias=epsc[:, 0:1], scale=1.0 / 256), r=["ss", "epsc"], w=["ss"])
                yield
                S.act(lambda e: e.activation(out=fr[:, :], in_=ss[:, :], func=AF.Exp, scale=-0.5), r=["ss"], w=["fr"])
                S.dve(lambda e: e.tensor_tensor(out=fr[:, :], in0=fr[:, :], in1=fac[:, :], op=ALU.mult), r=["fr", "fac"], w=["fr"])
                S.dve(lambda e, h=h: e.scalar_tensor_tensor(out=t1[:, :], in0=pD[:, 0:256], scalar=fr[:, 0:1], in1=gnb[:, 256 * h:256 * (h + 1)],
                                                             op0=ALU.mult, op1=ALU.mult), r=["pb5", "fr", "gnb"], w=["t1"])
                for dc in range(2):
                    c = 2 * h + dc
                    S.pe(lambda e, c=c, dc=dc: e.matmul(pG[:, 128 * dc:128 * (dc + 1)], lhsT=xc[:, c, ts_], rhs=dsk[:, c, :], start=True, stop=True),
                         r=[xctag, "dsk"], w=["pb1"])
                yield
                S.dve(lambda e: e.tensor_tensor(out=t1[:, :], in0=pG[:, 0:256], in1=t1[:, :], op=ALU.add), r=["pb1", "t1"], w=["t1"])
                yield
                S.pool(lambda e, h=h: e.tensor_tensor(out=ysb[:, h, :], in0=t1[:, :], in1=silz[p2][:, 256 * h:256 * (h + 1)], op=ALU.mult), r=["t1", ("silz", p2)], w=[("ysb", h)])
                yield
                for vc in range(2):
                    S.pe(lambda e, h=h, vc=vc: e.transpose(ptr[:, 128 * vc:128 * (vc + 1)], ysb[:, h, 128 * vc:128 * (vc + 1)], C["ident_bf"][:, :]),
                         r=[("ysb", h), "c_ident_bf"], w=["ptr"])
                yield
                S.act(lambda e, h=h: e.copy(out=yTs[b2][:, 2 * h:2 * h + 2, ts_], in_=ptr[:, 0:256].rearrange("p (a b) -> p a b", a=2)),
                      r=["ptr"], w=[("yTs", b2)])
                yield
            if tt == 3:
                yT.put(S, 512, lambda d, cs_: d[512:1024, cs_].rearrange("(a p) t -> p a t", p=128), j, lambda cs_, b2=b2: yTs[b2][:, :, cs_], [("yTs", b2)])

        load_block(0)
        for _ in P(0):
            pass
        for n in range(NTL):
            gp = P(n + 1) if n + 1 < NTL else iter(())
            gx = X(n)
            while True:
                a_ = next(gp, "end")
                b_ = next(gx, "end")
                if a_ == "end" and b_ == "end":
                    break
        S.emit()


def bcast(v, n=128):
    v = np.asarray(v, np.float32)
    return np.ascontiguousarray(np.broadcast_to(v, (n,) + v.shape))


def block_diag_full(w):
    W = np.zeros((1024, 1024), np.float32)
    for c in range(4):
        for d in range(4):
            W[np.arange(256) * 4 + c, np.arange(256) * 4 + d] = w[:, c, d]
    return W


def prep_mlstm(p, conv_w, conv_b, wq, wk, wv, w_i, b_i, w_f, b_f, skip, norm):
    own = np.arange(512 * p, 512 * p + 512)
    oth = np.arange(512 * (1 - p), 512 * (1 - p) + 512)
    perm = np.concatenate([own, oth])
    P = {}
    P["cw"] = np.ascontiguousarray(conv_w[:, perm].reshape(4, 8, 128).transpose(2, 1, 0))
    P["cb"] = np.ascontiguousarray(conv_b[perm].reshape(8, 128).T)
    for nm, w in (("q", wq), ("k", wk), ("v", wv)):
        W = block_diag_full(w)[perm][:, perm]
        blocks = np.stack([W[128 * c:128 * (c + 1), 128 * c:128 * (c + 1)] for c in range(8)])
        P["w%s_bd" % nm] = np.ascontiguousarray(blocks[:4])
        P["w%sT_bd" % nm] = np.ascontiguousarray(blocks.transpose(0, 2, 1))
    cols = np.stack([w_i[:, 2 * p], w_i[:, 2 * p + 1], w_f[:, 2 * p], w_f[:, 2 * p + 1]], axis=1)
    wif = np.stack([cols[part * 1024 + perm] for part in range(3)])
    P["wif"] = np.ascontiguousarray(wif.reshape(3, 8, 128, 4).transpose(2, 0, 1, 3))
    P["gb"] = bcast(np.array([b_i[2 * p], b_i[2 * p + 1], b_f[2 * p], b_f[2 * p + 1]], np.float32))
    P["skc"] = np.ascontiguousarray(skip[own].reshape(4, 128).T)
    P["gnb"] = bcast(norm[own])
    return {k: np.ascontiguousarray(v, dtype=np.float32) for k, v in P.items()}, perm


def load_w_generic(S, wsb_view_fn, wdram_view, nchunks, ncols, G, stg, stgtag, wtag, k0=0):
    k = k0
    for c0 in range(0, ncols, G):
        n = min(G, ncols - c0)
        st = stg[k % 2]
        tg = (stgtag, k % 2)
        sv = st[:, 0:nchunks * n].rearrange("p (c n) -> p c n", c=nchunks)
        S.dma(sv, wdram_view[:, :, c0:c0 + n], w=[tg])
        eng = [S.pool, S.dve, S.act][k % 3]
        if k % 3 == 2:
            eng(lambda e, sv=sv, c0=c0, n=n: e.copy(out=wsb_view_fn(c0, n), in_=sv), r=[tg], w=[wtag])
        else:
            eng(lambda e, sv=sv, c0=c0, n=n: e.tensor_copy(out=wsb_view_fn(c0, n), in_=sv), r=[tg], w=[wtag])
        k += 1
    return k


def blocks_of(H, TO, NB):
    bl = [(0, H)] if H > 0 else []
    for k in range(TO // NB):
        bl.append((H + NB * k, NB))
    return bl


def phase_C1(nc, S, H, TO, xTo, ysrc, hmask_d, Wg, bg_d, Wb, Wout, gmix, x1T, yr=(), xr=(), ypre=False):
    NB = 256
    with ExitStack() as st:
        A = Tiles(nc, st)
        S.barrier()
        T_ = {}
        wg = A.sb([128, 8, 3072], BF16, "wg")
        wb = A.sb([128, 3, 8, 1024], BF16, "wb")
        wo = A.sb([128, 8, 1024], BF16, "wo")
        stg = [A.sb([128, 2048], F32, "stg") for _ in range(2)]
        xb = [A.sb([128, 8, NB], F32, "xb") for _ in range(1)]
        yb = [A.sb([128, 24, NB], BF16, "yb") for _ in range(2)]
        yb2 = A.sb([128, 24, NB], BF16, "yb2") if len(ysrc) == 2 else None
        hT = A.sb([128, 8, NB], BF16, "hT")
        T_["xsq"] = A.sb([128, 8, NB], BF16, "xsq")
        T_["std"] = A.sb([128, NB], F32, "std")
        T_["rstd"] = A.sb([128, NB], F32, "rstd")
        T_["epsc"] = A.sb([128, 1], F32, "epsc")
        ones_bf = A.sb([128, 128], BF16, "ones")
        gcol = A.sb([128, 8], F32, "gcol")
        bg = A.sb([128, 24], F32, "bg")
        hmask = A.sb([128, 2], F32, "hmask")
        mg = A.sb([128, 8, NB], F32, "mg")
        mgT = A.sb([128, 8, NB], BF16, "mgT")
        sg = [A.sb([128, NB], F32, "sg") for _ in range(2)]
        tmp = [A.sb([128, NB], F32, "tmp") for _ in range(2)]
        ps_ss = A.ps([128, 512], F32, "ps_ss")
        psg = [A.ps([128, 512], F32, "psg") for _ in range(2)]
        psb = [A.ps([128, 512], F32, "psb") for _ in range(2)]
        pso = [A.ps([128, 512], F32, "pso") for _ in range(2)]
        S.pool(lambda e: e.memset(ones_bf[:, :], 1.0), w=["ones"])
        S.pool(lambda e: e.memset(T_["epsc"][:, :], EPS), w=["epsc"])
        S.dma(gcol[:, :], gmix, w=["gcol"])
        S.dma(bg[:, :], bg_d, w=["bg"])
        S.dma(hmask[:, :], hmask_d, w=["hmask"])
        k = load_w_generic(S, lambda c0, n: wg[:, :, c0:c0 + n], Wg.rearrange("(c p) n -> p c n", p=128), 8, 3072, 256, stg, "stg", "wg")
        for n_ in range(3):
            k = load_w_generic(S, lambda c0, n, n_=n_: wb[:, n_, :, c0:c0 + n], Wb[n_].rearrange("(c p) n -> p c n", p=128), 8, 1024, 256, stg, "stg", "wb", k)
        k = load_w_generic(S, lambda c0, n: wo[:, :, c0:c0 + n], Wout.rearrange("(c p) n -> p c n", p=128), 8, 1024, 256, stg, "stg", "wo", k)
        xv = xTo.rearrange("(c p) t -> p c t", p=128)
        yvs = ysrc if ypre else [y_.rearrange("(c p) t -> p c t", p=128) for y_ in ysrc]
        yview = (lambda t_: t_.rearrange("p (r k) n -> p r k n", r=2)) if ypre else (lambda t_: t_)
        ov = x1T.rearrange("(c p) t -> p c t", p=128)
        kk = 0
        for bi, (c0, nb) in enumerate(blocks_of(H, TO, NB)):
            xt = xb[0]
            xtag = ("xb", 0)
            yt = yb[bi % 2]
            ytag = ("yb", bi % 2)
            S.dma(xt[:, :, :nb], xv[:, :, c0:c0 + nb], r=list(xr), w=[xtag])
            if ypre:
                for r_ in range(2):
                    S.dma(yt[:, 12 * r_:12 * (r_ + 1), :nb], yvs[0][:, r_, :, c0:c0 + nb], r=list(yr), w=[ytag])
            else:
                S.dma(yt[:, :, :nb], yvs[0][:, :, c0:c0 + nb], r=list(yr), w=[ytag])
            if len(ysrc) == 2:
                for r_ in range(2):
                    S.dma(yb2[:, 12 * r_:12 * (r_ + 1), :nb], yvs[1][:, r_, :, c0:c0 + nb], r=list(yr), w=["yb2"])
                S.dve(lambda e, yt=yt, nb=nb: e.tensor_scalar(out=yt[:, :, :nb], in0=yt[:, :, :nb], scalar1=hmask[:, 0:1], scalar2=None, op0=ALU.mult),
                      r=[ytag, "hmask"], w=[ytag])
                S.dve(lambda e, yt=yt, nb=nb: e.scalar_tensor_tensor(out=yt[:, :, :nb], in0=yb2[:, :, :nb], scalar=hmask[:, 1:2], in1=yt[:, :, :nb], op0=ALU.mult, op1=ALU.add),
                      r=[ytag, "yb2", "hmask"], w=[ytag])
            if bi == 0 and H > 0:
                S.dve(lambda e, xt=xt, nb=nb: e.tensor_scalar(out=xt[:, :, :nb], in0=xt[:, :, :nb], scalar1=hmask[:, 1:2], scalar2=None, op0=ALU.mult),
                      r=[xtag, "hmask"], w=[xtag])
            rms_block(S, T_, ones_bf, xt, gcol, hT, ps_ss, xtag, nb)
            htag = ("hT", id(hT))
            for dc in range(8):
                for n_ in range(3):
                    pg = psg[kk % 2]
                    pgt = ("psg", kk % 2)
                    pbk = psb[kk % 2]
                    pbt = ("psb", kk % 2)
                    sgt = sg[kk % 2]
                    sgtag = ("sg", kk % 2)
                    tm = tmp[kk % 2]
                    tmtag = ("tmp", kk % 2)
                    kk += 1
                    for c in range(8):
                        S.pe(lambda e, c=c, pg=pg, n_=n_, dc=dc, nb=nb: e.matmul(pg[:, :nb], lhsT=wg[:, c, n_ * 1024 + dc * 128:n_ * 1024 + (dc + 1) * 128], rhs=hT[:, c, :nb],
                                                                               start=(c == 0), stop=(c == 7)), r=["wg", htag], w=[pgt])
                    S.act(lambda e, pg=pg, sgt=sgt, n_=n_, dc=dc, nb=nb: e.activation(out=sgt[:, :nb], in_=pg[:, :nb], func=AF.Sigmoid, bias=bg[:, n_ * 8 + dc:n_ * 8 + dc + 1]),
                          r=[pgt, "bg"], w=[sgtag])
                    for c in range(8):
                        ych = (c // 4) * 12 + n_ * 4 + (c % 4)
                        S.pe(lambda e, c=c, pbk=pbk, n_=n_, dc=dc, nb=nb, ych=ych, yt=yt: e.matmul(pbk[:, :nb], lhsT=wb[:, n_, c, dc * 128:(dc + 1) * 128], rhs=yt[:, ych, :nb],
                                                                                                start=(c == 0), stop=(c == 7)), r=["wb", ytag], w=[pbt])
                    if n_ == 0:
                        S.dve(lambda e, pbk=pbk, sgt=sgt, dc=dc, nb=nb: e.tensor_tensor(out=mg[:, dc, :nb], in0=pbk[:, :nb], in1=sgt[:, :nb], op=ALU.mult),
                              r=[pbt, sgtag], w=[("mg", dc)])
                    else:
                        S.dve(lambda e, pbk=pbk, sgt=sgt, tm=tm, nb=nb: e.tensor_tensor(out=tm[:, :nb], in0=pbk[:, :nb], in1=sgt[:, :nb], op=ALU.mult),
                              r=[pbt, sgtag], w=[tmtag])
                        dst = mg if n_ == 1 else mgT
                        S.pool(lambda e, tm=tm, dc=dc, nb=nb, dst=dst: e.tensor_tensor(out=dst[:, dc, :nb], in0=mg[:, dc, :nb], in1=tm[:, :nb], op=ALU.add),
                               r=[("mg", dc), tmtag], w=[("mg", dc), ("mgT", dc)])
            for dc in range(8):
                po = pso[dc % 2]
                pot = ("pso", dc % 2)
                for c in range(8):
                    S.pe(lambda e, c=c, po=po, dc=dc, nb=nb: e.matmul(po[:, :nb], lhsT=wo[:, c, dc * 128:(dc + 1) * 128], rhs=mgT[:, c, :nb], start=(c == 0), stop=(c == 7)),
                         r=["wo"] + [("mgT", c_) for c_ in range(8)], w=[pot])
                S.dve(lambda e, po=po, dc=dc, nb=nb, xt=xt: e.tensor_tensor(out=xt[:, dc, :nb], in0=po[:, :nb], in1=xt[:, dc, :nb], op=ALU.add), r=[pot, xtag, htag], w=[xtag])
            S.dma(ov[:, :, c0:c0 + nb], xt[:, :, :nb], r=[xtag], w=[("x1T", c0)])
        S.emit()


def phase_C2(nc, S, H, TO, x1T, Wup, cw_d, cb_d, Wdown, gffn, x2T, final_g=None, outT=None, xsend=None):
    NB = 256
    with ExitStack() as st:
        A = Tiles(nc, st)
        S.barrier()
        T_ = {}
        wu = A.sb([128, 8, 5632], BF16, "wu")
        wd = A.sb([128, 22, 1024], BF16, "wd")
        stg = [A.sb([128, 2048], F32, "stg") for _ in range(2)]
        xb = [A.sb([128, 8, NB], F32, "xb") for _ in range(2)]
        hT = A.sb([128, 8, NB], BF16, "hT")
        T_["xsq"] = A.sb([128, 8, NB], BF16, "xsq")
        T_["std"] = A.sb([128, NB], F32, "std")
        T_["rstd"] = A.sb([128, NB], F32, "rstd")
        T_["epsc"] = A.sb([128, 1], F32, "epsc")
        ones_bf = A.sb([128, 128], BF16, "ones")
        gcol = A.sb([128, 8], F32, "gcol")
        gfin = A.sb([128, 8], F32, "gfin")
        cw = A.sb([128, 44, 3], F32, "cw")
        cb = A.sb([128, 44], F32, "cb")
        halo = A.sb([128, 44, 2], F32, "halo")
        ua = [A.sb([128, NB + 2], F32, "ua") for _ in range(2)]
        ug = [A.sb([128, NB + 2], F32, "ug") for _ in range(2)]
        aa = [A.sb([128, NB], F32, "aa") for _ in range(2)]
        ag = [A.sb([128, NB], F32, "ag") for _ in range(2)]
        actT = A.sb([128, 22, NB], BF16, "actT")
        oT = A.sb([128, 8, NB], F32, "oT")
        ps_ss = A.ps([128, 512], F32, "ps_ss")
        psa = [A.ps([128, 512], F32, "psa") for _ in range(2)]
        psgt = [A.ps([128, 512], F32, "psgt") for _ in range(2)]
        psd = [A.ps([128, 512], F32, "psd") for _ in range(2)]
        S.pool(lambda e: e.memset(ones_bf[:, :], 1.0), w=["ones"])
        S.pool(lambda e: e.memset(T_["epsc"][:, :], EPS), w=["epsc"])
        S.dve(lambda e: e.memset(halo[:, :, :], 0.0), w=["halo"])
        S.dma(gcol[:, :], gffn, w=["gcol"])
        if final_g is not None:
            S.dma(gfin[:, :], final_g, w=["gfin"])
        S.dma(cw[:, :, :], cw_d, w=["cw"])
        S.dma(cb[:, :], cb_d, w=["cb"])
        k = load_w_generic(S, lambda c0, n: wu[:, :, c0:c0 + n], Wup.rearrange("(c p) n -> p c n", p=128), 8, 5632, 256, stg, "stg", "wu")
        k = load_w_generic(S, lambda c0, n: wd[:, :, c0:c0 + n], Wdown.rearrange("(c p) n -> p c n", p=128), 22, 1024, 64, stg, "stg", "wd", k)
        xv = x1T.rearrange("(c p) t -> p c t", p=128)
        ov = x2T.rearrange("(c p) t -> p c t", p=128)
        kk = 0
        for bi, (c0, nb) in enumerate(blocks_of(H, TO, NB)):
            xt = xb[bi % 2]
            xtag = ("xb", bi % 2)
            S.dma(xt[:, :, :nb], xv[:, :, c0:c0 + nb], r=[("x1T", c0)], w=[xtag])
            rms_block(S, T_, ones_bf, xt, gcol, hT, ps_ss, xtag, nb)
            htag = ("hT", id(hT))
            for fc in range(22):
                b2 = kk % 2
                kk += 1
                for (ps, pst, off, ut, utag, acc, atag, hc) in ((psa[b2], ("psa", b2), 0, ua[b2], ("ua", b2), aa[b2], ("aa", b2), fc),
                                                                (psgt[b2], ("psgt", b2), 2816, ug[b2], ("ug", b2), ag[b2], ("ag", b2), 22 + fc)):
                    for c in range(8):
                        S.pe(lambda e, c=c, ps=ps, off=off, fc=fc, nb=nb: e.matmul(ps[:, :nb], lhsT=wu[:, c, off + fc * 128:off + (fc + 1) * 128], rhs=hT[:, c, :nb],
                                                                                 start=(c == 0), stop=(c == 7)), r=["wu", htag], w=[pst])
                    S.act(lambda e, ut=ut, hc=hc: e.copy(out=ut[:, 0:2], in_=halo[:, hc, :]), r=[("halo", hc)], w=[utag])
                    S.act(lambda e, ut=ut, ps=ps, nb=nb: e.copy(out=ut[:, 2:2 + nb], in_=ps[:, :nb]), r=[pst], w=[utag])
                    S.act(lambda e, ut=ut, hc=hc, nb=nb: e.copy(out=halo[:, hc, :], in_=ut[:, nb:nb + 2]), r=[utag], w=[("halo", hc)])
                    S.dve(lambda e, ut=ut, acc=acc, hc=hc, nb=nb: e.tensor_scalar(out=acc[:, :nb], in0=ut[:, 0:nb], scalar1=cw[:, hc, 0:1], scalar2=cb[:, hc:hc + 1],
                                                                                op0=ALU.mult, op1=ALU.add), r=[utag, "cw", "cb"], w=[atag])
                    for j in (1, 2):
                        S.dve(lambda e, ut=ut, acc=acc, hc=hc, nb=nb, j=j: e.scalar_tensor_tensor(out=acc[:, :nb], in0=ut[:, j:j + nb], scalar=cw[:, hc, j:j + 1], in1=acc[:, :nb],
                                                                                                  op0=ALU.mult, op1=ALU.add), r=[utag, "cw", atag], w=[atag])
                S.act(lambda e, b2=b2, nb=nb: e.activation(out=ag[b2][:, :nb], in_=ag[b2][:, :nb], func=AF.Silu), r=[("ag", b2)], w=[("ag", b2)])
                S.pool(lambda e, b2=b2, fc=fc, nb=nb: e.tensor_tensor(out=actT[:, fc, :nb], in0=aa[b2][:, :nb], in1=ag[b2][:, :nb], op=ALU.mult),
                       r=[("aa", b2), ("ag", b2)], w=[("actT", fc)])
            for dc in range(8):
                po = psd[dc % 2]
                pot = ("psd", dc % 2)
                for fc in range(22):
                    S.pe(lambda e, fc=fc, po=po, dc=dc, nb=nb: e.matmul(po[:, :nb], lhsT=wd[:, fc, dc * 128:(dc + 1) * 128], rhs=actT[:, fc, :nb], start=(fc == 0), stop=(fc == 21)),
                         r=["wd"] + [("actT", f_) for f_ in range(22)], w=[pot])
                S.dve(lambda e, po=po, dc=dc, nb=nb, xt=xt: e.tensor_tensor(out=xt[:, dc, :nb], in0=po[:, :nb], in1=xt[:, dc, :nb], op=ALU.add), r=[pot, xtag, htag], w=[xtag])
            if final_g is None:
                S.dma(ov[:, :, c0:c0 + nb], xt[:, :, :nb], r=[xtag], w=[("x2T", c0)])
                if xsend is not None and not (bi == 0 and H > 0):
                    S.dma(xsend.rearrange("(c p) t -> p c t", p=128)[:, :, c0 - H:c0 - H + nb], xt[:, :, :nb], r=[xtag], w=[("xsend", c0)])
            elif not (bi == 0 and H > 0):
                xsq, std, rstd = T_["xsq"], T_["std"], T_["rstd"]
                S.act(lambda e, xt=xt, nb=nb: e.activation(out=xsq[:, :, :nb], in_=xt[:, :, :nb], func=AF.Square), r=[xtag], w=["xsq"])
                for c in range(8):
                    S.pe(lambda e, c=c, nb=nb: e.matmul(ps_ss[:, :nb], lhsT=ones_bf[:, :], rhs=xsq[:, c, :nb], start=(c == 0), stop=(c == 7)), r=["xsq", "ones"], w=["ps_ss"])
                S.act(lambda e, nb=nb: e.activation(out=std[:, :nb], in_=ps_ss[:, :nb], func=AF.Sqrt, bias=T_["epsc"][:, 0:1], scale=1.0 / D), r=["ps_ss", "epsc"], w=["std"])
                S.dve(lambda e, nb=nb: e.reciprocal(out=rstd[:, :nb], in_=std[:, :nb]), r=["std"], w=["rstd"])
                for c in range(8):
                    S.dve(lambda e, c=c, nb=nb, xt=xt: e.scalar_tensor_tensor(out=oT[:, c, :nb], in0=xt[:, c, :nb], scalar=gfin[:, c:c + 1], in1=rstd[:, :nb], op0=ALU.mult, op1=ALU.mult),
                          r=[xtag, "rstd", "gfin"], w=["oT"])
                S.dma(outT.rearrange("(c p) t -> p c t", p=128)[:, :, c0 - H:c0 - H + nb], oT[:, :, :nb], r=["oT"], w=[("outT", c0)])
        S.emit()


T_FULL = 8192
TO_FULL = 4096
ML_SH = {"cw": [128, 8, 4], "cb": [128, 8], "wq_bd": [4, 128, 128], "wk_bd": [4, 128, 128], "wv_bd": [4, 128, 128], "wqT_bd": [8, 128, 128],
         "wkT_bd": [8, 128, 128], "wvT_bd": [8, 128, 128], "wif": [128, 3, 8, 4], "gb": [128, 4], "skc": [128, 4], "gnb": [128, 512]}


def build_AB(T):
    nc = bass.Bass("TRN2", target_bir_lowering=False)
    dt = lambda n, s, d, k: nc.dram_tensor(n, s, d, kind=k).ap()
    xT = dt("xT", [D, T], F32, "ExternalInput")
    wA = dt("wA", [D, NA], F32, "ExternalInput")
    gm = dt("gmix", [128, 8], F32, "ExternalInput")
    fb = dt("foxbf_t", [128, T // 128, 4], F32, "ExternalInput")
    wl = dt("wlr_aug", [17, 256], F32, "ExternalInput")
    gn = dt("gla_gnb", [128, 512], F32, "ExternalInput")
    P = {k: dt("m_" + k, v, F32, "ExternalInput") for k, v in ML_SH.items()}
    yT = dt("yT", [1536, T], BF16, "ExternalOutput")
    pFM = dt("pFM", [NFM, T], BF16, "Internal")
    pTM = dt("pTM", [T, NTM], BF16, "Internal")
    with ExitStack() as st:
        S = Sched(nc, st)
        phase_A(nc, S, T, xT, wA, gm, pFM, pTM)
        yd = YDst(T, yT=yT)
        phase_gla(nc, S, T, pFM, pTM, wl, gn, yd)
        phase_mlstm(nc, S, T, pFM, pTM, P, yd)
        phase_fox(nc, S, T, pFM, pTM, fb, yd)
        S.finish()
        S.emit()
    return nc


def build_C(H, TO, final):
    nc = bass.Bass("TRN2", target_bir_lowering=False)
    dt = lambda n, s, d, k: nc.dram_tensor(n, s, d, kind=k).ap()
    W = H + TO
    xTo = dt("xTo", [D, W], F32, "ExternalInput")
    yTall = dt("yTall", [3072, W], BF16, "ExternalInput")
    hm = dt("hmask", [128, 2], F32, "ExternalInput")
    Wg = dt("Wg", [D, 3072], F32, "ExternalInput")
    bg = dt("bg", [128, 24], F32, "ExternalInput")
    Wb = dt("Wb", [3, D, D], F32, "ExternalInput")
    Wo = dt("Wout", [D, D], F32, "ExternalInput")
    gm = dt("gmix", [128, 8], F32, "ExternalInput")
    Wup = dt("Wup", [D, 5632], F32, "ExternalInput")
    cw = dt("fcw", [128, 44, 3], F32, "ExternalInput")
    cb = dt("fcb", [128, 44], F32, "ExternalInput")
    Wd = dt("Wdown", [2816, D], F32, "ExternalInput")
    gf = dt("gffn", [128, 8], F32, "ExternalInput")
    x1T = dt("x1T", [D, W], F32, "Internal")
    if final:
        gfin = dt("gfin", [128, 8], F32, "ExternalInput")
        outT = dt("outT", [D, TO], F32, "ExternalOutput")
        x2T = x1T
    else:
        gfin, outT = None, None
        x2T = dt("x2T", [D, W], F32, "ExternalOutput")
    with ExitStack() as st:
        S = Sched(nc, st)
        phase_C1(nc, S, H, TO, xTo, [yTall], hm, Wg, bg, Wb, Wo, gm, x1T)
        phase_C2(nc, S, H, TO, x1T, Wup, cw, cb, Wd, gf, x2T, gfin, outT)
        S.finish()
        S.emit()
    return nc


def colvec(v):
    v = np.asarray(v, np.float32)
    return np.ascontiguousarray(v.reshape(-1, 128).T)


SPL = np.cumsum([0, 512, 512, 1024, 16, 1024, 1024, 1024, 1024, 1024, 1024, 8, 1024, 3072])
O_GQ, O_GK, O_GV, O_GLR, O_GR, O_MX, O_MZ, O_FQ, O_FK, O_FV, O_FF, O_FOG, O_GATES = [int(v) for v in SPL[:13]]


def prep_AB_inputs(p, l, inp, T):
    w_in = inp["w_in"][l]
    own = np.arange(512 * p, 512 * p + 512)
    oth = np.arange(512 * (1 - p), 512 * (1 - p) + 512)
    perm = np.concatenate([own, oth])
    r256 = np.arange(256 * p, 256 * p + 256)
    cols = np.concatenate([O_GQ + r256, O_GK + r256, O_MX + perm, O_FQ + own, O_FK + own, O_FOG + own, O_GLR + np.arange(16),
                           O_GK + r256, O_GV + own, O_GR + own, O_MZ + own, O_FV + own, O_FF + np.arange(4 * p, 4 * p + 4)])
    assert cols.size == NA
    d = {}
    d["wA"] = np.ascontiguousarray(w_in[:, cols])
    d["gmix"] = colvec(inp["norm_mix"][l])
    d["foxbf_t"] = np.ascontiguousarray(np.broadcast_to(inp["fox_b_f"][l][4 * p:4 * p + 4], (128, T // 128, 4))).astype(np.float32)
    d["wlr_aug"] = np.ascontiguousarray(np.concatenate([inp["gla_w_lr"][l][:, r256], inp["gla_b_lr"][l][r256][None]], 0)).astype(np.float32)
    d["gla_gnb"] = bcast(inp["gla_norm"][l][own])
    P, _ = prep_mlstm(p, inp["mlstm_conv_w"][l], inp["mlstm_conv_b"][l], inp["mlstm_wq"][l], inp["mlstm_wk"][l], inp["mlstm_wv"][l],
                      inp["mlstm_w_i"][l], inp["mlstm_b_i"][l], inp["mlstm_w_f"][l], inp["mlstm_b_f"][l], inp["mlstm_skip"][l], inp["mlstm_norm"][l])
    for k, v in P.items():
        d["m_" + k] = v
    return d


def prep_C_inputs(l, inp, final):
    d = {}
    d["Wg"] = np.ascontiguousarray(inp["w_in"][l][:, O_GATES:O_GATES + 3072])
    d["bg"] = colvec(inp["b_gate"][l].reshape(-1))
    d["Wb"] = np.ascontiguousarray(inp["w_branch"][l])
    d["Wout"] = np.ascontiguousarray(inp["w_out"][l])
    d["gmix"] = colvec(inp["norm_mix"][l])
    d["Wup"] = np.ascontiguousarray(inp["ffn_w_up"][l])
    d["fcw"] = np.ascontiguousarray(inp["ffn_conv_w"][l].reshape(3, 44, 128).transpose(2, 1, 0))
    d["fcb"] = colvec(inp["ffn_conv_b"][l])
    d["Wdown"] = np.ascontiguousarray(inp["ffn_w_down"][l])
    d["gffn"] = colvec(inp["norm_ffn"][l])
    if final:
        d["gfin"] = colvec(inp["norm_final"])
    return d


_NC_CACHE = {}


def _get_nc(key, fn):
    if key not in _NC_CACHE:
        _NC_CACHE[key] = fn()
    return _NC_CACHE[key]


def kernel_unfused(inp):
    x = inp["x"]
    B, T, _ = x.shape
    TO = T // 2
    ncore = 2 * B
    HS = [4, 2]
    xT_full = [np.ascontiguousarray(x[b].T) for b in range(B)]
    def own_slice(arr, p, H):
        if p == 0:
            return np.ascontiguousarray(np.concatenate([np.zeros((arr.shape[0], H), arr.dtype), arr[:, :TO]], axis=1))
        return np.ascontiguousarray(arr[:, TO - H:])
    xTo = [own_slice(xT_full[c // 2], c % 2, HS[0]) for c in range(ncore)]
    out = None
    for l in range(2):
        H = HS[l]
        final = (l == 1)
        nc_ab = _get_nc(("AB", T), lambda: build_AB(T))
        in_maps = []
        for c in range(ncore):
            d = prep_AB_inputs(c % 2, l, inp, T)
            d["xT"] = xT_full[c // 2]
            in_maps.append(d)
        res = run_bass_kernel_spmd(nc_ab, in_maps, core_ids=list(range(ncore)))
        yTs = [np.asarray(r["yT"]) for r in res.results]
        nc_c = _get_nc(("C", H, TO, final), lambda: build_C(H, TO, final))
        cw = prep_C_inputs(l, inp, final)
        in_maps = []
        for c in range(ncore):
            b, p = c // 2, c % 2
            d = dict(cw)
            d["xTo"] = xTo[c]
            d["yTall"] = np.ascontiguousarray(np.concatenate([own_slice(yTs[2 * b + rr], p, H) for rr in range(2)], axis=0))
            d["hmask"] = np.ascontiguousarray(np.broadcast_to(np.array([1.0 - p, float(p)], np.float32), (128, 2)))
            in_maps.append(d)
        res = run_bass_kernel_spmd(nc_c, in_maps, core_ids=list(range(ncore)))
        if not final:
            x2 = [np.asarray(r["x2T"]) for r in res.results]
            xT_full = [np.ascontiguousarray(np.concatenate([x2[2 * b][:, H:], x2[2 * b + 1][:, H:]], axis=1)) for b in range(B)]
            xTo = [np.ascontiguousarray(x2[c][:, H - HS[1]:]) for c in range(ncore)]
        else:
            o = [np.asarray(r["outT"]) for r in res.results]
            out = np.stack([np.concatenate([o[2 * b], o[2 * b + 1]], axis=1).T for b in range(B)]).astype(np.float32)
    return np.ascontiguousarray(out)


AB_IN = [("wA", [D, NA]), ("gmix", [128, 8]), ("foxbf_t", None), ("wlr_aug", [17, 256]), ("gla_gnb", [128, 512])] + [("m_" + k, v) for k, v in ML_SH.items()]
C_IN = [("Wg", [D, 3072]), ("bg", [128, 24]), ("Wb", [3, D, D]), ("Wout", [D, D]), ("Wup", [D, 5632]), ("fcw", [128, 44, 3]), ("fcb", [128, 44]),
        ("Wdown", [2816, D]), ("gffn", [128, 8])]


def build_fused(T, ncore):
    TO = T // 2
    HS = [4, 2]
    NBK = T // 512
    half = NBK // 2
    groups = [[i, i + 1] for i in range(0, ncore, 2)]
    nc = bass.Bass("TRN2", target_bir_lowering=False)
    dt = lambda n, s, d, k: nc.dram_tensor(n, s, d, kind=k).ap()
    xT = dt("xT", [D, T], F32, "ExternalInput")
    xTo = dt("xTo", [D, HS[0] + TO], F32, "ExternalInput")
    hm = dt("hmask", [128, 2], F32, "ExternalInput")
    gfin = dt("gfin", [128, 8], F32, "ExternalInput")
    outT = dt("outT", [D, TO], F32, "ExternalOutput")
    W = []
    for l in range(2):
        d = {}
        for n, shp in AB_IN:
            d[n] = dt(f"{n}_l{l}", shp if shp is not None else [128, T // 128, 4], F32, "ExternalInput")
        for n, shp in C_IN:
            d[n] = dt(f"{n}_l{l}", shp, F32, "ExternalInput")
        W.append(d)
    pFM = dt("pFM", [NFM, T], BF16, "Internal")
    pTM = dt("pTM", [T, NTM], BF16, "Internal")
    ys = [[dt(f"ys_{l}_{s_}", [1536, HS[l] + TO], BF16, "Internal") for s_ in range(2)] for l in range(2)]
    yg = [[dt(f"yg_{l}_{s_}", [12, 2, 128, HS[l] + TO], BF16, "Internal") for s_ in range(2)] for l in range(2)]
    x1T = [dt(f"x1T_{l}", [D, HS[l] + TO], F32, "Internal") for l in range(2)]
    x2T0 = dt("x2T_0", [D, HS[0] + TO], F32, "Internal")
    xsend = dt("xsend", [D, TO], F32, "Internal")
    sgT = dt("sgT", [3072, HS[0] + TO], BF16, "Internal")
    actD = dt("actD", [2816, HS[0] + TO], BF16, "Internal")
    yselD = dt("yselD", [3072, HS[0] + TO], BF16, "Internal")
    xg = dt("xg", [8, 2, 128, TO], F32, "Internal")
    with ExitStack() as st:
        S = Sched(nc, st)
        for l in range(2):
            H = HS[l]
            w = W[l]
            if l == 0:
                phase_A(nc, S, T, xT, w["wA"], w["gmix"], pFM, pTM)
            else:
                xgv = xg.rearrange("c r p t -> r p c t")
                xblk = lambda j: xgv[j // half][:, :, 512 * (j % half):512 * (j % half + 1)]
                phase_A(nc, S, T, None, w["wA"], w["gmix"], pFM, pTM, xblk=xblk, xr=["xg"])
            yd = YDst(T, ys=ys[l], H=H, key=("y", l))
            P = {k: w["m_" + k] for k in ML_SH}
            def gather_rows(i0, i1, row_lo, row_hi):
                toks = [t for t in yd.tokens if t[1] == "zero" or (isinstance(t[1], int) and row_lo <= t[1] < row_hi)]
                for s_ in range(2):
                    for i in range(i0, i1):
                        S.collective("AllGather", [ys[l][s_][128 * i:128 * (i + 1), :]], [yg[l][s_][i].rearrange("r p w -> (r p) w")], groups,
                                     r=toks, w=[("yg", l, s_, i)])

            phase_gla(nc, S, T, pFM, pTM, w["wlr_aug"], w["gla_gnb"], yd, zero_halo=True)
            gather_rows(0, 4, 0, 512)
            phase_mlstm(nc, S, T, pFM, pTM, P, yd)
            gather_rows(4, 8, 512, 1024)
            phase_fox(nc, S, T, pFM, pTM, w["foxbf_t"], yd)
            gather_rows(8, 12, 1024, 1536)
            xin = xTo if l == 0 else x2T0[:, HS[0] - HS[1]:]
            xr = [] if l == 0 else [("x2T", c0) for (c0, nb) in blocks_of(HS[0], TO, 256)]
            W_ = H + TO
            sg_l = sgT[:, 0:W_]
            act_l = actD[:, 0:W_]
            blk = blocks_of(HS[0], TO, 512)
            xr2 = [] if l == 0 else [("x2T", c0) for (c0, nb) in blk]
            ysel_l = yselD[:, 0:W_]
            phase_C1a(nc, S, H, TO, xin, hm, w["Wg"], w["bg"], w["gmix"], sg_l, xr=xr2,
                      ysrc=[y_.rearrange("k r p w -> p r k w") for y_ in yg[l]], ysel=ysel_l, yr=[("yg", l, s_, i) for s_ in range(2) for i in range(12)])
            phase_C1b(nc, S, H, TO, xin, ysel_l, hm, sg_l, w["Wb"], w["Wout"], x1T[l], xr=xr2)
            phase_C2a(nc, S, H, TO, x1T[l], w["Wup"], w["fcw"], w["fcb"], w["gffn"], act_l)
            if l == 0:
                phase_C2b(nc, S, H, TO, x1T[l], act_l, w["Wdown"], x2T0, xsend=xsend)
                for i in range(8):
                    S.collective("AllGather", [xsend[128 * i:128 * (i + 1), :]], [xg[i].rearrange("r p t -> (r p) t")], groups,
                                 r=[("xsend", c0) for (c0, nb) in blocks_of(H, TO, 512)], w=["xg"])
            else:
                phase_C2b(nc, S, H, TO, x1T[l], act_l, w["Wdown"], x1T[l], gfin, outT)
        S.finish()
        S.emit()
    return nc


def kernel_fused(inp):
    x = inp["x"]
    B, T, _ = x.shape
    TO = T // 2
    ncore = 2 * B
    nc = _get_nc(("F", T, ncore), lambda: build_fused(T, ncore))
    in_maps = []
    per_rank = {}
    for p in range(2):
        d = {}
        for l in range(2):
            for k, v in prep_AB_inputs(p, l, inp, T).items():
                d[f"{k}_l{l}"] = v
            for k, v in prep_C_inputs(l, inp, False).items():
                if k != "gmix":
                    d[f"{k}_l{l}"] = v
        d["gfin"] = colvec(inp["norm_final"])
        d["hmask"] = np.ascontiguousarray(np.broadcast_to(np.array([1.0 - p, float(p)], np.float32), (128, 2)))
        per_rank[p] = d
    for c in range(ncore):
        b, p = c // 2, c % 2
        d = dict(per_rank[p])
        xTb = np.ascontiguousarray(x[b].T)
        d["xT"] = xTb
        if p == 0:
            d["xTo"] = np.ascontiguousarray(np.concatenate([np.zeros((D, 4), np.float32), xTb[:, :TO]], axis=1))
        else:
            d["xTo"] = np.ascontiguousarray(xTb[:, TO - 4:])
        in_maps.append(d)
    res = run_bass_kernel_spmd(nc, in_maps, core_ids=list(range(ncore)))
    o = [np.asarray(r["outT"]) for r in res.results]
    out = np.stack([np.concatenate([o[2 * b], o[2 * b + 1]], axis=1).T for b in range(B)]).astype(np.float32)
    return np.ascontiguousarray(out)


FUSED = True


def kernel(**inputs):
    inp = {k: np.asarray(v, dtype=np.float32) for k, v in inputs.items()}
    if FUSED:
        return kernel_fused(inp)
    return kernel_unfused(inp)


def phase_C1a(nc, S, H, TO, xTo, hmask_d, Wg, bg_d, gmix, sgT, xr=(), ysrc=None, ysel=None, yr=()):
    NB = 512
    with ExitStack() as st:
        A = Tiles(nc, st)
        S.barrier()
        T_ = {}
        wg = A.sb([128, 8, 3072], BF16, "wg")
        stg = [A.sb([128, 2048], F32, "stg") for _ in range(2)]
        xb = [A.sb([128, 8, NB], F32, "xb") for _ in range(2)]
        hT = A.sb([128, 8, NB], BF16, "hT")
        T_["xsq"] = A.sb([128, 8, NB], BF16, "xsq")
        T_["std"] = A.sb([128, NB], F32, "std")
        T_["rstd"] = A.sb([128, NB], F32, "rstd")
        T_["epsc"] = A.sb([128, 1], F32, "epsc")
        ones_bf = A.sb([128, 128], BF16, "ones")
        gcol = A.sb([128, 8], F32, "gcol")
        bg = A.sb([128, 24], F32, "bg")
        hmask = A.sb([128, 2], F32, "hmask")
        sgo = [A.sb([128, NB], BF16, "sgo") for _ in range(4)]
        if ysrc is not None:
            ya = A.sb([128, 24, NB], BF16, "ya")
            yb_ = A.sb([128, 24, NB], BF16, "yb_")
            ysv = ysel.rearrange("(c p) t -> p c t", p=128)
        ps_ss = A.ps([128, 512], F32, "ps_ss")
        psg = [A.ps([128, 512], F32, "psg") for _ in range(4)]
        S.pool(lambda e: e.memset(ones_bf[:, :], 1.0), w=["ones"])
        S.pool(lambda e: e.memset(T_["epsc"][:, :], EPS), w=["epsc"])
        S.dma(gcol[:, :], gmix, w=["gcol"])
        S.dma(bg[:, :], bg_d, w=["bg"])
        S.dma(hmask[:, :], hmask_d, w=["hmask"])
        load_w_generic(S, lambda c0, n: wg[:, :, c0:c0 + n], Wg.rearrange("(c p) n -> p c n", p=128), 8, 3072, 256, stg, "stg", "wg")
        xv = xTo.rearrange("(c p) t -> p c t", p=128)
        kk = 0
        blks = blocks_of(H, TO, NB)

        def load_x(bi):
            c0_, nb_ = blks[bi]
            S.dma(xb[bi % 2][:, :, :nb_], xv[:, :, c0_:c0_ + nb_], r=list(xr), w=[("xb", bi % 2)])

        load_x(0)
        for bi, (c0, nb) in enumerate(blks):
            xt = xb[bi % 2]
            xtag = ("xb", bi % 2)
            if bi + 1 < len(blks):
                load_x(bi + 1)
            if ysrc is not None:
                for r_ in range(2):
                    S.dma(ya[:, 12 * r_:12 * (r_ + 1), :nb], ysrc[0][:, r_, :, c0:c0 + nb], r=list(yr), w=["ya"])
                    S.dma(yb_[:, 12 * r_:12 * (r_ + 1), :nb], ysrc[1][:, r_, :, c0:c0 + nb], r=list(yr), w=["yb_"])
                S.dve(lambda e, nb=nb: e.tensor_scalar(out=ya[:, :, :nb], in0=ya[:, :, :nb], scalar1=hmask[:, 0:1], scalar2=None, op0=ALU.mult),
                      r=["ya", "hmask"], w=["ya"])
                S.dve(lambda e, nb=nb: e.scalar_tensor_tensor(out=ya[:, :, :nb], in0=yb_[:, :, :nb], scalar=hmask[:, 1:2], in1=ya[:, :, :nb], op0=ALU.mult, op1=ALU.add),
                      r=["ya", "yb_", "hmask"], w=["ya"])
                S.dma(ysv[:, :, c0:c0 + nb], ya[:, :, :nb], r=["ya"], w=[("ysel", c0)])
            if bi == 0 and H > 0:
                S.dve(lambda e, xt=xt, nb=nb: e.tensor_scalar(out=xt[:, :, :nb], in0=xt[:, :, :nb], scalar1=hmask[:, 1:2], scalar2=None, op0=ALU.mult),
                      r=[xtag, "hmask"], w=[xtag])
            rms_block(S, T_, ones_bf, xt, gcol, hT, ps_ss, xtag, nb)
            htag = ("hT", id(hT))
            for n_ in range(3):
                for dc in range(8):
                    pg = psg[kk % 4]
                    pgt = ("psg", kk % 4)
                    so = sgo[kk % 4]
                    sot = ("sgo", kk % 4)
                    kk += 1
                    for c in range(8):
                        S.pe(lambda e, c=c, pg=pg, n_=n_, dc=dc, nb=nb: e.matmul(pg[:, :nb], lhsT=wg[:, c, n_ * 1024 + dc * 128:n_ * 1024 + (dc + 1) * 128], rhs=hT[:, c, :nb],
                                                                               start=(c == 0), stop=(c == 7)), r=["wg", htag], w=[pgt])
                    S.act(lambda e, pg=pg, so=so, n_=n_, dc=dc, nb=nb: e.activation(out=so[:, :nb], in_=pg[:, :nb], func=AF.Sigmoid, bias=bg[:, n_ * 8 + dc:n_ * 8 + dc + 1]),
                          r=[pgt, "bg"], w=[sot])
                    k_ = n_ * 8 + dc
                    S.dma(sgT[k_ * 128:(k_ + 1) * 128, c0:c0 + nb], so[:, :nb], r=[sot], w=[("sgT", c0)])
        S.emit()


def phase_C1b(nc, S, H, TO, xTo, ysel, hmask_d, sgT, Wb, Wout, x1T, xr=()):
    NB = 512
    with ExitStack() as st:
        A = Tiles(nc, st)
        S.barrier()
        wb = A.sb([128, 3, 8, 1024], BF16, "wb")
        wo = A.sb([128, 8, 1024], BF16, "wo")
        stg = [A.sb([128, 2048], F32, "stg") for _ in range(2)]
        xb = A.sb([128, 8, NB], F32, "xb")
        yb = [A.sb([128, 24, NB], BF16, "yb") for _ in range(2)]
        sgr = [A.sb([128, 3, NB], BF16, "sgr") for _ in range(4)]
        hmask = A.sb([128, 2], F32, "hmask")
        mg = [A.sb([128, NB], F32, "mg") for _ in range(2)]
        mgT = A.sb([128, 8, NB], BF16, "mgT")
        tmp = [A.sb([128, NB], F32, "tmp") for _ in range(2)]
        psb = [A.ps([128, 512], F32, "psb") for _ in range(4)]
        pso = [A.ps([128, 512], F32, "pso") for _ in range(2)]
        S.dma(hmask[:, :], hmask_d, w=["hmask"])
        k = 0
        for n_ in range(3):
            k = load_w_generic(S, lambda c0, n, n_=n_: wb[:, n_, :, c0:c0 + n], Wb[n_].rearrange("(c p) n -> p c n", p=128), 8, 1024, 256, stg, "stg", "wb", k)
        k = load_w_generic(S, lambda c0, n: wo[:, :, c0:c0 + n], Wout.rearrange("(c p) n -> p c n", p=128), 8, 1024, 256, stg, "stg", "wo", k)
        xv = xTo.rearrange("(c p) t -> p c t", p=128)
        yv = ysel.rearrange("(c p) t -> p c t", p=128)
        sv = sgT.rearrange("(n d p) t -> p d n t", n=3, p=128)
        ov = x1T.rearrange("(c p) t -> p c t", p=128)
        blks = blocks_of(H, TO, NB)

        def load_y(bi):
            c0_, nb_ = blks[bi]
            S.dma(yb[bi % 2][:, :, :nb_], yv[:, :, c0_:c0_ + nb_], r=[("ysel", c0_)], w=[("yb", bi % 2)])

        kk = 0
        ks = 0
        load_y(0)
        for bi, (c0, nb) in enumerate(blks):
            yt = yb[bi % 2]
            ytag = ("yb", bi % 2)
            if bi + 1 < len(blks):
                load_y(bi + 1)
            S.dma(xb[:, :, :nb], xv[:, :, c0:c0 + nb], r=list(xr), w=["xb"])
            if bi == 0 and H > 0:
                S.dve(lambda e, nb=nb: e.tensor_scalar(out=xb[:, :, :nb], in0=xb[:, :, :nb], scalar1=hmask[:, 1:2], scalar2=None, op0=ALU.mult),
                      r=["xb", "hmask"], w=["xb"])
            for dc in range(8):
                sg = sgr[ks % 4]
                sgtag = ("sgr", ks % 4)
                ks += 1
                S.dma(sg[:, :, :nb], sv[:, dc, :, c0:c0 + nb], r=[("sgT", c0)], w=[sgtag])
                mgd = mg[dc % 2]
                mgtag = ("mg", dc % 2)
                for n_ in range(3):
                    pbk = psb[kk % 4]
                    pbt = ("psb", kk % 4)
                    tm = tmp[kk % 2]
                    tmtag = ("tmp", kk % 2)
                    kk += 1
                    for c in range(8):
                        ych = (c // 4) * 12 + n_ * 4 + (c % 4)
                        S.pe(lambda e, c=c, pbk=pbk, n_=n_, dc=dc, nb=nb, ych=ych, yt=yt: e.matmul(pbk[:, :nb], lhsT=wb[:, n_, c, dc * 128:(dc + 1) * 128], rhs=yt[:, ych, :nb],
                                                                                                start=(c == 0), stop=(c == 7)), r=["wb", ytag], w=[pbt])
                    if n_ == 0:
                        S.dve(lambda e, pbk=pbk, nb=nb, sg=sg, mgd=mgd: e.tensor_tensor(out=mgd[:, :nb], in0=pbk[:, :nb], in1=sg[:, 0, :nb], op=ALU.mult),
                              r=[pbt, sgtag], w=[mgtag])
                    else:
                        S.dve(lambda e, pbk=pbk, tm=tm, nb=nb, sg=sg, n_=n_: e.tensor_tensor(out=tm[:, :nb], in0=pbk[:, :nb], in1=sg[:, n_, :nb], op=ALU.mult),
                              r=[pbt, sgtag], w=[tmtag])
                        if n_ == 1:
                            S.pool(lambda e, tm=tm, nb=nb, mgd=mgd: e.tensor_tensor(out=mgd[:, :nb], in0=mgd[:, :nb], in1=tm[:, :nb], op=ALU.add),
                                   r=[mgtag, tmtag], w=[mgtag])
                        else:
                            S.pool(lambda e, tm=tm, dc=dc, nb=nb, mgd=mgd: e.tensor_tensor(out=mgT[:, dc, :nb], in0=mgd[:, :nb], in1=tm[:, :nb], op=ALU.add),
                                   r=[mgtag, tmtag], w=[("mgT", dc)])
            for dc in range(8):
                po = pso[dc % 2]
                pot = ("pso", dc % 2)
                for c in range(8):
                    S.pe(lambda e, c=c, po=po, dc=dc, nb=nb: e.matmul(po[:, :nb], lhsT=wo[:, c, dc * 128:(dc + 1) * 128], rhs=mgT[:, c, :nb], start=(c == 0), stop=(c == 7)),
                         r=["wo"] + [("mgT", c_) for c_ in range(8)], w=[pot])
                S.dve(lambda e, po=po, dc=dc, nb=nb: e.tensor_tensor(out=xb[:, dc, :nb], in0=po[:, :nb], in1=xb[:, dc, :nb], op=ALU.add), r=[pot, "xb"], w=["xb"])
            S.dma(ov[:, :, c0:c0 + nb], xb[:, :, :nb], r=["xb"], w=[("x1T", c0)])
        S.emit()


def phase_C2a(nc, S, H, TO, x1T, Wup, cw_d, cb_d, gffn, actD):
    NB = 512
    with ExitStack() as st:
        A = Tiles(nc, st)
        S.barrier()
        T_ = {}
        wu = A.sb([128, 8, 5632], BF16, "wu")
        stg = [A.sb([128, 2048], F32, "stg") for _ in range(2)]
        xb = [A.sb([128, 8, NB], F32, "xb") for _ in range(2)]
        hT = A.sb([128, 8, NB], BF16, "hT")
        T_["xsq"] = A.sb([128, 8, NB], BF16, "xsq")
        T_["std"] = A.sb([128, NB], F32, "std")
        T_["rstd"] = A.sb([128, NB], F32, "rstd")
        T_["epsc"] = A.sb([128, 1], F32, "epsc")
        ones_bf = A.sb([128, 128], BF16, "ones")
        gcol = A.sb([128, 8], F32, "gcol")
        cw = A.sb([128, 44, 3], F32, "cw")
        cb = A.sb([128, 44], F32, "cb")
        halo = A.sb([128, 44, 2], F32, "halo")
        ua = [A.sb([128, NB + 2], F32, "ua") for _ in range(2)]
        ug = [A.sb([128, NB + 2], F32, "ug") for _ in range(2)]
        aa = [A.sb([128, NB], F32, "aa") for _ in range(2)]
        ag = [A.sb([128, NB], F32, "ag") for _ in range(2)]
        acto = [A.sb([128, NB], BF16, "acto") for _ in range(4)]
        ps_ss = A.ps([128, 512], F32, "ps_ss")
        psa = [A.ps([128, 512], F32, "psa") for _ in range(3)]
        psgt = [A.ps([128, 512], F32, "psgt") for _ in range(3)]
        S.pool(lambda e: e.memset(ones_bf[:, :], 1.0), w=["ones"])
        S.pool(lambda e: e.memset(T_["epsc"][:, :], EPS), w=["epsc"])
        S.dve(lambda e: e.memset(halo[:, :, :], 0.0), w=["halo"])
        S.dma(gcol[:, :], gffn, w=["gcol"])
        S.dma(cw[:, :, :], cw_d, w=["cw"])
        S.dma(cb[:, :], cb_d, w=["cb"])
        load_w_generic(S, lambda c0, n: wu[:, :, c0:c0 + n], Wup.rearrange("(c p) n -> p c n", p=128), 8, 5632, 256, stg, "stg", "wu")
        xv = x1T.rearrange("(c p) t -> p c t", p=128)
        kk = 0
        blks = blocks_of(H, TO, NB)

        def load_x(bi):
            c0_, nb_ = blks[bi]
            S.dma(xb[bi % 2][:, :, :nb_], xv[:, :, c0_:c0_ + nb_], r=[("x1T", c0_)], w=[("xb", bi % 2)])

        load_x(0)
        for bi, (c0, nb) in enumerate(blks):
            xt = xb[bi % 2]
            xtag = ("xb", bi % 2)
            if bi + 1 < len(blks):
                load_x(bi + 1)
            rms_block(S, T_, ones_bf, xt, gcol, hT, ps_ss, xtag, nb)
            htag = ("hT", id(hT))
            for fc in range(22):
                b2 = kk % 2
                b3 = kk % 3
                b4 = kk % 4
                kk += 1
                for (ps, pst, off, ut, utag, acc, atag, hc) in ((psa[b3], ("psa", b3), 0, ua[b2], ("ua", b2), aa[b2], ("aa", b2), fc),
                                                                (psgt[b3], ("psgt", b3), 2816, ug[b2], ("ug", b2), ag[b2], ("ag", b2), 22 + fc)):
                    for c in range(8):
                        S.pe(lambda e, c=c, ps=ps, off=off, fc=fc, nb=nb: e.matmul(ps[:, :nb], lhsT=wu[:, c, off + fc * 128:off + (fc + 1) * 128], rhs=hT[:, c, :nb],
                                                                                 start=(c == 0), stop=(c == 7)), r=["wu", htag], w=[pst])
                    S.act(lambda e, ut=ut, hc=hc: e.copy(out=ut[:, 0:2], in_=halo[:, hc, :]), r=[("halo", hc)], w=[utag])
                    S.act(lambda e, ut=ut, ps=ps, nb=nb: e.copy(out=ut[:, 2:2 + nb], in_=ps[:, :nb]), r=[pst], w=[utag])
                    S.act(lambda e, ut=ut, hc=hc, nb=nb: e.copy(out=halo[:, hc, :], in_=ut[:, nb:nb + 2]), r=[utag], w=[("halo", hc)])
                    S.dve(lambda e, ut=ut, acc=acc, hc=hc, nb=nb: e.tensor_scalar(out=acc[:, :nb], in0=ut[:, 0:nb], scalar1=cw[:, hc, 0:1], scalar2=cb[:, hc:hc + 1],
                                                                                op0=ALU.mult, op1=ALU.add), r=[utag, "cw", "cb"], w=[atag])
                    for j in (1, 2):
                        S.dve(lambda e, ut=ut, acc=acc, hc=hc, nb=nb, j=j: e.scalar_tensor_tensor(out=acc[:, :nb], in0=ut[:, j:j + nb], scalar=cw[:, hc, j:j + 1], in1=acc[:, :nb],
                                                                                                  op0=ALU.mult, op1=ALU.add), r=[utag, "cw", atag], w=[atag])
                S.act(lambda e, b2=b2, nb=nb: e.activation(out=ag[b2][:, :nb], in_=ag[b2][:, :nb], func=AF.Silu), r=[("ag", b2)], w=[("ag", b2)])
                S.pool(lambda e, b2=b2, b4=b4, nb=nb: e.tensor_tensor(out=acto[b4][:, :nb], in0=aa[b2][:, :nb], in1=ag[b2][:, :nb], op=ALU.mult),
                       r=[("aa", b2), ("ag", b2)], w=[("acto", b4)])
                S.dma(actD[fc * 128:(fc + 1) * 128, c0:c0 + nb], acto[b4][:, :nb], r=[("acto", b4)], w=[("actD", c0)])
        S.emit()


def phase_C2b(nc, S, H, TO, x1T, actD, Wdown, x2T, final_g=None, outT=None, xsend=None):
    NB = 512
    with ExitStack() as st:
        A = Tiles(nc, st)
        S.barrier()
        T_ = {}
        wd = A.sb([128, 22, 1024], BF16, "wd")
        stg = [A.sb([128, 2048], F32, "stg") for _ in range(2)]
        xb = [A.sb([128, 8, NB], F32, "xb") for _ in range(2)]
        actb = [A.sb([128, 22, NB], BF16, "actb") for _ in range(2)]
        T_["xsq"] = A.sb([128, 8, NB], BF16, "xsq")
        T_["std"] = A.sb([128, NB], F32, "std")
        T_["rstd"] = A.sb([128, NB], F32, "rstd")
        T_["epsc"] = A.sb([128, 1], F32, "epsc")
        ones_bf = A.sb([128, 128], BF16, "ones")
        gfin = A.sb([128, 8], F32, "gfin")
        oT = A.sb([128, 8, NB], F32, "oT")
        ps_ss = A.ps([128, 512], F32, "ps_ss")
        psd = [A.ps([128, 512], F32, "psd") for _ in range(4)]
        S.pool(lambda e: e.memset(ones_bf[:, :], 1.0), w=["ones"])
        S.pool(lambda e: e.memset(T_["epsc"][:, :], EPS), w=["epsc"])
        if final_g is not None:
            S.dma(gfin[:, :], final_g, w=["gfin"])
        load_w_generic(S, lambda c0, n: wd[:, :, c0:c0 + n], Wdown.rearrange("(c p) n -> p c n", p=128), 22, 1024, 64, stg, "stg", "wd")
        xv = x1T.rearrange("(c p) t -> p c t", p=128)
        av = actD.rearrange("(c p) t -> p c t", p=128)
        ov = x2T.rearrange("(c p) t -> p c t", p=128)
        kk = 0
        blks = blocks_of(H, TO, NB)

        def load_xa(bi):
            c0_, nb_ = blks[bi]
            S.dma(xb[bi % 2][:, :, :nb_], xv[:, :, c0_:c0_ + nb_], r=[("x1T", c0_)], w=[("xb", bi % 2)])
            S.dma(actb[bi % 2][:, :, :nb_], av[:, :, c0_:c0_ + nb_], r=[("actD", c0_)], w=[("actb", bi % 2)])

        load_xa(0)
        for bi, (c0, nb) in enumerate(blks):
            xt = xb[bi % 2]
            xtag = ("xb", bi % 2)
            at = actb[bi % 2]
            attag = ("actb", bi % 2)
            if bi + 1 < len(blks):
                load_xa(bi + 1)
            for dc in range(8):
                po = psd[kk % 4]
                pot = ("psd", kk % 4)
                kk += 1
                for fc in range(22):
                    S.pe(lambda e, fc=fc, po=po, dc=dc, nb=nb, at=at: e.matmul(po[:, :nb], lhsT=wd[:, fc, dc * 128:(dc + 1) * 128], rhs=at[:, fc, :nb], start=(fc == 0), stop=(fc == 21)),
                         r=["wd", attag], w=[pot])
                S.dve(lambda e, po=po, dc=dc, nb=nb, xt=xt: e.tensor_tensor(out=xt[:, dc, :nb], in0=po[:, :nb], in1=xt[:, dc, :nb], op=ALU.add), r=[pot, xtag], w=[xtag])
            if final_g is None:
                S.dma(ov[:, :, c0:c0 + nb], xt[:, :, :nb], r=[xtag], w=[("x2T", c0)])
                if xsend is not None and not (bi == 0 and H > 0):
                    S.dma(xsend.rearrange("(c p) t -> p c t", p=128)[:, :, c0 - H:c0 - H + nb], xt[:, :, :nb], r=[xtag], w=[("xsend", c0)])
            elif not (bi == 0 and H > 0):
                xsq, std, rstd = T_["xsq"], T_["std"], T_["rstd"]
                S.act(lambda e, xt=xt, nb=nb: e.activation(out=xsq[:, :, :nb], in_=xt[:, :, :nb], func=AF.Square), r=[xtag], w=["xsq"])
                for c in range(8):
                    S.pe(lambda e, c=c, nb=nb: e.matmul(ps_ss[:, :nb], lhsT=ones_bf[:, :], rhs=xsq[:, c, :nb], start=(c == 0), stop=(c == 7)), r=["xsq", "ones"], w=["ps_ss"])
                S.act(lambda e, nb=nb: e.activation(out=std[:, :nb], in_=ps_ss[:, :nb], func=AF.Sqrt, bias=T_["epsc"][:, 0:1], scale=1.0 / D), r=["ps_ss", "epsc"], w=["std"])
                S.dve(lambda e, nb=nb: e.reciprocal(out=rstd[:, :nb], in_=std[:, :nb]), r=["std"], w=["rstd"])
                for c in range(8):
                    S.dve(lambda e, c=c, nb=nb, xt=xt: e.scalar_tensor_tensor(out=oT[:, c, :nb], in0=xt[:, c, :nb], scalar=gfin[:, c:c + 1], in1=rstd[:, :nb], op0=ALU.mult, op1=ALU.mult),
                          r=[xtag, "rstd", "gfin"], w=["oT"])
                S.dma(outT.rearrange("(c p) t -> p c t", p=128)[:, :, c0 - H:c0 - H + nb], oT[:, :, :nb], r=["oT"], w=[("outT", c0)])
        S.emit()
```

```python
import numpy as np
from contextlib import ExitStack
import concourse.bass as bass
import concourse.mybir as mybir
from concourse.bass_utils import run_bass_kernel_spmd

F32 = mybir.dt.float32
BF16 = mybir.dt.bfloat16
AF = mybir.ActivationFunctionType
ALU = mybir.AluOpType
AX = mybir.AxisListType

COMPUTE = ("pe", "act", "dve", "pool")
EPOCH = 4096
NEPS = 3
NDMASEM = 20
NCSEM = 56
SAME_ENGINE_SYNC = True


def _is_ps(t):
    t0 = t[0] if isinstance(t, tuple) else t
    return isinstance(t0, str) and (t0[:2] in ("ps", "pb", "po", "pu", "pg") or t0 == "ptr")


class Op:
    __slots__ = ("eng", "fn", "deps", "flag", "sem", "target", "dma", "idx", "pre", "cc")

    def __init__(self, eng, fn, dma):
        self.eng = eng
        self.fn = fn
        self.dma = dma
        self.deps = []
        self.flag = False
        self.sem = None
        self.target = 0
        self.idx = 0
        self.pre = None
        self.cc = False


class Sched:
    def __init__(self, nc, stack):
        self.nc = nc
        self.ops = {e: [] for e in COMPUTE + ("sp",)}
        self.lastw = {}
        self.ps_true_w = {}
        self.readers = {}
        self.nops = {e: 0 for e in COMPUTE + ("sp",)}
        self.flagcnt = {e: 0 for e in COMPUTE}
        self.esem = {e: [stack.enter_context(nc.semaphore(f"s_{e}{i}")) for i in range(NEPS)] for e in COMPUTE}
        self.dsem = [stack.enter_context(nc.semaphore(f"s_dma{i}")) for i in range(NDMASEM)]
        self.ndma = 0
        self.csem = [stack.enter_context(nc.semaphore(f"s_cc{i}")) for i in range(NCSEM)]
        self.ncoll = 0
        self.dma_hist = []
        self.barrier_pending = {}
        self.last_op = {}
        self.waited = {e: {} for e in COMPUTE + ("sp",)}
        self.all_dma = []
        self.dma_barriered = 0

    def op(self, eng, fn, r=(), w=(), dma=False):
        ps_r = [t for t in r if _is_ps(t)]
        if ps_r:
            w = list(w) + [t for t in ps_r if t not in w]
        o = Op(eng, fn, dma)
        o.idx = self.nops[eng]
        self.nops[eng] += 1
        deps = {}

        def add(d, hazard):
            if d is None or d is o:
                return
            if d.dma:
                deps[id(d)] = d
            else:
                if d.eng == eng and not dma and (eng == "pe" or not SAME_ENGINE_SYNC or not hazard):
                    return
                k = ("e", d.eng)
                if k not in deps or deps[k].idx < d.idx:
                    deps[k] = d

        for t in r:
            if t in ps_r:
                add(self.ps_true_w.get(t), True)
            else:
                add(self.lastw.get(t), True)
        for t in w:
            hz = t not in ps_r
            add(self.lastw.get(t), hz)
            rd = self.readers.get(t)
            if rd:
                for d in rd.values():
                    add(d, hz)
            if hz and _is_ps(t):
                self.ps_true_w[t] = o
        bp = self.barrier_pending.pop(eng, None)
        if bp:
            for d in bp:
                add(d, True)
        for t in r:
            rd = self.readers.setdefault(t, {})
            if dma:
                rd[id(o)] = o
            else:
                rd[eng] = o
        for t in w:
            self.lastw[t] = o
            self.readers[t] = {}
        o.deps = list(deps.values())
        for d in o.deps:
            d.flag = True
        if dma:
            o.flag = True
            self.all_dma.append(o)
        self.ops[eng].append(o)
        self.last_op[eng] = o
        return o

    def pe(self, fn, r=(), w=()):
        return self.op("pe", fn, r, w)

    def act(self, fn, r=(), w=()):
        return self.op("act", fn, r, w)

    def dve(self, fn, r=(), w=()):
        return self.op("dve", fn, r, w)

    def pool(self, fn, r=(), w=()):
        return self.op("pool", fn, r, w)

    def dma(self, out, in_, r=(), w=(), eng="sp"):
        return self.op(eng, lambda e: e.dma_start(out=out, in_=in_), r, w, dma=True)

    def collective(self, kind, ins, outs, groups, r=(), w=()):
        o = self.op("pool", lambda e: e.collective_compute(kind, ALU.bypass, replica_groups=groups, ins=ins, outs=outs), r, w, dma=True)
        o.sem = self.csem[self.ncoll]
        self.ncoll += 1
        o.target = 1
        o.cc = True
        return o

    def barrier(self):
        b = [o for o in self.last_op.values() if not o.dma]
        b += [o for o in self.all_dma[self.dma_barriered:] if not o.cc]
        self.dma_barriered = len(self.all_dma)
        for o in b:
            o.flag = True
        self.barrier_pending = {e: b for e in COMPUTE + ("sp",)}

    def emit(self):
        nc = self.nc
        for e in COMPUTE:
            for o in self.ops[e]:
                if o.flag and o.sem is None:
                    c = self.flagcnt[e]
                    self.flagcnt[e] += 1
                    ep = c // EPOCH
                    o.sem = self.esem[e][ep % NEPS]
                    o.target = (ep // NEPS) * EPOCH + (c % EPOCH) + 1
        for e in COMPUTE + ("sp",):
            for o in self.ops[e]:
                if o.dma and o.sem is None:
                    n = self.ndma
                    self.ndma += 1
                    o.sem = self.dsem[n % NDMASEM]
                    o.target = 16 * (n // NDMASEM + 1)
                    if n >= NDMASEM:
                        o.pre = (o.sem, 16 * (n // NDMASEM))
        with nc.Block() as block:
            @block.tensor
            def _(eng):
                self._emit_eng("pe", eng)

            @block.scalar
            def _(eng):
                self._emit_eng("act", eng)

            @block.vector
            def _(eng):
                self._emit_eng("dve", eng)

            @block.gpsimd
            def _(eng):
                self._emit_eng("pool", eng)

            @block.sync
            def _(eng):
                self._emit_eng("sp", eng)
        for e in self.ops:
            self.ops[e] = []

    def _emit_eng(self, name, eng):
        waited = self.waited[name]
        for o in self.ops[name]:
            for d in o.deps:
                key = id(d.sem)
                if waited.get(key, 0) < d.target:
                    eng.wait_ge(d.sem, d.target)
                    waited[key] = d.target
            if o.pre is not None:
                key = id(o.pre[0])
                if waited.get(key, 0) < o.pre[1]:
                    eng.wait_ge(o.pre[0], o.pre[1])
                    waited[key] = o.pre[1]
            inst = o.fn(eng)
            if o.flag:
                inst.then_inc(o.sem, 16 if (o.dma and not o.cc) else 1)

    def finish(self, eng="sp"):
        self.barrier()
        self.op(eng, lambda e: e.nop(), r=(), w=())


class Tiles:
    CNT = [0]

    def __init__(self, nc, stack):
        self.nc = nc
        self.stack = stack

    def sb(self, shape, dtype, name=None):
        Tiles.CNT[0] += 1
        return self.stack.enter_context(self.nc.sbuf_tensor(f"{name or 't'}_{Tiles.CNT[0]}", list(shape), dtype))

    def ps(self, shape, dtype=F32, name=None):
        Tiles.CNT[0] += 1
        return self.stack.enter_context(self.nc.psum_tensor(f"{name or 'p'}_{Tiles.CNT[0]}", list(shape), dtype))


D = 1024
NFM = 3088
NTM = 2308
NA = NFM + NTM
EPS = 1e-6
FM_GQ, FM_GK, FM_MX, FM_FQ, FM_FK, FM_OG, FM_LR = 0, 256, 512, 1536, 2048, 2560, 3072
TM_GK, TM_GV, TM_GR, TM_MZ, TM_FV, TM_FF = 0, 256, 768, 1280, 1792, 2304


def cdiv(a, b):
    return (a + b - 1) // b


def rms_block(S, T_, ones_bf, xt, gcol, hT, ps_ss, tag, nb):
    xsq, std, rstd = T_["xsq"], T_["std"], T_["rstd"]
    S.act(lambda e: e.activation(out=xsq[:, :, :nb], in_=xt[:, :, :nb], func=AF.Square), r=[tag], w=["xsq"])
    for c in range(8):
        S.pe(lambda e, c=c: e.matmul(ps_ss[:, :nb], lhsT=ones_bf[:, :], rhs=xsq[:, c, :nb], start=(c == 0), stop=(c == 7)),
             r=["xsq", "ones"], w=["ps_ss"])
    S.act(lambda e: e.activation(out=std[:, :nb], in_=ps_ss[:, :nb], func=AF.Sqrt, bias=T_["epsc"][:, 0:1], scale=1.0 / D),
          r=["ps_ss", "epsc"], w=["std"])
    S.dve(lambda e: e.reciprocal(out=rstd[:, :nb], in_=std[:, :nb]), r=["std"], w=["rstd"])
    for c in range(8):
        S.dve(lambda e, c=c: e.scalar_tensor_tensor(out=hT[:, c, :nb], in0=xt[:, c, :nb], scalar=gcol[:, c:c + 1],
                                                     in1=rstd[:, :nb], op0=ALU.mult, op1=ALU.mult),
              r=[tag, "rstd", "gcol"], w=[("hT", id(hT))])


def load_weights_bf16(S, T_, wsb, wdram, ncols, wtag, stg, stgtag):
    wv = wdram.rearrange("(c p) n -> p c n", p=128)
    G = 512
    for gi in range(cdiv(ncols, G)):
        c0 = gi * G
        n = min(G, ncols - c0)
        st = stg[gi % 2]
        tg = (stgtag, gi % 2)
        S.dma(st[:, :, :n], wv[:, :, c0:c0 + n], r=[], w=[tg])
        eng = [S.pool, S.dve][gi % 2]
        eng(lambda e, st=st, c0=c0, n=n: e.tensor_copy(out=wsb[:, :, c0:c0 + n], in_=st[:, :, :n]), r=[tg], w=[wtag])


def phase_A(nc, S, T, xT, wA, gmix, pFM, pTM, xblk=None, xr=()):
    NB = 512
    with ExitStack() as st:
        A = Tiles(nc, st)
        T_ = {}
        wsb = A.sb([128, 8, NA], BF16, "wsb")
        stg = [A.sb([128, 8, 512], F32, "wstg") for _ in range(2)]
        xb = [A.sb([128, 8, NB], F32, "xb") for _ in range(2)]
        hTs = [A.sb([128, 8, NB], BF16, "hT") for _ in range(2)]
        T_["xsq"] = A.sb([128, 8, NB], BF16, "xsq")
        T_["std"] = A.sb([128, NB], F32, "std")
        T_["rstd"] = A.sb([128, NB], F32, "rstd")
        T_["epsc"] = A.sb([128, 1], F32, "epsc")
        ones_bf = A.sb([128, 128], BF16, "ones")
        gcol = A.sb([128, 8], F32, "gcol")
        fmst = [A.sb([128, NB], BF16, "fmst") for _ in range(4)]
        tmst = [A.sb([128, NTM], BF16, "tmst") for _ in range(2)]
        ps_ss = A.ps([128, NB], F32, "ps_ss")
        psr = [A.ps([128, NB], F32, "psr") for _ in range(4)]
        S.barrier()
        S.pool(lambda e: e.memset(ones_bf[:, :], 1.0), w=["ones"])
        S.pool(lambda e: e.memset(T_["epsc"][:, :], EPS), w=["epsc"])
        S.dma(gcol[:, :], gmix, w=["gcol"])
        load_weights_bf16(S, T_, wsb, wA, NA, "wsb", stg, "wstg")
        if xblk is None:
            xv = xT.rearrange("(c p) t -> p c t", p=128)
            xblk = lambda j: xv[:, :, j * NB:(j + 1) * NB]
        k = 0
        S.dma(xb[0][:, :, :], xblk(0), r=list(xr), w=[("xb", 0)])
        for j in range(T // NB):
            xt = xb[j % 2]
            xtag = ("xb", j % 2)
            hT = hTs[j % 2]
            htag = ("hT", id(hT))
            if j + 1 < T // NB:
                S.dma(xb[(j + 1) % 2][:, :, :], xblk(j + 1), r=list(xr), w=[("xb", (j + 1) % 2)])
            rms_block(S, T_, ones_bf, xt, gcol, hT, ps_ss, xtag, NB)
            for m in range(cdiv(NFM, 128)):
                mm = min(128, NFM - m * 128)
                ps = psr[k % 4]
                ptag = ("psr", k % 4)
                so = fmst[k % 4]
                stag = ("fmst", k % 4)
                for c in range(8):
                    S.pe(lambda e, c=c, ps=ps, m=m, mm=mm, hT=hT: e.matmul(ps[:mm, :], lhsT=wsb[:, c, m * 128:m * 128 + mm], rhs=hT[:, c, :],
                                                                        start=(c == 0), stop=(c == 7)), r=["wsb", htag], w=[ptag])
                if k % 2 == 0:
                    S.act(lambda e, ps=ps, so=so, mm=mm: e.copy(out=so[:mm, :], in_=ps[:mm, :]), r=[ptag], w=[stag])
                else:
                    S.dve(lambda e, ps=ps, so=so, mm=mm: e.tensor_copy(out=so[:mm, :], in_=ps[:mm, :]), r=[ptag], w=[stag])
                S.dma(pFM[m * 128:m * 128 + mm, j * NB:(j + 1) * NB], so[:mm, :], r=[stag], w=[("pFM", j)])
                k += 1
            for tt in range(NB // 128):
                ti = j * (NB // 128) + tt
                so = tmst[ti % 2]
                stag = ("tmst", ti % 2)
                for n in range(cdiv(NTM, 512)):
                    nn = min(512, NTM - n * 512)
                    ps = psr[k % 4]
                    ptag = ("psr", k % 4)
                    for c in range(8):
                        S.pe(lambda e, c=c, ps=ps, n=n, nn=nn, hT=hT, tt=tt: e.matmul(ps[:, :nn], lhsT=hT[:, c, tt * 128:(tt + 1) * 128],
                                                                                  rhs=wsb[:, c, NFM + n * 512:NFM + n * 512 + nn],
                                                                                  start=(c == 0), stop=(c == 7)), r=["wsb", htag], w=[ptag])
                    if k % 2 == 0:
                        S.act(lambda e, ps=ps, so=so, n=n, nn=nn: e.copy(out=so[:, n * 512:n * 512 + nn], in_=ps[:, :nn]), r=[ptag], w=[stag])
                    else:
                        S.dve(lambda e, ps=ps, so=so, n=n, nn=nn: e.tensor_copy(out=so[:, n * 512:n * 512 + nn], in_=ps[:, :nn]), r=[ptag], w=[stag])
                    k += 1
                S.dma(pTM[ti * 128:(ti + 1) * 128, :], so[:, :], r=[stag], w=[("pTM", ti)])
        S.emit()


class YDst:
    def __init__(self, T, yT=None, ys=None, H=0, key="y"):
        self.T, self.yT, self.ys, self.H, self.key = T, yT, ys, H, key
        self.tokens = []

    def put(self, S, row0, rows_ap_fn, j, tile_fn, rtags):
        NBK = self.T // 512
        half = NBK // 2
        tok = (self.key, row0, j)
        self.tokens.append(tok)
        if self.ys is None:
            S.dma(rows_ap_fn(self.yT, slice(512 * j, 512 * (j + 1))), tile_fn(slice(0, 512)), r=rtags, w=[tok])
            return
        H = self.H
        if j < half:
            S.dma(rows_ap_fn(self.ys[0], slice(H + 512 * j, H + 512 * (j + 1))), tile_fn(slice(0, 512)), r=rtags, w=[tok])
        else:
            S.dma(rows_ap_fn(self.ys[1], slice(H + 512 * (j - half), H + 512 * (j - half + 1))), tile_fn(slice(0, 512)), r=rtags, w=[tok])
        if j == half - 1:
            tok2 = (self.key, row0, "halo")
            self.tokens.append(tok2)
            S.dma(rows_ap_fn(self.ys[1], slice(0, H)), tile_fn(slice(512 - H, 512)), r=rtags, w=[tok2])

    def zero_halo(self, S, A):
        if self.ys is None:
            return
        z = A.sb([128, 12, self.H], BF16, "zhalo")
        S.dve(lambda e: e.memset(z[:, :, :], 0.0), w=["zhalo"])
        tok = (self.key, "zero")
        self.tokens.append(tok)
        S.dma(self.ys[0][:, 0:self.H].rearrange("(a p) t -> p a t", p=128), z[:, :, :], r=["zhalo"], w=[tok])


def make_masks(S, A):
    nc = A.nc
    C = {}
    C["ones_bf"] = A.sb([128, 128], BF16, "ones_bf")
    C["ones_f"] = A.sb([128, 128], F32, "ones_f")
    C["tri_f"] = A.sb([128, 128], F32, "tri_f")
    C["tri_bf"] = A.sb([128, 128], BF16, "tri_bf")
    C["ident_bf"] = A.sb([128, 128], BF16, "ident_bf")
    C["ident_f"] = A.sb([128, 128], F32, "ident_f")
    S.pool(lambda e: e.memset(C["ones_bf"][:, :], 1.0), w=["c_ones_bf"])
    S.pool(lambda e: e.memset(C["ones_f"][:, :], 1.0), w=["c_ones_f"])
    S.pool(lambda e: e.affine_select(out=C["tri_f"][:, :], in_=C["ones_f"][:, :], pattern=[[1, 128]],
                                     compare_op=ALU.is_ge, fill=0.0, base=0, channel_multiplier=-1),
           r=["c_ones_f"], w=["c_tri_f"])
    S.pool(lambda e: e.tensor_copy(out=C["tri_bf"][:, :], in_=C["tri_f"][:, :]), r=["c_tri_f"], w=["c_tri_bf"])
    S.pool(lambda e: e.affine_select(out=C["ident_f"][:, :], in_=C["ones_f"][:, :], pattern=[[1, 128]],
                                     compare_op=ALU.is_equal, fill=0.0, base=0, channel_multiplier=-1),
           r=["c_ones_f"], w=["c_ident_f"])
    S.pool(lambda e: e.tensor_copy(out=C["ident_bf"][:, :], in_=C["ident_f"][:, :]), r=["c_ident_f"], w=["c_ident_bf"])
    return C


def phase_fox(nc, S, T, pFM, pTM, foxbf_t, yT, zero_halo=False):
    NT = T // 128
    NQ = T // 512
    SCALE = 128 ** -0.5
    with ExitStack() as st:
        A = Tiles(nc, st)
        S.barrier()
        C = make_masks(S, A)
        if zero_halo:
            yT.zero_halo(S, A)
        ff = A.sb([128, NT, 4], BF16, "ff")
        bfb = A.sb([128, NT, 4], F32, "bfb")
        u = A.sb([128, NT, 4], F32, "u")
        sp = A.sb([128, NT, 4], F32, "sp")
        inc = A.sb([128, NT, 4], F32, "inc")
        zer = A.sb([128, NT], F32, "zer")
        Pk = A.sb([128, NT, 4], F32, "Pk")
        Bb = A.sb([128, T // 256, NT], F32, "Bb")
        kT = A.sb([128, T], BF16, "kT")
        Vall = A.sb([128, NT, 512], BF16, "Vall")
        qTb = [A.sb([128, 512], BF16, "qTb") for _ in range(2)]
        ogb = [A.sb([128, 512], BF16, "ogb") for _ in range(2)]
        PT = [A.sb([128, 512], BF16, "PT") for _ in range(3)]
        rl = A.sb([128, 512], F32, "rl")
        osb = A.sb([128, 512], F32, "osb")
        sg = A.sb([128, 512], F32, "sg")
        yb = [A.sb([128, 512], BF16, "yb") for _ in range(2)]
        ps_s = [A.ps([128, 512], F32, "ps_s") for _ in range(3)]
        ps_o = [A.ps([128, 512], F32, "ps_o") for _ in range(2)]
        ps_l = [A.ps([128, 512], F32, "ps_l") for _ in range(2)]
        ps_c = ps_s[0]
        ps_t = ps_s[1]
        ffv = pTM[:, TM_FF:TM_FF + 4].rearrange("(i p) h -> p i h", p=128)
        step = max(1, NT // 8)
        for i0 in range(0, NT, step):
            S.dma(ff[:, i0:i0 + step, :], ffv[:, i0:i0 + step, :], r=[("pTM", i) for i in range(i0, i0 + step)], w=["ff"])
        S.dma(bfb[:, :, :], foxbf_t, w=["bfb"])
        S.dve(lambda e: e.memset(zer[:, :], 0.0), w=["zer"])
        S.dve(lambda e: e.tensor_tensor(out=u[:, :, :], in0=ff[:, :, :], in1=bfb[:, :, :], op=ALU.add), r=["ff", "bfb"], w=["u"])
        S.act(lambda e: e.activation(out=u[:, :, :], in_=u[:, :, :], func=AF.Exp, scale=-1.0), r=["u"], w=["u"])
        S.act(lambda e: e.activation(out=sp[:, :, :], in_=u[:, :, :], func=AF.Ln, bias=1.0), r=["u"], w=["sp"])
        spf = sp[:, :, :].rearrange("p i h -> p (i h)")
        for n0 in range(0, NT * 4, 512):
            nn = min(512, NT * 4 - n0)
            S.pe(lambda e, n0=n0, nn=nn: e.matmul(ps_c[:, :nn], lhsT=C["tri_f"][:, :], rhs=spf[:, n0:n0 + nn], start=True, stop=True),
                 r=["sp", "c_tri_f"], w=["ps_s0"])
            S.pe(lambda e, n0=n0, nn=nn: e.matmul(ps_t[:, :nn], lhsT=C["ones_f"][:, :], rhs=spf[:, n0:n0 + nn], start=True, stop=True),
                 r=["sp", "c_ones_f"], w=["ps_s1"])
            Pf = Pk[:, :, :].rearrange("p i h -> p (i h)")
            If = inc[:, :, :].rearrange("p i h -> p (i h)")
            S.dve(lambda e, n0=n0, nn=nn, If=If: e.tensor_copy(out=If[:, n0:n0 + nn], in_=ps_t[:, :nn]), r=["ps_s1"], w=["inc"])
            S.dve(lambda e, n0=n0, nn=nn, Pf=Pf, If=If: e.tensor_tensor(out=Pf[:, n0:n0 + nn], in0=ps_c[:, :nn], in1=If[:, n0:n0 + nn], op=ALU.subtract),
                  r=["ps_s0", "inc"], w=["Pk"])
        for h in range(4):
            S.dve(lambda e, h=h: e.tensor_tensor_scan(out=inc[:, :, h], data0=inc[:, :, h], data1=zer[:, :], initial=0.0,
                                                       op0=ALU.add, op1=ALU.add), r=["inc", "zer"], w=["inc"])
        S.dve(lambda e: e.tensor_tensor(out=Pk[:, :, :], in0=Pk[:, :, :], in1=inc[:, :, :], op=ALU.add), r=["Pk", "inc"], w=["Pk"])
        vv = pTM[:, TM_FV:TM_FV + 512].rearrange("(i p) c -> p i c", p=128)
        for i0 in range(0, NT, step):
            S.dma(Vall[:, i0:i0 + step, :], vv[:, i0:i0 + step, :], r=[("pTM", i) for i in range(i0, i0 + step)], w=["Vall"])
        kq = 0
        for h in range(4):
            S.dma(kT[:, :], pFM[FM_FK + 128 * h:FM_FK + 128 * (h + 1), :], r=[("pFM", j) for j in range(NQ)], w=["kT"])
            for i2 in range(T // 256):
                nj = 2 * i2 + 2
                S.dve(lambda e, h=h, i2=i2, nj=nj: e.tensor_scalar(out=Bb[:, i2, :nj], in0=Pk[:, :nj, h], scalar1=inc[:, 2 * i2 + 1, h:h + 1],
                                                                   scalar2=None, op0=ALU.subtract), r=["Pk", "inc"], w=["Bb"])
            steps = []
            for I in range(NQ):
                nkt = 4 * I + 4
                for j in range(nkt):
                    steps.append((I, j, nkt))

            def emit_S(st_, kq_):
                I, j, nkt = st_
                qt = qTb[I % 2]
                qtag = ("qTb", I % 2)

                def load_q(I_):
                    S.dma(qTb[I_ % 2][:, :], pFM[FM_FQ + 128 * h:FM_FQ + 128 * (h + 1), 512 * I_:512 * (I_ + 1)], r=[("pFM", I_)], w=[("qTb", I_ % 2)])
                    S.dma(ogb[I_ % 2][:, :], pFM[FM_OG + 128 * h:FM_OG + 128 * (h + 1), 512 * I_:512 * (I_ + 1)], r=[("pFM", I_)], w=[("ogb", I_ % 2)])

                if I == 0 and j == 0:
                    load_q(0)
                if j == 2 and I + 1 < NQ:
                    load_q(I + 1)
                q0 = max(0, 128 * (j - 4 * I))
                pss = ps_s[kq_ % 3]
                S.pe(lambda e, pss=pss, j=j, qt=qt, q0=q0: e.matmul(pss[:, q0:512], lhsT=kT[:, 128 * j:128 * (j + 1)], rhs=qt[:, q0:512],
                                                                   start=True, stop=True), r=["kT", qtag], w=["ps_s%d" % (kq_ % 3)])

            def emit_rest(st_, kq_):
                I, j, nkt = st_
                r_ = j - 4 * I
                q0 = max(0, 128 * r_)
                pss = ps_s[kq_ % 3]
                pstag = "ps_s%d" % (kq_ % 3)
                pt = PT[kq_ % 3]
                po = ps_o[I % 2]
                pl = ps_l[I % 2]
                potag = ("ps_o", I % 2)
                pltag = ("ps_l", I % 2)
                for sb in range(2):
                    c0 = max(q0, 256 * sb)
                    c1 = 256 * (sb + 1)
                    if c0 >= c1:
                        continue
                    S.act(lambda e, pt=pt, pss=pss, c0=c0, c1=c1, I=I, sb=sb, j=j: e.activation(
                        out=pt[:, c0:c1], in_=pss[:, c0:c1], func=AF.Exp, scale=SCALE, bias=Bb[:, 2 * I + sb, j:j + 1]),
                        r=[pstag, "Bb"], w=[("PT", kq_ % 3, sb)])
                pttags = [("PT", kq_ % 3, 0), ("PT", kq_ % 3, 1)]
                if r_ >= 0:
                    S.pool(lambda e, pt=pt, q0=q0: e.tensor_tensor(out=pt[:, q0:q0 + 128], in0=pt[:, q0:q0 + 128], in1=C["tri_bf"][:, :], op=ALU.mult),
                           r=pttags + ["c_tri_bf"], w=pttags)
                S.pe(lambda e, po=po, j=j, pt=pt, q0=q0, nkt=nkt, h=h: e.matmul(po[:, q0:512], lhsT=Vall[:, j, 128 * h:128 * (h + 1)], rhs=pt[:, q0:512],
                                                                        start=(j == 0), stop=(j == nkt - 1)), r=["Vall"] + pttags, w=[potag])
                S.pe(lambda e, pl=pl, j=j, pt=pt, q0=q0, nkt=nkt: e.matmul(pl[:, q0:512], lhsT=C["ones_bf"][:, :], rhs=pt[:, q0:512],
                                                                        start=(j == 0), stop=(j == nkt - 1)), r=["c_ones_bf"] + pttags, w=[pltag])
                if j == nkt - 1:
                    og = ogb[I % 2]
                    ogtag = ("ogb", I % 2)
                    y = yb[I % 2]
                    ytag = ("yb", I % 2)
                    S.dve(lambda e, pl=pl: e.reciprocal(out=rl[:, :], in_=pl[:, :]), r=[pltag], w=["rl"])
                    S.dve(lambda e, po=po: e.tensor_tensor(out=osb[:, :], in0=po[:, :], in1=rl[:, :], op=ALU.mult), r=[potag, "rl"], w=["osb"])
                    S.act(lambda e, og=og: e.activation(out=sg[:, :], in_=og[:, :], func=AF.Exp, scale=-1.0), r=[ogtag], w=["sg"])
                    S.dve(lambda e: e.tensor_scalar_add(out=sg[:, :], in0=sg[:, :], scalar1=1.0), r=["sg"], w=["sg"])
                    S.dve(lambda e: e.reciprocal(out=sg[:, :], in_=sg[:, :]), r=["sg"], w=["sg"])
                    S.pool(lambda e, y=y: e.tensor_tensor(out=y[:, :], in0=osb[:, :], in1=sg[:, :], op=ALU.mult), r=["osb", "sg"], w=[ytag])
                    yT.put(S, 1024 + 128 * h, lambda d, cs_, h=h: d[1024 + 128 * h:1024 + 128 * (h + 1), cs_], I, lambda cs_, y=y: y[:, cs_], [ytag])

            LOOK = 2
            n = len(steps)
            for i in range(min(LOOK, n)):
                emit_S(steps[i], kq + i)
            for i in range(n):
                if i + LOOK < n:
                    emit_S(steps[i + LOOK], kq + i + LOOK)
                emit_rest(steps[i], kq + i)
            kq += n
        S.emit()


def phase_gla(nc, S, T, pFM, pTM, wlr_aug, gnb_d, yT, zero_halo=False):
    NBK = T // 512
    NCH = T // 128
    with ExitStack() as st:
        A = Tiles(nc, st)
        S.barrier()
        C = make_masks(S, A)
        if zero_halo:
            yT.zero_halo(S, A)
        rt_f = A.sb([128, 128], F32, "rt_f")
        S.pool(lambda e: e.affine_select(out=rt_f[:, :], in_=C["ones_f"][:, :], pattern=[[-1, 128]], compare_op=ALU.is_ge,
                                         fill=0.0, base=-1, channel_multiplier=1), r=["c_ones_f"], w=["rt_f"])
        wl_f = A.sb([17, 256], F32, "wl_f")
        wl = A.sb([17, 256], BF16, "wl")
        gnb = A.sb([128, 512], F32, "gnb")
        epsc = A.sb([128, 1], F32, "epsc")
        S.pool(lambda e: e.memset(epsc[:, :], EPS), w=["epsc"])
        S.dma(wl_f[:, :], wlr_aug, w=["wl_f"])
        S.dve(lambda e: e.tensor_copy(out=wl[:, :], in_=wl_f[:, :]), r=["wl_f"], w=["wl"])
        S.dma(gnb[:, :], gnb_d, w=["gnb"])
        laug = [A.sb([17, 512], BF16, "laug") for _ in range(2)]
        qkb = [A.sb([128, 4, 512], BF16, "qkb") for _ in range(2)]
        tmb = [A.sb([128, 4, 1280], BF16, "tmb") for _ in range(2)]
        for i in range(2):
            S.pool(lambda e, i=i: e.memset(laug[i][:, :], 1.0), w=[("laug", i)])
        e_sb = A.sb([128, 256], F32, "e_sb")
        esr = A.sb([128, 512], F32, "esr")
        sp_sb = A.sb([128, 256], F32, "sp_sb")
        ek = A.sb([128, 256], F32, "ek")
        ekk = A.sb([128, 2, 128], F32, "ekk")
        kin = A.sb([128, 2, 128], BF16, "kin")
        kst = [A.sb([128, 256], BF16, "kst") for _ in range(2)]
        eq = [A.sb([128, 2, 128], F32, "eq") for _ in range(2)]
        qin = [A.sb([128, 2, 128], BF16, "qin") for _ in range(2)]
        att = [A.sb([128, 2, 128], BF16, "att") for _ in range(2)]
        silr = [A.sb([128, 512], F32, "silr") for _ in range(2)]
        St = A.sb([128, 2, 256], F32, "St")
        Sb = A.sb([128, 2, 256], BF16, "Sb")
        junk = A.sb([128, 256], F32, "junk")
        ssum = A.sb([128, 2], F32, "ssum")
        rstd = A.sb([128, 2], F32, "rstd")
        t1 = A.sb([128, 256], F32, "t1")
        ysb = A.sb([128, 2, 256], BF16, "ysb")
        yTs = [A.sb([128, 4, 512], BF16, "yTs") for _ in range(2)]
        pb = [A.ps([128, 512], F32, "pb") for _ in range(7)]
        ptr = A.ps([128, 1024], BF16, "ptr")
        S.dve(lambda e: e.memset(St[:, :, :], 0.0), w=["St"])
        S.dve(lambda e: e.memset(Sb[:, :, :], 0.0), w=["Sb"])
        tmv = pTM[:, TM_GK:TM_GK + 1280].rearrange("(i p) c -> p i c", p=128)

        def load_block(j):
            b2 = j % 2
            bs = slice(512 * j, 512 * (j + 1))
            S.dma(laug[b2][0:16, :], pFM[FM_LR:FM_LR + 16, bs], r=[("pFM", j)], w=[("laug", b2)])
            S.dma(qkb[b2][:, 0:2, :], pFM[FM_GQ:FM_GQ + 256, bs].rearrange("(h p) t -> p h t", p=128), r=[("pFM", j)], w=[("qkb", b2)])
            S.dma(qkb[b2][:, 2:4, :], pFM[FM_GK:FM_GK + 256, bs].rearrange("(h p) t -> p h t", p=128), r=[("pFM", j)], w=[("qkb", b2)])
            S.dma(tmb[b2][:, :, :], tmv[:, 4 * j:4 * j + 4, :], r=[("pTM", 4 * j + i) for i in range(4)], w=[("tmb", b2)])

        def P(n):
            j, ch = n // 4, n % 4
            b2 = j % 2
            p2 = n % 2
            if ch == 1 and j + 1 < NBK:
                load_block(j + 1)
            cs = slice(128 * ch, 128 * (ch + 1))
            kt = tmb[b2][:, ch, 0:256]
            rt = tmb[b2][:, ch, 768:1280]
            S.pe(lambda e: e.matmul(pb[0][:, 0:256], lhsT=laug[b2][0:17, cs], rhs=wl[0:17, :], start=True, stop=True),
                 r=[("laug", b2), "wl"], w=["pg0"])
            yield
            S.act(lambda e: e.activation(out=e_sb[:, :], in_=pb[0][:, 0:256], func=AF.Exp, scale=-1.0), r=["pg0"], w=["e_sb"])
            yield
            S.act(lambda e: e.activation(out=sp_sb[:, :], in_=e_sb[:, :], func=AF.Ln, bias=1.0), r=["e_sb"], w=["sp_sb"])
            yield
            S.pe(lambda e: e.matmul(pb[0][:, 256:512], lhsT=rt_f[:, :], rhs=sp_sb[:, :], start=True, stop=True),
                 r=["rt_f", "sp_sb"], w=["pg0"])
            for h in range(2):
                S.pe(lambda e, h=h: e.matmul(pb[1][:, 128 * h:128 * (h + 1)], lhsT=sp_sb[:, 128 * h:128 * (h + 1)], rhs=C["tri_f"][:, :],
                                            start=True, stop=True), r=["sp_sb", "c_tri_f"], w=["pg1"])
            yield
            S.act(lambda e: e.activation(out=ek[:, :], in_=pb[0][:, 256:512], func=AF.Exp, scale=-1.0 / 16), r=["pg0"], w=["ek"])
            yield
            S.act(lambda e: e.activation(out=eq[p2][:, :, :].rearrange("p a b -> p (a b)"), in_=pb[1][:, 0:256], func=AF.Exp, scale=-1.0 / 16),
                  r=["pg1"], w=[("eq", p2)])
            yield
            S.act(lambda e: e.activation(out=ekk[:, :, :].rearrange("p a b -> p (a b)"), in_=pb[1][:, 0:256], func=AF.Exp, scale=1.0 / 16),
                  r=["pg1"], w=["ekk"])
            S.dve(lambda e: e.tensor_tensor(out=kst[p2][:, :], in0=kt, in1=ek[:, :], op=ALU.mult), r=[("tmb", b2), "ek"], w=[("kst", p2)])
            yield
            S.dve(lambda e: e.scalar_tensor_tensor(out=qin[p2][:, :, :], in0=qkb[b2][:, 0:2, cs], scalar=128 ** -0.5, in1=eq[p2][:, :, :],
                                                    op0=ALU.mult, op1=ALU.mult), r=[("qkb", b2), ("eq", p2)], w=[("qin", p2)])
            yield
            S.dve(lambda e: e.tensor_tensor(out=kin[:, :, :], in0=qkb[b2][:, 2:4, cs], in1=ekk[:, :, :], op=ALU.mult),
                  r=[("qkb", b2), "ekk"], w=["kin"])
            yield
            for h in range(2):
                S.pe(lambda e, h=h: e.matmul(pb[6][:, 128 * h:128 * (h + 1)], lhsT=kin[:, h, :], rhs=qin[p2][:, h, :], start=True, stop=True),
                     r=["kin", ("qin", p2)], w=["pg6"])
            yield
            S.dve(lambda e: e.tensor_tensor(out=att[p2][:, :, :], in0=pb[6][:, 0:256].rearrange("p (a b) -> p a b", a=2),
                                            in1=C["tri_f"][:, None, :].to_broadcast([128, 2, 128]), op=ALU.mult),
                  r=["pg6", "c_tri_f"], w=[("att", p2)])
            yield
            S.act(lambda e: e.activation(out=silr[p2][:, :], in_=rt, func=AF.Silu), r=[("tmb", b2)], w=[("silr", p2)])
            yield

        def X(n):
            j, ch = n // 4, n % 4
            b2 = j % 2
            p2 = n % 2
            cs = slice(128 * ch, 128 * (ch + 1))
            vt = tmb[b2][:, ch, 256:768]
            for h in range(2):
                po = pb[2 + h]
                pu = pb[4 + h]
                S.pe(lambda e, h=h, po=po: e.matmul(po[:, 0:256], lhsT=att[p2][:, h, :], rhs=vt[:, 256 * h:256 * (h + 1)], start=True, stop=False),
                     r=[("att", p2), ("tmb", b2)], w=[("po", h)])
                S.pe(lambda e, h=h, po=po: e.matmul(po[:, 0:256], lhsT=qin[p2][:, h, :], rhs=Sb[:, h, :], start=False, stop=True),
                     r=[("qin", p2), ("Sb", h)], w=[("po", h)])
                S.pe(lambda e, h=h, pu=pu: e.matmul(pu[:, 0:256], lhsT=kst[p2][:, 128 * h:128 * (h + 1)], rhs=vt[:, 256 * h:256 * (h + 1)], start=True, stop=True),
                     r=[("kst", p2), ("tmb", b2)], w=[("pu", h)])
                yield
                S.dve(lambda e, h=h, pu=pu: e.scalar_tensor_tensor(out=St[:, h, :], in0=St[:, h, :], scalar=eq[p2][:, h, 127:128], in1=pu[:, 0:256],
                                                                    op0=ALU.mult, op1=ALU.add), r=[("St", h), ("eq", p2), ("pu", h)], w=[("St", h)])
                yield
                S.act(lambda e, h=h: e.copy(out=Sb[:, h, :], in_=St[:, h, :]), r=[("St", h)], w=[("Sb", h)])
                yield
                S.act(lambda e, h=h, po=po: e.activation(out=junk[:, :], in_=po[:, 0:256], func=AF.Square, accum_out=ssum[:, h:h + 1]),
                      r=[("po", h)], w=["junk", ("ssum", h)])
                yield
                S.act(lambda e, h=h: e.activation(out=ssum[:, h:h + 1], in_=ssum[:, h:h + 1], func=AF.Ln, bias=epsc[:, 0:1], scale=1.0 / 256),
                      r=[("ssum", h), "epsc"], w=[("ssum", h)])
                yield
                S.act(lambda e, h=h: e.activation(out=rstd[:, h:h + 1], in_=ssum[:, h:h + 1], func=AF.Exp, scale=-0.5), r=[("ssum", h)], w=[("rstd", h)])
                yield
                S.dve(lambda e, h=h, po=po: e.scalar_tensor_tensor(out=t1[:, :], in0=po[:, 0:256], scalar=rstd[:, h:h + 1], in1=gnb[:, 256 * h:256 * (h + 1)],
                                                                    op0=ALU.mult, op1=ALU.mult), r=[("po", h), ("rstd", h), "gnb"], w=["t1"])
                yield
                S.pool(lambda e, h=h: e.tensor_tensor(out=ysb[:, h, :], in0=t1[:, :], in1=silr[p2][:, 256 * h:256 * (h + 1)], op=ALU.mult),
                       r=["t1", ("silr", p2)], w=[("ysb", h)])
                yield
                for vc in range(2):
                    S.pe(lambda e, h=h, vc=vc: e.transpose(ptr[:, 128 * vc:128 * (vc + 1)], ysb[:, h, 128 * vc:128 * (vc + 1)], C["ident_bf"][:, :]),
                         r=[("ysb", h), "c_ident_bf"], w=["ptr"])
                yield
                S.act(lambda e, h=h: e.copy(out=yTs[b2][:, 2 * h:2 * h + 2, cs], in_=ptr[:, 0:256].rearrange("p (a b) -> p a b", a=2)),
                      r=["ptr"], w=[("yTs", b2)])
                yield
            if ch == 3:
                yT.put(S, 0, lambda d, cs_: d[0:512, cs_].rearrange("(a p) t -> p a t", p=128), j, lambda cs_, b2=b2: yTs[b2][:, :, cs_], [("yTs", b2)])

        load_block(0)
        for _ in P(0):
            pass
        for n in range(NCH):
            gp = P(n + 1) if n + 1 < NCH else iter(())
            gx = X(n)
            while True:
                a_ = next(gp, "end")
                b_ = next(gx, "end")
                if a_ == "end" and b_ == "end":
                    break
        S.emit()


MLSTM_STOP = 0


def phase_mlstm(nc, S, T, pFM, pTM, P, yT):
    NBK = T // 512
    with ExitStack() as st:
        A = Tiles(nc, st)
        S.barrier()
        C = make_masks(S, A)
        epsc = A.sb([128, 1], F32, "epsc")
        S.pool(lambda e: e.memset(epsc[:, :], EPS), w=["epsc"])
        cw = A.sb([128, 8, 4], F32, "cw")
        cb = A.sb([128, 8], F32, "cb")
        gb = A.sb([128, 4], F32, "gb")
        skc = A.sb([128, 4], F32, "skc")
        gnb = A.sb([128, 512], F32, "gnb")
        wif = A.sb([128, 3, 8, 4], F32, "wif")
        for nm, tl in (("cw", cw), ("cb", cb), ("gb", gb), ("skc", skc), ("gnb", gnb), ("wif", wif)):
            S.dma(tl[tuple(slice(None) for _ in tl.shape)], P[nm], w=[nm])
        wbd_f = A.sb([128, 3, 4, 128], F32, "wbd_f")
        wbd = A.sb([128, 3, 4, 128], BF16, "wbd")
        wT_f = A.sb([128, 3, 8, 128], F32, "wT_f")
        for i, nm in enumerate(("wq_bd", "wk_bd", "wv_bd")):
            S.dma(wbd_f[:, i, :, :], P[nm].rearrange("c p o -> p c o"), w=["wbd_f"])
        S.dve(lambda e: e.tensor_copy(out=wbd[:, :, :, :], in_=wbd_f[:, :, :, :]), r=["wbd_f"], w=["wbd"])
        for i, nm in enumerate(("wqT_bd", "wkT_bd", "wvT_bd")):
            S.dma(wT_f[:, i, :, :], P[nm].rearrange("c p o -> p c o"), w=["wT_f"])
        dcw = A.sb([128, 8, 4, 128], BF16, "dcw")
        dsk = A.sb([128, 4, 128], BF16, "dsk")
        for c in range(8):
            for j in range(4):
                S.dve(lambda e, c=c, j=j: e.tensor_scalar(out=dcw[:, c, j, :], in0=C["ident_f"][:, :], scalar1=cw[:, c, j:j + 1], scalar2=None, op0=ALU.mult),
                      r=["c_ident_f", "cw"], w=["dcw"])
        for c in range(4):
            S.dve(lambda e, c=c: e.tensor_scalar(out=dsk[:, c, :], in0=C["ident_f"][:, :], scalar1=skc[:, c:c + 1], scalar2=None, op0=ALU.mult),
                  r=["c_ident_f", "skc"], w=["dsk"])
        pb = [A.ps([128, 512], F32, "pb") for _ in range(7)]
        ptr = A.ps([128, 1024], BF16, "ptr")
        weff = A.sb([128, 2, 8, 4], BF16, "weff")
        for c in range(8):
            S.pe(lambda e, c=c: e.matmul(pb[0][:, 8 * c:8 * c + 4], lhsT=wT_f[:, 0, c, :], rhs=wif[:, 0, c, :], start=True, stop=False), r=["wT_f", "wif"], w=["pb0"])
            S.pe(lambda e, c=c: e.matmul(pb[0][:, 8 * c:8 * c + 4], lhsT=wT_f[:, 1, c, :], rhs=wif[:, 1, c, :], start=False, stop=True), r=["wT_f", "wif"], w=["pb0"])
            S.pe(lambda e, c=c: e.matmul(pb[0][:, 8 * c + 4:8 * c + 8], lhsT=wT_f[:, 2, c, :], rhs=wif[:, 2, c, :], start=True, stop=True), r=["wT_f", "wif"], w=["pb0"])
        pw = pb[0][:, 0:64].rearrange("p (c k f) -> p k c f", c=8, k=2)
        S.dve(lambda e: e.tensor_copy(out=weff[:, :, :, :], in_=pw), r=["pb0"], w=["weff"])
        if MLSTM_STOP == 1:
            S.emit()
            return
        mxb = [A.sb([128, 8, 516], BF16, "mxb") for _ in range(2)]
        mxs = A.sb([128, 8, 516], BF16, "mxs")
        mzb = [A.sb([128, 4, 512], BF16, "mzb") for _ in range(2)]
        xcT = [A.sb([128, 8, 512], BF16, "xcT") for _ in range(2)]
        gsb = A.sb([128, 4], F32, "gsb")
        ef = A.sb([128, 2], F32, "ef")
        nlf = A.sb([128, 2], F32, "nlf")
        tmp2 = A.sb([128, 2], F32, "tmp2")
        wv = A.sb([128, 2], F32, "wv")
        wveg = A.sb([128, 2], F32, "wveg")
        eb = [A.sb([128, 2], F32, "eb") for _ in range(2)]
        eg = [A.sb([128, 2], F32, "eg") for _ in range(2)]
        silz = [A.sb([128, 512], F32, "silz") for _ in range(2)]
        qk = [[A.sb([128, 4, 128], BF16, "qk") for _ in range(2)] for _ in range(2)]
        ksb = [[A.sb([128, 256], BF16, "ksb") for _ in range(2)] for _ in range(2)]
        vw = [[A.sb([128, 260], BF16, "vw") for _ in range(2)] for _ in range(2)]
        vw2 = [[A.sb([128, 260], BF16, "vw2") for _ in range(2)] for _ in range(2)]
        att = [[A.sb([128, 128], BF16, "att") for _ in range(2)] for _ in range(2)]
        Cst = A.sb([128, 2, 2, 257], F32, "Cst")
        Cb = A.sb([128, 2, 2, 260], BF16, "Cb")
        den = A.sb([128, 1], F32, "den")
        fac = A.sb([128, 1], F32, "fac")
        ss = A.sb([128, 1], F32, "ss")
        fr = A.sb([128, 1], F32, "fr")
        junk = A.sb([128, 256], F32, "junk")
        t1 = A.sb([128, 256], F32, "t1")
        ysb = A.sb([128, 2, 256], BF16, "ysb")
        yTs = [A.sb([128, 4, 512], BF16, "yTs") for _ in range(2)]
        ln16c = A.sb([128, 1], F32, "ln16c")
        ncb = A.sb([128, 8], F32, "ncb")
        S.dve(lambda e: e.tensor_scalar(out=ncb[:, :], in0=cb[:, :], scalar1=-1.0, scalar2=None, op0=ALU.mult), r=["cb"], w=["ncb"])
        ecv = A.sb([128, 512], F32, "ecv")
        zcv = A.sb([128, 512], F32, "zcv")
        esz = A.sb([128, 512], F32, "esz")
        S.pool(lambda e: e.memset(ln16c[:, :], float(np.log(1.0 / 16.0))), w=["ln16c"])
        S.dve(lambda e: e.memset(Cst[:, :, :, :], 0.0), w=["Cst"])
        S.dve(lambda e: e.memset(Cb[:, :, :, :], 0.0), w=["Cb"])
        for p_ in range(2):
            for h_ in range(2):
                S.dve(lambda e, p_=p_, h_=h_: e.memset(vw[p_][h_][:, :], 0.0), w=[("vw", p_, h_)])
                S.dve(lambda e, p_=p_, h_=h_: e.memset(vw2[p_][h_][:, :], 0.0), w=[("vw2", p_, h_)])
        S.pool(lambda e: e.memset(mxb[0][:, :, 0:4], 0.0), w=[("mxb", 0)])
        S.pool(lambda e: e.memset(mxs[:, :, :], 0.0), w=["mxs"])
        mzv = pTM[:, TM_MZ:TM_MZ + 512].rearrange("(i p) c -> p i c", p=128)
        NTL = T // 128

        def load_block(j):
            b2 = j % 2
            bs = slice(512 * j, 512 * (j + 1))
            S.dma(mxb[b2][:, :, 4:516], pFM[FM_MX:FM_MX + 1024, bs].rearrange("(c p) t -> p c t", p=128), r=[("pFM", j)], w=[("mxb", b2)])
            S.dma(mzb[b2][:, :, :], mzv[:, 4 * j:4 * j + 4, :], r=[("pTM", 4 * j + i) for i in range(4)], w=[("mzb", b2)])

        def P(n):
            j, tt = n // 4, n % 4
            b2 = j % 2
            p2 = n % 2
            mx = mxb[b2]
            xc = xcT[b2]
            xctag = ("xcT", b2)
            if tt == 1 and j + 1 < NBK:
                load_block(j + 1)
            if tt == 0:
                if j > 0:
                    S.dve(lambda e: e.tensor_copy(out=mxb[b2][:, :, 0:4], in_=mxb[1 - b2][:, :, 512:516]), r=[("mxb", 1 - b2)], w=[("mxb", b2)])
                S.dve(lambda e: e.tensor_copy(out=mxs[:, :, 0:514], in_=mx[:, :, 1:515]), r=[("mxb", b2)], w=["mxs"])
                yield
                for c in range(8):
                    pbi = (2, 4)[c % 2]
                    pc = pb[pbi]
                    for tp in range(4):
                        S.pe(lambda e, c=c, tp=tp, pc=pc: e.matmul(pc[:, :], lhsT=dcw[:, c, tp, :], rhs=(mx[:, c, tp + 1:tp + 513] if tp % 2 == 1 else mxs[:, c, tp:tp + 512]),
                                                                  start=(tp == 0), stop=(tp == 3)), r=["dcw", ("mxb", b2), "mxs"], w=["pb%d" % pbi])
                    S.act(lambda e, c=c, pc=pc: e.activation(out=xc[:, c, :], in_=pc[:, :], func=AF.Silu, bias=cb[:, c:c + 1]), r=["pb%d" % pbi, "cb"], w=[xctag])
                    yield
            ts_ = slice(128 * tt, 128 * (tt + 1))
            tsx = slice(4 + 128 * tt, 4 + 128 * (tt + 1))
            for c in range(8):
                S.pe(lambda e, c=c: e.matmul(pb[3][:, 0:4], lhsT=xc[:, c, ts_], rhs=weff[:, 0, c, :], start=(c == 0), stop=False), r=[xctag, "weff"], w=["pb3"])
            for c in range(8):
                S.pe(lambda e, c=c: e.matmul(pb[3][:, 0:4], lhsT=mx[:, c, tsx], rhs=weff[:, 1, c, :], start=False, stop=(c == 7)), r=[("mxb", b2), "weff"], w=["pb3"])
            yield
            S.dve(lambda e: e.tensor_tensor(out=gsb[:, :], in0=pb[3][:, 0:4], in1=gb[:, :], op=ALU.add), r=["pb3", "gb"], w=["gsb"])
            yield
            S.act(lambda e: e.activation(out=ef[:, :], in_=gsb[:, 2:4], func=AF.Exp, scale=-1.0), r=["gsb"], w=["ef"])
            yield
            S.act(lambda e: e.activation(out=nlf[:, :], in_=ef[:, :], func=AF.Ln, bias=1.0), r=["ef"], w=["nlf"])
            yield
            S.pe(lambda e: e.matmul(pb[3][:, 8:10], lhsT=C["tri_f"][:, :], rhs=nlf[:, :], start=True, stop=True), r=["c_tri_f", "nlf"], w=["pb3"])
            S.pe(lambda e: e.matmul(pb[3][:, 16:18], lhsT=C["ones_f"][:, :], rhs=nlf[:, :], start=True, stop=True), r=["c_ones_f", "nlf"], w=["pb3"])
            yield
            S.dve(lambda e: e.tensor_tensor(out=tmp2[:, :], in0=pb[3][:, 8:10], in1=gsb[:, 0:2], op=ALU.add), r=["pb3", "gsb"], w=["tmp2"])
            yield
            S.act(lambda e: e.activation(out=wv[:, :], in_=tmp2[:, :], func=AF.Exp), r=["tmp2"], w=["wv"])
            S.act(lambda e: e.activation(out=eb[p2][:, :], in_=pb[3][:, 8:10], func=AF.Exp, scale=-1.0, bias=ln16c[:, 0:1]), r=["pb3", "ln16c"], w=[("eb", p2)])
            S.act(lambda e: e.activation(out=eg[p2][:, :], in_=pb[3][:, 16:18], func=AF.Exp, scale=-1.0), r=["pb3"], w=[("eg", p2)])
            yield
            S.dve(lambda e: e.tensor_tensor(out=wveg[:, :], in0=wv[:, :], in1=eg[p2][:, :], op=ALU.mult), r=["wv", ("eg", p2)], w=["wveg"])
            yield
            for h in range(2):
                pA, pB = pb[3], pb[4]
                for dc in range(2):
                    c = 2 * h + dc
                    S.pe(lambda e, c=c, dc=dc: e.matmul(pA[:, 128 * dc:128 * (dc + 1)], lhsT=wbd[:, 0, c, :], rhs=xc[:, c, ts_], start=True, stop=True), r=["wbd", xctag], w=["pb3"])
                    S.pe(lambda e, c=c, dc=dc: e.matmul(pA[:, 256 + 128 * dc:256 + 128 * (dc + 1)], lhsT=wbd[:, 1, c, :], rhs=xc[:, c, ts_], start=True, stop=True), r=["wbd", xctag], w=["pb3"])
                    S.pe(lambda e, c=c, dc=dc: e.matmul(pB[:, 128 * dc:128 * (dc + 1)], lhsT=xc[:, c, ts_], rhs=wbd[:, 1, c, :], start=True, stop=True), r=["wbd", xctag], w=["pb4"])
                    S.pe(lambda e, c=c, dc=dc: e.matmul(pB[:, 256 + 128 * dc:256 + 128 * (dc + 1)], lhsT=mx[:, c, tsx], rhs=wbd[:, 2, c, :], start=True, stop=True), r=["wbd", ("mxb", b2)], w=["pb4"])
                yield
                S.act(lambda e, h=h: e.copy(out=qk[p2][h][:, :, :].rearrange("p a b -> p (a b)"), in_=pA[:, :]), r=["pb3"], w=[("qk", p2, h)])
                yield
                S.act(lambda e, h=h: e.copy(out=ksb[p2][h][:, :], in_=pB[:, 0:256]), r=["pb4"], w=[("ksb", p2, h)])
                S.dve(lambda e, h=h: e.tensor_scalar(out=vw[p2][h][:, 0:256], in0=pB[:, 256:512], scalar1=wv[:, h:h + 1], scalar2=None, op0=ALU.mult), r=["pb4", "wv"], w=[("vw", p2, h)])
                yield
                S.dve(lambda e, h=h: e.tensor_scalar(out=vw2[p2][h][:, 0:256], in0=pB[:, 256:512], scalar1=wveg[:, h:h + 1], scalar2=None, op0=ALU.mult), r=["pb4", "wveg"], w=[("vw2", p2, h)])
                S.pool(lambda e, h=h: e.tensor_copy(out=vw[p2][h][:, 256:257], in_=wv[:, h:h + 1]), r=["wv"], w=[("vw", p2, h)])
                S.pool(lambda e, h=h: e.tensor_copy(out=vw2[p2][h][:, 256:257], in_=wveg[:, h:h + 1]), r=["wveg"], w=[("vw2", p2, h)])
                yield
                for dc in range(2):
                    S.pe(lambda e, dc=dc, h=h: e.matmul(pb[2][:, 0:128], lhsT=qk[p2][h][:, 2 + dc, :], rhs=qk[p2][h][:, dc, :], start=(dc == 0), stop=(dc == 1)), r=[("qk", p2, h)], w=["pb2"])
                yield
                S.dve(lambda e, h=h: e.tensor_tensor(out=att[p2][h][:, :], in0=pb[2][:, 0:128], in1=C["tri_f"][:, :], op=ALU.mult), r=["pb2", "c_tri_f"], w=[("att", p2, h)])
                yield
            S.act(lambda e: e.activation(out=silz[p2][:, :], in_=mzb[b2][:, tt, :], func=AF.Silu), r=[("mzb", b2)], w=[("silz", p2)])
            yield

        def X(n):
            j, tt = n // 4, n % 4
            b2 = j % 2
            p2 = n % 2
            xc = xcT[b2]
            xctag = ("xcT", b2)
            ts_ = slice(128 * tt, 128 * (tt + 1))
            for h in range(2):
                pD, pE, pF, pG = pb[5], pb[6], pb[0], pb[1]
                S.pe(lambda e, h=h: e.matmul(pD[:, 0:258], lhsT=att[p2][h][:, :], rhs=vw[p2][h][:, 0:258], start=True, stop=False), r=[("att", p2, h), ("vw", p2, h)], w=["pb5"])
                for dc in range(2):
                    S.pe(lambda e, dc=dc, h=h: e.matmul(pD[:, 0:258], lhsT=qk[p2][h][:, dc, :], rhs=Cb[:, h, dc, 0:258], start=False, stop=(dc == 1)), r=[("qk", p2, h), ("Cb", h)], w=["pb5"])
                for dc, pU in ((0, pE), (1, pF)):
                    S.pe(lambda e, dc=dc, pU=pU, h=h: e.matmul(pU[:, 0:258], lhsT=ksb[p2][h][:, 128 * dc:128 * (dc + 1)], rhs=vw2[p2][h][:, 0:258], start=True, stop=True),
                         r=[("ksb", p2, h), ("vw2", p2, h)], w=[("pb6", "pb0")[dc]])
                yield
                for dc, pU in ((0, pE), (1, pF)):
                    S.dve(lambda e, dc=dc, pU=pU, h=h: e.scalar_tensor_tensor(out=Cst[:, h, dc, :], in0=Cst[:, h, dc, :], scalar=eg[p2][:, h:h + 1], in1=pU[:, 0:257],
                                                                             op0=ALU.mult, op1=ALU.add), r=[("Cst", h), ("eg", p2), ("pb6", "pb0")[dc]], w=[("Cst", h)])
                yield
                S.act(lambda e, h=h: e.copy(out=Cb[:, h, :, 0:257], in_=Cst[:, h, :, :]), r=[("Cst", h)], w=[("Cb", h)])
                yield
                S.act(lambda e, h=h: e.activation(out=den[:, :], in_=pD[:, 256:257], func=AF.Abs, scale=eb[p2][:, h:h + 1]), r=["pb5", ("eb", p2)], w=["den"])
                yield
                S.dve(lambda e: e.tensor_scalar_max(out=den[:, :], in0=den[:, :], scalar1=1.0), r=["den"], w=["den"])
                S.dve(lambda e: e.reciprocal(out=den[:, :], in_=den[:, :]), r=["den"], w=["den"])
                S.dve(lambda e, h=h: e.tensor_tensor(out=fac[:, :], in0=den[:, :], in1=eb[p2][:, h:h + 1], op=ALU.mult), r=["den", ("eb", p2)], w=["fac"])
                yield
                S.act(lambda e: e.activation(out=junk[:, :], in_=pD[:, 0:256], func=AF.Square, scale=fac[:, 0:1], accum_out=ss[:, 0:1]), r=["pb5", "fac"], w=["junk", "ss"])
                yield
                S.act(lambda e: e.activation(out=ss[:, :], in_=ss[:, :], func=AF.Ln, bias=epsc[:, 0:1], scale=1.0 / 256), r=["ss", "epsc"], w=["ss"])
                yield
                S.act(lambda e: e.activation(out=fr[:, :], in_=ss[:, :], func=AF.Exp, scale=-0.5), r=["ss"], w=["fr"])
                S.dve(lambda e: e.tensor_tensor(out=fr[:, :], in0=fr[:, :], in1=fac[:, :], op=ALU.mult), r=["fr", "fac"], w=["fr"])
                S.dve(lambda e, h=h: e.scalar_tensor_tensor(out=t1[:, :], in0=pD[:, 0:256], scalar=fr[:, 0:1], in1=gnb[:, 256 * h:256 * (h + 1)],
                                                             op0=ALU.mult, op1=ALU.mult), r=["pb5", "fr", "gnb"], w=["t1"])
                for dc in range(2):
                    c = 2 * h + dc
                    S.pe(lambda e, c=c, dc=dc: e.matmul(pG[:, 128 * dc:128 * (dc + 1)], lhsT=xc[:, c, ts_], rhs=dsk[:, c, :], start=True, stop=True),
                         r=[xctag, "dsk"], w=["pb1"])
                yield
                S.dve(lambda e: e.tensor_tensor(out=t1[:, :], in0=pG[:, 0:256], in1=t1[:, :], op=ALU.add), r=["pb1", "t1"], w=["t1"])
                yield
                S.pool(lambda e, h=h: e.tensor_tensor(out=ysb[:, h, :], in0=t1[:, :], in1=silz[p2][:, 256 * h:256 * (h + 1)], op=ALU.mult), r=["t1", ("silz", p2)], w=[("ysb", h)])
                yield
                for vc in range(2):
                    S.pe(lambda e, h=h, vc=vc: e.transpose(ptr[:, 128 * vc:128 * (vc + 1)], ysb[:, h, 128 * vc:128 * (vc + 1)], C["ident_bf"][:, :]),
                         r=[("ysb", h), "c_ident_bf"], w=["ptr"])
                yield
                S.act(lambda e, h=h: e.copy(out=yTs[b2][:, 2 * h:2 * h + 2, ts_], in_=ptr[:, 0:256].rearrange("p (a b) -> p a b", a=2)),
                      r=["ptr"], w=[("yTs", b2)])
                yield
            if tt == 3:
                yT.put(S, 512, lambda d, cs_: d[512:1024, cs_].rearrange("(a p) t -> p a t", p=128), j, lambda cs_, b2=b2: yTs[b2][:, :, cs_], [("yTs", b2)])

        load_block(0)
        for _ in P(0):
            pass
        for n in range(NTL):
            gp = P(n + 1) if n + 1 < NTL else iter(())
            gx = X(n)
            while True:
                a_ = next(gp, "end")
                b_ = next(gx, "end")
                if a_ == "end" and b_ == "end":
                    break
        S.emit()


def bcast(v, n=128):
    v = np.asarray(v, np.float32)
    return np.ascontiguousarray(np.broadcast_to(v, (n,) + v.shape))


def block_diag_full(w):
    W = np.zeros((1024, 1024), np.float32)
    for c in range(4):
        for d in range(4):
            W[np.arange(256) * 4 + c, np.arange(256) * 4 + d] = w[:, c, d]
    return W


def prep_mlstm(p, conv_w, conv_b, wq, wk, wv, w_i, b_i, w_f, b_f, skip, norm):
    own = np.arange(512 * p, 512 * p + 512)
    oth = np.arange(512 * (1 - p), 512 * (1 - p) + 512)
    perm = np.concatenate([own, oth])
    P = {}
    P["cw"] = np.ascontiguousarray(conv_w[:, perm].reshape(4, 8, 128).transpose(2, 1, 0))
    P["cb"] = np.ascontiguousarray(conv_b[perm].reshape(8, 128).T)
    for nm, w in (("q", wq), ("k", wk), ("v", wv)):
        W = block_diag_full(w)[perm][:, perm]
        blocks = np.stack([W[128 * c:128 * (c + 1), 128 * c:128 * (c + 1)] for c in range(8)])
        P["w%s_bd" % nm] = np.ascontiguousarray(blocks[:4])
        P["w%sT_bd" % nm] = np.ascontiguousarray(blocks.transpose(0, 2, 1))
    cols = np.stack([w_i[:, 2 * p], w_i[:, 2 * p + 1], w_f[:, 2 * p], w_f[:, 2 * p + 1]], axis=1)
    wif = np.stack([cols[part * 1024 + perm] for part in range(3)])
    P["wif"] = np.ascontiguousarray(wif.reshape(3, 8, 128, 4).transpose(2, 0, 1, 3))
    P["gb"] = bcast(np.array([b_i[2 * p], b_i[2 * p + 1], b_f[2 * p], b_f[2 * p + 1]], np.float32))
    P["skc"] = np.ascontiguousarray(skip[own].reshape(4, 128).T)
    P["gnb"] = bcast(norm[own])
    return {k: np.ascontiguousarray(v, dtype=np.float32) for k, v in P.items()}, perm


def load_w_generic(S, wsb_view_fn, wdram_view, nchunks, ncols, G, stg, stgtag, wtag, k0=0):
    k = k0
    for c0 in range(0, ncols, G):
        n = min(G, ncols - c0)
        st = stg[k % 2]
        tg = (stgtag, k % 2)
        sv = st[:, 0:nchunks * n].rearrange("p (c n) -> p c n", c=nchunks)
        S.dma(sv, wdram_view[:, :, c0:c0 + n], w=[tg])
        eng = [S.pool, S.dve, S.act][k % 3]
        if k % 3 == 2:
            eng(lambda e, sv=sv, c0=c0, n=n: e.copy(out=wsb_view_fn(c0, n), in_=sv), r=[tg], w=[wtag])
        else:
            eng(lambda e, sv=sv, c0=c0, n=n: e.tensor_copy(out=wsb_view_fn(c0, n), in_=sv), r=[tg], w=[wtag])
        k += 1
    return k


def blocks_of(H, TO, NB):
    bl = [(0, H)] if H > 0 else []
    for k in range(TO // NB):
        bl.append((H + NB * k, NB))
    return bl


def phase_C1(nc, S, H, TO, xTo, ysrc, hmask_d, Wg, bg_d, Wb, Wout, gmix, x1T, yr=(), xr=(), ypre=False):
    NB = 256
    with ExitStack() as st:
        A = Tiles(nc, st)
        S.barrier()
        T_ = {}
        wg = A.sb([128, 8, 3072], BF16, "wg")
        wb = A.sb([128, 3, 8, 1024], BF16, "wb")
        wo = A.sb([128, 8, 1024], BF16, "wo")
        stg = [A.sb([128, 2048], F32, "stg") for _ in range(2)]
        xb = [A.sb([128, 8, NB], F32, "xb") for _ in range(1)]
        yb = [A.sb([128, 24, NB], BF16, "yb") for _ in range(2)]
        yb2 = A.sb([128, 24, NB], BF16, "yb2") if len(ysrc) == 2 else None
        hT = A.sb([128, 8, NB], BF16, "hT")
        T_["xsq"] = A.sb([128, 8, NB], BF16, "xsq")
        T_["std"] = A.sb([128, NB], F32, "std")
        T_["rstd"] = A.sb([128, NB], F32, "rstd")
        T_["epsc"] = A.sb([128, 1], F32, "epsc")
        ones_bf = A.sb([128, 128], BF16, "ones")
        gcol = A.sb([128, 8], F32, "gcol")
        bg = A.sb([128, 24], F32, "bg")
        hmask = A.sb([128, 2], F32, "hmask")
        mg = A.sb([128, 8, NB], F32, "mg")
        mgT = A.sb([128, 8, NB], BF16, "mgT")
        sg = [A.sb([128, NB], F32, "sg") for _ in range(2)]
        tmp = [A.sb([128, NB], F32, "tmp") for _ in range(2)]
        ps_ss = A.ps([128, 512], F32, "ps_ss")
        psg = [A.ps([128, 512], F32, "psg") for _ in range(2)]
        psb = [A.ps([128, 512], F32, "psb") for _ in range(2)]
        pso = [A.ps([128, 512], F32, "pso") for _ in range(2)]
        S.pool(lambda e: e.memset(ones_bf[:, :], 1.0), w=["ones"])
        S.pool(lambda e: e.memset(T_["epsc"][:, :], EPS), w=["epsc"])
        S.dma(gcol[:, :], gmix, w=["gcol"])
        S.dma(bg[:, :], bg_d, w=["bg"])
        S.dma(hmask[:, :], hmask_d, w=["hmask"])
        k = load_w_generic(S, lambda c0, n: wg[:, :, c0:c0 + n], Wg.rearrange("(c p) n -> p c n", p=128), 8, 3072, 256, stg, "stg", "wg")
        for n_ in range(3):
            k = load_w_generic(S, lambda c0, n, n_=n_: wb[:, n_, :, c0:c0 + n], Wb[n_].rearrange("(c p) n -> p c n", p=128), 8, 1024, 256, stg, "stg", "wb", k)
        k = load_w_generic(S, lambda c0, n: wo[:, :, c0:c0 + n], Wout.rearrange("(c p) n -> p c n", p=128), 8, 1024, 256, stg, "stg", "wo", k)
        xv = xTo.rearrange("(c p) t -> p c t", p=128)
        yvs = ysrc if ypre else [y_.rearrange("(c p) t -> p c t", p=128) for y_ in ysrc]
        yview = (lambda t_: t_.rearrange("p (r k) n -> p r k n", r=2)) if ypre else (lambda t_: t_)
        ov = x1T.rearrange("(c p) t -> p c t", p=128)
        kk = 0
        for bi, (c0, nb) in enumerate(blocks_of(H, TO, NB)):
            xt = xb[0]
            xtag = ("xb", 0)
            yt = yb[bi % 2]
            ytag = ("yb", bi % 2)
            S.dma(xt[:, :, :nb], xv[:, :, c0:c0 + nb], r=list(xr), w=[xtag])
            if ypre:
                for r_ in range(2):
                    S.dma(yt[:, 12 * r_:12 * (r_ + 1), :nb], yvs[0][:, r_, :, c0:c0 + nb], r=list(yr), w=[ytag])
            else:
                S.dma(yt[:, :, :nb], yvs[0][:, :, c0:c0 + nb], r=list(yr), w=[ytag])
            if len(ysrc) == 2:
                for r_ in range(2):
                    S.dma(yb2[:, 12 * r_:12 * (r_ + 1), :nb], yvs[1][:, r_, :, c0:c0 + nb], r=list(yr), w=["yb2"])
                S.dve(lambda e, yt=yt, nb=nb: e.tensor_scalar(out=yt[:, :, :nb], in0=yt[:, :, :nb], scalar1=hmask[:, 0:1], scalar2=None, op0=ALU.mult),
                      r=[ytag, "hmask"], w=[ytag])
                S.dve(lambda e, yt=yt, nb=nb: e.scalar_tensor_tensor(out=yt[:, :, :nb], in0=yb2[:, :, :nb], scalar=hmask[:, 1:2], in1=yt[:, :, :nb], op0=ALU.mult, op1=ALU.add),
                      r=[ytag, "yb2", "hmask"], w=[ytag])
            if bi == 0 and H > 0:
                S.dve(lambda e, xt=xt, nb=nb: e.tensor_scalar(out=xt[:, :, :nb], in0=xt[:, :, :nb], scalar1=hmask[:, 1:2], scalar2=None, op0=ALU.mult),
                      r=[xtag, "hmask"], w=[xtag])
            rms_block(S, T_, ones_bf, xt, gcol, hT, ps_ss, xtag, nb)
            htag = ("hT", id(hT))
            for dc in range(8):
                for n_ in range(3):
                    pg = psg[kk % 2]
                    pgt = ("psg", kk % 2)
                    pbk = psb[kk % 2]
                    pbt = ("psb", kk % 2)
                    sgt = sg[kk % 2]
                    sgtag = ("sg", kk % 2)
                    tm = tmp[kk % 2]
                    tmtag = ("tmp", kk % 2)
                    kk += 1
                    for c in range(8):
                        S.pe(lambda e, c=c, pg=pg, n_=n_, dc=dc, nb=nb: e.matmul(pg[:, :nb], lhsT=wg[:, c, n_ * 1024 + dc * 128:n_ * 1024 + (dc + 1) * 128], rhs=hT[:, c, :nb],
                                                                               start=(c == 0), stop=(c == 7)), r=["wg", htag], w=[pgt])
                    S.act(lambda e, pg=pg, sgt=sgt, n_=n_, dc=dc, nb=nb: e.activation(out=sgt[:, :nb], in_=pg[:, :nb], func=AF.Sigmoid, bias=bg[:, n_ * 8 + dc:n_ * 8 + dc + 1]),
                          r=[pgt, "bg"], w=[sgtag])
                    for c in range(8):
                        ych = (c // 4) * 12 + n_ * 4 + (c % 4)
                        S.pe(lambda e, c=c, pbk=pbk, n_=n_, dc=dc, nb=nb, ych=ych, yt=yt: e.matmul(pbk[:, :nb], lhsT=wb[:, n_, c, dc * 128:(dc + 1) * 128], rhs=yt[:, ych, :nb],
                                                                                                start=(c == 0), stop=(c == 7)), r=["wb", ytag], w=[pbt])
                    if n_ == 0:
                        S.dve(lambda e, pbk=pbk, sgt=sgt, dc=dc, nb=nb: e.tensor_tensor(out=mg[:, dc, :nb], in0=pbk[:, :nb], in1=sgt[:, :nb], op=ALU.mult),
                              r=[pbt, sgtag], w=[("mg", dc)])
                    else:
                        S.dve(lambda e, pbk=pbk, sgt=sgt, tm=tm, nb=nb: e.tensor_tensor(out=tm[:, :nb], in0=pbk[:, :nb], in1=sgt[:, :nb], op=ALU.mult),
                              r=[pbt, sgtag], w=[tmtag])
                        dst = mg if n_ == 1 else mgT
                        S.pool(lambda e, tm=tm, dc=dc, nb=nb, dst=dst: e.tensor_tensor(out=dst[:, dc, :nb], in0=mg[:, dc, :nb], in1=tm[:, :nb], op=ALU.add),
                               r=[("mg", dc), tmtag], w=[("mg", dc), ("mgT", dc)])
            for dc in range(8):
                po = pso[dc % 2]
                pot = ("pso", dc % 2)
                for c in range(8):
                    S.pe(lambda e, c=c, po=po, dc=dc, nb=nb: e.matmul(po[:, :nb], lhsT=wo[:, c, dc * 128:(dc + 1) * 128], rhs=mgT[:, c, :nb], start=(c == 0), stop=(c == 7)),
                         r=["wo"] + [("mgT", c_) for c_ in range(8)], w=[pot])
                S.dve(lambda e, po=po, dc=dc, nb=nb, xt=xt: e.tensor_tensor(out=xt[:, dc, :nb], in0=po[:, :nb], in1=xt[:, dc, :nb], op=ALU.add), r=[pot, xtag, htag], w=[xtag])
            S.dma(ov[:, :, c0:c0 + nb], xt[:, :, :nb], r=[xtag], w=[("x1T", c0)])
        S.emit()


def phase_C2(nc, S, H, TO, x1T, Wup, cw_d, cb_d, Wdown, gffn, x2T, final_g=None, outT=None, xsend=None):
    NB = 256
    with ExitStack() as st:
        A = Tiles(nc, st)
        S.barrier()
        T_ = {}
        wu = A.sb([128, 8, 5632], BF16, "wu")
        wd = A.sb([128, 22, 1024], BF16, "wd")
        stg = [A.sb([128, 2048], F32, "stg") for _ in range(2)]
        xb = [A.sb([128, 8, NB], F32, "xb") for _ in range(2)]
        hT = A.sb([128, 8, NB], BF16, "hT")
        T_["xsq"] = A.sb([128, 8, NB], BF16, "xsq")
        T_["std"] = A.sb([128, NB], F32, "std")
        T_["rstd"] = A.sb([128, NB], F32, "rstd")
        T_["epsc"] = A.sb([128, 1], F32, "epsc")
        ones_bf = A.sb([128, 128], BF16, "ones")
        gcol = A.sb([128, 8], F32, "gcol")
        gfin = A.sb([128, 8], F32, "gfin")
        cw = A.sb([128, 44, 3], F32, "cw")
        cb = A.sb([128, 44], F32, "cb")
        halo = A.sb([128, 44, 2], F32, "halo")
        ua = [A.sb([128, NB + 2], F32, "ua") for _ in range(2)]
        ug = [A.sb([128, NB + 2], F32, "ug") for _ in range(2)]
        aa = [A.sb([128, NB], F32, "aa") for _ in range(2)]
        ag = [A.sb([128, NB], F32, "ag") for _ in range(2)]
        actT = A.sb([128, 22, NB], BF16, "actT")
        oT = A.sb([128, 8, NB], F32, "oT")
        ps_ss = A.ps([128, 512], F32, "ps_ss")
        psa = [A.ps([128, 512], F32, "psa") for _ in range(2)]
        psgt = [A.ps([128, 512], F32, "psgt") for _ in range(2)]
        psd = [A.ps([128, 512], F32, "psd") for _ in range(2)]
        S.pool(lambda e: e.memset(ones_bf[:, :], 1.0), w=["ones"])
        S.pool(lambda e: e.memset(T_["epsc"][:, :], EPS), w=["epsc"])
        S.dve(lambda e: e.memset(halo[:, :, :], 0.0), w=["halo"])
        S.dma(gcol[:, :], gffn, w=["gcol"])
        if final_g is not None:
            S.dma(gfin[:, :], final_g, w=["gfin"])
        S.dma(cw[:, :, :], cw_d, w=["cw"])
        S.dma(cb[:, :], cb_d, w=["cb"])
        k = load_w_generic(S, lambda c0, n: wu[:, :, c0:c0 + n], Wup.rearrange("(c p) n -> p c n", p=128), 8, 5632, 256, stg, "stg", "wu")
        k = load_w_generic(S, lambda c0, n: wd[:, :, c0:c0 + n], Wdown.rearrange("(c p) n -> p c n", p=128), 22, 1024, 64, stg, "stg", "wd", k)
        xv = x1T.rearrange("(c p) t -> p c t", p=128)
        ov = x2T.rearrange("(c p) t -> p c t", p=128)
        kk = 0
        for bi, (c0, nb) in enumerate(blocks_of(H, TO, NB)):
            xt = xb[bi % 2]
            xtag = ("xb", bi % 2)
            S.dma(xt[:, :, :nb], xv[:, :, c0:c0 + nb], r=[("x1T", c0)], w=[xtag])
            rms_block(S, T_, ones_bf, xt, gcol, hT, ps_ss, xtag, nb)
            htag = ("hT", id(hT))
            for fc in range(22):
                b2 = kk % 2
                kk += 1
                for (ps, pst, off, ut, utag, acc, atag, hc) in ((psa[b2], ("psa", b2), 0, ua[b2], ("ua", b2), aa[b2], ("aa", b2), fc),
                                                                (psgt[b2], ("psgt", b2), 2816, ug[b2], ("ug", b2), ag[b2], ("ag", b2), 22 + fc)):
                    for c in range(8):
                        S.pe(lambda e, c=c, ps=ps, off=off, fc=fc, nb=nb: e.matmul(ps[:, :nb], lhsT=wu[:, c, off + fc * 128:off + (fc + 1) * 128], rhs=hT[:, c, :nb],
                                                                                 start=(c == 0), stop=(c == 7)), r=["wu", htag], w=[pst])
                    S.act(lambda e, ut=ut, hc=hc: e.copy(out=ut[:, 0:2], in_=halo[:, hc, :]), r=[("halo", hc)], w=[utag])
                    S.act(lambda e, ut=ut, ps=ps, nb=nb: e.copy(out=ut[:, 2:2 + nb], in_=ps[:, :nb]), r=[pst], w=[utag])
                    S.act(lambda e, ut=ut, hc=hc, nb=nb: e.copy(out=halo[:, hc, :], in_=ut[:, nb:nb + 2]), r=[utag], w=[("halo", hc)])
                    S.dve(lambda e, ut=ut, acc=acc, hc=hc, nb=nb: e.tensor_scalar(out=acc[:, :nb], in0=ut[:, 0:nb], scalar1=cw[:, hc, 0:1], scalar2=cb[:, hc:hc + 1],
                                                                                op0=ALU.mult, op1=ALU.add), r=[utag, "cw", "cb"], w=[atag])
                    for j in (1, 2):
                        S.dve(lambda e, ut=ut, acc=acc, hc=hc, nb=nb, j=j: e.scalar_tensor_tensor(out=acc[:, :nb], in0=ut[:, j:j + nb], scalar=cw[:, hc, j:j + 1], in1=acc[:, :nb],
                                                                                                  op0=ALU.mult, op1=ALU.add), r=[utag, "cw", atag], w=[atag])
                S.act(lambda e, b2=b2, nb=nb: e.activation(out=ag[b2][:, :nb], in_=ag[b2][:, :nb], func=AF.Silu), r=[("ag", b2)], w=[("ag", b2)])
                S.pool(lambda e, b2=b2, fc=fc, nb=nb: e.tensor_tensor(out=actT[:, fc, :nb], in0=aa[b2][:, :nb], in1=ag[b2][:, :nb], op=ALU.mult),
                       r=[("aa", b2), ("ag", b2)], w=[("actT", fc)])
            for dc in range(8):
                po = psd[dc % 2]
                pot = ("psd", dc % 2)
                for fc in range(22):
                    S.pe(lambda e, fc=fc, po=po, dc=dc, nb=nb: e.matmul(po[:, :nb], lhsT=wd[:, fc, dc * 128:(dc + 1) * 128], rhs=actT[:, fc, :nb], start=(fc == 0), stop=(fc == 21)),
                         r=["wd"] + [("actT", f_) for f_ in range(22)], w=[pot])
                S.dve(lambda e, po=po, dc=dc, nb=nb, xt=xt: e.tensor_tensor(out=xt[:, dc, :nb], in0=po[:, :nb], in1=xt[:, dc, :nb], op=ALU.add), r=[pot, xtag, htag], w=[xtag])
            if final_g is None:
                S.dma(ov[:, :, c0:c0 + nb], xt[:, :, :nb], r=[xtag], w=[("x2T", c0)])
                if xsend is not None and not (bi == 0 and H > 0):
                    S.dma(xsend.rearrange("(c p) t -> p c t", p=128)[:, :, c0 - H:c0 - H + nb], xt[:, :, :nb], r=[xtag], w=[("xsend", c0)])
            elif not (bi == 0 and H > 0):
                xsq, std, rstd = T_["xsq"], T_["std"], T_["rstd"]
                S.act(lambda e, xt=xt, nb=nb: e.activation(out=xsq[:, :, :nb], in_=xt[:, :, :nb], func=AF.Square), r=[xtag], w=["xsq"])
                for c in range(8):
                    S.pe(lambda e, c=c, nb=nb: e.matmul(ps_ss[:, :nb], lhsT=ones_bf[:, :], rhs=xsq[:, c, :nb], start=(c == 0), stop=(c == 7)), r=["xsq", "ones"], w=["ps_ss"])
                S.act(lambda e, nb=nb: e.activation(out=std[:, :nb], in_=ps_ss[:, :nb], func=AF.Sqrt, bias=T_["epsc"][:, 0:1], scale=1.0 / D), r=["ps_ss", "epsc"], w=["std"])
                S.dve(lambda e, nb=nb: e.reciprocal(out=rstd[:, :nb], in_=std[:, :nb]), r=["std"], w=["rstd"])
                for c in range(8):
                    S.dve(lambda e, c=c, nb=nb, xt=xt: e.scalar_tensor_tensor(out=oT[:, c, :nb], in0=xt[:, c, :nb], scalar=gfin[:, c:c + 1], in1=rstd[:, :nb], op0=ALU.mult, op1=ALU.mult),
                          r=[xtag, "rstd", "gfin"], w=["oT"])
                S.dma(outT.rearrange("(c p) t -> p c t", p=128)[:, :, c0 - H:c0 - H + nb], oT[:, :, :nb], r=["oT"], w=[("outT", c0)])
        S.emit()


T_FULL = 8192
TO_FULL = 4096
ML_SH = {"cw": [128, 8, 4], "cb": [128, 8], "wq_bd": [4, 128, 128], "wk_bd": [4, 128, 128], "wv_bd": [4, 128, 128], "wqT_bd": [8, 128, 128],
         "wkT_bd": [8, 128, 128], "wvT_bd": [8, 128, 128], "wif": [128, 3, 8, 4], "gb": [128, 4], "skc": [128, 4], "gnb": [128, 512]}


def build_AB(T):
    nc = bass.Bass("TRN2", target_bir_lowering=False)
    dt = lambda n, s, d, k: nc.dram_tensor(n, s, d, kind=k).ap()
    xT = dt("xT", [D, T], F32, "ExternalInput")
    wA = dt("wA", [D, NA], F32, "ExternalInput")
    gm = dt("gmix", [128, 8], F32, "ExternalInput")
    fb = dt("foxbf_t", [128, T // 128, 4], F32, "ExternalInput")
    wl = dt("wlr_aug", [17, 256], F32, "ExternalInput")
    gn = dt("gla_gnb", [128, 512], F32, "ExternalInput")
    P = {k: dt("m_" + k, v, F32, "ExternalInput") for k, v in ML_SH.items()}
    yT = dt("yT", [1536, T], BF16, "ExternalOutput")
    pFM = dt("pFM", [NFM, T], BF16, "Internal")
    pTM = dt("pTM", [T, NTM], BF16, "Internal")
    with ExitStack() as st:
        S = Sched(nc, st)
        phase_A(nc, S, T, xT, wA, gm, pFM, pTM)
        yd = YDst(T, yT=yT)
        phase_gla(nc, S, T, pFM, pTM, wl, gn, yd)
        phase_mlstm(nc, S, T, pFM, pTM, P, yd)
        phase_fox(nc, S, T, pFM, pTM, fb, yd)
        S.finish()
        S.emit()
    return nc


def build_C(H, TO, final):
    nc = bass.Bass("TRN2", target_bir_lowering=False)
    dt = lambda n, s, d, k: nc.dram_tensor(n, s, d, kind=k).ap()
    W = H + TO
    xTo = dt("xTo", [D, W], F32, "ExternalInput")
    yTall = dt("yTall", [3072, W], BF16, "ExternalInput")
    hm = dt("hmask", [128, 2], F32, "ExternalInput")
    Wg = dt("Wg", [D, 3072], F32, "ExternalInput")
    bg = dt("bg", [128, 24], F32, "ExternalInput")
    Wb = dt("Wb", [3, D, D], F32, "ExternalInput")
    Wo = dt("Wout", [D, D], F32, "ExternalInput")
    gm = dt("gmix", [128, 8], F32, "ExternalInput")
    Wup = dt("Wup", [D, 5632], F32, "ExternalInput")
    cw = dt("fcw", [128, 44, 3], F32, "ExternalInput")
    cb = dt("fcb", [128, 44], F32, "ExternalInput")
    Wd = dt("Wdown", [2816, D], F32, "ExternalInput")
    gf = dt("gffn", [128, 8], F32, "ExternalInput")
    x1T = dt("x1T", [D, W], F32, "Internal")
    if final:
        gfin = dt("gfin", [128, 8], F32, "ExternalInput")
        outT = dt("outT", [D, TO], F32, "ExternalOutput")
        x2T = x1T
    else:
        gfin, outT = None, None
        x2T = dt("x2T", [D, W], F32, "ExternalOutput")
    with ExitStack() as st:
        S = Sched(nc, st)
        phase_C1(nc, S, H, TO, xTo, [yTall], hm, Wg, bg, Wb, Wo, gm, x1T)
        phase_C2(nc, S, H, TO, x1T, Wup, cw, cb, Wd, gf, x2T, gfin, outT)
        S.finish()
        S.emit()
    return nc


def colvec(v):
    v = np.asarray(v, np.float32)
    return np.ascontiguousarray(v.reshape(-1, 128).T)


SPL = np.cumsum([0, 512, 512, 1024, 16, 1024, 1024, 1024, 1024, 1024, 1024, 8, 1024, 3072])
O_GQ, O_GK, O_GV, O_GLR, O_GR, O_MX, O_MZ, O_FQ, O_FK, O_FV, O_FF, O_FOG, O_GATES = [int(v) for v in SPL[:13]]


def prep_AB_inputs(p, l, inp, T):
    w_in = inp["w_in"][l]
    own = np.arange(512 * p, 512 * p + 512)
    oth = np.arange(512 * (1 - p), 512 * (1 - p) + 512)
    perm = np.concatenate([own, oth])
    r256 = np.arange(256 * p, 256 * p + 256)
    cols = np.concatenate([O_GQ + r256, O_GK + r256, O_MX + perm, O_FQ + own, O_FK + own, O_FOG + own, O_GLR + np.arange(16),
                           O_GK + r256, O_GV + own, O_GR + own, O_MZ + own, O_FV + own, O_FF + np.arange(4 * p, 4 * p + 4)])
    assert cols.size == NA
    d = {}
    d["wA"] = np.ascontiguousarray(w_in[:, cols])
    d["gmix"] = colvec(inp["norm_mix"][l])
    d["foxbf_t"] = np.ascontiguousarray(np.broadcast_to(inp["fox_b_f"][l][4 * p:4 * p + 4], (128, T // 128, 4))).astype(np.float32)
    d["wlr_aug"] = np.ascontiguousarray(np.concatenate([inp["gla_w_lr"][l][:, r256], inp["gla_b_lr"][l][r256][None]], 0)).astype(np.float32)
    d["gla_gnb"] = bcast(inp["gla_norm"][l][own])
    P, _ = prep_mlstm(p, inp["mlstm_conv_w"][l], inp["mlstm_conv_b"][l], inp["mlstm_wq"][l], inp["mlstm_wk"][l], inp["mlstm_wv"][l],
                      inp["mlstm_w_i"][l], inp["mlstm_b_i"][l], inp["mlstm_w_f"][l], inp["mlstm_b_f"][l], inp["mlstm_skip"][l], inp["mlstm_norm"][l])
    for k, v in P.items():
        d["m_" + k] = v
    return d


def prep_C_inputs(l, inp, final):
    d = {}
    d["Wg"] = np.ascontiguousarray(inp["w_in"][l][:, O_GATES:O_GATES + 3072])
    d["bg"] = colvec(inp["b_gate"][l].reshape(-1))
    d["Wb"] = np.ascontiguousarray(inp["w_branch"][l])
    d["Wout"] = np.ascontiguousarray(inp["w_out"][l])
    d["gmix"] = colvec(inp["norm_mix"][l])
    d["Wup"] = np.ascontiguousarray(inp["ffn_w_up"][l])
    d["fcw"] = np.ascontiguousarray(inp["ffn_conv_w"][l].reshape(3, 44, 128).transpose(2, 1, 0))
    d["fcb"] = colvec(inp["ffn_conv_b"][l])
    d["Wdown"] = np.ascontiguousarray(inp["ffn_w_down"][l])
    d["gffn"] = colvec(inp["norm_ffn"][l])
    if final:
        d["gfin"] = colvec(inp["norm_final"])
    return d


_NC_CACHE = {}


def _get_nc(key, fn):
    if key not in _NC_CACHE:
        _NC_CACHE[key] = fn()
    return _NC_CACHE[key]


def kernel_unfused(inp):
    x = inp["x"]
    B, T, _ = x.shape
    TO = T // 2
    ncore = 2 * B
    HS = [4, 2]
    xT_full = [np.ascontiguousarray(x[b].T) for b in range(B)]
    def own_slice(arr, p, H):
        if p == 0:
            return np.ascontiguousarray(np.concatenate([np.zeros((arr.shape[0], H), arr.dtype), arr[:, :TO]], axis=1))
        return np.ascontiguousarray(arr[:, TO - H:])
    xTo = [own_slice(xT_full[c // 2], c % 2, HS[0]) for c in range(ncore)]
    out = None
    for l in range(2):
        H = HS[l]
        final = (l == 1)
        nc_ab = _get_nc(("AB", T), lambda: build_AB(T))
        in_maps = []
        for c in range(ncore):
            d = prep_AB_inputs(c % 2, l, inp, T)
            d["xT"] = xT_full[c // 2]
            in_maps.append(d)
        res = run_bass_kernel_spmd(nc_ab, in_maps, core_ids=list(range(ncore)))
        yTs = [np.asarray(r["yT"]) for r in res.results]
        nc_c = _get_nc(("C", H, TO, final), lambda: build_C(H, TO, final))
        cw = prep_C_inputs(l, inp, final)
        in_maps = []
        for c in range(ncore):
            b, p = c // 2, c % 2
            d = dict(cw)
            d["xTo"] = xTo[c]
            d["yTall"] = np.ascontiguousarray(np.concatenate([own_slice(yTs[2 * b + rr], p, H) for rr in range(2)], axis=0))
            d["hmask"] = np.ascontiguousarray(np.broadcast_to(np.array([1.0 - p, float(p)], np.float32), (128, 2)))
            in_maps.append(d)
        res = run_bass_kernel_spmd(nc_c, in_maps, core_ids=list(range(ncore)))
        if not final:
            x2 = [np.asarray(r["x2T"]) for r in res.results]
            xT_full = [np.ascontiguousarray(np.concatenate([x2[2 * b][:, H:], x2[2 * b + 1][:, H:]], axis=1)) for b in range(B)]
            xTo = [np.ascontiguousarray(x2[c][:, H - HS[1]:]) for c in range(ncore)]
        else:
            o = [np.asarray(r["outT"]) for r in res.results]
            out = np.stack([np.concatenate([o[2 * b], o[2 * b + 1]], axis=1).T for b in range(B)]).astype(np.float32)
    return np.ascontiguousarray(out)


AB_IN = [("wA", [D, NA]), ("gmix", [128, 8]), ("foxbf_t", None), ("wlr_aug", [17, 256]), ("gla_gnb", [128, 512])] + [("m_" + k, v) for k, v in ML_SH.items()]
C_IN = [("Wg", [D, 3072]), ("bg", [128, 24]), ("Wb", [3, D, D]), ("Wout", [D, D]), ("Wup", [D, 5632]), ("fcw", [128, 44, 3]), ("fcb", [128, 44]),
        ("Wdown", [2816, D]), ("gffn", [128, 8])]


def build_fused(T, ncore):
    TO = T // 2
    HS = [4, 2]
    NBK = T // 512
    half = NBK // 2
    groups = [[i, i + 1] for i in range(0, ncore, 2)]
    nc = bass.Bass("TRN2", target_bir_lowering=False)
    dt = lambda n, s, d, k: nc.dram_tensor(n, s, d, kind=k).ap()
    xT = dt("xT", [D, T], F32, "ExternalInput")
    xTo = dt("xTo", [D, HS[0] + TO], F32, "ExternalInput")
    hm = dt("hmask", [128, 2], F32, "ExternalInput")
    gfin = dt("gfin", [128, 8], F32, "ExternalInput")
    outT = dt("outT", [D, TO], F32, "ExternalOutput")
    W = []
    for l in range(2):
        d = {}
        for n, shp in AB_IN:
            d[n] = dt(f"{n}_l{l}", shp if shp is not None else [128, T // 128, 4], F32, "ExternalInput")
        for n, shp in C_IN:
            d[n] = dt(f"{n}_l{l}", shp, F32, "ExternalInput")
        W.append(d)
    pFM = dt("pFM", [NFM, T], BF16, "Internal")
    pTM = dt("pTM", [T, NTM], BF16, "Internal")
    ys = [[dt(f"ys_{l}_{s_}", [1536, HS[l] + TO], BF16, "Internal") for s_ in range(2)] for l in range(2)]
    yg = [[dt(f"yg_{l}_{s_}", [12, 2, 128, HS[l] + TO], BF16, "Internal") for s_ in range(2)] for l in range(2)]
    x1T = [dt(f"x1T_{l}", [D, HS[l] + TO], F32, "Internal") for l in range(2)]
    x2T0 = dt("x2T_0", [D, HS[0] + TO], F32, "Internal")
    xsend = dt("xsend", [D, TO], F32, "Internal")
    sgT = dt("sgT", [3072, HS[0] + TO], BF16, "Internal")
    actD = dt("actD", [2816, HS[0] + TO], BF16, "Internal")
    yselD = dt("yselD", [3072, HS[0] + TO], BF16, "Internal")
    xg = dt("xg", [8, 2, 128, TO], F32, "Internal")
    with ExitStack() as st:
        S = Sched(nc, st)
        for l in range(2):
            H = HS[l]
            w = W[l]
            if l == 0:
                phase_A(nc, S, T, xT, w["wA"], w["gmix"], pFM, pTM)
            else:
                xgv = xg.rearrange("c r p t -> r p c t")
                xblk = lambda j: xgv[j // half][:, :, 512 * (j % half):512 * (j % half + 1)]
                phase_A(nc, S, T, None, w["wA"], w["gmix"], pFM, pTM, xblk=xblk, xr=["xg"])
            yd = YDst(T, ys=ys[l], H=H, key=("y", l))
            P = {k: w["m_" + k] for k in ML_SH}
            def gather_rows(i0, i1, row_lo, row_hi):
                toks = [t for t in yd.tokens if t[1] == "zero" or (isinstance(t[1], int) and row_lo <= t[1] < row_hi)]
                for s_ in range(2):
                    for i in range(i0, i1):
                        S.collective("AllGather", [ys[l][s_][128 * i:128 * (i + 1), :]], [yg[l][s_][i].rearrange("r p w -> (r p) w")], groups,
                                     r=toks, w=[("yg", l, s_, i)])

            phase_gla(nc, S, T, pFM, pTM, w["wlr_aug"], w["gla_gnb"], yd, zero_halo=True)
            gather_rows(0, 4, 0, 512)
            phase_mlstm(nc, S, T, pFM, pTM, P, yd)
            gather_rows(4, 8, 512, 1024)
            phase_fox(nc, S, T, pFM, pTM, w["foxbf_t"], yd)
            gather_rows(8, 12, 1024, 1536)
            xin = xTo if l == 0 else x2T0[:, HS[0] - HS[1]:]
            xr = [] if l == 0 else [("x2T", c0) for (c0, nb) in blocks_of(HS[0], TO, 256)]
            W_ = H + TO
            sg_l = sgT[:, 0:W_]
            act_l = actD[:, 0:W_]
            blk = blocks_of(HS[0], TO, 512)
            xr2 = [] if l == 0 else [("x2T", c0) for (c0, nb) in blk]
            ysel_l = yselD[:, 0:W_]
            phase_C1a(nc, S, H, TO, xin, hm, w["Wg"], w["bg"], w["gmix"], sg_l, xr=xr2,
                      ysrc=[y_.rearrange("k r p w -> p r k w") for y_ in yg[l]], ysel=ysel_l, yr=[("yg", l, s_, i) for s_ in range(2) for i in range(12)])
            phase_C1b(nc, S, H, TO, xin, ysel_l, hm, sg_l, w["Wb"], w["Wout"], x1T[l], xr=xr2)
            phase_C2a(nc, S, H, TO, x1T[l], w["Wup"], w["fcw"], w["fcb"], w["gffn"], act_l)
            if l == 0:
                phase_C2b(nc, S, H, TO, x1T[l], act_l, w["Wdown"], x2T0, xsend=xsend)
                for i in range(8):
                    S.collective("AllGather", [xsend[128 * i:128 * (i + 1), :]], [xg[i].rearrange("r p t -> (r p) t")], groups,
                                 r=[("xsend", c0) for (c0, nb) in blocks_of(H, TO, 512)], w=["xg"])
            else:
                phase_C2b(nc, S, H, TO, x1T[l], act_l, w["Wdown"], x1T[l], gfin, outT)
        S.finish()
        S.emit()
    return nc


def kernel_fused(inp):
    x = inp["x"]
    B, T, _ = x.shape
    TO = T // 2
    ncore = 2 * B
    nc = _get_nc(("F", T, ncore), lambda: build_fused(T, ncore))
    in_maps = []
    per_rank = {}
    for p in range(2):
        d = {}
        for l in range(2):
            for k, v in prep_AB_inputs(p, l, inp, T).items():
                d[f"{k}_l{l}"] = v
            for k, v in prep_C_inputs(l, inp, False).items():
                if k != "gmix":
                    d[f"{k}_l{l}"] = v
        d["gfin"] = colvec(inp["norm_final"])
        d["hmask"] = np.ascontiguousarray(np.broadcast_to(np.array([1.0 - p, float(p)], np.float32), (128, 2)))
        per_rank[p] = d
    for c in range(ncore):
        b, p = c // 2, c % 2
        d = dict(per_rank[p])
        xTb = np.ascontiguousarray(x[b].T)
        d["xT"] = xTb
        if p == 0:
            d["xTo"] = np.ascontiguousarray(np.concatenate([np.zeros((D, 4), np.float32), xTb[:, :TO]], axis=1))
        else:
            d["xTo"] = np.ascontiguousarray(xTb[:, TO - 4:])
        in_maps.append(d)
    res = run_bass_kernel_spmd(nc, in_maps, core_ids=list(range(ncore)))
    o = [np.asarray(r["outT"]) for r in res.results]
    out = np.stack([np.concatenate([o[2 * b], o[2 * b + 1]], axis=1).T for b in range(B)]).astype(np.float32)
    return np.ascontiguousarray(out)


FUSED = True


def kernel(**inputs):
    inp = {k: np.asarray(v, dtype=np.float32) for k, v in inputs.items()}
    if FUSED:
        return kernel_fused(inp)
    return kernel_unfused(inp)


def phase_C1a(nc, S, H, TO, xTo, hmask_d, Wg, bg_d, gmix, sgT, xr=(), ysrc=None, ysel=None, yr=()):
    NB = 512
    with ExitStack() as st:
        A = Tiles(nc, st)
        S.barrier()
        T_ = {}
        wg = A.sb([128, 8, 3072], BF16, "wg")
        stg = [A.sb([128, 2048], F32, "stg") for _ in range(2)]
        xb = [A.sb([128, 8, NB], F32, "xb") for _ in range(2)]
        hT = A.sb([128, 8, NB], BF16, "hT")
        T_["xsq"] = A.sb([128, 8, NB], BF16, "xsq")
        T_["std"] = A.sb([128, NB], F32, "std")
        T_["rstd"] = A.sb([128, NB], F32, "rstd")
        T_["epsc"] = A.sb([128, 1], F32, "epsc")
        ones_bf = A.sb([128, 128], BF16, "ones")
        gcol = A.sb([128, 8], F32, "gcol")
        bg = A.sb([128, 24], F32, "bg")
        hmask = A.sb([128, 2], F32, "hmask")
        sgo = [A.sb([128, NB], BF16, "sgo") for _ in range(4)]
        if ysrc is not None:
            ya = A.sb([128, 24, NB], BF16, "ya")
            yb_ = A.sb([128, 24, NB], BF16, "yb_")
            ysv = ysel.rearrange("(c p) t -> p c t", p=128)
        ps_ss = A.ps([128, 512], F32, "ps_ss")
        psg = [A.ps([128, 512], F32, "psg") for _ in range(4)]
        S.pool(lambda e: e.memset(ones_bf[:, :], 1.0), w=["ones"])
        S.pool(lambda e: e.memset(T_["epsc"][:, :], EPS), w=["epsc"])
        S.dma(gcol[:, :], gmix, w=["gcol"])
        S.dma(bg[:, :], bg_d, w=["bg"])
        S.dma(hmask[:, :], hmask_d, w=["hmask"])
        load_w_generic(S, lambda c0, n: wg[:, :, c0:c0 + n], Wg.rearrange("(c p) n -> p c n", p=128), 8, 3072, 256, stg, "stg", "wg")
        xv = xTo.rearrange("(c p) t -> p c t", p=128)
        kk = 0
        blks = blocks_of(H, TO, NB)

        def load_x(bi):
            c0_, nb_ = blks[bi]
            S.dma(xb[bi % 2][:, :, :nb_], xv[:, :, c0_:c0_ + nb_], r=list(xr), w=[("xb", bi % 2)])

        load_x(0)
        for bi, (c0, nb) in enumerate(blks):
            xt = xb[bi % 2]
            xtag = ("xb", bi % 2)
            if bi + 1 < len(blks):
                load_x(bi + 1)
            if ysrc is not None:
                for r_ in range(2):
                    S.dma(ya[:, 12 * r_:12 * (r_ + 1), :nb], ysrc[0][:, r_, :, c0:c0 + nb], r=list(yr), w=["ya"])
                    S.dma(yb_[:, 12 * r_:12 * (r_ + 1), :nb], ysrc[1][:, r_, :, c0:c0 + nb], r=list(yr), w=["yb_"])
                S.dve(lambda e, nb=nb: e.tensor_scalar(out=ya[:, :, :nb], in0=ya[:, :, :nb], scalar1=hmask[:, 0:1], scalar2=None, op0=ALU.mult),
                      r=["ya", "hmask"], w=["ya"])
                S.dve(lambda e, nb=nb: e.scalar_tensor_tensor(out=ya[:, :, :nb], in0=yb_[:, :, :nb], scalar=hmask[:, 1:2], in1=ya[:, :, :nb], op0=ALU.mult, op1=ALU.add),
                      r=["ya", "yb_", "hmask"], w=["ya"])
                S.dma(ysv[:, :, c0:c0 + nb], ya[:, :, :nb], r=["ya"], w=[("ysel", c0)])
            if bi == 0 and H > 0:
                S.dve(lambda e, xt=xt, nb=nb: e.tensor_scalar(out=xt[:, :, :nb], in0=xt[:, :, :nb], scalar1=hmask[:, 1:2], scalar2=None, op0=ALU.mult),
                      r=[xtag, "hmask"], w=[xtag])
            rms_block(S, T_, ones_bf, xt, gcol, hT, ps_ss, xtag, nb)
            htag = ("hT", id(hT))
            for n_ in range(3):
                for dc in range(8):
                    pg = psg[kk % 4]
                    pgt = ("psg", kk % 4)
                    so = sgo[kk % 4]
                    sot = ("sgo", kk % 4)
                    kk += 1
                    for c in range(8):
                        S.pe(lambda e, c=c, pg=pg, n_=n_, dc=dc, nb=nb: e.matmul(pg[:, :nb], lhsT=wg[:, c, n_ * 1024 + dc * 128:n_ * 1024 + (dc + 1) * 128], rhs=hT[:, c, :nb],
                                                                               start=(c == 0), stop=(c == 7)), r=["wg", htag], w=[pgt])
                    S.act(lambda e, pg=pg, so=so, n_=n_, dc=dc, nb=nb: e.activation(out=so[:, :nb], in_=pg[:, :nb], func=AF.Sigmoid, bias=bg[:, n_ * 8 + dc:n_ * 8 + dc + 1]),
                          r=[pgt, "bg"], w=[sot])
                    k_ = n_ * 8 + dc
                    S.dma(sgT[k_ * 128:(k_ + 1) * 128, c0:c0 + nb], so[:, :nb], r=[sot], w=[("sgT", c0)])
        S.emit()


def phase_C1b(nc, S, H, TO, xTo, ysel, hmask_d, sgT, Wb, Wout, x1T, xr=()):
    NB = 512
    with ExitStack() as st:
        A = Tiles(nc, st)
        S.barrier()
        wb = A.sb([128, 3, 8, 1024], BF16, "wb")
        wo = A.sb([128, 8, 1024], BF16, "wo")
        stg = [A.sb([128, 2048], F32, "stg") for _ in range(2)]
        xb = A.sb([128, 8, NB], F32, "xb")
        yb = [A.sb([128, 24, NB], BF16, "yb") for _ in range(2)]
        sgr = [A.sb([128, 3, NB], BF16, "sgr") for _ in range(4)]
        hmask = A.sb([128, 2], F32, "hmask")
        mg = [A.sb([128, NB], F32, "mg") for _ in range(2)]
        mgT = A.sb([128, 8, NB], BF16, "mgT")
        tmp = [A.sb([128, NB], F32, "tmp") for _ in range(2)]
        psb = [A.ps([128, 512], F32, "psb") for _ in range(4)]
        pso = [A.ps([128, 512], F32, "pso") for _ in range(2)]
        S.dma(hmask[:, :], hmask_d, w=["hmask"])
        k = 0
        for n_ in range(3):
            k = load_w_generic(S, lambda c0, n, n_=n_: wb[:, n_, :, c0:c0 + n], Wb[n_].rearrange("(c p) n -> p c n", p=128), 8, 1024, 256, stg, "stg", "wb", k)
        k = load_w_generic(S, lambda c0, n: wo[:, :, c0:c0 + n], Wout.rearrange("(c p) n -> p c n", p=128), 8, 1024, 256, stg, "stg", "wo", k)
        xv = xTo.rearrange("(c p) t -> p c t", p=128)
        yv = ysel.rearrange("(c p) t -> p c t", p=128)
        sv = sgT.rearrange("(n d p) t -> p d n t", n=3, p=128)
        ov = x1T.rearrange("(c p) t -> p c t", p=128)
        blks = blocks_of(H, TO, NB)

        def load_y(bi):
            c0_, nb_ = blks[bi]
            S.dma(yb[bi % 2][:, :, :nb_], yv[:, :, c0_:c0_ + nb_], r=[("ysel", c0_)], w=[("yb", bi % 2)])

        kk = 0
        ks = 0
        load_y(0)
        for bi, (c0, nb) in enumerate(blks):
            yt = yb[bi % 2]
            ytag = ("yb", bi % 2)
            if bi + 1 < len(blks):
                load_y(bi + 1)
            S.dma(xb[:, :, :nb], xv[:, :, c0:c0 + nb], r=list(xr), w=["xb"])
            if bi == 0 and H > 0:
                S.dve(lambda e, nb=nb: e.tensor_scalar(out=xb[:, :, :nb], in0=xb[:, :, :nb], scalar1=hmask[:, 1:2], scalar2=None, op0=ALU.mult),
                      r=["xb", "hmask"], w=["xb"])
            for dc in range(8):
                sg = sgr[ks % 4]
                sgtag = ("sgr", ks % 4)
                ks += 1
                S.dma(sg[:, :, :nb], sv[:, dc, :, c0:c0 + nb], r=[("sgT", c0)], w=[sgtag])
                mgd = mg[dc % 2]
                mgtag = ("mg", dc % 2)
                for n_ in range(3):
                    pbk = psb[kk % 4]
                    pbt = ("psb", kk % 4)
                    tm = tmp[kk % 2]
                    tmtag = ("tmp", kk % 2)
                    kk += 1
                    for c in range(8):
                        ych = (c // 4) * 12 + n_ * 4 + (c % 4)
                        S.pe(lambda e, c=c, pbk=pbk, n_=n_, dc=dc, nb=nb, ych=ych, yt=yt: e.matmul(pbk[:, :nb], lhsT=wb[:, n_, c, dc * 128:(dc + 1) * 128], rhs=yt[:, ych, :nb],
                                                                                                start=(c == 0), stop=(c == 7)), r=["wb", ytag], w=[pbt])
                    if n_ == 0:
                        S.dve(lambda e, pbk=pbk, nb=nb, sg=sg, mgd=mgd: e.tensor_tensor(out=mgd[:, :nb], in0=pbk[:, :nb], in1=sg[:, 0, :nb], op=ALU.mult),
                              r=[pbt, sgtag], w=[mgtag])
                    else:
                        S.dve(lambda e, pbk=pbk, tm=tm, nb=nb, sg=sg, n_=n_: e.tensor_tensor(out=tm[:, :nb], in0=pbk[:, :nb], in1=sg[:, n_, :nb], op=ALU.mult),
                              r=[pbt, sgtag], w=[tmtag])
                        if n_ == 1:
                            S.pool(lambda e, tm=tm, nb=nb, mgd=mgd: e.tensor_tensor(out=mgd[:, :nb], in0=mgd[:, :nb], in1=tm[:, :nb], op=ALU.add),
                                   r=[mgtag, tmtag], w=[mgtag])
                        else:
                            S.pool(lambda e, tm=tm, dc=dc, nb=nb, mgd=mgd: e.tensor_tensor(out=mgT[:, dc, :nb], in0=mgd[:, :nb], in1=tm[:, :nb], op=ALU.add),
                                   r=[mgtag, tmtag], w=[("mgT", dc)])
            for dc in range(8):
                po = pso[dc % 2]
                pot = ("pso", dc % 2)
                for c in range(8):
                    S.pe(lambda e, c=c, po=po, dc=dc, nb=nb: e.matmul(po[:, :nb], lhsT=wo[:, c, dc * 128:(dc + 1) * 128], rhs=mgT[:, c, :nb], start=(c == 0), stop=(c == 7)),
                         r=["wo"] + [("mgT", c_) for c_ in range(8)], w=[pot])
                S.dve(lambda e, po=po, dc=dc, nb=nb: e.tensor_tensor(out=xb[:, dc, :nb], in0=po[:, :nb], in1=xb[:, dc, :nb], op=ALU.add), r=[pot, "xb"], w=["xb"])
            S.dma(ov[:, :, c0:c0 + nb], xb[:, :, :nb], r=["xb"], w=[("x1T", c0)])
        S.emit()


def phase_C2a(nc, S, H, TO, x1T, Wup, cw_d, cb_d, gffn, actD):
    NB = 512
    with ExitStack() as st:
        A = Tiles(nc, st)
        S.barrier()
        T_ = {}
        wu = A.sb([128, 8, 5632], BF16, "wu")
        stg = [A.sb([128, 2048], F32, "stg") for _ in range(2)]
        xb = [A.sb([128, 8, NB], F32, "xb") for _ in range(2)]
        hT = A.sb([128, 8, NB], BF16, "hT")
        T_["xsq"] = A.sb([128, 8, NB], BF16, "xsq")
        T_["std"] = A.sb([128, NB], F32, "std")
        T_["rstd"] = A.sb([128, NB], F32, "rstd")
        T_["epsc"] = A.sb([128, 1], F32, "epsc")
        ones_bf = A.sb([128, 128], BF16, "ones")
        gcol = A.sb([128, 8], F32, "gcol")
        cw = A.sb([128, 44, 3], F32, "cw")
        cb = A.sb([128, 44], F32, "cb")
        halo = A.sb([128, 44, 2], F32, "halo")
        ua = [A.sb([128, NB + 2], F32, "ua") for _ in range(2)]
        ug = [A.sb([128, NB + 2], F32, "ug") for _ in range(2)]
        aa = [A.sb([128, NB], F32, "aa") for _ in range(2)]
        ag = [A.sb([128, NB], F32, "ag") for _ in range(2)]
        acto = [A.sb([128, NB], BF16, "acto") for _ in range(4)]
        ps_ss = A.ps([128, 512], F32, "ps_ss")
        psa = [A.ps([128, 512], F32, "psa") for _ in range(3)]
        psgt = [A.ps([128, 512], F32, "psgt") for _ in range(3)]
        S.pool(lambda e: e.memset(ones_bf[:, :], 1.0), w=["ones"])
        S.pool(lambda e: e.memset(T_["epsc"][:, :], EPS), w=["epsc"])
        S.dve(lambda e: e.memset(halo[:, :, :], 0.0), w=["halo"])
        S.dma(gcol[:, :], gffn, w=["gcol"])
        S.dma(cw[:, :, :], cw_d, w=["cw"])
        S.dma(cb[:, :], cb_d, w=["cb"])
        load_w_generic(S, lambda c0, n: wu[:, :, c0:c0 + n], Wup.rearrange("(c p) n -> p c n", p=128), 8, 5632, 256, stg, "stg", "wu")
        xv = x1T.rearrange("(c p) t -> p c t", p=128)
        kk = 0
        blks = blocks_of(H, TO, NB)

        def load_x(bi):
            c0_, nb_ = blks[bi]
            S.dma(xb[bi % 2][:, :, :nb_], xv[:, :, c0_:c0_ + nb_], r=[("x1T", c0_)], w=[("xb", bi % 2)])

        load_x(0)
        for bi, (c0, nb) in enumerate(blks):
            xt = xb[bi % 2]
            xtag = ("xb", bi % 2)
            if bi + 1 < len(blks):
                load_x(bi + 1)
            rms_block(S, T_, ones_bf, xt, gcol, hT, ps_ss, xtag, nb)
            htag = ("hT", id(hT))
            for fc in range(22):
                b2 = kk % 2
                b3 = kk % 3
                b4 = kk % 4
                kk += 1
                for (ps, pst, off, ut, utag, acc, atag, hc) in ((psa[b3], ("psa", b3), 0, ua[b2], ("ua", b2), aa[b2], ("aa", b2), fc),
                                                                (psgt[b3], ("psgt", b3), 2816, ug[b2], ("ug", b2), ag[b2], ("ag", b2), 22 + fc)):
                    for c in range(8):
                        S.pe(lambda e, c=c, ps=ps, off=off, fc=fc, nb=nb: e.matmul(ps[:, :nb], lhsT=wu[:, c, off + fc * 128:off + (fc + 1) * 128], rhs=hT[:, c, :nb],
                                                                                 start=(c == 0), stop=(c == 7)), r=["wu", htag], w=[pst])
                    S.act(lambda e, ut=ut, hc=hc: e.copy(out=ut[:, 0:2], in_=halo[:, hc, :]), r=[("halo", hc)], w=[utag])
                    S.act(lambda e, ut=ut, ps=ps, nb=nb: e.copy(out=ut[:, 2:2 + nb], in_=ps[:, :nb]), r=[pst], w=[utag])
                    S.act(lambda e, ut=ut, hc=hc, nb=nb: e.copy(out=halo[:, hc, :], in_=ut[:, nb:nb + 2]), r=[utag], w=[("halo", hc)])
                    S.dve(lambda e, ut=ut, acc=acc, hc=hc, nb=nb: e.tensor_scalar(out=acc[:, :nb], in0=ut[:, 0:nb], scalar1=cw[:, hc, 0:1], scalar2=cb[:, hc:hc + 1],
                                                                                op0=ALU.mult, op1=ALU.add), r=[utag, "cw", "cb"], w=[atag])
                    for j in (1, 2):
                        S.dve(lambda e, ut=ut, acc=acc, hc=hc, nb=nb, j=j: e.scalar_tensor_tensor(out=acc[:, :nb], in0=ut[:, j:j + nb], scalar=cw[:, hc, j:j + 1], in1=acc[:, :nb],
                                                                                                  op0=ALU.mult, op1=ALU.add), r=[utag, "cw", atag], w=[atag])
                S.act(lambda e, b2=b2, nb=nb: e.activation(out=ag[b2][:, :nb], in_=ag[b2][:, :nb], func=AF.Silu), r=[("ag", b2)], w=[("ag", b2)])
                S.pool(lambda e, b2=b2, b4=b4, nb=nb: e.tensor_tensor(out=acto[b4][:, :nb], in0=aa[b2][:, :nb], in1=ag[b2][:, :nb], op=ALU.mult),
                       r=[("aa", b2), ("ag", b2)], w=[("acto", b4)])
                S.dma(actD[fc * 128:(fc + 1) * 128, c0:c0 + nb], acto[b4][:, :nb], r=[("acto", b4)], w=[("actD", c0)])
        S.emit()


def phase_C2b(nc, S, H, TO, x1T, actD, Wdown, x2T, final_g=None, outT=None, xsend=None):
    NB = 512
    with ExitStack() as st:
        A = Tiles(nc, st)
        S.barrier()
        T_ = {}
        wd = A.sb([128, 22, 1024], BF16, "wd")
        stg = [A.sb([128, 2048], F32, "stg") for _ in range(2)]
        xb = [A.sb([128, 8, NB], F32, "xb") for _ in range(2)]
        actb = [A.sb([128, 22, NB], BF16, "actb") for _ in range(2)]
        T_["xsq"] = A.sb([128, 8, NB], BF16, "xsq")
        T_["std"] = A.sb([128, NB], F32, "std")
        T_["rstd"] = A.sb([128, NB], F32, "rstd")
        T_["epsc"] = A.sb([128, 1], F32, "epsc")
        ones_bf = A.sb([128, 128], BF16, "ones")
        gfin = A.sb([128, 8], F32, "gfin")
        oT = A.sb([128, 8, NB], F32, "oT")
        ps_ss = A.ps([128, 512], F32, "ps_ss")
        psd = [A.ps([128, 512], F32, "psd") for _ in range(4)]
        S.pool(lambda e: e.memset(ones_bf[:, :], 1.0), w=["ones"])
        S.pool(lambda e: e.memset(T_["epsc"][:, :], EPS), w=["epsc"])
        if final_g is not None:
            S.dma(gfin[:, :], final_g, w=["gfin"])
        load_w_generic(S, lambda c0, n: wd[:, :, c0:c0 + n], Wdown.rearrange("(c p) n -> p c n", p=128), 22, 1024, 64, stg, "stg", "wd")
        xv = x1T.rearrange("(c p) t -> p c t", p=128)
        av = actD.rearrange("(c p) t -> p c t", p=128)
        ov = x2T.rearrange("(c p) t -> p c t", p=128)
        kk = 0
        blks = blocks_of(H, TO, NB)

        def load_xa(bi):
            c0_, nb_ = blks[bi]
            S.dma(xb[bi % 2][:, :, :nb_], xv[:, :, c0_:c0_ + nb_], r=[("x1T", c0_)], w=[("xb", bi % 2)])
            S.dma(actb[bi % 2][:, :, :nb_], av[:, :, c0_:c0_ + nb_], r=[("actD", c0_)], w=[("actb", bi % 2)])

        load_xa(0)
        for bi, (c0, nb) in enumerate(blks):
            xt = xb[bi % 2]
            xtag = ("xb", bi % 2)
            at = actb[bi % 2]
            attag = ("actb", bi % 2)
            if bi + 1 < len(blks):
                load_xa(bi + 1)
            for dc in range(8):
                po = psd[kk % 4]
                pot = ("psd", kk % 4)
                kk += 1
                for fc in range(22):
                    S.pe(lambda e, fc=fc, po=po, dc=dc, nb=nb, at=at: e.matmul(po[:, :nb], lhsT=wd[:, fc, dc * 128:(dc + 1) * 128], rhs=at[:, fc, :nb], start=(fc == 0), stop=(fc == 21)),
                         r=["wd", attag], w=[pot])
                S.dve(lambda e, po=po, dc=dc, nb=nb, xt=xt: e.tensor_tensor(out=xt[:, dc, :nb], in0=po[:, :nb], in1=xt[:, dc, :nb], op=ALU.add), r=[pot, xtag], w=[xtag])
            if final_g is None:
                S.dma(ov[:, :, c0:c0 + nb], xt[:, :, :nb], r=[xtag], w=[("x2T", c0)])
                if xsend is not None and not (bi == 0 and H > 0):
                    S.dma(xsend.rearrange("(c p) t -> p c t", p=128)[:, :, c0 - H:c0 - H + nb], xt[:, :, :nb], r=[xtag], w=[("xsend", c0)])
            elif not (bi == 0 and H > 0):
                xsq, std, rstd = T_["xsq"], T_["std"], T_["rstd"]
                S.act(lambda e, xt=xt, nb=nb: e.activation(out=xsq[:, :, :nb], in_=xt[:, :, :nb], func=AF.Square), r=[xtag], w=["xsq"])
                for c in range(8):
                    S.pe(lambda e, c=c, nb=nb: e.matmul(ps_ss[:, :nb], lhsT=ones_bf[:, :], rhs=xsq[:, c, :nb], start=(c == 0), stop=(c == 7)), r=["xsq", "ones"], w=["ps_ss"])
                S.act(lambda e, nb=nb: e.activation(out=std[:, :nb], in_=ps_ss[:, :nb], func=AF.Sqrt, bias=T_["epsc"][:, 0:1], scale=1.0 / D), r=["ps_ss", "epsc"], w=["std"])
                S.dve(lambda e, nb=nb: e.reciprocal(out=rstd[:, :nb], in_=std[:, :nb]), r=["std"], w=["rstd"])
                for c in range(8):
                    S.dve(lambda e, c=c, nb=nb, xt=xt: e.scalar_tensor_tensor(out=oT[:, c, :nb], in0=xt[:, c, :nb], scalar=gfin[:, c:c + 1], in1=rstd[:, :nb], op0=ALU.mult, op1=ALU.mult),
                          r=[xtag, "rstd", "gfin"], w=["oT"])
                S.dma(outT.rearrange("(c p) t -> p c t", p=128)[:, :, c0 - H:c0 - H + nb], oT[:, :, :nb], r=["oT"], w=[("outT", c0)])
        S.emit()
```

```python
import numpy as np
from contextlib import ExitStack
import concourse.bass as bass
import concourse.mybir as mybir
from concourse.bass_utils import run_bass_kernel_spmd

F32 = mybir.dt.float32
BF16 = mybir.dt.bfloat16
AF = mybir.ActivationFunctionType
ALU = mybir.AluOpType
AX = mybir.AxisListType

COMPUTE = ("pe", "act", "dve", "pool")
EPOCH = 4096
NEPS = 3
NDMASEM = 20
NCSEM = 56
SAME_ENGINE_SYNC = True


def _is_ps(t):
    t0 = t[0] if isinstance(t, tuple) else t
    return isinstance(t0, str) and (t0[:2] in ("ps", "pb", "po", "pu", "pg") or t0 == "ptr")


class Op:
    __slots__ = ("eng", "fn", "deps", "flag", "sem", "target", "dma", "idx", "pre", "cc")

    def __init__(self, eng, fn, dma):
        self.eng = eng
        self.fn = fn
        self.dma = dma
        self.deps = []
        self.flag = False
        self.sem = None
        self.target = 0
        self.idx = 0
        self.pre = None
        self.cc = False


class Sched:
    def __init__(self, nc, stack):
        self.nc = nc
        self.ops = {e: [] for e in COMPUTE + ("sp",)}
        self.lastw = {}
        self.ps_true_w = {}
        self.readers = {}
        self.nops = {e: 0 for e in COMPUTE + ("sp",)}
        self.flagcnt = {e: 0 for e in COMPUTE}
        self.esem = {e: [stack.enter_context(nc.semaphore(f"s_{e}{i}")) for i in range(NEPS)] for e in COMPUTE}
        self.dsem = [stack.enter_context(nc.semaphore(f"s_dma{i}")) for i in range(NDMASEM)]
        self.ndma = 0
        self.csem = [stack.enter_context(nc.semaphore(f"s_cc{i}")) for i in range(NCSEM)]
        self.ncoll = 0
        self.dma_hist = []
        self.barrier_pending = {}
        self.last_op = {}
        self.waited = {e: {} for e in COMPUTE + ("sp",)}
        self.all_dma = []
        self.dma_barriered = 0

    def op(self, eng, fn, r=(), w=(), dma=False):
        ps_r = [t for t in r if _is_ps(t)]
        if ps_r:
            w = list(w) + [t for t in ps_r if t not in w]
        o = Op(eng, fn, dma)
        o.idx = self.nops[eng]
        self.nops[eng] += 1
        deps = {}

        def add(d, hazard):
            if d is None or d is o:
                return
            if d.dma:
                deps[id(d)] = d
            else:
                if d.eng == eng and not dma and (eng == "pe" or not SAME_ENGINE_SYNC or not hazard):
                    return
                k = ("e", d.eng)
                if k not in deps or deps[k].idx < d.idx:
                    deps[k] = d

        for t in r:
            if t in ps_r:
                add(self.ps_true_w.get(t), True)
            else:
                add(self.lastw.get(t), True)
        for t in w:
            hz = t not in ps_r
            add(self.lastw.get(t), hz)
            rd = self.readers.get(t)
            if rd:
                for d in rd.values():
                    add(d, hz)
            if hz and _is_ps(t):
                self.ps_true_w[t] = o
        bp = self.barrier_pending.pop(eng, None)
        if bp:
            for d in bp:
                add(d, True)
        for t in r:
            rd = self.readers.setdefault(t, {})
            if dma:
                rd[id(o)] = o
            else:
                rd[eng] = o
        for t in w:
            self.lastw[t] = o
            self.readers[t] = {}
        o.deps = list(deps.values())
        for d in o.deps:
            d.flag = True
        if dma:
            o.flag = True
            self.all_dma.append(o)
        self.ops[eng].append(o)
        self.last_op[eng] = o
        return o

    def pe(self, fn, r=(), w=()):
        return self.op("pe", fn, r, w)

    def act(self, fn, r=(), w=()):
        return self.op("act", fn, r, w)

    def dve(self, fn, r=(), w=()):
        return self.op("dve", fn, r, w)

    def pool(self, fn, r=(), w=()):
        return self.op("pool", fn, r, w)

    def dma(self, out, in_, r=(), w=(), eng="sp"):
        return self.op(eng, lambda e: e.dma_start(out=out, in_=in_), r, w, dma=True)

    def collective(self, kind, ins, outs, groups, r=(), w=()):
        o = self.op("pool", lambda e: e.collective_compute(kind, ALU.bypass, replica_groups=groups, ins=ins, outs=outs), r, w, dma=True)
        o.sem = self.csem[self.ncoll]
        self.ncoll += 1
        o.target = 1
        o.cc = True
        return o

    def barrier(self):
        b = [o for o in self.last_op.values() if not o.dma]
        b += [o for o in self.all_dma[self.dma_barriered:] if not o.cc]
        self.dma_barriered = len(self.all_dma)
        for o in b:
            o.flag = True
        self.barrier_pending = {e: b for e in COMPUTE + ("sp",)}

    def emit(self):
        nc = self.nc
        for e in COMPUTE:
            for o in self.ops[e]:
                if o.flag and o.sem is None:
                    c = self.flagcnt[e]
                    self.flagcnt[e] += 1
                    ep = c // EPOCH
                    o.sem = self.esem[e][ep % NEPS]
                    o.target = (ep // NEPS) * EPOCH + (c % EPOCH) + 1
        for e in COMPUTE + ("sp",):
            for o in self.ops[e]:
                if o.dma and o.sem is None:
                    n = self.ndma
                    self.ndma += 1
                    o.sem = self.dsem[n % NDMASEM]
                    o.target = 16 * (n // NDMASEM + 1)
                    if n >= NDMASEM:
                        o.pre = (o.sem, 16 * (n // NDMASEM))
        with nc.Block() as block:
            @block.tensor
            def _(eng):
                self._emit_eng("pe", eng)

            @block.scalar
            def _(eng):
                self._emit_eng("act", eng)

            @block.vector
            def _(eng):
                self._emit_eng("dve", eng)

            @block.gpsimd
            def _(eng):
                self._emit_eng("pool", eng)

            @block.sync
            def _(eng):
                self._emit_eng("sp", eng)
        for e in self.ops:
            self.ops[e] = []

    def _emit_eng(self, name, eng):
        waited = self.waited[name]
        for o in self.ops[name]:
            for d in o.deps:
                key = id(d.sem)
                if waited.get(key, 0) < d.target:
                    eng.wait_ge(d.sem, d.target)
                    waited[key] = d.target
            if o.pre is not None:
                key = id(o.pre[0])
                if waited.get(key, 0) < o.pre[1]:
                    eng.wait_ge(o.pre[0], o.pre[1])
                    waited[key] = o.pre[1]
            inst = o.fn(eng)
            if o.flag:
                inst.then_inc(o.sem, 16 if (o.dma and not o.cc) else 1)

    def finish(self, eng="sp"):
        self.barrier()
        self.op(eng, lambda e: e.nop(), r=(), w=())


class Tiles:
    CNT = [0]

    def __init__(self, nc, stack):
        self.nc = nc
        self.stack = stack

    def sb(self, shape, dtype, name=None):
        Tiles.CNT[0] += 1
        return self.stack.enter_context(self.nc.sbuf_tensor(f"{name or 't'}_{Tiles.CNT[0]}", list(shape), dtype))

    def ps(self, shape, dtype=F32, name=None):
        Tiles.CNT[0] += 1
        return self.stack.enter_context(self.nc.psum_tensor(f"{name or 'p'}_{Tiles.CNT[0]}", list(shape), dtype))


D = 1024
NFM = 3088
NTM = 2308
NA = NFM + NTM
EPS = 1e-6
FM_GQ, FM_GK, FM_MX, FM_FQ, FM_FK, FM_OG, FM_LR = 0, 256, 512, 1536, 2048, 2560, 3072
TM_GK, TM_GV, TM_GR, TM_MZ, TM_FV, TM_FF = 0, 256, 768, 1280, 1792, 2304


def cdiv(a, b):
    return (a + b - 1) // b


def rms_block(S, T_, ones_bf, xt, gcol, hT, ps_ss, tag, nb):
    xsq, std, rstd = T_["xsq"], T_["std"], T_["rstd"]
    S.act(lambda e: e.activation(out=xsq[:, :, :nb], in_=xt[:, :, :nb], func=AF.Square), r=[tag], w=["xsq"])
    for c in range(8):
        S.pe(lambda e, c=c: e.matmul(ps_ss[:, :nb], lhsT=ones_bf[:, :], rhs=xsq[:, c, :nb], start=(c == 0), stop=(c == 7)),
             r=["xsq", "ones"], w=["ps_ss"])
    S.act(lambda e: e.activation(out=std[:, :nb], in_=ps_ss[:, :nb], func=AF.Sqrt, bias=T_["epsc"][:, 0:1], scale=1.0 / D),
          r=["ps_ss", "epsc"], w=["std"])
    S.dve(lambda e: e.reciprocal(out=rstd[:, :nb], in_=std[:, :nb]), r=["std"], w=["rstd"])
    for c in range(8):
        S.dve(lambda e, c=c: e.scalar_tensor_tensor(out=hT[:, c, :nb], in0=xt[:, c, :nb], scalar=gcol[:, c:c + 1],
                                                     in1=rstd[:, :nb], op0=ALU.mult, op1=ALU.mult),
              r=[tag, "rstd", "gcol"], w=[("hT", id(hT))])


def load_weights_bf16(S, T_, wsb, wdram, ncols, wtag, stg, stgtag):
    wv = wdram.rearrange("(c p) n -> p c n", p=128)
    G = 512
    for gi in range(cdiv(ncols, G)):
        c0 = gi * G
        n = min(G, ncols - c0)
        st = stg[gi % 2]
        tg = (stgtag, gi % 2)
        S.dma(st[:, :, :n], wv[:, :, c0:c0 + n], r=[], w=[tg])
        eng = [S.pool, S.dve][gi % 2]
        eng(lambda e, st=st, c0=c0, n=n: e.tensor_copy(out=wsb[:, :, c0:c0 + n], in_=st[:, :, :n]), r=[tg], w=[wtag])


def phase_A(nc, S, T, xT, wA, gmix, pFM, pTM, xblk=None, xr=()):
    NB = 512
    with ExitStack() as st:
        A = Tiles(nc, st)
        T_ = {}
        wsb = A.sb([128, 8, NA], BF16, "wsb")
        stg = [A.sb([128, 8, 512], F32, "wstg") for _ in range(2)]
        xb = [A.sb([128, 8, NB], F32, "xb") for _ in range(2)]
        hTs = [A.sb([128, 8, NB], BF16, "hT") for _ in range(2)]
        T_["xsq"] = A.sb([128, 8, NB], BF16, "xsq")
        T_["std"] = A.sb([128, NB], F32, "std")
        T_["rstd"] = A.sb([128, NB], F32, "rstd")
        T_["epsc"] = A.sb([128, 1], F32, "epsc")
        ones_bf = A.sb([128, 128], BF16, "ones")
        gcol = A.sb([128, 8], F32, "gcol")
        fmst = [A.sb([128, NB], BF16, "fmst") for _ in range(4)]
        tmst = [A.sb([128, NTM], BF16, "tmst") for _ in range(2)]
        ps_ss = A.ps([128, NB], F32, "ps_ss")
        psr = [A.ps([128, NB], F32, "psr") for _ in range(4)]
        S.barrier()
        S.pool(lambda e: e.memset(ones_bf[:, :], 1.0), w=["ones"])
        S.pool(lambda e: e.memset(T_["epsc"][:, :], EPS), w=["epsc"])
        S.dma(gcol[:, :], gmix, w=["gcol"])
        load_weights_bf16(S, T_, wsb, wA, NA, "wsb", stg, "wstg")
        if xblk is None:
            xv = xT.rearrange("(c p) t -> p c t", p=128)
            xblk = lambda j: xv[:, :, j * NB:(j + 1) * NB]
        k = 0
        xrf = xr if callable(xr) else (lambda j_: list(xr))
        S.dma(xb[0][:, :, :], xblk(0), r=xrf(0), w=[("xb", 0)])
        for j in range(T // NB):
            xt = xb[j % 2]
            xtag = ("xb", j % 2)
            hT = hTs[j % 2]
            htag = ("hT", id(hT))
            if j + 1 < T // NB:
                S.dma(xb[(j + 1) % 2][:, :, :], xblk(j + 1), r=xrf(j + 1), w=[("xb", (j + 1) % 2)])
            rms_block(S, T_, ones_bf, xt, gcol, hT, ps_ss, xtag, NB)
            for m in range(cdiv(NFM, 128)):
                mm = min(128, NFM - m * 128)
                ps = psr[k % 4]
                ptag = ("psr", k % 4)
                so = fmst[k % 4]
                stag = ("fmst", k % 4)
                for c in range(8):
                    S.pe(lambda e, c=c, ps=ps, m=m, mm=mm, hT=hT: e.matmul(ps[:mm, :], lhsT=wsb[:, c, m * 128:m * 128 + mm], rhs=hT[:, c, :],
                                                                        start=(c == 0), stop=(c == 7)), r=["wsb", htag], w=[ptag])
                if k % 2 == 0:
                    S.act(lambda e, ps=ps, so=so, mm=mm: e.copy(out=so[:mm, :], in_=ps[:mm, :]), r=[ptag], w=[stag])
                else:
                    S.dve(lambda e, ps=ps, so=so, mm=mm: e.tensor_copy(out=so[:mm, :], in_=ps[:mm, :]), r=[ptag], w=[stag])
                S.dma(pFM[m * 128:m * 128 + mm, j * NB:(j + 1) * NB], so[:mm, :], r=[stag], w=[("pFM", j)])
                k += 1
            for tt in range(NB // 128):
                ti = j * (NB // 128) + tt
                so = tmst[ti % 2]
                stag = ("tmst", ti % 2)
                for n in range(cdiv(NTM, 512)):
                    nn = min(512, NTM - n * 512)
                    ps = psr[k % 4]
                    ptag = ("psr", k % 4)
                    for c in range(8):
                        S.pe(lambda e, c=c, ps=ps, n=n, nn=nn, hT=hT, tt=tt: e.matmul(ps[:, :nn], lhsT=hT[:, c, tt * 128:(tt + 1) * 128],
                                                                                  rhs=wsb[:, c, NFM + n * 512:NFM + n * 512 + nn],
                                                                                  start=(c == 0), stop=(c == 7)), r=["wsb", htag], w=[ptag])
                    if k % 2 == 0:
                        S.act(lambda e, ps=ps, so=so, n=n, nn=nn: e.copy(out=so[:, n * 512:n * 512 + nn], in_=ps[:, :nn]), r=[ptag], w=[stag])
                    else:
                        S.dve(lambda e, ps=ps, so=so, n=n, nn=nn: e.tensor_copy(out=so[:, n * 512:n * 512 + nn], in_=ps[:, :nn]), r=[ptag], w=[stag])
                    k += 1
                S.dma(pTM[ti * 128:(ti + 1) * 128, :], so[:, :], r=[stag], w=[("pTM", ti)])
        S.emit()


class YDst:
    def __init__(self, T, yT=None, ys=None, H=0, key="y"):
        self.T, self.yT, self.ys, self.H, self.key = T, yT, ys, H, key
        self.tokens = []

    def put(self, S, row0, rows_ap_fn, j, tile_fn, rtags):
        NBK = self.T // 512
        half = NBK // 2
        tok = (self.key, row0, j)
        self.tokens.append(tok)
        if self.ys is None:
            S.dma(rows_ap_fn(self.yT, slice(512 * j, 512 * (j + 1))), tile_fn(slice(0, 512)), r=rtags, w=[tok])
            return
        H = self.H
        if j < half:
            S.dma(rows_ap_fn(self.ys[0], slice(H + 512 * j, H + 512 * (j + 1))), tile_fn(slice(0, 512)), r=rtags, w=[tok])
        else:
            S.dma(rows_ap_fn(self.ys[1], slice(H + 512 * (j - half), H + 512 * (j - half + 1))), tile_fn(slice(0, 512)), r=rtags, w=[tok])
        if j == half - 1:
            tok2 = (self.key, row0, "halo")
            self.tokens.append(tok2)
            S.dma(rows_ap_fn(self.ys[1], slice(0, H)), tile_fn(slice(512 - H, 512)), r=rtags, w=[tok2])

    def zero_halo(self, S, A):
        if self.ys is None:
            return
        z = A.sb([128, 12, self.H], BF16, "zhalo")
        S.dve(lambda e: e.memset(z[:, :, :], 0.0), w=["zhalo"])
        tok = (self.key, "zero")
        self.tokens.append(tok)
        S.dma(self.ys[0][:, 0:self.H].rearrange("(a p) t -> p a t", p=128), z[:, :, :], r=["zhalo"], w=[tok])


def make_masks(S, A):
    nc = A.nc
    C = {}
    C["ones_bf"] = A.sb([128, 128], BF16, "ones_bf")
    C["ones_f"] = A.sb([128, 128], F32, "ones_f")
    C["tri_f"] = A.sb([128, 128], F32, "tri_f")
    C["tri_bf"] = A.sb([128, 128], BF16, "tri_bf")
    C["ident_bf"] = A.sb([128, 128], BF16, "ident_bf")
    C["ident_f"] = A.sb([128, 128], F32, "ident_f")
    S.pool(lambda e: e.memset(C["ones_bf"][:, :], 1.0), w=["c_ones_bf"])
    S.pool(lambda e: e.memset(C["ones_f"][:, :], 1.0), w=["c_ones_f"])
    S.pool(lambda e: e.affine_select(out=C["tri_f"][:, :], in_=C["ones_f"][:, :], pattern=[[1, 128]],
                                     compare_op=ALU.is_ge, fill=0.0, base=0, channel_multiplier=-1),
           r=["c_ones_f"], w=["c_tri_f"])
    S.pool(lambda e: e.tensor_copy(out=C["tri_bf"][:, :], in_=C["tri_f"][:, :]), r=["c_tri_f"], w=["c_tri_bf"])
    S.pool(lambda e: e.affine_select(out=C["ident_f"][:, :], in_=C["ones_f"][:, :], pattern=[[1, 128]],
                                     compare_op=ALU.is_equal, fill=0.0, base=0, channel_multiplier=-1),
           r=["c_ones_f"], w=["c_ident_f"])
    S.pool(lambda e: e.tensor_copy(out=C["ident_bf"][:, :], in_=C["ident_f"][:, :]), r=["c_ident_f"], w=["c_ident_bf"])
    return C


def phase_fox(nc, S, T, pFM, pTM, foxbf_t, yT, zero_halo=False, after_head=None):
    NT = T // 128
    NQ = T // 512
    SCALE = 128 ** -0.5
    with ExitStack() as st:
        A = Tiles(nc, st)
        S.barrier()
        C = make_masks(S, A)
        if zero_halo:
            yT.zero_halo(S, A)
        ff = A.sb([128, NT, 4], BF16, "ff")
        bfb = A.sb([128, NT, 4], F32, "bfb")
        u = A.sb([128, NT, 4], F32, "u")
        sp = A.sb([128, NT, 4], F32, "sp")
        inc = A.sb([128, NT, 4], F32, "inc")
        zer = A.sb([128, NT], F32, "zer")
        Pk = A.sb([128, NT, 4], F32, "Pk")
        Bb = A.sb([128, T // 256, NT], F32, "Bb")
        kT = A.sb([128, T], BF16, "kT")
        Vall = A.sb([128, NT, 512], BF16, "Vall")
        qTb = [A.sb([128, 512], BF16, "qTb") for _ in range(2)]
        ogb = [A.sb([128, 512], BF16, "ogb") for _ in range(2)]
        PT = [A.sb([128, 512], BF16, "PT") for _ in range(3)]
        rl = A.sb([128, 512], F32, "rl")
        osb = A.sb([128, 512], F32, "osb")
        sg = A.sb([128, 512], F32, "sg")
        yb = [A.sb([128, 512], BF16, "yb") for _ in range(2)]
        ps_s = [A.ps([128, 512], F32, "ps_s") for _ in range(3)]
        ps_o = [A.ps([128, 512], F32, "ps_o") for _ in range(2)]
        ps_l = [A.ps([128, 512], F32, "ps_l") for _ in range(2)]
        ps_c = ps_s[0]
        ps_t = ps_s[1]
        ffv = pTM[:, TM_FF:TM_FF + 4].rearrange("(i p) h -> p i h", p=128)
        step = max(1, NT // 8)
        for i0 in range(0, NT, step):
            S.dma(ff[:, i0:i0 + step, :], ffv[:, i0:i0 + step, :], r=[("pTM", i) for i in range(i0, i0 + step)], w=["ff"])
        S.dma(bfb[:, :, :], foxbf_t, w=["bfb"])
        S.dve(lambda e: e.memset(zer[:, :], 0.0), w=["zer"])
        S.dve(lambda e: e.tensor_tensor(out=u[:, :, :], in0=ff[:, :, :], in1=bfb[:, :, :], op=ALU.add), r=["ff", "bfb"], w=["u"])
        S.act(lambda e: e.activation(out=u[:, :, :], in_=u[:, :, :], func=AF.Exp, scale=-1.0), r=["u"], w=["u"])
        S.act(lambda e: e.activation(out=sp[:, :, :], in_=u[:, :, :], func=AF.Ln, bias=1.0), r=["u"], w=["sp"])
        spf = sp[:, :, :].rearrange("p i h -> p (i h)")
        for n0 in range(0, NT * 4, 512):
            nn = min(512, NT * 4 - n0)
            S.pe(lambda e, n0=n0, nn=nn: e.matmul(ps_c[:, :nn], lhsT=C["tri_f"][:, :], rhs=spf[:, n0:n0 + nn], start=True, stop=True),
                 r=["sp", "c_tri_f"], w=["ps_s0"])
            S.pe(lambda e, n0=n0, nn=nn: e.matmul(ps_t[:, :nn], lhsT=C["ones_f"][:, :], rhs=spf[:, n0:n0 + nn], start=True, stop=True),
                 r=["sp", "c_ones_f"], w=["ps_s1"])
            Pf = Pk[:, :, :].rearrange("p i h -> p (i h)")
            If = inc[:, :, :].rearrange("p i h -> p (i h)")
            S.dve(lambda e, n0=n0, nn=nn, If=If: e.tensor_copy(out=If[:, n0:n0 + nn], in_=ps_t[:, :nn]), r=["ps_s1"], w=["inc"])
            S.dve(lambda e, n0=n0, nn=nn, Pf=Pf, If=If: e.tensor_tensor(out=Pf[:, n0:n0 + nn], in0=ps_c[:, :nn], in1=If[:, n0:n0 + nn], op=ALU.subtract),
                  r=["ps_s0", "inc"], w=["Pk"])
        for h in range(4):
            S.dve(lambda e, h=h: e.tensor_tensor_scan(out=inc[:, :, h], data0=inc[:, :, h], data1=zer[:, :], initial=0.0,
                                                       op0=ALU.add, op1=ALU.add), r=["inc", "zer"], w=["inc"])
        S.dve(lambda e: e.tensor_tensor(out=Pk[:, :, :], in0=Pk[:, :, :], in1=inc[:, :, :], op=ALU.add), r=["Pk", "inc"], w=["Pk"])
        vv = pTM[:, TM_FV:TM_FV + 512].rearrange("(i p) c -> p i c", p=128)
        for i0 in range(0, NT, step):
            S.dma(Vall[:, i0:i0 + step, :], vv[:, i0:i0 + step, :], r=[("pTM", i) for i in range(i0, i0 + step)], w=["Vall"])
        kq = 0
        for h in range(4):
            S.dma(kT[:, :], pFM[FM_FK + 128 * h:FM_FK + 128 * (h + 1), :], r=[("pFM", j) for j in range(NQ)], w=["kT"])
            for i2 in range(T // 256):
                nj = 2 * i2 + 2
                S.dve(lambda e, h=h, i2=i2, nj=nj: e.tensor_scalar(out=Bb[:, i2, :nj], in0=Pk[:, :nj, h], scalar1=inc[:, 2 * i2 + 1, h:h + 1],
                                                                   scalar2=None, op0=ALU.subtract), r=["Pk", "inc"], w=["Bb"])
            steps = []
            for I in range(NQ):
                nkt = 4 * I + 4
                for j in range(nkt):
                    steps.append((I, j, nkt))

            def emit_S(st_, kq_):
                I, j, nkt = st_
                qt = qTb[I % 2]
                qtag = ("qTb", I % 2)

                def load_q(I_):
                    S.dma(qTb[I_ % 2][:, :], pFM[FM_FQ + 128 * h:FM_FQ + 128 * (h + 1), 512 * I_:512 * (I_ + 1)], r=[("pFM", I_)], w=[("qTb", I_ % 2)])
                    S.dma(ogb[I_ % 2][:, :], pFM[FM_OG + 128 * h:FM_OG + 128 * (h + 1), 512 * I_:512 * (I_ + 1)], r=[("pFM", I_)], w=[("ogb", I_ % 2)])

                if I == 0 and j == 0:
                    load_q(0)
                if j == 2 and I + 1 < NQ:
                    load_q(I + 1)
                q0 = max(0, 128 * (j - 4 * I))
                pss = ps_s[kq_ % 3]
                S.pe(lambda e, pss=pss, j=j, qt=qt, q0=q0: e.matmul(pss[:, q0:512], lhsT=kT[:, 128 * j:128 * (j + 1)], rhs=qt[:, q0:512],
                                                                   start=True, stop=True), r=["kT", qtag], w=["ps_s%d" % (kq_ % 3)])

            def emit_rest(st_, kq_):
                I, j, nkt = st_
                r_ = j - 4 * I
                q0 = max(0, 128 * r_)
                pss = ps_s[kq_ % 3]
                pstag = "ps_s%d" % (kq_ % 3)
                pt = PT[kq_ % 3]
                po = ps_o[I % 2]
                pl = ps_l[I % 2]
                potag = ("ps_o", I % 2)
                pltag = ("ps_l", I % 2)
                for sb in range(2):
                    c0 = max(q0, 256 * sb)
                    c1 = 256 * (sb + 1)
                    if c0 >= c1:
                        continue
                    S.act(lambda e, pt=pt, pss=pss, c0=c0, c1=c1, I=I, sb=sb, j=j: e.activation(
                        out=pt[:, c0:c1], in_=pss[:, c0:c1], func=AF.Exp, scale=SCALE, bias=Bb[:, 2 * I + sb, j:j + 1]),
                        r=[pstag, "Bb"], w=[("PT", kq_ % 3, sb)])
                pttags = [("PT", kq_ % 3, 0), ("PT", kq_ % 3, 1)]
                if r_ >= 0:
                    S.pool(lambda e, pt=pt, q0=q0: e.tensor_tensor(out=pt[:, q0:q0 + 128], in0=pt[:, q0:q0 + 128], in1=C["tri_bf"][:, :], op=ALU.mult),
                           r=pttags + ["c_tri_bf"], w=pttags)
                S.pe(lambda e, po=po, j=j, pt=pt, q0=q0, nkt=nkt, h=h: e.matmul(po[:, q0:512], lhsT=Vall[:, j, 128 * h:128 * (h + 1)], rhs=pt[:, q0:512],
                                                                        start=(j == 0), stop=(j == nkt - 1)), r=["Vall"] + pttags, w=[potag])
                S.pe(lambda e, pl=pl, j=j, pt=pt, q0=q0, nkt=nkt: e.matmul(pl[:, q0:512], lhsT=C["ones_bf"][:, :], rhs=pt[:, q0:512],
                                                                        start=(j == 0), stop=(j == nkt - 1)), r=["c_ones_bf"] + pttags, w=[pltag])
                if j == nkt - 1:
                    og = ogb[I % 2]
                    ogtag = ("ogb", I % 2)
                    y = yb[I % 2]
                    ytag = ("yb", I % 2)
                    S.dve(lambda e, pl=pl: e.reciprocal(out=rl[:, :], in_=pl[:, :]), r=[pltag], w=["rl"])
                    S.dve(lambda e, po=po: e.tensor_tensor(out=osb[:, :], in0=po[:, :], in1=rl[:, :], op=ALU.mult), r=[potag, "rl"], w=["osb"])
                    S.act(lambda e, og=og: e.activation(out=sg[:, :], in_=og[:, :], func=AF.Exp, scale=-1.0), r=[ogtag], w=["sg"])
                    S.dve(lambda e: e.tensor_scalar_add(out=sg[:, :], in0=sg[:, :], scalar1=1.0), r=["sg"], w=["sg"])
                    S.dve(lambda e: e.reciprocal(out=sg[:, :], in_=sg[:, :]), r=["sg"], w=["sg"])
                    S.pool(lambda e, y=y: e.tensor_tensor(out=y[:, :], in0=osb[:, :], in1=sg[:, :], op=ALU.mult), r=["osb", "sg"], w=[ytag])
                    yT.put(S, 1024 + 128 * h, lambda d, cs_, h=h: d[1024 + 128 * h:1024 + 128 * (h + 1), cs_], I, lambda cs_, y=y: y[:, cs_], [ytag])

            LOOK = 2
            n = len(steps)
            for i in range(min(LOOK, n)):
                emit_S(steps[i], kq + i)
            for i in range(n):
                if i + LOOK < n:
                    emit_S(steps[i + LOOK], kq + i + LOOK)
                emit_rest(steps[i], kq + i)
            kq += n
            if after_head is not None:
                after_head(h)
        S.emit()


def phase_gla(nc, S, T, pFM, pTM, wlr_aug, gnb_d, yT, zero_halo=False):
    NBK = T // 512
    NCH = T // 128
    with ExitStack() as st:
        A = Tiles(nc, st)
        S.barrier()
        C = make_masks(S, A)
        if zero_halo:
            yT.zero_halo(S, A)
        rt_f = A.sb([128, 128], F32, "rt_f")
        S.pool(lambda e: e.affine_select(out=rt_f[:, :], in_=C["ones_f"][:, :], pattern=[[-1, 128]], compare_op=ALU.is_ge,
                                         fill=0.0, base=-1, channel_multiplier=1), r=["c_ones_f"], w=["rt_f"])
        wl_f = A.sb([17, 256], F32, "wl_f")
        wl = A.sb([17, 256], BF16, "wl")
        gnb = A.sb([128, 512], F32, "gnb")
        epsc = A.sb([128, 1], F32, "epsc")
        S.pool(lambda e: e.memset(epsc[:, :], EPS), w=["epsc"])
        S.dma(wl_f[:, :], wlr_aug, w=["wl_f"])
        S.dve(lambda e: e.tensor_copy(out=wl[:, :], in_=wl_f[:, :]), r=["wl_f"], w=["wl"])
        S.dma(gnb[:, :], gnb_d, w=["gnb"])
        laug = [A.sb([17, 512], BF16, "laug") for _ in range(2)]
        qkb = [A.sb([128, 4, 512], BF16, "qkb") for _ in range(2)]
        tmb = [A.sb([128, 4, 1280], BF16, "tmb") for _ in range(2)]
        for i in range(2):
            S.pool(lambda e, i=i: e.memset(laug[i][:, :], 1.0), w=[("laug", i)])
        e_sb = A.sb([128, 256], F32, "e_sb")
        esr = A.sb([128, 512], F32, "esr")
        sp_sb = A.sb([128, 256], F32, "sp_sb")
        ek = A.sb([128, 256], F32, "ek")
        ekk = A.sb([128, 2, 128], F32, "ekk")
        kin = A.sb([128, 2, 128], BF16, "kin")
        kst = [A.sb([128, 256], BF16, "kst") for _ in range(2)]
        eq = [A.sb([128, 2, 128], F32, "eq") for _ in range(2)]
        qin = [A.sb([128, 2, 128], BF16, "qin") for _ in range(2)]
        att = [A.sb([128, 2, 128], BF16, "att") for _ in range(2)]
        silr = [A.sb([128, 512], F32, "silr") for _ in range(2)]
        St = A.sb([128, 2, 256], F32, "St")
        Sb = A.sb([128, 2, 256], BF16, "Sb")
        junk = A.sb([128, 256], F32, "junk")
        ssum = A.sb([128, 2], F32, "ssum")
        rstd = A.sb([128, 2], F32, "rstd")
        t1 = A.sb([128, 256], F32, "t1")
        ysb = A.sb([128, 2, 256], BF16, "ysb")
        yTs = [A.sb([128, 4, 512], BF16, "yTs") for _ in range(2)]
        pb = [A.ps([128, 512], F32, "pb") for _ in range(7)]
        ptr = A.ps([128, 1024], BF16, "ptr")
        S.dve(lambda e: e.memset(St[:, :, :], 0.0), w=["St"])
        S.dve(lambda e: e.memset(Sb[:, :, :], 0.0), w=["Sb"])
        tmv = pTM[:, TM_GK:TM_GK + 1280].rearrange("(i p) c -> p i c", p=128)

        def load_block(j):
            b2 = j % 2
            bs = slice(512 * j, 512 * (j + 1))
            S.dma(laug[b2][0:16, :], pFM[FM_LR:FM_LR + 16, bs], r=[("pFM", j)], w=[("laug", b2)])
            S.dma(qkb[b2][:, 0:2, :], pFM[FM_GQ:FM_GQ + 256, bs].rearrange("(h p) t -> p h t", p=128), r=[("pFM", j)], w=[("qkb", b2)])
            S.dma(qkb[b2][:, 2:4, :], pFM[FM_GK:FM_GK + 256, bs].rearrange("(h p) t -> p h t", p=128), r=[("pFM", j)], w=[("qkb", b2)])
            S.dma(tmb[b2][:, :, :], tmv[:, 4 * j:4 * j + 4, :], r=[("pTM", 4 * j + i) for i in range(4)], w=[("tmb", b2)])

        def P(n):
            j, ch = n // 4, n % 4
            b2 = j % 2
            p2 = n % 2
            if ch == 1 and j + 1 < NBK:
                load_block(j + 1)
            cs = slice(128 * ch, 128 * (ch + 1))
            kt = tmb[b2][:, ch, 0:256]
            rt = tmb[b2][:, ch, 768:1280]
            S.pe(lambda e: e.matmul(pb[0][:, 0:256], lhsT=laug[b2][0:17, cs], rhs=wl[0:17, :], start=True, stop=True),
                 r=[("laug", b2), "wl"], w=["pg0"])
            yield
            S.act(lambda e: e.activation(out=e_sb[:, :], in_=pb[0][:, 0:256], func=AF.Exp, scale=-1.0), r=["pg0"], w=["e_sb"])
            yield
            S.act(lambda e: e.activation(out=sp_sb[:, :], in_=e_sb[:, :], func=AF.Ln, bias=1.0), r=["e_sb"], w=["sp_sb"])
            yield
            S.pe(lambda e: e.matmul(pb[0][:, 256:512], lhsT=rt_f[:, :], rhs=sp_sb[:, :], start=True, stop=True),
                 r=["rt_f", "sp_sb"], w=["pg0"])
            for h in range(2):
                S.pe(lambda e, h=h: e.matmul(pb[1][:, 128 * h:128 * (h + 1)], lhsT=sp_sb[:, 128 * h:128 * (h + 1)], rhs=C["tri_f"][:, :],
                                            start=True, stop=True), r=["sp_sb", "c_tri_f"], w=["pg1"])
            yield
            S.act(lambda e: e.activation(out=ek[:, :], in_=pb[0][:, 256:512], func=AF.Exp, scale=-1.0 / 16), r=["pg0"], w=["ek"])
            yield
            S.act(lambda e: e.activation(out=eq[p2][:, :, :].rearrange("p a b -> p (a b)"), in_=pb[1][:, 0:256], func=AF.Exp, scale=-1.0 / 16),
                  r=["pg1"], w=[("eq", p2)])
            yield
            S.act(lambda e: e.activation(out=ekk[:, :, :].rearrange("p a b -> p (a b)"), in_=pb[1][:, 0:256], func=AF.Exp, scale=1.0 / 16),
                  r=["pg1"], w=["ekk"])
            S.dve(lambda e: e.tensor_tensor(out=kst[p2][:, :], in0=kt, in1=ek[:, :], op=ALU.mult), r=[("tmb", b2), "ek"], w=[("kst", p2)])
            yield
            S.dve(lambda e: e.scalar_tensor_tensor(out=qin[p2][:, :, :], in0=qkb[b2][:, 0:2, cs], scalar=128 ** -0.5, in1=eq[p2][:, :, :],
                                                    op0=ALU.mult, op1=ALU.mult), r=[("qkb", b2), ("eq", p2)], w=[("qin", p2)])
            yield
            S.dve(lambda e: e.tensor_tensor(out=kin[:, :, :], in0=qkb[b2][:, 2:4, cs], in1=ekk[:, :, :], op=ALU.mult),
                  r=[("qkb", b2), "ekk"], w=["kin"])
            yield
            for h in range(2):
                S.pe(lambda e, h=h: e.matmul(pb[6][:, 128 * h:128 * (h + 1)], lhsT=kin[:, h, :], rhs=qin[p2][:, h, :], start=True, stop=True),
                     r=["kin", ("qin", p2)], w=["pg6"])
            yield
            S.dve(lambda e: e.tensor_tensor(out=att[p2][:, :, :], in0=pb[6][:, 0:256].rearrange("p (a b) -> p a b", a=2),
                                            in1=C["tri_f"][:, None, :].to_broadcast([128, 2, 128]), op=ALU.mult),
                  r=["pg6", "c_tri_f"], w=[("att", p2)])
            yield
            S.act(lambda e: e.activation(out=silr[p2][:, :], in_=rt, func=AF.Silu), r=[("tmb", b2)], w=[("silr", p2)])
            yield

        def X(n):
            j, ch = n // 4, n % 4
            b2 = j % 2
            p2 = n % 2
            cs = slice(128 * ch, 128 * (ch + 1))
            vt = tmb[b2][:, ch, 256:768]
            for h in range(2):
                po = pb[2 + h]
                pu = pb[4 + h]
                S.pe(lambda e, h=h, po=po: e.matmul(po[:, 0:256], lhsT=att[p2][:, h, :], rhs=vt[:, 256 * h:256 * (h + 1)], start=True, stop=False),
                     r=[("att", p2), ("tmb", b2)], w=[("po", h)])
                S.pe(lambda e, h=h, po=po: e.matmul(po[:, 0:256], lhsT=qin[p2][:, h, :], rhs=Sb[:, h, :], start=False, stop=True),
                     r=[("qin", p2), ("Sb", h)], w=[("po", h)])
                S.pe(lambda e, h=h, pu=pu: e.matmul(pu[:, 0:256], lhsT=kst[p2][:, 128 * h:128 * (h + 1)], rhs=vt[:, 256 * h:256 * (h + 1)], start=True, stop=True),
                     r=[("kst", p2), ("tmb", b2)], w=[("pu", h)])
                yield
                S.dve(lambda e, h=h, pu=pu: e.scalar_tensor_tensor(out=St[:, h, :], in0=St[:, h, :], scalar=eq[p2][:, h, 127:128], in1=pu[:, 0:256],
                                                                    op0=ALU.mult, op1=ALU.add), r=[("St", h), ("eq", p2), ("pu", h)], w=[("St", h)])
                yield
                S.act(lambda e, h=h: e.copy(out=Sb[:, h, :], in_=St[:, h, :]), r=[("St", h)], w=[("Sb", h)])
                yield
                S.act(lambda e, h=h, po=po: e.activation(out=junk[:, :], in_=po[:, 0:256], func=AF.Square, accum_out=ssum[:, h:h + 1]),
                      r=[("po", h)], w=["junk", ("ssum", h)])
                yield
                S.act(lambda e, h=h: e.activation(out=ssum[:, h:h + 1], in_=ssum[:, h:h + 1], func=AF.Ln, bias=epsc[:, 0:1], scale=1.0 / 256),
                      r=[("ssum", h), "epsc"], w=[("ssum", h)])
                yield
                S.act(lambda e, h=h: e.activation(out=rstd[:, h:h + 1], in_=ssum[:, h:h + 1], func=AF.Exp, scale=-0.5), r=[("ssum", h)], w=[("rstd", h)])
                yield
                S.dve(lambda e, h=h, po=po: e.scalar_tensor_tensor(out=t1[:, :], in0=po[:, 0:256], scalar=rstd[:, h:h + 1], in1=gnb[:, 256 * h:256 * (h + 1)],
                                                                    op0=ALU.mult, op1=ALU.mult), r=[("po", h), ("rstd", h), "gnb"], w=["t1"])
                yield
                S.pool(lambda e, h=h: e.tensor_tensor(out=ysb[:, h, :], in0=t1[:, :], in1=silr[p2][:, 256 * h:256 * (h + 1)], op=ALU.mult),
                       r=["t1", ("silr", p2)], w=[("ysb", h)])
                yield
                for vc in range(2):
                    S.pe(lambda e, h=h, vc=vc: e.transpose(ptr[:, 128 * vc:128 * (vc + 1)], ysb[:, h, 128 * vc:128 * (vc + 1)], C["ident_bf"][:, :]),
                         r=[("ysb", h), "c_ident_bf"], w=["ptr"])
                yield
                S.act(lambda e, h=h: e.copy(out=yTs[b2][:, 2 * h:2 * h + 2, cs], in_=ptr[:, 0:256].rearrange("p (a b) -> p a b", a=2)),
                      r=["ptr"], w=[("yTs", b2)])
                yield
            if ch == 3:
                yT.put(S, 0, lambda d, cs_: d[0:512, cs_].rearrange("(a p) t -> p a t", p=128), j, lambda cs_, b2=b2: yTs[b2][:, :, cs_], [("yTs", b2)])

        load_block(0)
        for _ in P(0):
            pass
        for n in range(NCH):
            gp = P(n + 1) if n + 1 < NCH else iter(())
            gx = X(n)
            while True:
                a_ = next(gp, "end")
                b_ = next(gx, "end")
                if a_ == "end" and b_ == "end":
                    break
        S.emit()


MLSTM_STOP = 0


def phase_mlstm(nc, S, T, pFM, pTM, P, yT):
    NBK = T // 512
    with ExitStack() as st:
        A = Tiles(nc, st)
        S.barrier()
        C = make_masks(S, A)
        epsc = A.sb([128, 1], F32, "epsc")
        S.pool(lambda e: e.memset(epsc[:, :], EPS), w=["epsc"])
        cw = A.sb([128, 8, 4], F32, "cw")
        cb = A.sb([128, 8], F32, "cb")
        gb = A.sb([128, 4], F32, "gb")
        skc = A.sb([128, 4], F32, "skc")
        gnb = A.sb([128, 512], F32, "gnb")
        wif = A.sb([128, 3, 8, 4], F32, "wif")
        for nm, tl in (("cw", cw), ("cb", cb), ("gb", gb), ("skc", skc), ("gnb", gnb), ("wif", wif)):
            S.dma(tl[tuple(slice(None) for _ in tl.shape)], P[nm], w=[nm])
        wbd_f = A.sb([128, 3, 4, 128], F32, "wbd_f")
        wbd = A.sb([128, 3, 4, 128], BF16, "wbd")
        wT_f = A.sb([128, 3, 8, 128], F32, "wT_f")
        for i, nm in enumerate(("wq_bd", "wk_bd", "wv_bd")):
            S.dma(wbd_f[:, i, :, :], P[nm].rearrange("c p o -> p c o"), w=["wbd_f"])
        S.dve(lambda e: e.tensor_copy(out=wbd[:, :, :, :], in_=wbd_f[:, :, :, :]), r=["wbd_f"], w=["wbd"])
        for i, nm in enumerate(("wqT_bd", "wkT_bd", "wvT_bd")):
            S.dma(wT_f[:, i, :, :], P[nm].rearrange("c p o -> p c o"), w=["wT_f"])
        dcw = A.sb([128, 8, 4, 128], BF16, "dcw")
        dsk = A.sb([128, 4, 128], BF16, "dsk")
        for c in range(8):
            for j in range(4):
                S.dve(lambda e, c=c, j=j: e.tensor_scalar(out=dcw[:, c, j, :], in0=C["ident_f"][:, :], scalar1=cw[:, c, j:j + 1], scalar2=None, op0=ALU.mult),
                      r=["c_ident_f", "cw"], w=["dcw"])
        for c in range(4):
            S.dve(lambda e, c=c: e.tensor_scalar(out=dsk[:, c, :], in0=C["ident_f"][:, :], scalar1=skc[:, c:c + 1], scalar2=None, op0=ALU.mult),
                  r=["c_ident_f", "skc"], w=["dsk"])
        pb = [A.ps([128, 512], F32, "pb") for _ in range(7)]
        ptr = A.ps([128, 1024], BF16, "ptr")
        weff = A.sb([128, 2, 8, 4], BF16, "weff")
        for c in range(8):
            S.pe(lambda e, c=c: e.matmul(pb[0][:, 8 * c:8 * c + 4], lhsT=wT_f[:, 0, c, :], rhs=wif[:, 0, c, :], start=True, stop=False), r=["wT_f", "wif"], w=["pb0"])
            S.pe(lambda e, c=c: e.matmul(pb[0][:, 8 * c:8 * c + 4], lhsT=wT_f[:, 1, c, :], rhs=wif[:, 1, c, :], start=False, stop=True), r=["wT_f", "wif"], w=["pb0"])
            S.pe(lambda e, c=c: e.matmul(pb[0][:, 8 * c + 4:8 * c + 8], lhsT=wT_f[:, 2, c, :], rhs=wif[:, 2, c, :], start=True, stop=True), r=["wT_f", "wif"], w=["pb0"])
        pw = pb[0][:, 0:64].rearrange("p (c k f) -> p k c f", c=8, k=2)
        S.dve(lambda e: e.tensor_copy(out=weff[:, :, :, :], in_=pw), r=["pb0"], w=["weff"])
        if MLSTM_STOP == 1:
            S.emit()
            return
        mxb = [A.sb([128, 8, 516], BF16, "mxb") for _ in range(2)]
        mxs = A.sb([128, 8, 516], BF16, "mxs")
        mzb = [A.sb([128, 4, 512], BF16, "mzb") for _ in range(2)]
        xcT = [A.sb([128, 8, 512], BF16, "xcT") for _ in range(2)]
        gsb = A.sb([128, 4], F32, "gsb")
        ef = A.sb([128, 2], F32, "ef")
        nlf = A.sb([128, 2], F32, "nlf")
        tmp2 = A.sb([128, 2], F32, "tmp2")
        wv = A.sb([128, 2], F32, "wv")
        wveg = A.sb([128, 2], F32, "wveg")
        eb = [A.sb([128, 2], F32, "eb") for _ in range(2)]
        eg = [A.sb([128, 2], F32, "eg") for _ in range(2)]
        silz = [A.sb([128, 512], F32, "silz") for _ in range(2)]
        qk = [[A.sb([128, 4, 128], BF16, "qk") for _ in range(2)] for _ in range(2)]
        ksb = [[A.sb([128, 256], BF16, "ksb") for _ in range(2)] for _ in range(2)]
        vw = [[A.sb([128, 260], BF16, "vw") for _ in range(2)] for _ in range(2)]
        vw2 = [[A.sb([128, 260], BF16, "vw2") for _ in range(2)] for _ in range(2)]
        att = [[A.sb([128, 128], BF16, "att") for _ in range(2)] for _ in range(2)]
        Cst = A.sb([128, 2, 2, 257], F32, "Cst")
        Cb = A.sb([128, 2, 2, 260], BF16, "Cb")
        den = A.sb([128, 1], F32, "den")
        fac = A.sb([128, 1], F32, "fac")
        ss = A.sb([128, 1], F32, "ss")
        fr = A.sb([128, 1], F32, "fr")
        junk = A.sb([128, 256], F32, "junk")
        t1 = A.sb([128, 256], F32, "t1")
        ysb = A.sb([128, 2, 256], BF16, "ysb")
        yTs = [A.sb([128, 4, 512], BF16, "yTs") for _ in range(2)]
        ln16c = A.sb([128, 1], F32, "ln16c")
        ncb = A.sb([128, 8], F32, "ncb")
        S.dve(lambda e: e.tensor_scalar(out=ncb[:, :], in0=cb[:, :], scalar1=-1.0, scalar2=None, op0=ALU.mult), r=["cb"], w=["ncb"])
        ecv = A.sb([128, 512], F32, "ecv")
        zcv = A.sb([128, 512], F32, "zcv")
        esz = A.sb([128, 512], F32, "esz")
        S.pool(lambda e: e.memset(ln16c[:, :], float(np.log(1.0 / 16.0))), w=["ln16c"])
        S.dve(lambda e: e.memset(Cst[:, :, :, :], 0.0), w=["Cst"])
        S.dve(lambda e: e.memset(Cb[:, :, :, :], 0.0), w=["Cb"])
        for p_ in range(2):
            for h_ in range(2):
                S.dve(lambda e, p_=p_, h_=h_: e.memset(vw[p_][h_][:, :], 0.0), w=[("vw", p_, h_)])
                S.dve(lambda e, p_=p_, h_=h_: e.memset(vw2[p_][h_][:, :], 0.0), w=[("vw2", p_, h_)])
        S.pool(lambda e: e.memset(mxb[0][:, :, 0:4], 0.0), w=[("mxb", 0)])
        S.pool(lambda e: e.memset(mxs[:, :, :], 0.0), w=["mxs"])
        mzv = pTM[:, TM_MZ:TM_MZ + 512].rearrange("(i p) c -> p i c", p=128)
        NTL = T // 128

        def load_block(j):
            b2 = j % 2
            bs = slice(512 * j, 512 * (j + 1))
            S.dma(mxb[b2][:, :, 4:516], pFM[FM_MX:FM_MX + 1024, bs].rearrange("(c p) t -> p c t", p=128), r=[("pFM", j)], w=[("mxb", b2)])
            S.dma(mzb[b2][:, :, :], mzv[:, 4 * j:4 * j + 4, :], r=[("pTM", 4 * j + i) for i in range(4)], w=[("mzb", b2)])

        def P(n):
            j, tt = n // 4, n % 4
            b2 = j % 2
            p2 = n % 2
            mx = mxb[b2]
            xc = xcT[b2]
            xctag = ("xcT", b2)
            if tt == 1 and j + 1 < NBK:
                load_block(j + 1)
            if tt == 0:
                if j > 0:
                    S.dve(lambda e: e.tensor_copy(out=mxb[b2][:, :, 0:4], in_=mxb[1 - b2][:, :, 512:516]), r=[("mxb", 1 - b2)], w=[("mxb", b2)])
                S.dve(lambda e: e.tensor_copy(out=mxs[:, :, 0:514], in_=mx[:, :, 1:515]), r=[("mxb", b2)], w=["mxs"])
                yield
                for c in range(8):
                    pbi = (2, 4)[c % 2]
                    pc = pb[pbi]
                    for tp in range(4):
                        S.pe(lambda e, c=c, tp=tp, pc=pc: e.matmul(pc[:, :], lhsT=dcw[:, c, tp, :], rhs=(mx[:, c, tp + 1:tp + 513] if tp % 2 == 1 else mxs[:, c, tp:tp + 512]),
                                                                  start=(tp == 0), stop=(tp == 3)), r=["dcw", ("mxb", b2), "mxs"], w=["pb%d" % pbi])
                    S.act(lambda e, c=c, pc=pc: e.activation(out=xc[:, c, :], in_=pc[:, :], func=AF.Silu, bias=cb[:, c:c + 1]), r=["pb%d" % pbi, "cb"], w=[xctag])
                    yield
            ts_ = slice(128 * tt, 128 * (tt + 1))
            tsx = slice(4 + 128 * tt, 4 + 128 * (tt + 1))
            for c in range(8):
                S.pe(lambda e, c=c: e.matmul(pb[3][:, 0:4], lhsT=xc[:, c, ts_], rhs=weff[:, 0, c, :], start=(c == 0), stop=False), r=[xctag, "weff"], w=["pb3"])
            for c in range(8):
                S.pe(lambda e, c=c: e.matmul(pb[3][:, 0:4], lhsT=mx[:, c, tsx], rhs=weff[:, 1, c, :], start=False, stop=(c == 7)), r=[("mxb", b2), "weff"], w=["pb3"])
            yield
            S.dve(lambda e: e.tensor_tensor(out=gsb[:, :], in0=pb[3][:, 0:4], in1=gb[:, :], op=ALU.add), r=["pb3", "gb"], w=["gsb"])
            yield
            S.act(lambda e: e.activation(out=ef[:, :], in_=gsb[:, 2:4], func=AF.Exp, scale=-1.0), r=["gsb"], w=["ef"])
            yield
            S.act(lambda e: e.activation(out=nlf[:, :], in_=ef[:, :], func=AF.Ln, bias=1.0), r=["ef"], w=["nlf"])
            yield
            S.pe(lambda e: e.matmul(pb[3][:, 8:10], lhsT=C["tri_f"][:, :], rhs=nlf[:, :], start=True, stop=True), r=["c_tri_f", "nlf"], w=["pb3"])
            S.pe(lambda e: e.matmul(pb[3][:, 16:18], lhsT=C["ones_f"][:, :], rhs=nlf[:, :], start=True, stop=True), r=["c_ones_f", "nlf"], w=["pb3"])
            yield
            S.dve(lambda e: e.tensor_tensor(out=tmp2[:, :], in0=pb[3][:, 8:10], in1=gsb[:, 0:2], op=ALU.add), r=["pb3", "gsb"], w=["tmp2"])
            yield
            S.act(lambda e: e.activation(out=wv[:, :], in_=tmp2[:, :], func=AF.Exp), r=["tmp2"], w=["wv"])
            S.act(lambda e: e.activation(out=eb[p2][:, :], in_=pb[3][:, 8:10], func=AF.Exp, scale=-1.0, bias=ln16c[:, 0:1]), r=["pb3", "ln16c"], w=[("eb", p2)])
            S.act(lambda e: e.activation(out=eg[p2][:, :], in_=pb[3][:, 16:18], func=AF.Exp, scale=-1.0), r=["pb3"], w=[("eg", p2)])
            yield
            S.dve(lambda e: e.tensor_tensor(out=wveg[:, :], in0=wv[:, :], in1=eg[p2][:, :], op=ALU.mult), r=["wv", ("eg", p2)], w=["wveg"])
            yield
            for h in range(2):
                pA, pB = pb[3], pb[4]
                for dc in range(2):
                    c = 2 * h + dc
                    S.pe(lambda e, c=c, dc=dc: e.matmul(pA[:, 128 * dc:128 * (dc + 1)], lhsT=wbd[:, 0, c, :], rhs=xc[:, c, ts_], start=True, stop=True), r=["wbd", xctag], w=["pb3"])
                    S.pe(lambda e, c=c, dc=dc: e.matmul(pA[:, 256 + 128 * dc:256 + 128 * (dc + 1)], lhsT=wbd[:, 1, c, :], rhs=xc[:, c, ts_], start=True, stop=True), r=["wbd", xctag], w=["pb3"])
                    S.pe(lambda e, c=c, dc=dc: e.matmul(pB[:, 128 * dc:128 * (dc + 1)], lhsT=xc[:, c, ts_], rhs=wbd[:, 1, c, :], start=True, stop=True), r=["wbd", xctag], w=["pb4"])
                    S.pe(lambda e, c=c, dc=dc: e.matmul(pB[:, 256 + 128 * dc:256 + 128 * (dc + 1)], lhsT=mx[:, c, tsx], rhs=wbd[:, 2, c, :], start=True, stop=True), r=["wbd", ("mxb", b2)], w=["pb4"])
                yield
                S.act(lambda e, h=h: e.copy(out=qk[p2][h][:, :, :].rearrange("p a b -> p (a b)"), in_=pA[:, :]), r=["pb3"], w=[("qk", p2, h)])
                yield
                S.act(lambda e, h=h: e.copy(out=ksb[p2][h][:, :], in_=pB[:, 0:256]), r=["pb4"], w=[("ksb", p2, h)])
                S.dve(lambda e, h=h: e.tensor_scalar(out=vw[p2][h][:, 0:256], in0=pB[:, 256:512], scalar1=wv[:, h:h + 1], scalar2=None, op0=ALU.mult), r=["pb4", "wv"], w=[("vw", p2, h)])
                yield
                S.dve(lambda e, h=h: e.tensor_scalar(out=vw2[p2][h][:, 0:256], in0=pB[:, 256:512], scalar1=wveg[:, h:h + 1], scalar2=None, op0=ALU.mult), r=["pb4", "wveg"], w=[("vw2", p2, h)])
                S.pool(lambda e, h=h: e.tensor_copy(out=vw[p2][h][:, 256:257], in_=wv[:, h:h + 1]), r=["wv"], w=[("vw", p2, h)])
                S.pool(lambda e, h=h: e.tensor_copy(out=vw2[p2][h][:, 256:257], in_=wveg[:, h:h + 1]), r=["wveg"], w=[("vw2", p2, h)])
                yield
                for dc in range(2):
                    S.pe(lambda e, dc=dc, h=h: e.matmul(pb[2][:, 0:128], lhsT=qk[p2][h][:, 2 + dc, :], rhs=qk[p2][h][:, dc, :], start=(dc == 0), stop=(dc == 1)), r=[("qk", p2, h)], w=["pb2"])
                yield
                S.dve(lambda e, h=h: e.tensor_tensor(out=att[p2][h][:, :], in0=pb[2][:, 0:128], in1=C["tri_f"][:, :], op=ALU.mult), r=["pb2", "c_tri_f"], w=[("att", p2, h)])
                yield
            S.act(lambda e: e.activation(out=silz[p2][:, :], in_=mzb[b2][:, tt, :], func=AF.Silu), r=[("mzb", b2)], w=[("silz", p2)])
            yield

        def X(n):
            j, tt = n // 4, n % 4
            b2 = j % 2
            p2 = n % 2
            xc = xcT[b2]
            xctag = ("xcT", b2)
            ts_ = slice(128 * tt, 128 * (tt + 1))
            for h in range(2):
                pD, pE, pF, pG = pb[5], pb[6], pb[0], pb[1]
                S.pe(lambda e, h=h: e.matmul(pD[:, 0:258], lhsT=att[p2][h][:, :], rhs=vw[p2][h][:, 0:258], start=True, stop=False), r=[("att", p2, h), ("vw", p2, h)], w=["pb5"])
                for dc in range(2):
                    S.pe(lambda e, dc=dc, h=h: e.matmul(pD[:, 0:258], lhsT=qk[p2][h][:, dc, :], rhs=Cb[:, h, dc, 0:258], start=False, stop=(dc == 1)), r=[("qk", p2, h), ("Cb", h)], w=["pb5"])
                for dc, pU in ((0, pE), (1, pF)):
                    S.pe(lambda e, dc=dc, pU=pU, h=h: e.matmul(pU[:, 0:258], lhsT=ksb[p2][h][:, 128 * dc:128 * (dc + 1)], rhs=vw2[p2][h][:, 0:258], start=True, stop=True),
                         r=[("ksb", p2, h), ("vw2", p2, h)], w=[("pb6", "pb0")[dc]])
                yield
                for dc, pU in ((0, pE), (1, pF)):
                    S.dve(lambda e, dc=dc, pU=pU, h=h: e.scalar_tensor_tensor(out=Cst[:, h, dc, :], in0=Cst[:, h, dc, :], scalar=eg[p2][:, h:h + 1], in1=pU[:, 0:257],
                                                                             op0=ALU.mult, op1=ALU.add), r=[("Cst", h), ("eg", p2), ("pb6", "pb0")[dc]], w=[("Cst", h)])
                yield
                S.act(lambda e, h=h: e.copy(out=Cb[:, h, :, 0:257], in_=Cst[:, h, :, :]), r=[("Cst", h)], w=[("Cb", h)])
                yield
                S.act(lambda e, h=h: e.activation(out=den[:, :], in_=pD[:, 256:257], func=AF.Abs, scale=eb[p2][:, h:h + 1]), r=["pb5", ("eb", p2)], w=["den"])
                yield
                S.dve(lambda e: e.tensor_scalar_max(out=den[:, :], in0=den[:, :], scalar1=1.0), r=["den"], w=["den"])
                S.dve(lambda e: e.reciprocal(out=den[:, :], in_=den[:, :]), r=["den"], w=["den"])
                S.dve(lambda e, h=h: e.tensor_tensor(out=fac[:, :], in0=den[:, :], in1=eb[p2][:, h:h + 1], op=ALU.mult), r=["den", ("eb", p2)], w=["fac"])
                yield
                S.act(lambda e: e.activation(out=junk[:, :], in_=pD[:, 0:256], func=AF.Square, scale=fac[:, 0:1], accum_out=ss[:, 0:1]), r=["pb5", "fac"], w=["junk", "ss"])
                yield
                S.act(lambda e: e.activation(out=ss[:, :], in_=ss[:, :], func=AF.Ln, bias=epsc[:, 0:1], scale=1.0 / 256), r=["ss", "epsc"], w=["ss"])
                yield
                S.act(lambda e: e.activation(out=fr[:, :], in_=ss[:, :], func=AF.Exp, scale=-0.5), r=["ss"], w=["fr"])
                S.dve(lambda e: e.tensor_tensor(out=fr[:, :], in0=fr[:, :], in1=fac[:, :], op=ALU.mult), r=["fr", "fac"], w=["fr"])
                S.dve(lambda e, h=h: e.scalar_tensor_tensor(out=t1[:, :], in0=pD[:, 0:256], scalar=fr[:, 0:1], in1=gnb[:, 256 * h:256 * (h + 1)],
                                                             op0=ALU.mult, op1=ALU.mult), r=["pb5", "fr", "gnb"], w=["t1"])
                for dc in range(2):
                    c = 2 * h + dc
                    S.pe(lambda e, c=c, dc=dc: e.matmul(pG[:, 128 * dc:128 * (dc + 1)], lhsT=xc[:, c, ts_], rhs=dsk[:, c, :], start=True, stop=True),
                         r=[xctag, "dsk"], w=["pb1"])
                yield
                S.dve(lambda e: e.tensor_tensor(out=t1[:, :], in0=pG[:, 0:256], in1=t1[:, :], op=ALU.add), r=["pb1", "t1"], w=["t1"])
                yield
                S.pool(lambda e, h=h: e.tensor_tensor(out=ysb[:, h, :], in0=t1[:, :], in1=silz[p2][:, 256 * h:256 * (h + 1)], op=ALU.mult), r=["t1", ("silz", p2)], w=[("ysb", h)])
                yield
                for vc in range(2):
                    S.pe(lambda e, h=h, vc=vc: e.transpose(ptr[:, 128 * vc:128 * (vc + 1)], ysb[:, h, 128 * vc:128 * (vc + 1)], C["ident_bf"][:, :]),
                         r=[("ysb", h), "c_ident_bf"], w=["ptr"])
                yield
                S.act(lambda e, h=h: e.copy(out=yTs[b2][:, 2 * h:2 * h + 2, ts_], in_=ptr[:, 0:256].rearrange("p (a b) -> p a b", a=2)),
                      r=["ptr"], w=[("yTs", b2)])
                yield
            if tt == 3:
                yT.put(S, 512, lambda d, cs_: d[512:1024, cs_].rearrange("(a p) t -> p a t", p=128), j, lambda cs_, b2=b2: yTs[b2][:, :, cs_], [("yTs", b2)])

        load_block(0)
        for _ in P(0):
            pass
        for n in range(NTL):
            gp = P(n + 1) if n + 1 < NTL else iter(())
            gx = X(n)
            while True:
                a_ = next(gp, "end")
                b_ = next(gx, "end")
                if a_ == "end" and b_ == "end":
                    break
        S.emit()


def bcast(v, n=128):
    v = np.asarray(v, np.float32)
    return np.ascontiguousarray(np.broadcast_to(v, (n,) + v.shape))


def block_diag_full(w):
    W = np.zeros((1024, 1024), np.float32)
    for c in range(4):
        for d in range(4):
            W[np.arange(256) * 4 + c, np.arange(256) * 4 + d] = w[:, c, d]
    return W


def prep_mlstm(p, conv_w, conv_b, wq, wk, wv, w_i, b_i, w_f, b_f, skip, norm):
    own = np.arange(512 * p, 512 * p + 512)
    oth = np.arange(512 * (1 - p), 512 * (1 - p) + 512)
    perm = np.concatenate([own, oth])
    P = {}
    P["cw"] = np.ascontiguousarray(conv_w[:, perm].reshape(4, 8, 128).transpose(2, 1, 0))
    P["cb"] = np.ascontiguousarray(conv_b[perm].reshape(8, 128).T)
    for nm, w in (("q", wq), ("k", wk), ("v", wv)):
        W = block_diag_full(w)[perm][:, perm]
        blocks = np.stack([W[128 * c:128 * (c + 1), 128 * c:128 * (c + 1)] for c in range(8)])
        P["w%s_bd" % nm] = np.ascontiguousarray(blocks[:4])
        P["w%sT_bd" % nm] = np.ascontiguousarray(blocks.transpose(0, 2, 1))
    cols = np.stack([w_i[:, 2 * p], w_i[:, 2 * p + 1], w_f[:, 2 * p], w_f[:, 2 * p + 1]], axis=1)
    wif = np.stack([cols[part * 1024 + perm] for part in range(3)])
    P["wif"] = np.ascontiguousarray(wif.reshape(3, 8, 128, 4).transpose(2, 0, 1, 3))
    P["gb"] = bcast(np.array([b_i[2 * p], b_i[2 * p + 1], b_f[2 * p], b_f[2 * p + 1]], np.float32))
    P["skc"] = np.ascontiguousarray(skip[own].reshape(4, 128).T)
    P["gnb"] = bcast(norm[own])
    return {k: np.ascontiguousarray(v, dtype=np.float32) for k, v in P.items()}, perm


def load_w_generic(S, wsb_view_fn, wdram_view, nchunks, ncols, G, stg, stgtag, wtag, k0=0):
    k = k0
    for c0 in range(0, ncols, G):
        n = min(G, ncols - c0)
        st = stg[k % 2]
        tg = (stgtag, k % 2)
        sv = st[:, 0:nchunks * n].rearrange("p (c n) -> p c n", c=nchunks)
        S.dma(sv, wdram_view[:, :, c0:c0 + n], w=[tg])
        eng = [S.pool, S.dve, S.act][k % 3]
        if k % 3 == 2:
            eng(lambda e, sv=sv, c0=c0, n=n: e.copy(out=wsb_view_fn(c0, n), in_=sv), r=[tg], w=[wtag])
        else:
            eng(lambda e, sv=sv, c0=c0, n=n: e.tensor_copy(out=wsb_view_fn(c0, n), in_=sv), r=[tg], w=[wtag])
        k += 1
    return k


def blocks_of(H, TO, NB):
    bl = [(0, H)] if H > 0 else []
    for k in range(TO // NB):
        bl.append((H + NB * k, NB))
    return bl


def phase_C1(nc, S, H, TO, xTo, ysrc, hmask_d, Wg, bg_d, Wb, Wout, gmix, x1T, yr=(), xr=(), ypre=False):
    NB = 256
    with ExitStack() as st:
        A = Tiles(nc, st)
        S.barrier()
        T_ = {}
        wg = A.sb([128, 8, 3072], BF16, "wg")
        wb = A.sb([128, 3, 8, 1024], BF16, "wb")
        wo = A.sb([128, 8, 1024], BF16, "wo")
        stg = [A.sb([128, 2048], F32, "stg") for _ in range(2)]
        xb = [A.sb([128, 8, NB], F32, "xb") for _ in range(1)]
        yb = [A.sb([128, 24, NB], BF16, "yb") for _ in range(2)]
        yb2 = A.sb([128, 24, NB], BF16, "yb2") if len(ysrc) == 2 else None
        hT = A.sb([128, 8, NB], BF16, "hT")
        T_["xsq"] = A.sb([128, 8, NB], BF16, "xsq")
        T_["std"] = A.sb([128, NB], F32, "std")
        T_["rstd"] = A.sb([128, NB], F32, "rstd")
        T_["epsc"] = A.sb([128, 1], F32, "epsc")
        ones_bf = A.sb([128, 128], BF16, "ones")
        gcol = A.sb([128, 8], F32, "gcol")
        bg = A.sb([128, 24], F32, "bg")
        hmask = A.sb([128, 2], F32, "hmask")
        mg = A.sb([128, 8, NB], F32, "mg")
        mgT = A.sb([128, 8, NB], BF16, "mgT")
        sg = [A.sb([128, NB], F32, "sg") for _ in range(2)]
        tmp = [A.sb([128, NB], F32, "tmp") for _ in range(2)]
        ps_ss = A.ps([128, 512], F32, "ps_ss")
        psg = [A.ps([128, 512], F32, "psg") for _ in range(2)]
        psb = [A.ps([128, 512], F32, "psb") for _ in range(2)]
        pso = [A.ps([128, 512], F32, "pso") for _ in range(2)]
        S.pool(lambda e: e.memset(ones_bf[:, :], 1.0), w=["ones"])
        S.pool(lambda e: e.memset(T_["epsc"][:, :], EPS), w=["epsc"])
        S.dma(gcol[:, :], gmix, w=["gcol"])
        S.dma(bg[:, :], bg_d, w=["bg"])
        S.dma(hmask[:, :], hmask_d, w=["hmask"])
        k = load_w_generic(S, lambda c0, n: wg[:, :, c0:c0 + n], Wg.rearrange("(c p) n -> p c n", p=128), 8, 3072, 256, stg, "stg", "wg")
        for n_ in range(3):
            k = load_w_generic(S, lambda c0, n, n_=n_: wb[:, n_, :, c0:c0 + n], Wb[n_].rearrange("(c p) n -> p c n", p=128), 8, 1024, 256, stg, "stg", "wb", k)
        k = load_w_generic(S, lambda c0, n: wo[:, :, c0:c0 + n], Wout.rearrange("(c p) n -> p c n", p=128), 8, 1024, 256, stg, "stg", "wo", k)
        xv = xTo.rearrange("(c p) t -> p c t", p=128)
        yvs = ysrc if ypre else [y_.rearrange("(c p) t -> p c t", p=128) for y_ in ysrc]
        yview = (lambda t_: t_.rearrange("p (r k) n -> p r k n", r=2)) if ypre else (lambda t_: t_)
        ov = x1T.rearrange("(c p) t -> p c t", p=128)
        kk = 0
        for bi, (c0, nb) in enumerate(blocks_of(H, TO, NB)):
            xt = xb[0]
            xtag = ("xb", 0)
            yt = yb[bi % 2]
            ytag = ("yb", bi % 2)
            S.dma(xt[:, :, :nb], xv[:, :, c0:c0 + nb], r=list(xr), w=[xtag])
            if ypre:
                for r_ in range(2):
                    S.dma(yt[:, 12 * r_:12 * (r_ + 1), :nb], yvs[0][:, r_, :, c0:c0 + nb], r=list(yr), w=[ytag])
            else:
                S.dma(yt[:, :, :nb], yvs[0][:, :, c0:c0 + nb], r=list(yr), w=[ytag])
            if len(ysrc) == 2:
                for r_ in range(2):
                    S.dma(yb2[:, 12 * r_:12 * (r_ + 1), :nb], yvs[1][:, r_, :, c0:c0 + nb], r=list(yr), w=["yb2"])
                S.dve(lambda e, yt=yt, nb=nb: e.tensor_scalar(out=yt[:, :, :nb], in0=yt[:, :, :nb], scalar1=hmask[:, 0:1], scalar2=None, op0=ALU.mult),
                      r=[ytag, "hmask"], w=[ytag])
                S.dve(lambda e, yt=yt, nb=nb: e.scalar_tensor_tensor(out=yt[:, :, :nb], in0=yb2[:, :, :nb], scalar=hmask[:, 1:2], in1=yt[:, :, :nb], op0=ALU.mult, op1=ALU.add),
                      r=[ytag, "yb2", "hmask"], w=[ytag])
            if bi == 0 and H > 0:
                S.dve(lambda e, xt=xt, nb=nb: e.tensor_scalar(out=xt[:, :, :nb], in0=xt[:, :, :nb], scalar1=hmask[:, 1:2], scalar2=None, op0=ALU.mult),
                      r=[xtag, "hmask"], w=[xtag])
            rms_block(S, T_, ones_bf, xt, gcol, hT, ps_ss, xtag, nb)
            htag = ("hT", id(hT))
            for dc in range(8):
                for n_ in range(3):
                    pg = psg[kk % 2]
                    pgt = ("psg", kk % 2)
                    pbk = psb[kk % 2]
                    pbt = ("psb", kk % 2)
                    sgt = sg[kk % 2]
                    sgtag = ("sg", kk % 2)
                    tm = tmp[kk % 2]
                    tmtag = ("tmp", kk % 2)
                    kk += 1
                    for c in range(8):
                        S.pe(lambda e, c=c, pg=pg, n_=n_, dc=dc, nb=nb: e.matmul(pg[:, :nb], lhsT=wg[:, c, n_ * 1024 + dc * 128:n_ * 1024 + (dc + 1) * 128], rhs=hT[:, c, :nb],
                                                                               start=(c == 0), stop=(c == 7)), r=["wg", htag], w=[pgt])
                    S.act(lambda e, pg=pg, sgt=sgt, n_=n_, dc=dc, nb=nb: e.activation(out=sgt[:, :nb], in_=pg[:, :nb], func=AF.Sigmoid, bias=bg[:, n_ * 8 + dc:n_ * 8 + dc + 1]),
                          r=[pgt, "bg"], w=[sgtag])
                    for c in range(8):
                        ych = (c // 4) * 12 + n_ * 4 + (c % 4)
                        S.pe(lambda e, c=c, pbk=pbk, n_=n_, dc=dc, nb=nb, ych=ych, yt=yt: e.matmul(pbk[:, :nb], lhsT=wb[:, n_, c, dc * 128:(dc + 1) * 128], rhs=yt[:, ych, :nb],
                                                                                                start=(c == 0), stop=(c == 7)), r=["wb", ytag], w=[pbt])
                    if n_ == 0:
                        S.dve(lambda e, pbk=pbk, sgt=sgt, dc=dc, nb=nb: e.tensor_tensor(out=mg[:, dc, :nb], in0=pbk[:, :nb], in1=sgt[:, :nb], op=ALU.mult),
                              r=[pbt, sgtag], w=[("mg", dc)])
                    else:
                        S.dve(lambda e, pbk=pbk, sgt=sgt, tm=tm, nb=nb: e.tensor_tensor(out=tm[:, :nb], in0=pbk[:, :nb], in1=sgt[:, :nb], op=ALU.mult),
                              r=[pbt, sgtag], w=[tmtag])
                        dst = mg if n_ == 1 else mgT
                        S.pool(lambda e, tm=tm, dc=dc, nb=nb, dst=dst: e.tensor_tensor(out=dst[:, dc, :nb], in0=mg[:, dc, :nb], in1=tm[:, :nb], op=ALU.add),
                               r=[("mg", dc), tmtag], w=[("mg", dc), ("mgT", dc)])
            for dc in range(8):
                po = pso[dc % 2]
                pot = ("pso", dc % 2)
                for c in range(8):
                    S.pe(lambda e, c=c, po=po, dc=dc, nb=nb: e.matmul(po[:, :nb], lhsT=wo[:, c, dc * 128:(dc + 1) * 128], rhs=mgT[:, c, :nb], start=(c == 0), stop=(c == 7)),
                         r=["wo"] + [("mgT", c_) for c_ in range(8)], w=[pot])
                S.dve(lambda e, po=po, dc=dc, nb=nb, xt=xt: e.tensor_tensor(out=xt[:, dc, :nb], in0=po[:, :nb], in1=xt[:, dc, :nb], op=ALU.add), r=[pot, xtag, htag], w=[xtag])
            S.dma(ov[:, :, c0:c0 + nb], xt[:, :, :nb], r=[xtag], w=[("x1T", c0)])
        S.emit()


def phase_C2(nc, S, H, TO, x1T, Wup, cw_d, cb_d, Wdown, gffn, x2T, final_g=None, outT=None, xsend=None):
    NB = 256
    with ExitStack() as st:
        A = Tiles(nc, st)
        S.barrier()
        T_ = {}
        wu = A.sb([128, 8, 5632], BF16, "wu")
        wd = A.sb([128, 22, 1024], BF16, "wd")
        stg = [A.sb([128, 2048], F32, "stg") for _ in range(2)]
        xb = [A.sb([128, 8, NB], F32, "xb") for _ in range(2)]
        hT = A.sb([128, 8, NB], BF16, "hT")
        T_["xsq"] = A.sb([128, 8, NB], BF16, "xsq")
        T_["std"] = A.sb([128, NB], F32, "std")
        T_["rstd"] = A.sb([128, NB], F32, "rstd")
        T_["epsc"] = A.sb([128, 1], F32, "epsc")
        ones_bf = A.sb([128, 128], BF16, "ones")
        gcol = A.sb([128, 8], F32, "gcol")
        gfin = A.sb([128, 8], F32, "gfin")
        cw = A.sb([128, 44, 3], F32, "cw")
        cb = A.sb([128, 44], F32, "cb")
        halo = A.sb([128, 44, 2], F32, "halo")
        ua = [A.sb([128, NB + 2], F32, "ua") for _ in range(2)]
        ug = [A.sb([128, NB + 2], F32, "ug") for _ in range(2)]
        aa = [A.sb([128, NB], F32, "aa") for _ in range(2)]
        ag = [A.sb([128, NB], F32, "ag") for _ in range(2)]
        actT = A.sb([128, 22, NB], BF16, "actT")
        oT = A.sb([128, 8, NB], F32, "oT")
        ps_ss = A.ps([128, 512], F32, "ps_ss")
        psa = [A.ps([128, 512], F32, "psa") for _ in range(2)]
        psgt = [A.ps([128, 512], F32, "psgt") for _ in range(2)]
        psd = [A.ps([128, 512], F32, "psd") for _ in range(2)]
        S.pool(lambda e: e.memset(ones_bf[:, :], 1.0), w=["ones"])
        S.pool(lambda e: e.memset(T_["epsc"][:, :], EPS), w=["epsc"])
        S.dve(lambda e: e.memset(halo[:, :, :], 0.0), w=["halo"])
        S.dma(gcol[:, :], gffn, w=["gcol"])
        if final_g is not None:
            S.dma(gfin[:, :], final_g, w=["gfin"])
        S.dma(cw[:, :, :], cw_d, w=["cw"])
        S.dma(cb[:, :], cb_d, w=["cb"])
        k = load_w_generic(S, lambda c0, n: wu[:, :, c0:c0 + n], Wup.rearrange("(c p) n -> p c n", p=128), 8, 5632, 256, stg, "stg", "wu")
        k = load_w_generic(S, lambda c0, n: wd[:, :, c0:c0 + n], Wdown.rearrange("(c p) n -> p c n", p=128), 22, 1024, 64, stg, "stg", "wd", k)
        xv = x1T.rearrange("(c p) t -> p c t", p=128)
        ov = x2T.rearrange("(c p) t -> p c t", p=128)
        kk = 0
        for bi, (c0, nb) in enumerate(blocks_of(H, TO, NB)):
            xt = xb[bi % 2]
            xtag = ("xb", bi % 2)
            S.dma(xt[:, :, :nb], xv[:, :, c0:c0 + nb], r=[("x1T", c0)], w=[xtag])
            rms_block(S, T_, ones_bf, xt, gcol, hT, ps_ss, xtag, nb)
            htag = ("hT", id(hT))
            for fc in range(22):
                b2 = kk % 2
                kk += 1
                for (ps, pst, off, ut, utag, acc, atag, hc) in ((psa[b2], ("psa", b2), 0, ua[b2], ("ua", b2), aa[b2], ("aa", b2), fc),
                                                                (psgt[b2], ("psgt", b2), 2816, ug[b2], ("ug", b2), ag[b2], ("ag", b2), 22 + fc)):
                    for c in range(8):
                        S.pe(lambda e, c=c, ps=ps, off=off, fc=fc, nb=nb: e.matmul(ps[:, :nb], lhsT=wu[:, c, off + fc * 128:off + (fc + 1) * 128], rhs=hT[:, c, :nb],
                                                                                 start=(c == 0), stop=(c == 7)), r=["wu", htag], w=[pst])
                    S.act(lambda e, ut=ut, hc=hc: e.copy(out=ut[:, 0:2], in_=halo[:, hc, :]), r=[("halo", hc)], w=[utag])
                    S.act(lambda e, ut=ut, ps=ps, nb=nb: e.copy(out=ut[:, 2:2 + nb], in_=ps[:, :nb]), r=[pst], w=[utag])
                    S.act(lambda e, ut=ut, hc=hc, nb=nb: e.copy(out=halo[:, hc, :], in_=ut[:, nb:nb + 2]), r=[utag], w=[("halo", hc)])
                    S.dve(lambda e, ut=ut, acc=acc, hc=hc, nb=nb: e.tensor_scalar(out=acc[:, :nb], in0=ut[:, 0:nb], scalar1=cw[:, hc, 0:1], scalar2=cb[:, hc:hc + 1],
                                                                                op0=ALU.mult, op1=ALU.add), r=[utag, "cw", "cb"], w=[atag])
                    for j in (1, 2):
                        S.dve(lambda e, ut=ut, acc=acc, hc=hc, nb=nb, j=j: e.scalar_tensor_tensor(out=acc[:, :nb], in0=ut[:, j:j + nb], scalar=cw[:, hc, j:j + 1], in1=acc[:, :nb],
                                                                                                  op0=ALU.mult, op1=ALU.add), r=[utag, "cw", atag], w=[atag])
                S.act(lambda e, b2=b2, nb=nb: e.activation(out=ag[b2][:, :nb], in_=ag[b2][:, :nb], func=AF.Silu), r=[("ag", b2)], w=[("ag", b2)])
                S.pool(lambda e, b2=b2, fc=fc, nb=nb: e.tensor_tensor(out=actT[:, fc, :nb], in0=aa[b2][:, :nb], in1=ag[b2][:, :nb], op=ALU.mult),
                       r=[("aa", b2), ("ag", b2)], w=[("actT", fc)])
            for dc in range(8):
                po = psd[dc % 2]
                pot = ("psd", dc % 2)
                for fc in range(22):
                    S.pe(lambda e, fc=fc, po=po, dc=dc, nb=nb: e.matmul(po[:, :nb], lhsT=wd[:, fc, dc * 128:(dc + 1) * 128], rhs=actT[:, fc, :nb], start=(fc == 0), stop=(fc == 21)),
                         r=["wd"] + [("actT", f_) for f_ in range(22)], w=[pot])
                S.dve(lambda e, po=po, dc=dc, nb=nb, xt=xt: e.tensor_tensor(out=xt[:, dc, :nb], in0=po[:, :nb], in1=xt[:, dc, :nb], op=ALU.add), r=[pot, xtag, htag], w=[xtag])
            if final_g is None:
                S.dma(ov[:, :, c0:c0 + nb], xt[:, :, :nb], r=[xtag], w=[("x2T", c0)])
                if xsend is not None and not (bi == 0 and H > 0):
                    S.dma(xsend.rearrange("(c p) t -> p c t", p=128)[:, :, c0 - H:c0 - H + nb], xt[:, :, :nb], r=[xtag], w=[("xsend", c0)])
            elif not (bi == 0 and H > 0):
                xsq, std, rstd = T_["xsq"], T_["std"], T_["rstd"]
                S.act(lambda e, xt=xt, nb=nb: e.activation(out=xsq[:, :, :nb], in_=xt[:, :, :nb], func=AF.Square), r=[xtag], w=["xsq"])
                for c in range(8):
                    S.pe(lambda e, c=c, nb=nb: e.matmul(ps_ss[:, :nb], lhsT=ones_bf[:, :], rhs=xsq[:, c, :nb], start=(c == 0), stop=(c == 7)), r=["xsq", "ones"], w=["ps_ss"])
                S.act(lambda e, nb=nb: e.activation(out=std[:, :nb], in_=ps_ss[:, :nb], func=AF.Sqrt, bias=T_["epsc"][:, 0:1], scale=1.0 / D), r=["ps_ss", "epsc"], w=["std"])
                S.dve(lambda e, nb=nb: e.reciprocal(out=rstd[:, :nb], in_=std[:, :nb]), r=["std"], w=["rstd"])
                for c in range(8):
                    S.dve(lambda e, c=c, nb=nb, xt=xt: e.scalar_tensor_tensor(out=oT[:, c, :nb], in0=xt[:, c, :nb], scalar=gfin[:, c:c + 1], in1=rstd[:, :nb], op0=ALU.mult, op1=ALU.mult),
                          r=[xtag, "rstd", "gfin"], w=["oT"])
                S.dma(outT.rearrange("(c p) t -> p c t", p=128)[:, :, c0 - H:c0 - H + nb], oT[:, :, :nb], r=["oT"], w=[("outT", c0)])
        S.emit()


T_FULL = 8192
TO_FULL = 4096
ML_SH = {"cw": [128, 8, 4], "cb": [128, 8], "wq_bd": [4, 128, 128], "wk_bd": [4, 128, 128], "wv_bd": [4, 128, 128], "wqT_bd": [8, 128, 128],
         "wkT_bd": [8, 128, 128], "wvT_bd": [8, 128, 128], "wif": [128, 3, 8, 4], "gb": [128, 4], "skc": [128, 4], "gnb": [128, 512]}


def build_AB(T):
    nc = bass.Bass("TRN2", target_bir_lowering=False)
    dt = lambda n, s, d, k: nc.dram_tensor(n, s, d, kind=k).ap()
    xT = dt("xT", [D, T], F32, "ExternalInput")
    wA = dt("wA", [D, NA], F32, "ExternalInput")
    gm = dt("gmix", [128, 8], F32, "ExternalInput")
    fb = dt("foxbf_t", [128, T // 128, 4], F32, "ExternalInput")
    wl = dt("wlr_aug", [17, 256], F32, "ExternalInput")
    gn = dt("gla_gnb", [128, 512], F32, "ExternalInput")
    P = {k: dt("m_" + k, v, F32, "ExternalInput") for k, v in ML_SH.items()}
    yT = dt("yT", [1536, T], BF16, "ExternalOutput")
    pFM = dt("pFM", [NFM, T], BF16, "Internal")
    pTM = dt("pTM", [T, NTM], BF16, "Internal")
    with ExitStack() as st:
        S = Sched(nc, st)
        phase_A(nc, S, T, xT, wA, gm, pFM, pTM)
        yd = YDst(T, yT=yT)
        phase_gla(nc, S, T, pFM, pTM, wl, gn, yd)
        phase_mlstm(nc, S, T, pFM, pTM, P, yd)
        phase_fox(nc, S, T, pFM, pTM, fb, yd)
        S.finish()
        S.emit()
    return nc


def build_C(H, TO, final):
    nc = bass.Bass("TRN2", target_bir_lowering=False)
    dt = lambda n, s, d, k: nc.dram_tensor(n, s, d, kind=k).ap()
    W = H + TO
    xTo = dt("xTo", [D, W], F32, "ExternalInput")
    yTall = dt("yTall", [3072, W], BF16, "ExternalInput")
    hm = dt("hmask", [128, 2], F32, "ExternalInput")
    Wg = dt("Wg", [D, 3072], F32, "ExternalInput")
    bg = dt("bg", [128, 24], F32, "ExternalInput")
    Wb = dt("Wb", [3, D, D], F32, "ExternalInput")
    Wo = dt("Wout", [D, D], F32, "ExternalInput")
    gm = dt("gmix", [128, 8], F32, "ExternalInput")
    Wup = dt("Wup", [D, 5632], F32, "ExternalInput")
    cw = dt("fcw", [128, 44, 3], F32, "ExternalInput")
    cb = dt("fcb", [128, 44], F32, "ExternalInput")
    Wd = dt("Wdown", [2816, D], F32, "ExternalInput")
    gf = dt("gffn", [128, 8], F32, "ExternalInput")
    x1T = dt("x1T", [D, W], F32, "Internal")
    if final:
        gfin = dt("gfin", [128, 8], F32, "ExternalInput")
        outT = dt("outT", [D, TO], F32, "ExternalOutput")
        x2T = x1T
    else:
        gfin, outT = None, None
        x2T = dt("x2T", [D, W], F32, "ExternalOutput")
    with ExitStack() as st:
        S = Sched(nc, st)
        phase_C1(nc, S, H, TO, xTo, [yTall], hm, Wg, bg, Wb, Wo, gm, x1T)
        phase_C2(nc, S, H, TO, x1T, Wup, cw, cb, Wd, gf, x2T, gfin, outT)
        S.finish()
        S.emit()
    return nc


def colvec(v):
    v = np.asarray(v, np.float32)
    return np.ascontiguousarray(v.reshape(-1, 128).T)


SPL = np.cumsum([0, 512, 512, 1024, 16, 1024, 1024, 1024, 1024, 1024, 1024, 8, 1024, 3072])
O_GQ, O_GK, O_GV, O_GLR, O_GR, O_MX, O_MZ, O_FQ, O_FK, O_FV, O_FF, O_FOG, O_GATES = [int(v) for v in SPL[:13]]


def prep_AB_inputs(p, l, inp, T):
    w_in = inp["w_in"][l]
    own = np.arange(512 * p, 512 * p + 512)
    oth = np.arange(512 * (1 - p), 512 * (1 - p) + 512)
    perm = np.concatenate([own, oth])
    r256 = np.arange(256 * p, 256 * p + 256)
    cols = np.concatenate([O_GQ + r256, O_GK + r256, O_MX + perm, O_FQ + own, O_FK + own, O_FOG + own, O_GLR + np.arange(16),
                           O_GK + r256, O_GV + own, O_GR + own, O_MZ + own, O_FV + own, O_FF + np.arange(4 * p, 4 * p + 4)])
    assert cols.size == NA
    d = {}
    d["wA"] = np.ascontiguousarray(w_in[:, cols])
    d["gmix"] = colvec(inp["norm_mix"][l])
    d["foxbf_t"] = np.ascontiguousarray(np.broadcast_to(inp["fox_b_f"][l][4 * p:4 * p + 4], (128, T // 128, 4))).astype(np.float32)
    d["wlr_aug"] = np.ascontiguousarray(np.concatenate([inp["gla_w_lr"][l][:, r256], inp["gla_b_lr"][l][r256][None]], 0)).astype(np.float32)
    d["gla_gnb"] = bcast(inp["gla_norm"][l][own])
    P, _ = prep_mlstm(p, inp["mlstm_conv_w"][l], inp["mlstm_conv_b"][l], inp["mlstm_wq"][l], inp["mlstm_wk"][l], inp["mlstm_wv"][l],
                      inp["mlstm_w_i"][l], inp["mlstm_b_i"][l], inp["mlstm_w_f"][l], inp["mlstm_b_f"][l], inp["mlstm_skip"][l], inp["mlstm_norm"][l])
    for k, v in P.items():
        d["m_" + k] = v
    return d


def prep_C_inputs(l, inp, final):
    d = {}
    d["Wg"] = np.ascontiguousarray(inp["w_in"][l][:, O_GATES:O_GATES + 3072])
    d["bg"] = colvec(inp["b_gate"][l].reshape(-1))
    d["Wb"] = np.ascontiguousarray(inp["w_branch"][l])
    d["Wout"] = np.ascontiguousarray(inp["w_out"][l])
    d["gmix"] = colvec(inp["norm_mix"][l])
    d["Wup"] = np.ascontiguousarray(inp["ffn_w_up"][l])
    d["fcw"] = np.ascontiguousarray(inp["ffn_conv_w"][l].reshape(3, 44, 128).transpose(2, 1, 0))
    d["fcb"] = colvec(inp["ffn_conv_b"][l])
    d["Wdown"] = np.ascontiguousarray(inp["ffn_w_down"][l])
    d["gffn"] = colvec(inp["norm_ffn"][l])
    if final:
        d["gfin"] = colvec(inp["norm_final"])
    return d


_NC_CACHE = {}


def _get_nc(key, fn):
    if key not in _NC_CACHE:
        _NC_CACHE[key] = fn()
    return _NC_CACHE[key]


def kernel_unfused(inp):
    x = inp["x"]
    B, T, _ = x.shape
    TO = T // 2
    ncore = 2 * B
    HS = [4, 2]
    xT_full = [np.ascontiguousarray(x[b].T) for b in range(B)]
    def own_slice(arr, p, H):
        if p == 0:
            return np.ascontiguousarray(np.concatenate([np.zeros((arr.shape[0], H), arr.dtype), arr[:, :TO]], axis=1))
        return np.ascontiguousarray(arr[:, TO - H:])
    xTo = [own_slice(xT_full[c // 2], c % 2, HS[0]) for c in range(ncore)]
    out = None
    for l in range(2):
        H = HS[l]
        final = (l == 1)
        nc_ab = _get_nc(("AB", T), lambda: build_AB(T))
        in_maps = []
        for c in range(ncore):
            d = prep_AB_inputs(c % 2, l, inp, T)
            d["xT"] = xT_full[c // 2]
            in_maps.append(d)
        res = run_bass_kernel_spmd(nc_ab, in_maps, core_ids=list(range(ncore)))
        yTs = [np.asarray(r["yT"]) for r in res.results]
        nc_c = _get_nc(("C", H, TO, final), lambda: build_C(H, TO, final))
        cw = prep_C_inputs(l, inp, final)
        in_maps = []
        for c in range(ncore):
            b, p = c // 2, c % 2
            d = dict(cw)
            d["xTo"] = xTo[c]
            d["yTall"] = np.ascontiguousarray(np.concatenate([own_slice(yTs[2 * b + rr], p, H) for rr in range(2)], axis=0))
            d["hmask"] = np.ascontiguousarray(np.broadcast_to(np.array([1.0 - p, float(p)], np.float32), (128, 2)))
            in_maps.append(d)
        res = run_bass_kernel_spmd(nc_c, in_maps, core_ids=list(range(ncore)))
        if not final:
            x2 = [np.asarray(r["x2T"]) for r in res.results]
            xT_full = [np.ascontiguousarray(np.concatenate([x2[2 * b][:, H:], x2[2 * b + 1][:, H:]], axis=1)) for b in range(B)]
            xTo = [np.ascontiguousarray(x2[c][:, H - HS[1]:]) for c in range(ncore)]
        else:
            o = [np.asarray(r["outT"]) for r in res.results]
            out = np.stack([np.concatenate([o[2 * b], o[2 * b + 1]], axis=1).T for b in range(B)]).astype(np.float32)
    return np.ascontiguousarray(out)


AB_IN = [("wA", [D, NA]), ("gmix", [128, 8]), ("foxbf_t", None), ("wlr_aug", [17, 256]), ("gla_gnb", [128, 512])] + [("m_" + k, v) for k, v in ML_SH.items()]
C_IN = [("Wg", [D, 3072]), ("bg", [128, 24]), ("Wb", [3, D, D]), ("Wout", [D, D]), ("Wup", [D, 5632]), ("fcw", [128, 44, 3]), ("fcb", [128, 44]),
        ("Wdown", [2816, D]), ("gffn", [128, 8])]


def build_fused(T, ncore):
    TO = T // 2
    HS = [4, 2]
    NBK = T // 512
    half = NBK // 2
    groups = [[i, i + 1] for i in range(0, ncore, 2)]
    nc = bass.Bass("TRN2", target_bir_lowering=False)
    dt = lambda n, s, d, k: nc.dram_tensor(n, s, d, kind=k).ap()
    xT = dt("xT", [D, T], F32, "ExternalInput")
    xTo = dt("xTo", [D, HS[0] + TO], F32, "ExternalInput")
    hm = dt("hmask", [128, 2], F32, "ExternalInput")
    gfin = dt("gfin", [128, 8], F32, "ExternalInput")
    outT = dt("outT", [D, TO], F32, "ExternalOutput")
    W = []
    for l in range(2):
        d = {}
        for n, shp in AB_IN:
            d[n] = dt(f"{n}_l{l}", shp if shp is not None else [128, T // 128, 4], F32, "ExternalInput")
        for n, shp in C_IN:
            d[n] = dt(f"{n}_l{l}", shp, F32, "ExternalInput")
        W.append(d)
    pFM = dt("pFM", [NFM, T], BF16, "Internal")
    pTM = dt("pTM", [T, NTM], BF16, "Internal")
    ys = [[dt(f"ys_{l}_{s_}", [1536, HS[l] + TO], BF16, "Internal") for s_ in range(2)] for l in range(2)]
    yg = [[dt(f"yg_{l}_{s_}", [12, 2, 128, HS[l] + TO], BF16, "Internal") for s_ in range(2)] for l in range(2)]
    x1T = [dt(f"x1T_{l}", [D, HS[l] + TO], F32, "Internal") for l in range(2)]
    x2T0 = dt("x2T_0", [D, HS[0] + TO], F32, "Internal")
    NBX = TO // 512
    xsend = dt("xsend", [NBX, D, 512], F32, "Internal")
    sgT = dt("sgT", [3072, HS[0] + TO], BF16, "Internal")
    actD = dt("actD", [2816, HS[0] + TO], BF16, "Internal")
    yselD = dt("yselD", [3072, HS[0] + TO], BF16, "Internal")
    xg = dt("xg", [NBX, 2, D, 512], F32, "Internal")
    with ExitStack() as st:
        S = Sched(nc, st)
        for l in range(2):
            H = HS[l]
            w = W[l]
            if l == 0:
                phase_A(nc, S, T, xT, w["wA"], w["gmix"], pFM, pTM)
            else:
                xblk = lambda j: xg[j % half, j // half].rearrange("(c p) t -> p c t", p=128)
                phase_A(nc, S, T, None, w["wA"], w["gmix"], pFM, pTM, xblk=xblk, xr=lambda j: [("xg", j % half)])
            yd = YDst(T, ys=ys[l], H=H, key=("y", l))
            P = {k: w["m_" + k] for k in ML_SH}
            def gather_rows(i0, i1, row_lo, row_hi):
                toks = [t for t in yd.tokens if t[1] == "zero" or (isinstance(t[1], int) and row_lo <= t[1] < row_hi)]
                for s_ in range(2):
                    for i in range(i0, i1):
                        S.collective("AllGather", [ys[l][s_][128 * i:128 * (i + 1), :]], [yg[l][s_][i].rearrange("r p w -> (r p) w")], groups,
                                     r=toks, w=[("yg", l, s_, i)])

            phase_gla(nc, S, T, pFM, pTM, w["wlr_aug"], w["gla_gnb"], yd, zero_halo=True)
            gather_rows(0, 4, 0, 512)
            phase_mlstm(nc, S, T, pFM, pTM, P, yd)
            gather_rows(4, 8, 512, 1024)
            phase_fox(nc, S, T, pFM, pTM, w["foxbf_t"], yd,
                      after_head=lambda h: gather_rows(8 + h, 9 + h, 1024 + 128 * h, 1024 + 128 * (h + 1)))
            xin = xTo if l == 0 else x2T0[:, HS[0] - HS[1]:]
            xr = [] if l == 0 else [("x2T", c0) for (c0, nb) in blocks_of(HS[0], TO, 256)]
            W_ = H + TO
            sg_l = sgT[:, 0:W_]
            act_l = actD[:, 0:W_]
            blk = blocks_of(HS[0], TO, 512)
            xr2 = [] if l == 0 else [("x2T", c0) for (c0, nb) in blk]
            ysel_l = yselD[:, 0:W_]
            phase_C1a(nc, S, H, TO, xin, hm, w["Wg"], w["bg"], w["gmix"], sg_l, xr=xr2,
                      ysrc=[y_.rearrange("k r p w -> p r k w") for y_ in yg[l]], ysel=ysel_l, yr=[("yg", l, s_, i) for s_ in range(2) for i in range(12)])
            phase_C1b(nc, S, H, TO, xin, ysel_l, hm, sg_l, w["Wb"], w["Wout"], x1T[l], xr=xr2)
            phase_C2a(nc, S, H, TO, x1T[l], w["Wup"], w["fcw"], w["fcb"], w["gffn"], act_l)
            if l == 0:
                def gather_x(kb):
                    S.collective("AllGather", [xsend[kb]], [xg[kb].rearrange("r d t -> (r d) t")], groups, r=[("xsend", kb)], w=[("xg", kb)])

                phase_C2b(nc, S, H, TO, x1T[l], act_l, w["Wdown"], x2T0, xsend=xsend, after_block=gather_x)
            else:
                phase_C2b(nc, S, H, TO, x1T[l], act_l, w["Wdown"], x1T[l], gfin, outT)
        S.finish()
        S.emit()
    return nc


def kernel_fused(inp):
    x = inp["x"]
    B, T, _ = x.shape
    TO = T // 2
    ncore = 2 * B
    nc = _get_nc(("F", T, ncore), lambda: build_fused(T, ncore))
    in_maps = []
    per_rank = {}
    for p in range(2):
        d = {}
        for l in range(2):
            for k, v in prep_AB_inputs(p, l, inp, T).items():
                d[f"{k}_l{l}"] = v
            for k, v in prep_C_inputs(l, inp, False).items():
                if k != "gmix":
                    d[f"{k}_l{l}"] = v
        d["gfin"] = colvec(inp["norm_final"])
        d["hmask"] = np.ascontiguousarray(np.broadcast_to(np.array([1.0 - p, float(p)], np.float32), (128, 2)))
        per_rank[p] = d
    for c in range(ncore):
        b, p = c // 2, c % 2
        d = dict(per_rank[p])
        xTb = np.ascontiguousarray(x[b].T)
        d["xT"] = xTb
        if p == 0:
            d["xTo"] = np.ascontiguousarray(np.concatenate([np.zeros((D, 4), np.float32), xTb[:, :TO]], axis=1))
        else:
            d["xTo"] = np.ascontiguousarray(xTb[:, TO - 4:])
        in_maps.append(d)
    res = run_bass_kernel_spmd(nc, in_maps, core_ids=list(range(ncore)))
    o = [np.asarray(r["outT"]) for r in res.results]
    out = np.stack([np.concatenate([o[2 * b], o[2 * b + 1]], axis=1).T for b in range(B)]).astype(np.float32)
    return np.ascontiguousarray(out)


FUSED = True


def kernel(**inputs):
    inp = {k: np.asarray(v, dtype=np.float32) for k, v in inputs.items()}
    if FUSED:
        return kernel_fused(inp)
    return kernel_unfused(inp)


def phase_C1a(nc, S, H, TO, xTo, hmask_d, Wg, bg_d, gmix, sgT, xr=(), ysrc=None, ysel=None, yr=()):
    NB = 512
    with ExitStack() as st:
        A = Tiles(nc, st)
        S.barrier()
        T_ = {}
        wg = A.sb([128, 8, 3072], BF16, "wg")
        stg = [A.sb([128, 2048], F32, "stg") for _ in range(2)]
        xb = [A.sb([128, 8, NB], F32, "xb") for _ in range(2)]
        hT = A.sb([128, 8, NB], BF16, "hT")
        T_["xsq"] = A.sb([128, 8, NB], BF16, "xsq")
        T_["std"] = A.sb([128, NB], F32, "std")
        T_["rstd"] = A.sb([128, NB], F32, "rstd")
        T_["epsc"] = A.sb([128, 1], F32, "epsc")
        ones_bf = A.sb([128, 128], BF16, "ones")
        gcol = A.sb([128, 8], F32, "gcol")
        bg = A.sb([128, 24], F32, "bg")
        hmask = A.sb([128, 2], F32, "hmask")
        sgo = [A.sb([128, NB], BF16, "sgo") for _ in range(4)]
        if ysrc is not None:
            ya = A.sb([128, 24, NB], BF16, "ya")
            yb_ = A.sb([128, 24, NB], BF16, "yb_")
            ysv = ysel.rearrange("(c p) t -> p c t", p=128)
        ps_ss = A.ps([128, 512], F32, "ps_ss")
        psg = [A.ps([128, 512], F32, "psg") for _ in range(4)]
        S.pool(lambda e: e.memset(ones_bf[:, :], 1.0), w=["ones"])
        S.pool(lambda e: e.memset(T_["epsc"][:, :], EPS), w=["epsc"])
        S.dma(gcol[:, :], gmix, w=["gcol"])
        S.dma(bg[:, :], bg_d, w=["bg"])
        S.dma(hmask[:, :], hmask_d, w=["hmask"])
        load_w_generic(S, lambda c0, n: wg[:, :, c0:c0 + n], Wg.rearrange("(c p) n -> p c n", p=128), 8, 3072, 256, stg, "stg", "wg")
        xv = xTo.rearrange("(c p) t -> p c t", p=128)
        kk = 0
        blks = blocks_of(H, TO, NB)

        def load_x(bi):
            c0_, nb_ = blks[bi]
            S.dma(xb[bi % 2][:, :, :nb_], xv[:, :, c0_:c0_ + nb_], r=list(xr), w=[("xb", bi % 2)])

        load_x(0)
        for bi, (c0, nb) in enumerate(blks):
            xt = xb[bi % 2]
            xtag = ("xb", bi % 2)
            if bi + 1 < len(blks):
                load_x(bi + 1)
            if ysrc is not None:
                for r_ in range(2):
                    S.dma(ya[:, 12 * r_:12 * (r_ + 1), :nb], ysrc[0][:, r_, :, c0:c0 + nb], r=list(yr), w=["ya"])
                    S.dma(yb_[:, 12 * r_:12 * (r_ + 1), :nb], ysrc[1][:, r_, :, c0:c0 + nb], r=list(yr), w=["yb_"])
                S.dve(lambda e, nb=nb: e.tensor_scalar(out=ya[:, :, :nb], in0=ya[:, :, :nb], scalar1=hmask[:, 0:1], scalar2=None, op0=ALU.mult),
                      r=["ya", "hmask"], w=["ya"])
                S.dve(lambda e, nb=nb: e.scalar_tensor_tensor(out=ya[:, :, :nb], in0=yb_[:, :, :nb], scalar=hmask[:, 1:2], in1=ya[:, :, :nb], op0=ALU.mult, op1=ALU.add),
                      r=["ya", "yb_", "hmask"], w=["ya"])
                S.dma(ysv[:, :, c0:c0 + nb], ya[:, :, :nb], r=["ya"], w=[("ysel", c0)])
            if bi == 0 and H > 0:
                S.dve(lambda e, xt=xt, nb=nb: e.tensor_scalar(out=xt[:, :, :nb], in0=xt[:, :, :nb], scalar1=hmask[:, 1:2], scalar2=None, op0=ALU.mult),
                      r=[xtag, "hmask"], w=[xtag])
            rms_block(S, T_, ones_bf, xt, gcol, hT, ps_ss, xtag, nb)
            htag = ("hT", id(hT))
            for n_ in range(3):
                for dc in range(8):
                    pg = psg[kk % 4]
                    pgt = ("psg", kk % 4)
                    so = sgo[kk % 4]
                    sot = ("sgo", kk % 4)
                    kk += 1
                    for c in range(8):
                        S.pe(lambda e, c=c, pg=pg, n_=n_, dc=dc, nb=nb: e.matmul(pg[:, :nb], lhsT=wg[:, c, n_ * 1024 + dc * 128:n_ * 1024 + (dc + 1) * 128], rhs=hT[:, c, :nb],
                                                                               start=(c == 0), stop=(c == 7)), r=["wg", htag], w=[pgt])
                    S.act(lambda e, pg=pg, so=so, n_=n_, dc=dc, nb=nb: e.activation(out=so[:, :nb], in_=pg[:, :nb], func=AF.Sigmoid, bias=bg[:, n_ * 8 + dc:n_ * 8 + dc + 1]),
                          r=[pgt, "bg"], w=[sot])
                    k_ = n_ * 8 + dc
                    S.dma(sgT[k_ * 128:(k_ + 1) * 128, c0:c0 + nb], so[:, :nb], r=[sot], w=[("sgT", c0)])
        S.emit()


def phase_C1b(nc, S, H, TO, xTo, ysel, hmask_d, sgT, Wb, Wout, x1T, xr=()):
    NB = 512
    with ExitStack() as st:
        A = Tiles(nc, st)
        S.barrier()
        wb = A.sb([128, 3, 8, 1024], BF16, "wb")
        wo = A.sb([128, 8, 1024], BF16, "wo")
        stg = [A.sb([128, 2048], F32, "stg") for _ in range(2)]
        xb = A.sb([128, 8, NB], F32, "xb")
        yb = [A.sb([128, 24, NB], BF16, "yb") for _ in range(2)]
        sgr = [A.sb([128, 3, NB], BF16, "sgr") for _ in range(4)]
        hmask = A.sb([128, 2], F32, "hmask")
        mg = [A.sb([128, NB], F32, "mg") for _ in range(2)]
        mgT = A.sb([128, 8, NB], BF16, "mgT")
        tmp = [A.sb([128, NB], F32, "tmp") for _ in range(2)]
        psb = [A.ps([128, 512], F32, "psb") for _ in range(4)]
        pso = [A.ps([128, 512], F32, "pso") for _ in range(2)]
        S.dma(hmask[:, :], hmask_d, w=["hmask"])
        k = 0
        for n_ in range(3):
            k = load_w_generic(S, lambda c0, n, n_=n_: wb[:, n_, :, c0:c0 + n], Wb[n_].rearrange("(c p) n -> p c n", p=128), 8, 1024, 256, stg, "stg", "wb", k)
        k = load_w_generic(S, lambda c0, n: wo[:, :, c0:c0 + n], Wout.rearrange("(c p) n -> p c n", p=128), 8, 1024, 256, stg, "stg", "wo", k)
        xv = xTo.rearrange("(c p) t -> p c t", p=128)
        yv = ysel.rearrange("(c p) t -> p c t", p=128)
        sv = sgT.rearrange("(n d p) t -> p d n t", n=3, p=128)
        ov = x1T.rearrange("(c p) t -> p c t", p=128)
        blks = blocks_of(H, TO, NB)

        def load_y(bi):
            c0_, nb_ = blks[bi]
            S.dma(yb[bi % 2][:, :, :nb_], yv[:, :, c0_:c0_ + nb_], r=[("ysel", c0_)], w=[("yb", bi % 2)])

        kk = 0
        ks = 0
        load_y(0)
        for bi, (c0, nb) in enumerate(blks):
            yt = yb[bi % 2]
            ytag = ("yb", bi % 2)
            if bi + 1 < len(blks):
                load_y(bi + 1)
            S.dma(xb[:, :, :nb], xv[:, :, c0:c0 + nb], r=list(xr), w=["xb"])
            if bi == 0 and H > 0:
                S.dve(lambda e, nb=nb: e.tensor_scalar(out=xb[:, :, :nb], in0=xb[:, :, :nb], scalar1=hmask[:, 1:2], scalar2=None, op0=ALU.mult),
                      r=["xb", "hmask"], w=["xb"])
            for dc in range(8):
                sg = sgr[ks % 4]
                sgtag = ("sgr", ks % 4)
                ks += 1
                S.dma(sg[:, :, :nb], sv[:, dc, :, c0:c0 + nb], r=[("sgT", c0)], w=[sgtag])
                mgd = mg[dc % 2]
                mgtag = ("mg", dc % 2)
                for n_ in range(3):
                    pbk = psb[kk % 4]
                    pbt = ("psb", kk % 4)
                    tm = tmp[kk % 2]
                    tmtag = ("tmp", kk % 2)
                    kk += 1
                    for c in range(8):
                        ych = (c // 4) * 12 + n_ * 4 + (c % 4)
                        S.pe(lambda e, c=c, pbk=pbk, n_=n_, dc=dc, nb=nb, ych=ych, yt=yt: e.matmul(pbk[:, :nb], lhsT=wb[:, n_, c, dc * 128:(dc + 1) * 128], rhs=yt[:, ych, :nb],
                                                                                                start=(c == 0), stop=(c == 7)), r=["wb", ytag], w=[pbt])
                    if n_ == 0:
                        S.dve(lambda e, pbk=pbk, nb=nb, sg=sg, mgd=mgd: e.tensor_tensor(out=mgd[:, :nb], in0=pbk[:, :nb], in1=sg[:, 0, :nb], op=ALU.mult),
                              r=[pbt, sgtag], w=[mgtag])
                    else:
                        S.dve(lambda e, pbk=pbk, tm=tm, nb=nb, sg=sg, n_=n_: e.tensor_tensor(out=tm[:, :nb], in0=pbk[:, :nb], in1=sg[:, n_, :nb], op=ALU.mult),
                              r=[pbt, sgtag], w=[tmtag])
                        if n_ == 1:
                            S.pool(lambda e, tm=tm, nb=nb, mgd=mgd: e.tensor_tensor(out=mgd[:, :nb], in0=mgd[:, :nb], in1=tm[:, :nb], op=ALU.add),
                                   r=[mgtag, tmtag], w=[mgtag])
                        else:
                            S.pool(lambda e, tm=tm, dc=dc, nb=nb, mgd=mgd: e.tensor_tensor(out=mgT[:, dc, :nb], in0=mgd[:, :nb], in1=tm[:, :nb], op=ALU.add),
                                   r=[mgtag, tmtag], w=[("mgT", dc)])
            for dc in range(8):
                po = pso[dc % 2]
                pot = ("pso", dc % 2)
                for c in range(8):
                    S.pe(lambda e, c=c, po=po, dc=dc, nb=nb: e.matmul(po[:, :nb], lhsT=wo[:, c, dc * 128:(dc + 1) * 128], rhs=mgT[:, c, :nb], start=(c == 0), stop=(c == 7)),
                         r=["wo"] + [("mgT", c_) for c_ in range(8)], w=[pot])
                S.dve(lambda e, po=po, dc=dc, nb=nb: e.tensor_tensor(out=xb[:, dc, :nb], in0=po[:, :nb], in1=xb[:, dc, :nb], op=ALU.add), r=[pot, "xb"], w=["xb"])
            S.dma(ov[:, :, c0:c0 + nb], xb[:, :, :nb], r=["xb"], w=[("x1T", c0)])
        S.emit()


def phase_C2a(nc, S, H, TO, x1T, Wup, cw_d, cb_d, gffn, actD):
    NB = 512
    with ExitStack() as st:
        A = Tiles(nc, st)
        S.barrier()
        T_ = {}
        wu = A.sb([128, 8, 5632], BF16, "wu")
        stg = [A.sb([128, 2048], F32, "stg") for _ in range(2)]
        xb = [A.sb([128, 8, NB], F32, "xb") for _ in range(2)]
        hT = A.sb([128, 8, NB], BF16, "hT")
        T_["xsq"] = A.sb([128, 8, NB], BF16, "xsq")
        T_["std"] = A.sb([128, NB], F32, "std")
        T_["rstd"] = A.sb([128, NB], F32, "rstd")
        T_["epsc"] = A.sb([128, 1], F32, "epsc")
        ones_bf = A.sb([128, 128], BF16, "ones")
        gcol = A.sb([128, 8], F32, "gcol")
        cw = A.sb([128, 44, 3], F32, "cw")
        cb = A.sb([128, 44], F32, "cb")
        halo = A.sb([128, 44, 2], F32, "halo")
        ua = [A.sb([128, NB + 2], F32, "ua") for _ in range(2)]
        ug = [A.sb([128, NB + 2], F32, "ug") for _ in range(2)]
        aa = [A.sb([128, NB], F32, "aa") for _ in range(2)]
        ag = [A.sb([128, NB], F32, "ag") for _ in range(2)]
        acto = [A.sb([128, NB], BF16, "acto") for _ in range(4)]
        ps_ss = A.ps([128, 512], F32, "ps_ss")
        psa = [A.ps([128, 512], F32, "psa") for _ in range(3)]
        psgt = [A.ps([128, 512], F32, "psgt") for _ in range(3)]
        S.pool(lambda e: e.memset(ones_bf[:, :], 1.0), w=["ones"])
        S.pool(lambda e: e.memset(T_["epsc"][:, :], EPS), w=["epsc"])
        S.dve(lambda e: e.memset(halo[:, :, :], 0.0), w=["halo"])
        S.dma(gcol[:, :], gffn, w=["gcol"])
        S.dma(cw[:, :, :], cw_d, w=["cw"])
        S.dma(cb[:, :], cb_d, w=["cb"])
        load_w_generic(S, lambda c0, n: wu[:, :, c0:c0 + n], Wup.rearrange("(c p) n -> p c n", p=128), 8, 5632, 256, stg, "stg", "wu")
        xv = x1T.rearrange("(c p) t -> p c t", p=128)
        kk = 0
        blks = blocks_of(H, TO, NB)

        def load_x(bi):
            c0_, nb_ = blks[bi]
            S.dma(xb[bi % 2][:, :, :nb_], xv[:, :, c0_:c0_ + nb_], r=[("x1T", c0_)], w=[("xb", bi % 2)])

        load_x(0)
        for bi, (c0, nb) in enumerate(blks):
            xt = xb[bi % 2]
            xtag = ("xb", bi % 2)
            if bi + 1 < len(blks):
                load_x(bi + 1)
            rms_block(S, T_, ones_bf, xt, gcol, hT, ps_ss, xtag, nb)
            htag = ("hT", id(hT))
            for fc in range(22):
                b2 = kk % 2
                b3 = kk % 3
                b4 = kk % 4
                kk += 1
                for (ps, pst, off, ut, utag, acc, atag, hc) in ((psa[b3], ("psa", b3), 0, ua[b2], ("ua", b2), aa[b2], ("aa", b2), fc),
                                                                (psgt[b3], ("psgt", b3), 2816, ug[b2], ("ug", b2), ag[b2], ("ag", b2), 22 + fc)):
                    for c in range(8):
                        S.pe(lambda e, c=c, ps=ps, off=off, fc=fc, nb=nb: e.matmul(ps[:, :nb], lhsT=wu[:, c, off + fc * 128:off + (fc + 1) * 128], rhs=hT[:, c, :nb],
                                                                                 start=(c == 0), stop=(c == 7)), r=["wu", htag], w=[pst])
                    S.act(lambda e, ut=ut, hc=hc: e.copy(out=ut[:, 0:2], in_=halo[:, hc, :]), r=[("halo", hc)], w=[utag])
                    S.act(lambda e, ut=ut, ps=ps, nb=nb: e.copy(out=ut[:, 2:2 + nb], in_=ps[:, :nb]), r=[pst], w=[utag])
                    S.act(lambda e, ut=ut, hc=hc, nb=nb: e.copy(out=halo[:, hc, :], in_=ut[:, nb:nb + 2]), r=[utag], w=[("halo", hc)])
                    S.dve(lambda e, ut=ut, acc=acc, hc=hc, nb=nb: e.tensor_scalar(out=acc[:, :nb], in0=ut[:, 0:nb], scalar1=cw[:, hc, 0:1], scalar2=cb[:, hc:hc + 1],
                                                                                op0=ALU.mult, op1=ALU.add), r=[utag, "cw", "cb"], w=[atag])
                    for j in (1, 2):
                        S.dve(lambda e, ut=ut, acc=acc, hc=hc, nb=nb, j=j: e.scalar_tensor_tensor(out=acc[:, :nb], in0=ut[:, j:j + nb], scalar=cw[:, hc, j:j + 1], in1=acc[:, :nb],
                                                                                                  op0=ALU.mult, op1=ALU.add), r=[utag, "cw", atag], w=[atag])
                S.act(lambda e, b2=b2, nb=nb: e.activation(out=ag[b2][:, :nb], in_=ag[b2][:, :nb], func=AF.Silu), r=[("ag", b2)], w=[("ag", b2)])
                S.pool(lambda e, b2=b2, b4=b4, nb=nb: e.tensor_tensor(out=acto[b4][:, :nb], in0=aa[b2][:, :nb], in1=ag[b2][:, :nb], op=ALU.mult),
                       r=[("aa", b2), ("ag", b2)], w=[("acto", b4)])
                S.dma(actD[fc * 128:(fc + 1) * 128, c0:c0 + nb], acto[b4][:, :nb], r=[("acto", b4)], w=[("actD", c0)])
        S.emit()


def phase_C2b(nc, S, H, TO, x1T, actD, Wdown, x2T, final_g=None, outT=None, xsend=None, after_block=None):
    NB = 512
    with ExitStack() as st:
        A = Tiles(nc, st)
        S.barrier()
        T_ = {}
        wd = A.sb([128, 22, 1024], BF16, "wd")
        stg = [A.sb([128, 2048], F32, "stg") for _ in range(2)]
        xb = [A.sb([128, 8, NB], F32, "xb") for _ in range(2)]
        actb = [A.sb([128, 22, NB], BF16, "actb") for _ in range(2)]
        T_["xsq"] = A.sb([128, 8, NB], BF16, "xsq")
        T_["std"] = A.sb([128, NB], F32, "std")
        T_["rstd"] = A.sb([128, NB], F32, "rstd")
        T_["epsc"] = A.sb([128, 1], F32, "epsc")
        ones_bf = A.sb([128, 128], BF16, "ones")
        gfin = A.sb([128, 8], F32, "gfin")
        oT = A.sb([128, 8, NB], F32, "oT")
        ps_ss = A.ps([128, 512], F32, "ps_ss")
        psd = [A.ps([128, 512], F32, "psd") for _ in range(4)]
        S.pool(lambda e: e.memset(ones_bf[:, :], 1.0), w=["ones"])
        S.pool(lambda e: e.memset(T_["epsc"][:, :], EPS), w=["epsc"])
        if final_g is not None:
            S.dma(gfin[:, :], final_g, w=["gfin"])
        load_w_generic(S, lambda c0, n: wd[:, :, c0:c0 + n], Wdown.rearrange("(c p) n -> p c n", p=128), 22, 1024, 64, stg, "stg", "wd")
        xv = x1T.rearrange("(c p) t -> p c t", p=128)
        av = actD.rearrange("(c p) t -> p c t", p=128)
        ov = x2T.rearrange("(c p) t -> p c t", p=128)
        kk = 0
        blks = blocks_of(H, TO, NB)

        def load_xa(bi):
            c0_, nb_ = blks[bi]
            S.dma(xb[bi % 2][:, :, :nb_], xv[:, :, c0_:c0_ + nb_], r=[("x1T", c0_)], w=[("xb", bi % 2)])
            S.dma(actb[bi % 2][:, :, :nb_], av[:, :, c0_:c0_ + nb_], r=[("actD", c0_)], w=[("actb", bi % 2)])

        load_xa(0)
        for bi, (c0, nb) in enumerate(blks):
            xt = xb[bi % 2]
            xtag = ("xb", bi % 2)
            at = actb[bi % 2]
            attag = ("actb", bi % 2)
            if bi + 1 < len(blks):
                load_xa(bi + 1)
            for dc in range(8):
                po = psd[kk % 4]
                pot = ("psd", kk % 4)
                kk += 1
                for fc in range(22):
                    S.pe(lambda e, fc=fc, po=po, dc=dc, nb=nb, at=at: e.matmul(po[:, :nb], lhsT=wd[:, fc, dc * 128:(dc + 1) * 128], rhs=at[:, fc, :nb], start=(fc == 0), stop=(fc == 21)),
                         r=["wd", attag], w=[pot])
                S.dve(lambda e, po=po, dc=dc, nb=nb, xt=xt: e.tensor_tensor(out=xt[:, dc, :nb], in0=po[:, :nb], in1=xt[:, dc, :nb], op=ALU.add), r=[pot, xtag], w=[xtag])
            if final_g is None:
                S.dma(ov[:, :, c0:c0 + nb], xt[:, :, :nb], r=[xtag], w=[("x2T", c0)])
                if xsend is not None and not (bi == 0 and H > 0):
                    kb = (c0 - H) // NB
                    S.dma(xsend[kb].rearrange("(c p) t -> p c t", p=128)[:, :, :nb], xt[:, :, :nb], r=[xtag], w=[("xsend", kb)])
                    if after_block is not None:
                        after_block(kb)
            elif not (bi == 0 and H > 0):
                xsq, std, rstd = T_["xsq"], T_["std"], T_["rstd"]
                S.act(lambda e, xt=xt, nb=nb: e.activation(out=xsq[:, :, :nb], in_=xt[:, :, :nb], func=AF.Square), r=[xtag], w=["xsq"])
                for c in range(8):
                    S.pe(lambda e, c=c, nb=nb: e.matmul(ps_ss[:, :nb], lhsT=ones_bf[:, :], rhs=xsq[:, c, :nb], start=(c == 0), stop=(c == 7)), r=["xsq", "ones"], w=["ps_ss"])
                S.act(lambda e, nb=nb: e.activation(out=std[:, :nb], in_=ps_ss[:, :nb], func=AF.Sqrt, bias=T_["epsc"][:, 0:1], scale=1.0 / D), r=["ps_ss", "epsc"], w=["std"])
                S.dve(lambda e, nb=nb: e.reciprocal(out=rstd[:, :nb], in_=std[:, :nb]), r=["std"], w=["rstd"])
                for c in range(8):
                    S.dve(lambda e, c=c, nb=nb, xt=xt: e.scalar_tensor_tensor(out=oT[:, c, :nb], in0=xt[:, c, :nb], scalar=gfin[:, c:c + 1], in1=rstd[:, :nb], op0=ALU.mult, op1=ALU.mult),
                          r=[xtag, "rstd", "gfin"], w=["oT"])
                S.dma(outT.rearrange("(c p) t -> p c t", p=128)[:, :, c0 - H:c0 - H + nb], oT[:, :, :nb], r=["oT"], w=[("outT", c0)])
        S.emit()
```
